# Optimizing a Trainium2 kernel written in Bass

```python
import jax, jax.numpy as jnp
from jax import lax
import numpy as np

D_MODEL = 1024
BATCH = 8
SEQ = 8192
DEPTH = 4
DEC_BATCH = 8
DEC_SEQ = 32
PAST_LEN = 1024

CHUNK = 64
D_RWKV = D_MODEL
RWKV_HEAD_DIM = 64
N_RWKV_HEADS = D_RWKV // RWKV_HEAD_DIM
DECAY_LORA = 64
ICLR_LORA = 64
GATE_LORA = 128
RWKV_COLS = 3 * D_RWKV + DECAY_LORA + ICLR_LORA + GATE_LORA
RWKV_SPLITS = (D_RWKV, 2 * D_RWKV, 3 * D_RWKV, 3 * D_RWKV + DECAY_LORA, 3 * D_RWKV + DECAY_LORA + ICLR_LORA)
GMLP_WIDTH = D_MODEL
GMLP_CHUNK = 128
GMLP_GROUP_CH = 128
GMLP_GROUPS = GMLP_WIDTH // GMLP_GROUP_CH
N_MEM = 256
X_HEADS = 4
X_HEAD_DIM = D_MODEL // X_HEADS
D_X = X_HEADS * X_HEAD_DIM
N_BRANCH = 3
BRANCH_WIDTH = D_MODEL
COL_GMLP = RWKV_COLS
COL_Q = COL_GMLP + 2 * GMLP_WIDTH
COL_GATE = COL_Q + D_X
IN_COLS = COL_GATE + N_BRANCH * D_MODEL
D_FF = 4 * D_MODEL
RMS_EPS = 1e-6
LN_EPS = 1e-5
GN_EPS = 64e-5

kernel_name = 'hybrid_rwkv7_sgu_memxattn_stream_step'


def _rmsnorm(x, g):
    xf = x.astype(jnp.float32)
    y = xf * lax.rsqrt(jnp.mean(xf * xf, axis=-1, keepdims=True) + RMS_EPS)
    return (y * g.astype(jnp.float32)).astype(x.dtype)


def _layernorm(x, g, b, eps):
    xf = x.astype(jnp.float32)
    xc = xf - jnp.mean(xf, axis=-1, keepdims=True)
    var = jnp.mean(xc * xc, axis=-1, keepdims=True)
    y = xc * lax.rsqrt(var + eps) * g.astype(jnp.float32) + b.astype(jnp.float32)
    return y.astype(x.dtype)


def _rwkv7_recurrence(r, decay, k, v, a_vec, b_vec, s0):
    def step(s, inp):
        r_t, w_t, k_t, v_t, a_t, b_t = inp
        sa = jnp.einsum('bhvk,bhk->bhv', s, a_t)
        s = s * w_t[:, :, None, :] + sa[..., None] * b_t[:, :, None, :] + v_t[..., None] * k_t[:, :, None, :]
        y = jnp.einsum('bhvk,bhk->bhv', s, r_t)
        return s, y
    xs = tuple(jnp.swapaxes(t, 0, 1) for t in (r, decay, k, v, a_vec, b_vec))
    s_final, ys = lax.scan(step, s0, xs)
    return jnp.swapaxes(ys, 0, 1), s_final


def _rwkv7_mixer(cols, prev_row, wkv0, mu, w0, w2, a0, a2, g2, k_k, k_a, r_k, lnx_g, lnx_b):
    B, T, _ = cols.shape
    dt = cols.dtype
    f32 = jnp.float32
    H, N = N_RWKV_HEADS, RWKV_HEAD_DIM
    shifted = jnp.concatenate([prev_row.astype(dt), cols[:, :-1]], axis=1)
    mixed = cols + (shifted - cols) * mu.astype(dt)
    r, k, v, wl, al, gl = jnp.split(mixed, RWKV_SPLITS, axis=-1)
    z = (w0 + jnp.tanh(wl) @ w2).astype(f32)
    decay = jnp.exp(-jnp.exp(-jax.nn.softplus(-z) - 0.5))
    a = jax.nn.sigmoid((a0 + al @ a2).astype(f32))
    g = jax.nn.sigmoid(gl) @ g2
    heads = lambda t: t.astype(f32).reshape(B, T, H, N)
    r_h, k_h, v_h, a_h, decay_h = heads(r), heads(k), heads(v), heads(a), heads(decay)
    kk = k_h * k_k.astype(f32).reshape(H, N)
    kk = kk * lax.rsqrt(jnp.maximum(jnp.sum(kk * kk, axis=-1, keepdims=True), 1e-24))
    k_h = k_h * (1.0 + (a_h - 1.0) * k_a.astype(f32).reshape(H, N))
    y, wkv = _rwkv7_recurrence(r_h, decay_h, k_h, v_h, -kk, kk * a_h, wkv0.astype(f32))
    y = _layernorm(y, lnx_g.reshape(H, N), lnx_b.reshape(H, N), GN_EPS)
    y = y + jnp.sum(r_h * k_h * r_k.astype(f32), axis=-1, keepdims=True) * v_h
    y = y.reshape(B, T, D_RWKV).astype(dt) * g
    return y, cols[:, -1:], wkv


def _sgu_mixer(cols, ln_g, ln_b, w_s, b_s):
    B, T, _ = cols.shape
    z = jax.nn.gelu(cols)
    u, v = jnp.split(z, 2, axis=-1)
    v = _layernorm(v, ln_g, ln_b, LN_EPS)
    pad = (-T) % GMLP_CHUNK
    n_chunks = (T + pad) // GMLP_CHUNK
    vc = jnp.pad(v, ((0, 0), (0, pad), (0, 0))).reshape(B, n_chunks, GMLP_CHUNK, GMLP_GROUPS, GMLP_GROUP_CH)
    mask = jnp.tril(jnp.ones((GMLP_CHUNK, GMLP_CHUNK), dtype=bool))
    ws = jnp.where(mask, w_s, jnp.zeros((), w_s.dtype)).astype(v.dtype)
    sv = jnp.einsum('gij,bnjgc->bnigc', ws, vc) + b_s.T.astype(v.dtype)[None, None, :, :, None]
    sv = sv.reshape(B, n_chunks * GMLP_CHUNK, GMLP_WIDTH)[:, :T]
    return u * sv, v


def _memory_kv(mem, g, w_kv):
    B = mem.shape[0]
    kv = _rmsnorm(mem, g) @ w_kv
    k, v = jnp.split(kv, 2, axis=-1)
    return k.reshape(B, N_MEM, X_HEADS, X_HEAD_DIM), v.reshape(B, N_MEM, X_HEADS, X_HEAD_DIM)


def _memory_attention(q_cols, mem_k, mem_v):
    B, T, _ = q_cols.shape
    q = q_cols.reshape(B, T, X_HEADS, X_HEAD_DIM)
    s = jnp.einsum('bthd,bmhd->bhtm', q, mem_k.astype(q.dtype)).astype(jnp.float32) * (X_HEAD_DIM ** -0.5)
    p = jax.nn.softmax(s, axis=-1).astype(q.dtype)
    o = jnp.einsum('bhtm,bmhd->bthd', p, mem_v.astype(q.dtype))
    return o.reshape(B, T, D_X)


def _layer(x, mem_k, mem_v, prev_row, wkv0, lp):
    h = _rmsnorm(x, lp['norm_mix_g'])
    w_in = lp['w_in']
    y_a, new_row, wkv = _rwkv7_mixer(h @ w_in[:, :COL_GMLP], prev_row, wkv0, lp['rwkv_mu'], lp['rwkv_w0'],
                                     lp['rwkv_w2'], lp['rwkv_a0'], lp['rwkv_a2'], lp['rwkv_g2'], lp['rwkv_k_k'],
                                     lp['rwkv_k_a'], lp['rwkv_r_k'], lp['rwkv_lnx_g'], lp['rwkv_lnx_b'])
    y_b, v_rows = _sgu_mixer(h @ w_in[:, COL_GMLP:COL_Q], lp['sgu_ln_g'], lp['sgu_ln_b'], lp['sgu_w_s'], lp['sgu_b_s'])
    y_c = _memory_attention(h @ w_in[:, COL_Q:COL_GATE], mem_k, mem_v)
    gate = jax.nn.sigmoid((h @ w_in[:, COL_GATE:]).astype(jnp.float32)).astype(x.dtype)
    w_b = lp['w_branch']
    merged = (gate[..., :D_MODEL] * (y_a @ w_b[0])
              + gate[..., D_MODEL:2 * D_MODEL] * (y_b @ w_b[1])
              + gate[..., 2 * D_MODEL:] * (y_c @ w_b[2]))
    x = x + merged @ lp['w_out']
    up = _rmsnorm(x, lp['norm_ffn_g']) @ lp['w_ffn_up']
    x = x + jnp.square(jax.nn.relu(up)) @ lp['w_ffn_down']
    return x, new_row, wkv, v_rows


def setup_inputs(seed: int = 0) -> dict:
    key = jax.random.key(seed)
    ks = iter(jax.random.split(key, 40))
    f32 = jnp.float32
    nrm = lambda shape, scale: scale * jax.random.normal(next(ks), shape, f32)
    uni = lambda shape: jax.random.uniform(next(ks), shape, f32)
    H, N = N_RWKV_HEADS, RWKV_HEAD_DIM
    return {
        'x_prompt': nrm((BATCH, SEQ, D_MODEL), 1.0),
        'x_sample': nrm((DEC_BATCH, DEC_SEQ, D_MODEL), 1.0),
        'cache_mem_k': nrm((DEPTH, DEC_BATCH, N_MEM, X_HEADS, X_HEAD_DIM), 1.0),
        'cache_mem_v': nrm((DEPTH, DEC_BATCH, N_MEM, X_HEADS, X_HEAD_DIM), 1.0),
        'state_wkv': nrm((DEPTH, DEC_BATCH, H, N, N), 0.3),
        'state_shift': nrm((DEPTH, DEC_BATCH, 1, RWKV_COLS), 1.0),
        'mem_prompt': nrm((BATCH, N_MEM, D_MODEL), 1.0),
        'norm_mix_g': 1.0 + nrm((DEPTH, D_MODEL), 0.01),
        'norm_mem_g': 1.0 + nrm((DEPTH, D_MODEL), 0.01),
        'norm_ffn_g': 1.0 + nrm((DEPTH, D_MODEL), 0.01),
        'norm_final_g': 1.0 + nrm((D_MODEL,), 0.01),
        'w_in': nrm((DEPTH, D_MODEL, IN_COLS), D_MODEL ** -0.5),
        'w_mem_kv': nrm((DEPTH, D_MODEL, 2 * D_X), D_MODEL ** -0.5),
        'rwkv_mu': uni((DEPTH, RWKV_COLS)),
        'rwkv_w0': -7.0 + 5.0 * uni((DEPTH, D_RWKV)),
        'rwkv_w2': nrm((DEPTH, DECAY_LORA, D_RWKV), 0.1),
        'rwkv_a0': nrm((DEPTH, D_RWKV), 0.1),
        'rwkv_a2': nrm((DEPTH, ICLR_LORA, D_RWKV), 0.1),
        'rwkv_g2': nrm((DEPTH, GATE_LORA, D_RWKV), GATE_LORA ** -0.5),
        'rwkv_k_k': 0.85 + nrm((DEPTH, D_RWKV), 0.05),
        'rwkv_k_a': 1.0 + nrm((DEPTH, D_RWKV), 0.05),
        'rwkv_r_k': nrm((DEPTH, H, N), 0.1),
        'rwkv_lnx_g': 1.0 + nrm((DEPTH, D_RWKV), 0.01),
        'rwkv_lnx_b': nrm((DEPTH, D_RWKV), 0.01),
        'sgu_ln_g': 1.0 + nrm((DEPTH, GMLP_WIDTH), 0.01),
        'sgu_ln_b': nrm((DEPTH, GMLP_WIDTH), 0.01),
        'sgu_w_s': nrm((DEPTH, GMLP_GROUPS, GMLP_CHUNK, GMLP_CHUNK), GMLP_CHUNK ** -0.5),
        'sgu_b_s': 1.0 + nrm((DEPTH, GMLP_GROUPS, GMLP_CHUNK), 0.01),
        'w_branch': nrm((DEPTH, N_BRANCH, BRANCH_WIDTH, D_MODEL), BRANCH_WIDTH ** -0.5),
        'w_out': nrm((DEPTH, D_MODEL, D_MODEL), D_MODEL ** -0.5),
        'w_ffn_up': nrm((DEPTH, D_MODEL, D_FF), D_MODEL ** -0.5),
        'w_ffn_down': nrm((DEPTH, D_FF, D_MODEL), D_FF ** -0.5),
    }


def reference(x_prompt, x_sample, cache_mem_k, cache_mem_v, state_wkv, state_shift, mem_prompt,
              norm_mix_g, norm_mem_g, norm_ffn_g, norm_final_g, w_in, w_mem_kv,
              rwkv_mu, rwkv_w0, rwkv_w2, rwkv_a0, rwkv_a2, rwkv_g2, rwkv_k_k, rwkv_k_a, rwkv_r_k,
              rwkv_lnx_g, rwkv_lnx_b, sgu_ln_g, sgu_ln_b, sgu_w_s, sgu_b_s,
              w_branch, w_out, w_ffn_up, w_ffn_down):
    b_p = x_prompt.shape[0]
    H, N = N_RWKV_HEADS, RWKV_HEAD_DIM
    prompt_row0 = jnp.zeros((b_p, 1, RWKV_COLS), x_prompt.dtype)
    prompt_wkv0 = jnp.zeros((b_p, H, N, N), jnp.float32)
    xp, xs = x_prompt, x_sample
    mk_p, mv_p, wkv_p, row_p, wkv_s, row_s, v_s = [], [], [], [], [], [], []
    for l in range(DEPTH):
        lp = {
            'norm_mix_g': norm_mix_g[l], 'w_in': w_in[l],
            'rwkv_mu': rwkv_mu[l], 'rwkv_w0': rwkv_w0[l], 'rwkv_w2': rwkv_w2[l],
            'rwkv_a0': rwkv_a0[l], 'rwkv_a2': rwkv_a2[l], 'rwkv_g2': rwkv_g2[l],
            'rwkv_k_k': rwkv_k_k[l], 'rwkv_k_a': rwkv_k_a[l], 'rwkv_r_k': rwkv_r_k[l],
            'rwkv_lnx_g': rwkv_lnx_g[l], 'rwkv_lnx_b': rwkv_lnx_b[l],
            'sgu_ln_g': sgu_ln_g[l], 'sgu_ln_b': sgu_ln_b[l], 'sgu_w_s': sgu_w_s[l], 'sgu_b_s': sgu_b_s[l],
            'w_branch': w_branch[l], 'w_out': w_out[l],
            'norm_ffn_g': norm_ffn_g[l], 'w_ffn_up': w_ffn_up[l], 'w_ffn_down': w_ffn_down[l],
        }
        mem_k, mem_v = _memory_kv(mem_prompt, norm_mem_g[l], w_mem_kv[l])
        xp, r_p, s_p, _ = _layer(xp, mem_k, mem_v, prompt_row0, prompt_wkv0, lp)
        mk_p.append(mem_k)
        mv_p.append(mem_v)
        wkv_p.append(s_p.astype(state_wkv.dtype))
        row_p.append(r_p)
        xs, r_s, s_s, vr = _layer(xs, cache_mem_k[l], cache_mem_v[l], state_shift[l], state_wkv[l], lp)
        wkv_s.append(s_s.astype(state_wkv.dtype))
        row_s.append(r_s)
        v_s.append(vr)
    y_prompt = _rmsnorm(xp, norm_final_g)
    y_sample = _rmsnorm(xs, norm_final_g)
    new_cache_mem_k_prompt = jnp.stack(mk_p)
    new_cache_mem_v_prompt = jnp.stack(mv_p)
    new_state_wkv_prompt = jnp.stack(wkv_p)
    new_state_shift_prompt = jnp.stack(row_p)
    new_state_wkv_sample = jnp.stack(wkv_s)
    new_state_shift_sample = jnp.stack(row_s)
    new_sgu_v_sample = jnp.stack(v_s)
    return (y_prompt, y_sample, new_cache_mem_k_prompt, new_cache_mem_v_prompt, new_state_wkv_prompt,
            new_state_shift_prompt, new_state_wkv_sample, new_state_shift_sample, new_sgu_v_sample)
```

```python
import contextlib
import numpy as np
import concourse.bass as bass
import concourse.mybir as mybir
from concourse.bass_utils import run_bass_kernel_spmd

F32 = mybir.dt.float32
BF16 = mybir.dt.bfloat16
AF = mybir.ActivationFunctionType
ALU = mybir.AluOpType
AX = mybir.AxisListType

ENGS = ('pe', 'act', 'dve', 'pool', 'sp')


class Buf:
    __slots__ = ('w', 'r', 'id')
    _n = [0]

    def __init__(self):
        self.w = None
        self.r = {}
        Buf._n[0] += 1
        self.id = Buf._n[0]


class Rec:
    def __init__(self, nc):
        self.nc = nc
        self.streams = {e: [] for e in ENGS}
        self.cnt = {e: 0 for e in ENGS}
        self.dcnt = {}
        self.waited = {e: {} for e in ENGS}
        self.pending = {e: [] for e in ENGS}

    def _wait(self, eng, tok):
        if tok is None:
            return
        key, val = tok
        if self.waited[eng].get(key, 0) >= val:
            return
        self.waited[eng][key] = val
        self.streams[eng].append(('w', key, val))

    def _deps(self, eng, reads, writes, extra):
        for t in extra:
            self._wait(eng, t)
        for b in reads:
            self._wait(eng, b.w)
        for b in writes:
            self._wait(eng, b.w)
            for t in b.r.values():
                self._wait(eng, t)
            for e2 in ENGS:
                if e2 != eng and any(pb_ is b for pb_ in self.pending[e2]):
                    raise RuntimeError("write to buffer %d with pending unmarked reads on %s" % (b.id, e2))

    def op(self, eng, fn, reads=(), writes=(), mark=True, extra=(), wdeps=()):
        self._deps(eng, reads, tuple(writes) + tuple(wdeps), extra)
        if mark:
            self.cnt[eng] += 1
            tok = (eng, self.cnt[eng])
            self.streams[eng].append(('i', fn, True))
            for b in self.pending[eng]:
                b.r[eng] = tok
            self.pending[eng] = []
            for b in reads:
                b.r[eng] = tok
            for b in writes:
                b.w = tok
                b.r = {}
            return tok
        assert not writes
        self.streams[eng].append(('i', fn, False))
        self.pending[eng].extend(reads)
        return None

    def dma(self, q, sem, fn, reads=(), writes=(), extra=()):
        self._deps(q, reads, writes, extra)
        self.dcnt[sem] = self.dcnt.get(sem, 0) + 16
        tok = (sem, self.dcnt[sem])
        self.streams[q].append(('d', fn, sem))
        for b in reads:
            b.r[sem] = tok
        for b in writes:
            b.w = tok
            b.r = {}
        return tok

    def emit(self, stack):
        nc = self.nc
        sems = {}
        for e in ENGS:
            sems[e] = stack.enter_context(nc.semaphore('s_' + e))
        for d in self.dcnt:
            sems[d] = stack.enter_context(nc.semaphore('d_' + d))
        block = stack.enter_context(nc.Block())
        dec = {'pe': block.tensor, 'act': block.scalar, 'dve': block.vector, 'pool': block.gpsimd, 'sp': block.sync}
        for e in ENGS:
            stream = self.streams[e]
            own = sems[e]

            def body(engine, stream=stream, own=own):
                for item in stream:
                    if item[0] == 'w':
                        engine.wait_ge(sems[item[1]], item[2])
                    elif item[0] == 'i':
                        ins = item[1](engine)
                        if item[2]:
                            ins.then_inc(own, 1)
                    else:
                        item[1](engine).then_inc(sems[item[2]], 16)
            dec[e](body)


D = 1024
NCH = 8
RW = 3328
IN_COLS = 9472
NP_ = 49
PI_LORA, PI_RKV, PI_SGU, PI_Q, PI_GATE, PI_BR, PI_OUT, PI_UP, PI_DN, PI_MKV = 0, 1, 9, 13, 15, 21, 27, 29, 37, 45
C0 = -0.6065306597126334
V_MIXG, V_FFNG, V_MEMG, V_MU, V_OMM, V_W0, V_A0, V_KK, V_KA, V_OMKA, V_RK, V_SLG, V_SLB, V_LXG, V_LXB = (
    0, 8, 16, 24, 50, 76, 84, 92, 100, 108, 116, 124, 132, 140, 148)
VW = 156


class _Stop(Exception):
    pass


def build(SEQ, DEPTH, MEMN=256, TS=32, DBG=False, STOP=None):
    nc = bass.Bass("TRN2", target_bir_lowering=False)
    L = DEPTH
    T = 512
    NT = SEQ // T
    din = lambda name, shape, dt=F32: nc.dram_tensor(name, shape, dt, kind="ExternalInput").ap()
    dout = lambda name, shape, dt=F32: nc.dram_tensor(name, shape, dt, kind="ExternalOutput").ap()
    dscr = lambda name, shape, dt: nc.dram_tensor(name, shape, dt, kind="Internal").ap()
    xp = din("xp", [SEQ, D]); xs = din("xs", [TS, D])
    ck = din("ck", [L, MEMN, D]); cv = din("cv", [L, MEMN, D])
    swkv = din("swkv", [L, 16, 64, 64]); sshift = din("sshift", [L, RW]); memp = din("memp", [MEMN, D])
    vecs = din("vecs", [128, L, VW]); fing = din("fing", [128, 8])
    w_in = din("w_in", [L, D, IN_COLS]); w_mkv = din("w_mkv", [L, D, 2 * D])
    w2 = din("w2", [L, 64, D]); a2 = din("a2", [L, 64, D]); g2 = din("g2", [L, 128, D])
    w_s = din("w_s", [L, 8, 128, 128]); b_s = din("b_s", [L, 8 * 128])
    w_br = din("w_br", [L, 3, D, D]); w_out = din("w_out", [L, D, D])
    w_up = din("w_up", [L, D, 4 * D]); w_dn = din("w_dn", [L, 4 * D, D])
    yp = dout("yp", [SEQ, D]); ys = dout("ys", [TS, D])
    omk = dout("omk", [L, MEMN, D]); omv = dout("omv", [L, MEMN, D])
    owkv_p = dout("owkv_p", [L, 16, 64, 64]); oshift_p = dout("oshift_p", [L, RW])
    owkv_s = dout("owkv_s", [L, 16, 64, 64]); oshift_s = dout("oshift_s", [L, RW])
    osgu = dout("osgu", [L, TS, D])
    if DBG:
        dbg_y = dout("dbg_y", [128, 24, 512], BF16); dbg_x = dout("dbg_x", [128, 8, 512]); dbg_x2 = dout("dbg_x2", [128, 8, 512]); dbg_m = dout("dbg_m", [128, 8, 512], BF16)
    wsc = dscr("wsc", [L, NP_, 128, 4096], BF16)
    kvsc = dscr("kvsc", [L, 2, 128, 4096], BF16)
    smsc = dscr("smsc", [L, 128, 3072], BF16)

    rec = Rec(nc)
    st = contextlib.ExitStack()
    with st:
        def sb(name, shape, dt):
            return st.enter_context(nc.sbuf_tensor(name, shape, dt))

        def ps(name, shape, dt):
            return st.enter_context(nc.psum_tensor(name, shape, dt))

        def mm(out, lhsT, rhs, start, stop, reads, writes=None):
            rec.op('pe', lambda e: e.matmul(out, lhsT=lhsT, rhs=rhs, start=start, stop=stop),
                   reads=reads, writes=writes or (), mark=writes is not None, wdeps=() if writes is not None else psum_bufs[out.name])

        def tr(out, in_, ident, reads, writes=None):
            rec.op('pe', lambda e: e.transpose(out=out, in_=in_, identity=ident),
                   reads=reads, writes=writes or (), mark=writes is not None, wdeps=() if writes is not None else psum_bufs[out.name])

        def act(out, in_, func, reads, writes, scale=1.0, bias=None):
            if bias is None:
                rec.op('act', lambda e: e.activation(out=out, in_=in_, func=func, scale=scale), reads=reads, writes=writes)
            else:
                rec.op('act', lambda e: e.activation(out=out, in_=in_, func=func, scale=scale, bias=bias), reads=reads, writes=writes)

        def tt(eng, out, in0, in1, op, reads, writes):
            rec.op(eng, lambda e: e.tensor_tensor(out=out, in0=in0, in1=in1, op=op), reads=reads, writes=writes)

        def ts(eng, out, in0, s1, s2, op0, op1, reads, writes):
            if op1 is None:
                rec.op(eng, lambda e: e.tensor_scalar(out=out, in0=in0, scalar1=s1, scalar2=None, op0=op0), reads=reads, writes=writes)
            else:
                rec.op(eng, lambda e: e.tensor_scalar(out=out, in0=in0, scalar1=s1, scalar2=s2, op0=op0, op1=op1), reads=reads, writes=writes)

        def stt(out, in0, scalar, in1, op0, op1, reads, writes):
            rec.op('dve', lambda e: e.scalar_tensor_tensor(out=out, in0=in0, scalar=scalar, in1=in1, op0=op0, op1=op1), reads=reads, writes=writes)

        def cp(eng, out, in_, reads, writes):
            if eng == 'act':
                act(out, in_, AF.Copy, reads, writes)
            else:
                rec.op(eng, lambda e: e.tensor_copy(out=out, in_=in_), reads=reads, writes=writes)

        def recip(out, in_, reads, writes):
            rec.op('dve', lambda e: e.reciprocal(out=out, in_=in_), reads=reads, writes=writes)

        def memset(eng, ap, val, writes):
            rec.op(eng, lambda e: e.memset(ap, val), writes=writes)

        def dma(q, sem, out, in_, reads=(), writes=(), slow=False):
            if sem == 'W':
                sem = 'b%d' % writes[0].id
            elif sem == 'R':
                sem = 'b%d' % reads[0].id
            if slow:
                return rec.dma(q, sem, lambda e: e.dma_start(out=out, in_=in_, allow_slow_non_contiguous=True), reads=reads, writes=writes)
            return rec.dma(q, sem, lambda e: e.dma_start(out=out, in_=in_), reads=reads, writes=writes)

        x_t = sb("x_t", [128, 8, T], F32); Bx = [Buf() for _ in range(8)]
        h_t = sb("h_t", [128, 8, T], BF16); Bh = [Buf() for _ in range(8)]
        NS = 3
        ws_t = [sb(f"ws{i}", [128, 4096], BF16) for i in range(NS)]; Bws = [Buf() for _ in range(NS)]
        big = sb("big", [128, 16, T], F32); Bbig = [Buf() for _ in range(16)]
        bfp = sb("bfp", [128, 32, T], BF16); Bbfp = [Buf() for _ in range(32)]
        NTF = 12
        tf_t = sb("tf_t", [128, NTF, T + 1], F32); Btf = [Buf() for _ in range(NTF)]
        NTB = 10
        tb_t = sb("tb_t", [128, NTB, T], BF16); Btb = [Buf() for _ in range(NTB)]
        tfi = [0]; tbi = [0]

        def tmpf():
            i = tfi[0] % NTF; tfi[0] += 1
            return tf_t[:, i, :], Btf[i]

        def tmpb():
            i = tbi[0] % NTB; tbi[0] += 1
            return tb_t[:, i, :], Btb[i]

        S32 = sb("S32", [128, L, 8, 64], F32); BS32 = [[Buf() for _ in range(8)] for _ in range(L)]
        Sbd = sb("Sbd", [128, 8, 128], BF16); BSbd = [Buf() for _ in range(8)]
        prev = sb("prev", [128, L, 26], F32); Bprev = [Buf() for _ in range(L)]
        vec_t = sb("vec_t", [128, L, VW], F32); Bvec = Buf()
        fing_t = sb("fing_t", [128, 8], F32); Bfing = Buf()
        lw = sb("lw", [128, T], BF16); Blw = Buf(); sgl = sb("sgl", [128, T], BF16); Bsgl = Buf()
        smallw = sb("smallw", [128, 3072], BF16); Bsmall = Buf()
        bsb = sb("bsb", [128, 1024], F32); Bbsb = Buf()
        identf = sb("identf", [128, 128], F32); identb = sb("identb", [128, 128], BF16)
        onesb = sb("onesb", [128, 128], BF16); blkb = sb("blkb", [128, 128], BF16)
        onesf = sb("onesf", [128, 128], F32)
        msu = sb("msu", [128, 128], F32); miu = sb("miu", [128, 128], F32); msl = sb("msl", [128, 128], F32)
        cmask = sb("cmask", [128, T], F32)
        epsc = sb("epsc", [128, 4], F32)
        Bconst = Buf()
        vT = sb("vT", [128, 4, 128], BF16); kT = sb("kT", [128, 4, 128], BF16); bT = sb("bT", [128, 4, 128], BF16)
        BvT, BkT, BbT = Buf(), Buf(), Buf()
        NQt = [sb(f"NQt{i}", [128, 2, 2, 128], BF16) for i in range(2)]; BNQ = [Buf(), Buf()]
        MTt = [sb(f"MTt{i}", [128, 2, 2, 128], BF16) for i in range(2)]; BMT = [Buf(), Buf()]
        Xbt = [sb(f"Xbt{i}", [128, 2, 2, 128], BF16) for i in range(2)]; BXb = [Buf(), Buf()]
        m2d = sb("m2d", [128, 7, 2, 128], BF16)
        AKt = [sb(f"AKt{i}", [128, 2, 128], BF16) for i in range(2)]; BAK = [Buf(), Buf()]
        RRt = [sb(f"RRt{i}", [128, 4, 128], BF16) for i in range(2)]; BRR = [Buf(), Buf()]
        Zb = [sb(f"Zb{i}", [128, 128], BF16) for i in range(2)]; BZb = [Buf(), Buf()]
        Ub = [sb(f"Ub{i}", [128, 128], BF16) for i in range(2)]; BUb = [Buf(), Buf()]
        Swt = [sb(f"Swt{i}", [128, 64], F32) for i in range(2)]; BSw = [Buf(), Buf()]
        wc_t = sb("wc_t", [128, 4], F32); Bwc = Buf()
        yTM = sb("yTM", [128, 4, 128], F32); ByTM = Buf()
        ysq = sb("ysq", [128, 4, 128], F32); Bysq = Buf()
        ynb = sb("ynb", [128, 4, 128], BF16); Bynb = Buf()
        gst = sb("gst", [128, 4, 8], F32); Bgst = Buf()

        NB = 7
        pb = [ps(f"pb{i}", [128, 512], F32) for i in range(NB)]; Bpb = [Buf() for _ in range(NB)]
        pT = ps("pT", [128, 1024], BF16); _bpt = Buf(); BpT = [_bpt, _bpt]
        psum_bufs = {f"pb{i}": [Bpb[i]] for i in range(NB)}
        psum_bufs["pT"] = [_bpt]
        bki = [0]

        def bank():
            i = bki[0] % NB; bki[0] += 1
            return pb[i], Bpb[i]

        pti = [0]

        def ptbank():
            i = pti[0] % 2; pti[0] += 1
            return pT[:, i * 512:(i + 1) * 512], BpT[i]

        def checkpoint(name):
            if STOP == name:
                raise _Stop()

        def finish():
            for k_, v_ in list(rec.dcnt.items()):
                rec._wait('sp', (k_, v_))
            rec.emit(st)

        try:
            memset('pool', onesf[:], 1.0, [Bconst])
            memset('pool', identf[:], 0.0, [Bconst])
            rec.op('pool', lambda e: e.affine_select(out=identf[:], in_=identf[:], pattern=[[-1, 128]], compare_op=ALU.not_equal, fill=1.0, base=0, channel_multiplier=1), reads=[Bconst], writes=[Bconst])
            cp('pool', identb[:], identf[:], [Bconst], [Bconst])
            cp('pool', onesb[:], onesf[:], [Bconst], [Bconst])
            memset('pool', blkb[:], 0.0, [Bconst])
            memset('pool', blkb[0:64, 0:64], 1.0, [Bconst])
            memset('pool', blkb[64:128, 64:128], 1.0, [Bconst])
            rec.op('pool', lambda e: e.affine_select(out=msu[:], in_=onesf[:], pattern=[[1, 128]], compare_op=ALU.is_gt, fill=0.0, base=0, channel_multiplier=-1), reads=[Bconst], writes=[Bconst])
            rec.op('pool', lambda e: e.affine_select(out=miu[:], in_=onesf[:], pattern=[[1, 128]], compare_op=ALU.is_ge, fill=0.0, base=0, channel_multiplier=-1), reads=[Bconst], writes=[Bconst])
            rec.op('pool', lambda e: e.affine_select(out=msl[:], in_=onesf[:], pattern=[[-1, 128]], compare_op=ALU.is_gt, fill=0.0, base=0, channel_multiplier=1), reads=[Bconst], writes=[Bconst])
            Ea, Eb, Ec = tf_t[:, 0, 0:128], tf_t[:, 1, 0:128], tf_t[:, 2, 0:128]
            BE_ = [Btf[0], Btf[1], Btf[2]]
            cp('pool', Ea, identf[:], [Bconst], [BE_[0]])
            Ecur, Bcur, Enxt, Bnxt = Ea, BE_[0], Eb, BE_[1]
            for j in range(7):
                s_ = 2 ** (j + 1); nb_ = 128 // s_
                if j == 6:
                    cp('pool', Enxt, onesf[:], [Bconst], [Bnxt])
                else:
                    def sel1(e, out=Enxt, s_=s_, nb_=nb_):
                        return e.affine_select(out=out.rearrange("p (a b) -> p a b", b=s_), in_=onesf[:].rearrange("p (a b) -> p a b", b=s_),
                                               pattern=[[-s_, nb_], [0, s_]], compare_op=ALU.is_ge, fill=0.0, base=0, channel_multiplier=1)

                    def sel2(e, out=Enxt, s_=s_, nb_=nb_):
                        return e.affine_select(out=out.rearrange("p (a b) -> p a b", b=s_), in_=out.rearrange("p (a b) -> p a b", b=s_),
                                               pattern=[[s_, nb_], [0, s_]], compare_op=ALU.is_ge, fill=0.0, base=s_ - 1, channel_multiplier=-1)
                    rec.op('pool', sel1, reads=[Bconst], writes=[Bnxt])
                    rec.op('pool', sel2, reads=[Bnxt], writes=[Bnxt])
                tt('pool', Ec, Enxt, Ecur, ALU.subtract, [Bnxt, Bcur], [BE_[2]])
                tt('pool', m2d[:, j, 0, :], Ec, msu[:], ALU.mult, [BE_[2], Bconst], [Bconst])
                tt('pool', m2d[:, j, 1, :], Ec, msl[:], ALU.mult, [BE_[2], Bconst], [Bconst])
                Ecur, Bcur, Enxt, Bnxt = Enxt, Bnxt, Ecur, Bcur
            memset('pool', epsc[:, 0:1], 1e-6, [Bconst])
            memset('pool', epsc[:, 1:2], 1e-5, [Bconst])
            memset('pool', epsc[:, 2:3], 64e-5, [Bconst])
            dma('pool', 'W', vec_t[:], vecs, writes=[Bvec])
            dma('pool', 'W', fing_t[:], fing, writes=[Bfing])
            ts('dve', vec_t[:, :, V_OMM:V_OMM + 26], vec_t[:, :, V_MU:V_MU + 26], -1.0, 1.0, ALU.mult, ALU.add, [Bvec], [Bvec])
            ts('dve', vec_t[:, :, V_OMKA:V_OMKA + 8], vec_t[:, :, V_KA:V_KA + 8], -1.0, 1.0, ALU.mult, ALU.add, [Bvec], [Bvec])

            def vcol(l, base, c):
                return vec_t[:, l, base + c:base + c + 1]

            checkpoint('const')
            Bwsc = Buf()
            Bsm = [[] for _ in range(L)]
            Bkv = [[[], []] for _ in range(L)]

            def conv(l, pi, src2d, ncols, col0=0):
                dst = wsc[l, pi].rearrange("p (kc n) -> p kc n", kc=8)[:, :, col0:col0 + ncols]
                dma('pool', 'cv', dst, src2d.rearrange("(kc p) n -> p kc n", p=128), writes=[Bwsc])

            for l in range(L):
                conv(l, PI_LORA, w_in[l][:, 3072:3328], 256)
                for c in range(8):
                    for j in range(3):
                        conv(l, PI_RKV + c, w_in[l][:, j * 1024 + c * 128: j * 1024 + (c + 1) * 128], 128, col0=j * 128)
                for i in range(4):
                    conv(l, PI_SGU + i, w_in[l][:, 3328 + i * 512: 3328 + (i + 1) * 512], 512)
                for i in range(2):
                    conv(l, PI_Q + i, w_in[l][:, 5376 + i * 512: 5376 + (i + 1) * 512], 512)
                for i in range(6):
                    conv(l, PI_GATE + i, w_in[l][:, 6400 + i * 512: 6400 + (i + 1) * 512], 512)
                for b in range(3):
                    for hf in range(2):
                        conv(l, PI_BR + b * 2 + hf, w_br[l, b][:, hf * 512:(hf + 1) * 512], 512)
                for hf in range(2):
                    conv(l, PI_OUT + hf, w_out[l][:, hf * 512:(hf + 1) * 512], 512)
                for i in range(8):
                    conv(l, PI_UP + i, w_up[l][:, i * 512:(i + 1) * 512], 512)
                for g in range(2):
                    for q in range(4):
                        conv(l, PI_DN + g * 4 + q, w_dn[l][q * 1024:(q + 1) * 1024, g * 512:(g + 1) * 512], 512)
                for i in range(4):
                    conv(l, PI_MKV + i, w_mkv[l][:, i * 512:(i + 1) * 512], 512)
                dma('pool', 'cv', smsc[l][0:64, 1024:2048], w2[l], writes=[Bwsc])
                dma('pool', 'cv', smsc[l][64:128, 1024:2048], a2[l], writes=[Bwsc])
                dma('pool', 'cv', smsc[l][:, 2048:3072], g2[l], writes=[Bwsc])
                dma('pool', 'cv', kvsc[l, 1][:, 2048:4096].rearrange("p (mb n) -> p mb n", mb=2),
                    cv[l].rearrange("(mb p) n -> p mb n", p=128), writes=[Bwsc])
            checkpoint('conv')
            for l in range(L):
                wsf, Bw_ = big[:, 0:2, :], [Bbig[0], Bbig[1]]
                dma('pool', 'W', big[:, 0:2, :].rearrange("p a (g j) -> p (a g) j", j=128), w_s[l].rearrange("g i j -> i g j"), writes=Bw_)
                for g in range(8):
                    if g % 4 == 0:
                        stg, Bstg = tmpb()
                    bk, Bbk = bank()
                    tr(bk[:, 0:128], big[:, g // 4, (g % 4) * 128:(g % 4 + 1) * 128], identf[:], reads=Bw_ + [Bconst], writes=[Bbk])
                    tt('dve', stg[:, (g % 4) * 128:(g % 4 + 1) * 128], bk[:, 0:128], miu[:], ALU.mult, [Bbk, Bconst], [Bstg])
                    if g % 4 == 3:
                        Bd = Buf(); Bsm[l].append(Bd)
                        dma('pool', 'R', smsc[l][:, (g // 4) * 512:(g // 4 + 1) * 512], stg, reads=[Bstg], writes=[Bd])
            for l in range(L):
                Bk_ = [Bbig[i] for i in range(4)]
                dma('pool', 'W', big[:, 0:4, :].rearrange("p (mb a) n -> p mb (a n)", mb=2), ck[l].rearrange("(mb p) n -> p mb n", p=128), writes=Bk_)
                for j in range(8):
                    if j % 2 == 0:
                        stg, Bstg = tmpb()
                    bk, Bbk = bank()
                    for mb in range(2):
                        tr(bk[:, mb * 128:(mb + 1) * 128], big[:, mb * 2 + j // 4, (j % 4) * 128:(j % 4 + 1) * 128], identf[:],
                           reads=Bk_ + [Bconst], writes=[Bbk] if mb == 1 else None)
                    cp('act', stg[:, (j % 2) * 256:(j % 2 + 1) * 256], bk[:, 0:256], [Bbk], [Bstg])
                    if j % 2 == 1:
                        Bd = Buf(); Bkv[l][1].append(Bd)
                        dma('pool', 'R', kvsc[l, 1][:, (j - 1) * 256:(j + 1) * 256], stg, reads=[Bstg], writes=[Bd])

            checkpoint('prep')
            uses = []
            state = {'k': 0, 'loaded': 0}

            def plan_tile_layer(l, grp):
                seq = [('w', l, PI_LORA)] + [('w', l, PI_RKV + c) for c in range(8)] + [('w', l, PI_SGU + i) for i in range(4)]
                seq += [('w', l, PI_Q), ('w', l, PI_Q + 1), ('kv', l, grp)]
                for b in range(3):
                    for hf in range(2):
                        seq += [('w', l, PI_GATE + b * 2 + hf), ('w', l, PI_BR + b * 2 + hf)]
                seq += [('w', l, PI_OUT), ('w', l, PI_OUT + 1)] + [('w', l, PI_UP + i) for i in range(8)]
                seq += [('w', l, PI_DN + i) for i in range(8)]
                return seq

            for l in range(L):
                uses += [('w', l, PI_MKV + i) for i in range(4)]
            for l in range(L):
                uses += plan_tile_layer(l, 1)
            for t_ in range(NT):
                for l in range(L):
                    uses += plan_tile_layer(l, 0)
            def _issue_load(k):
                kind, l, idx = uses[k]
                slot = k % NS
                if kind == 'w':
                    ncols = 256 if idx == PI_LORA else (384 if PI_RKV <= idx < PI_RKV + 8 else 512)
                    if ncols == 512:
                        dma('sp', f'w{slot}', ws_t[slot][:], wsc[l, idx], reads=[Bwsc], writes=[Bws[slot]])
                    else:
                        dma('sp', f'w{slot}', ws_t[slot][:].rearrange("p (kc n) -> p kc n", kc=8)[:, :, 0:ncols],
                            wsc[l, idx].rearrange("p (kc n) -> p kc n", kc=8)[:, :, 0:ncols], reads=[Bwsc], writes=[Bws[slot]])
                else:
                    dma('sp', f'w{slot}', ws_t[slot][:], kvsc[l, idx], reads=[Bwsc] + Bkv[l][idx], writes=[Bws[slot]])

            def piece(kind, l, idx):
                k = state['k']
                assert uses[k] == (kind, l, idx), (uses[k], kind, l, idx)
                while state['loaded'] < min(len(uses), k + NS - 1):
                    _issue_load(state['loaded']); state['loaded'] += 1
                state['k'] += 1
                slot = k % NS
                return ws_t[slot], Bws[slot]

            def wpiece(l, idx):
                t_, B_ = piece('w', l, idx)
                return t_[:].rearrange("p (kc n) -> p kc n", kc=8), B_

            def rmsnorm_to_h(l, gbase, TT):
                bk, Bbk = bank()
                for c in range(8):
                    sq, Bsq = tmpb()
                    act(sq[:, :TT], x_t[:, c, :TT], AF.Square, [Bx[c]], [Bsq])
                    mm(bk[:, :TT], onesb[:], sq[:, :TT], c == 0, c == 7, [Bsq, Bconst], [Bbk] if c == 7 else None)
                rs, Brs = tmpf()
                act(rs[:, :TT], bk[:, :TT], AF.Sqrt, [Bbk, Bconst], [Brs], scale=1.0 / D, bias=epsc[:, 0:1])
                recip(rs[:, :TT], rs[:, :TT], [Brs], [Brs])
                for c in range(8):
                    gcol = vcol(l, gbase, c) if l is not None else fing_t[:, c:c + 1]
                    yield_out = h_t[:, c, :TT]
                    stt(yield_out, x_t[:, c, :TT], gcol, rs[:, :TT], ALU.mult, ALU.mult, [Bx[c], Brs, Bvec], [Bh[c]])

            def big_mm(wt, Bw, oc4, rhs_list, Brhs, TT, out_bk=None, start=True, stop=True, kc0=0):
                if out_bk is None:
                    bk, Bbk = bank()
                else:
                    bk, Bbk = out_bk
                n = len(rhs_list)
                for kc in range(n):
                    last = (kc == n - 1)
                    mm(bk[:, :TT], wt[:, kc, oc4 * 128:(oc4 + 1) * 128], rhs_list[kc], start and kc == 0, stop and last,
                       [Bw, Brhs[kc]], [Bbk] if last else None)
                return bk, Bbk

            def hrhs(TT):
                return [h_t[:, kc, :TT] for kc in range(8)]

            def rwkv_pair(l, c, TT, C, NSTEP):
                NC_ = TT // C
                wt, Bw = wpiece(l, PI_RKV + c)
                w2a2 = smallw[:, 1024:2048]; g2t = smallw[:, 2048:3072]
                F = lambda i: (tf_t[:, i, :], Btf[i])
                Bq = lambda i: (tb_t[:, i, :], Btb[i])
                CB = [F(0), F(1)]
                (r_, Br), (k_, Bk), (v_, Bv) = F(2), F(3), F(4)
                (sig, Bsig), (a_, Ba), (kk, Bkk), (Lc, BL), (E, BE), (E2, BE2), (t1, Bt1) = F(5), F(6), F(7), F(8), F(9), F(10), F(11)
                (g_, Bg), (sq, Bsq), (rt, Brt), (kt, Bkt), (bt, Bbt), (at, Bat), (pr, Bpr), (vb, Bvb) = [Bq(i) for i in range(8)]
                mixed = [(r_, Br), (k_, Bk), (v_, Bv)]
                for j in range(3):
                    col = j * 8 + c
                    bk, Bbk = big_mm(wt, Bw, j, hrhs(TT), Bh, TT)
                    cb, Bcb = CB[j % 2]
                    mx, Bmx = mixed[j]
                    cp('act', cb[:, 1:TT + 1], bk[:, :TT], [Bbk], [Bcb])
                    cp('pool', cb[:, 0:1], prev[:, l, col:col + 1], [Bprev[l]], [Bcb])
                    ts('dve', t1[:, :TT], cb[:, 0:TT], vcol(l, V_MU, col), None, ALU.mult, None, [Bcb, Bvec], [Bt1])
                    stt(mx[:, :TT], cb[:, 1:TT + 1], vcol(l, V_OMM, col), t1[:, :TT], ALU.mult, ALU.add, [Bcb, Bt1, Bvec], [Bmx])
                    cp('pool', prev[:, l, col:col + 1], cb[:, TT:TT + 1], [Bcb], [Bprev[l]])
                bk, Bbk = bank()
                mm(bk[:, :TT], w2a2[0:64, c * 128:(c + 1) * 128], lw[0:64, :TT], True, True, [Bsmall, Blw], [Bbk])
                act(sig[:, :TT], bk[:, :TT], AF.Sigmoid, [Bbk, Bvec], [Bsig], bias=vcol(l, V_W0, c))
                bk, Bbk = bank()
                mm(bk[:, :TT], w2a2[64:128, c * 128:(c + 1) * 128], lw[64:128, :TT], True, True, [Bsmall, Blw], [Bbk])
                act(a_[:, :TT], bk[:, :TT], AF.Sigmoid, [Bbk, Bvec], [Ba], bias=vcol(l, V_A0, c))
                bk, Bbk = bank()
                mm(bk[:, :TT], g2t[:, c * 128:(c + 1) * 128], sgl[:, :TT], True, True, [Bsmall, Bsgl], [Bbk])
                cp('act', g_[:, :TT], bk[:, :TT], [Bbk], [Bg])
                ts('pool', kk[:, :TT], k_[:, :TT], vcol(l, V_KK, c), None, ALU.mult, None, [Bk, Bvec], [Bkk])
                act(sq[:, :TT], kk[:, :TT], AF.Square, [Bkk], [Bsq])
                bk, Bbk = bank()
                mm(bk[:, :TT], blkb[:], sq[:, :TT], True, True, [Bconst, Bsq], [Bbk])
                rn, Brn = E2, BE2
                act(rn[:, :TT], bk[:, :TT], AF.Sqrt, [Bbk], [Brn])
                ts('dve', rn[:, :TT], rn[:, :TT], 1e-12, None, ALU.max, None, [Brn], [Brn])
                recip(rn[:, :TT], rn[:, :TT], [Brn], [Brn])
                tt('pool', kk[:, :TT], kk[:, :TT], rn[:, :TT], ALU.mult, [Bkk, Brn], [Bkk])
                ts('dve', t1[:, :TT], a_[:, :TT], vcol(l, V_KA, c), vcol(l, V_OMKA, c), ALU.mult, ALU.add, [Ba, Bvec], [Bt1])
                tt('pool', k_[:, :TT], k_[:, :TT], t1[:, :TT], ALU.mult, [Bk, Bt1], [Bk])
                scan_mask = cmask[:, :TT] if C == 128 else onesf[:, :TT]
                rec.op('dve', lambda e: e.tensor_tensor_scan(out=Lc[:, :TT], data0=scan_mask, data1=sig[:, :TT],
                                                             initial=0.0, op0=ALU.mult, op1=ALU.add), reads=[Bsig, Bconst], writes=[BL])
                act(E[:, :TT], Lc[:, :TT], AF.Exp, [BL], [BE], scale=C0)
                tt('dve', rt[:, :TT], r_[:, :TT], E[:, :TT], ALU.mult, [Br, BE], [Brt])
                cp('pool', wc_t[:, 0:NC_], E[:, :TT].rearrange("p (a b) -> p a b", b=C)[:, :, C - 1], [BE], [Bwc])
                act(E2[:, :TT], Lc[:, :TT], AF.Exp, [BL], [BE2], scale=-C0)
                tt('dve', kt[:, :TT], k_[:, :TT], E2[:, :TT], ALU.mult, [Bk, BE2], [Bkt])
                tt('pool', t1[:, :TT], kk[:, :TT], a_[:, :TT], ALU.mult, [Bkk, Ba], [Bt1])
                tt('dve', bt[:, :TT], t1[:, :TT], E2[:, :TT], ALU.mult, [Bt1, BE2], [Bbt])
                tt('pool', Lc[:, :TT], Lc[:, :TT], sig[:, :TT], ALU.subtract, [BL, Bsig], [BL])
                act(E[:, :TT], Lc[:, :TT], AF.Exp, [BL], [BE], scale=C0)
                stt(at[:, :TT], kk[:, :TT], -1.0, E[:, :TT], ALU.mult, ALU.mult, [Bkk, BE], [Bat])
                stt(pr[:, :TT], r_[:, :TT], vcol(l, V_RK, c), k_[:, :TT], ALU.mult, ALU.mult, [Br, Bk, Bvec], [Bpr])
                bk, Bbk = bank()
                mm(bk[:, :TT], blkb[:], pr[:, :TT], True, True, [Bconst, Bpr], [Bbk])
                bv, Bbv = sig, Bsig
                tt('dve', bv[:, :TT], bk[:, :TT], v_[:, :TT], ALU.mult, [Bbk, Bv], [Bbv])
                cp('pool', vb[:, :TT], v_[:, :TT], [Bv], [Bvb])
                checkpoint('pelem')
                for (src, Bsrc, dst, Bdst) in ((vb, Bvb, vT, BvT), (kt, Bkt, kT, BkT), (bt, Bbt, bT, BbT)):
                    pt, Bpt = ptbank()
                    for ci in range(NC_):
                        tr(pt[0:C, ci * 128:(ci + 1) * 128], src[:, ci * C:(ci + 1) * C], identb[:], [Bsrc, Bconst], [Bpt] if ci == NC_ - 1 else None)
                    cp('act', dst[0:C, 0:NC_, :], pt[0:C, 0:NC_ * 128].rearrange("p (a b) -> p a b", b=128), [Bpt], [Bdst])
                checkpoint('ptm')
                memset('pool', Sbd[:, c, :], 0.0, [BSbd[c]])
                for hd in range(2):
                    cp('pool', Sbd[hd * 64:(hd + 1) * 64, c, hd * 64:(hd + 1) * 64], S32[hd * 64:(hd + 1) * 64, l, c, :], [BS32[l][c]], [BSbd[c]])
                checkpoint('psbd')
                for ci in range(NC_):
                    par = ci % 2
                    cs = slice(ci * C, (ci + 1) * C)
                    AK, RR = AKt[par], RRt[par]
                    hr = lambda hd: slice(hd * 64, (hd + 1) * 64)
                    NQ, MT, XB_ = NQt[par], MTt[par], Xbt[par]
                    for hd in range(2):
                        bkX, BX = bank()
                        X3 = bkX[0:C, :].rearrange("p (a b) -> p a b", b=128)
                        mm(X3[:, 0, 0:C], bt[hr(hd), cs], at[hr(hd), cs], True, True, [Bbt, Bat])
                        mm(X3[:, 1, 0:C], kt[hr(hd), cs], at[hr(hd), cs], True, True, [Bkt, Bat])
                        mm(X3[:, 2, 0:C], bt[hr(hd), cs], rt[hr(hd), cs], True, True, [Bbt, Brt])
                        mm(X3[:, 3, 0:C], kt[hr(hd), cs], rt[hr(hd), cs], True, True, [Bkt, Brt], [BX])
                        tt('dve', NQ[0:C, 0, hd, 0:C], X3[:, 0, 0:C], msu[0:C, 0:C], ALU.mult, [BX, Bconst], [BNQ[par]])
                        tt('dve', AK[0:C, hd, 0:C], X3[:, 1, 0:C], msu[0:C, 0:C], ALU.mult, [BX, Bconst], [BAK[par]])
                        tt('dve', RR[0:C, hd:4:2, 0:C], X3[:, 2:4, 0:C], miu[0:C, 0:C].unsqueeze(1).to_broadcast([C, 2, C]), ALU.mult, [BX, Bconst], [BRR[par]])
                        bkY, BY = bank()
                        mm(bkY[0:C, 0:C], at[hr(hd), cs], bt[hr(hd), cs], True, True, [Bbt, Bat], [BY])
                        tt('dve', NQ[0:C, 1, hd, 0:C], bkY[0:C, 0:C], msl[0:C, 0:C], ALU.mult, [BY, Bconst], [BNQ[par]])
                    cp('pool', MT[0:C].rearrange("p f h q -> p (f h) q")[:, :, 0:C], identb[0:C, 0:C].unsqueeze(1).to_broadcast([C, 4, C]), [Bconst], [BMT[par]])
                    checkpoint('patype')
                    NL = C.bit_length() - 1
                    for j in range(NL):
                        last = j == NL - 1
                        nf = 1 if last else 2
                        bkX, BX = bank()
                        Xv = bkX[0:C, :].rearrange("p (f h q) -> p f h q", f=2, h=2)
                        for hd in range(2):
                            mm(Xv[:, 0, hd, 0:C], NQ[0:C, 1, hd, 0:C], MT[0:C, 0, hd, 0:C], True, True, [BNQ[par], BMT[par]], [BX] if (last and hd == 1) else None)
                            if not last:
                                mm(Xv[:, 1, hd, 0:C], NQ[0:C, 0, hd, 0:C], MT[0:C, 1, hd, 0:C], True, True, [BNQ[par], BMT[par]], [BX] if hd == 1 else None)
                        tt('dve', XB_[0:C, 0:nf, :, 0:C], Xv[:, 0:nf, :, 0:C], m2d[0:C, j, 0:nf, 0:C].unsqueeze(2).to_broadcast([C, nf, 2, C]), ALU.mult,
                           [BX, Bconst], [BXb[par]])
                        bkY, BY = bank()
                        Yv = bkY[0:C, :].rearrange("p (f h q) -> p f h q", f=2, h=2)
                        for hd in range(2):
                            mm(Yv[:, 0, hd, 0:C], MT[0:C, 1, hd, 0:C], XB_[0:C, 0, hd, 0:C], True, True, [BMT[par], BXb[par]], [BY] if (last and hd == 1) else None)
                            if not last:
                                mm(Yv[:, 1, hd, 0:C], MT[0:C, 0, hd, 0:C], XB_[0:C, 1, hd, 0:C], True, True, [BMT[par], BXb[par]], [BY] if hd == 1 else None)
                        tt('dve', MT[0:C, 0:nf, :, 0:C], Yv[:, 0:nf, :, 0:C], MT[0:C, 0:nf, :, 0:C], ALU.add, [BY, BMT[par]], [BMT[par]])
                    checkpoint('pdbl')
                    ts('pool', Swt[par][:, :], S32[:, l, c, :], wc_t[:, ci:ci + 1], None, ALU.mult, None, [BS32[l][c], Bwc], [BSw[par]])
                    bkW, BW = bank()
                    mm(bkW[0:C, 0:128], at[:, cs], Sbd[:, c, :], True, False, [Bat, BSbd[c]])
                    for hd in range(2):
                        mm(bkW[0:C, hr(hd)], AK[0:C, hd, 0:C], vT[0:C, ci, hr(hd)], False, hd == 1, [BAK[par], BvT], [BW] if hd == 1 else None)
                    cp('act', Zb[par][0:C, :], bkW[0:C, 0:128], [BW], [BZb[par]])
                    bkU, BU = bank()
                    for hd in range(2):
                        mm(bkU[0:C, hr(hd)], MT[0:C, 0, hd, 0:C], Zb[par][0:C, hr(hd)], True, True, [BMT[par], BZb[par]], [BU] if hd == 1 else None)
                    cp('act', Ub[par][0:C, :], bkU[0:C, 0:128], [BU], [BUb[par]])
                    bkY, BY = bank()
                    mm(bkY[0:C, 0:128], rt[:, cs], Sbd[:, c, :], True, False, [Brt, BSbd[c]])
                    for hd in range(2):
                        mm(bkY[0:C, hr(hd)], RR[0:C, hd, 0:C], Ub[par][0:C, hr(hd)], False, False, [BRR[par], BUb[par]])
                        mm(bkY[0:C, hr(hd)], RR[0:C, 2 + hd, 0:C], vT[0:C, ci, hr(hd)], False, hd == 1, [BRR[par], BvT], [BY] if hd == 1 else None)
                    cp('act', yTM[0:C, ci, :], bkY[0:C, 0:128], [BY], [ByTM])
                    bkS, BS = bank()
                    mm(bkS[:, 0:128], bT[0:C, ci, :], Ub[par][0:C, :], True, False, [BbT, BUb[par]])
                    mm(bkS[:, 0:128], kT[0:C, ci, :], vT[0:C, ci, :], False, True, [BkT, BvT], [BS])
                    for hd in range(2):
                        stt(S32[hr(hd), l, c, :], bkS[hr(hd), hr(hd)], wc_t[hr(hd), ci:ci + 1], Swt[par][hr(hd), :], ALU.mult, ALU.add,
                            [BS, Bwc, BSw[par]], [BS32[l][c]])
                    if ci < NC_ - 1:
                        for hd in range(2):
                            cp('pool', Sbd[hr(hd), c, hr(hd)], S32[hr(hd), l, c, :], [BS32[l][c]], [BSbd[c]])
                checkpoint('pseq')
                NG = NC_ * 2
                y3 = yTM[0:C, 0:NC_, :].rearrange("p a (h v) -> p (a h) v", v=64)
                rec.op('dve', lambda e: e.tensor_reduce(out=gst[0:C, 0, 0:NG], in_=y3, axis=AX.X, op=ALU.add), reads=[ByTM], writes=[Bgst])
                tt('pool', ysq[0:C, 0:NC_, :], yTM[0:C, 0:NC_, :], yTM[0:C, 0:NC_, :], ALU.mult, [ByTM], [Bysq])
                s3 = ysq[0:C, 0:NC_, :].rearrange("p a (h v) -> p (a h) v", v=64)
                rec.op('dve', lambda e: e.tensor_reduce(out=gst[0:C, 1, 0:NG], in_=s3, axis=AX.X, op=ALU.add), reads=[Bysq], writes=[Bgst])
                ts('dve', gst[0:C, 2, 0:NG], gst[0:C, 0, 0:NG], 1.0 / 64, None, ALU.mult, None, [Bgst], [Bgst])
                tt('dve', gst[0:C, 3, 0:NG], gst[0:C, 2, 0:NG], gst[0:C, 2, 0:NG], ALU.mult, [Bgst], [Bgst])
                stt(gst[0:C, 3, 0:NG], gst[0:C, 1, 0:NG], 1.0 / 64, gst[0:C, 3, 0:NG], ALU.mult, ALU.subtract, [Bgst], [Bgst])
                act(gst[0:C, 3, 0:NG], gst[0:C, 3, 0:NG], AF.Sqrt, [Bgst, Bconst], [Bgst], bias=epsc[0:C, 2:3])
                recip(gst[0:C, 3, 0:NG], gst[0:C, 3, 0:NG], [Bgst], [Bgst])
                tt('dve', ysq[0:C, 0:NC_, :].rearrange("p a (h v) -> p (a h) v", v=64), y3,
                   gst[0:C, 2, 0:NG].unsqueeze(2).to_broadcast([C, NG, 64]), ALU.subtract, [ByTM, Bgst], [Bysq])
                tt('dve', ynb[0:C, 0:NC_, :].rearrange("p a (h v) -> p (a h) v", v=64), s3,
                   gst[0:C, 3, 0:NG].unsqueeze(2).to_broadcast([C, NG, 64]), ALU.mult, [Bysq, Bgst], [Bynb])
                pt, Bpt = ptbank()
                for ci in range(NC_):
                    tr(pt[:, ci * C:(ci + 1) * C], ynb[0:C, ci, :], identb[0:C, 0:C], [Bynb, Bconst], [Bpt] if ci == NC_ - 1 else None)
                y1, By1 = E2, BE2
                act(y1[:, :TT], pt[:, :TT], AF.Identity, [Bpt, Bvec], [By1], scale=vcol(l, V_LXG, c), bias=vcol(l, V_LXB, c))
                tt('pool', y1[:, :TT], y1[:, :TT], bv[:, :TT], ALU.add, [By1, Bbv], [By1])
                tt('dve', bfp[:, c, :TT], y1[:, :TT], g_[:, :TT], ALU.mult, [By1, Bg], [Bbfp[c]])

            def tile_layer(l, grp, TT, C, NSTEP):
                NTB_ = max(1, TT // 128)
                TB = min(TT, 128)
                dma('pool', 'W', smallw[:], smsc[l], reads=[Bwsc] + Bsm[l], writes=[Bsmall])
                dma('pool', 'W', bsb[:], b_s[l].unsqueeze(0).partition_broadcast(128).rearrange("p o n -> p (o n)"), writes=[Bbsb])
                rmsnorm_to_h(l, V_MIXG, TT)
                wt, Bw = wpiece(l, PI_LORA)
                for j in range(2):
                    col = 24 + j
                    bk, Bbk = big_mm(wt, Bw, j, hrhs(TT), Bh, TT)
                    cb, Bcb = tmpf()
                    cp('act', cb[:, 1:TT + 1], bk[:, :TT], [Bbk], [Bcb])
                    cp('pool', cb[:, 0:1], prev[:, l, col:col + 1], [Bprev[l]], [Bcb])
                    t1, Bt1 = tmpf()
                    ts('dve', t1[:, :TT], cb[:, 0:TT], vcol(l, V_MU, col), None, ALU.mult, None, [Bcb, Bvec], [Bt1])
                    mx, Bmx = tmpf()
                    stt(mx[:, :TT], cb[:, 1:TT + 1], vcol(l, V_OMM, col), t1[:, :TT], ALU.mult, ALU.add, [Bcb, Bt1, Bvec], [Bmx])
                    cp('pool', prev[:, l, col:col + 1], cb[:, TT:TT + 1], [Bcb], [Bprev[l]])
                    if j == 0:
                        act(lw[0:64, :TT], mx[0:64, :TT], AF.Tanh, [Bmx], [Blw])
                        cp('act', lw[64:128, :TT], mx[64:128, :TT], [Bmx], [Blw])
                    else:
                        act(sgl[:, :TT], mx[:, :TT], AF.Sigmoid, [Bmx], [Bsgl])
                checkpoint('lora')
                for c in range(8):
                    rwkv_pair(l, c, TT, C, NSTEP)
                    checkpoint('pair0')
                checkpoint('rwkv')
                for i in range(4):
                    wt, Bw = wpiece(l, PI_SGU + i)
                    for o4 in range(4):
                        oc = i * 4 + o4
                        bk, Bbk = big_mm(wt, Bw, o4, hrhs(TT), Bh, TT)
                        act(big[:, oc, :TT], bk[:, :TT], AF.Gelu_apprx_tanh, [Bbk], [Bbig[oc]])
                bk1, Bb1 = bank(); bk2, Bb2 = bank()
                for c in range(8):
                    vb, Bvb = tmpb(); vs, Bvs = tmpb()
                    cp('pool', vb[:, :TT], big[:, 8 + c, :TT], [Bbig[8 + c]], [Bvb])
                    act(vs[:, :TT], big[:, 8 + c, :TT], AF.Square, [Bbig[8 + c]], [Bvs])
                    mm(bk1[:, :TT], onesb[:], vb[:, :TT], c == 0, c == 7, [Bvb, Bconst], [Bb1])
                    mm(bk2[:, :TT], onesb[:], vs[:, :TT], c == 0, c == 7, [Bvs, Bconst], [Bb2])
                mean, Bmean = tmpf(); rstd, Brstd = tmpf()
                act(mean[:, :TT], bk1[:, :TT], AF.Copy, [Bb1], [Bmean], scale=1.0 / D)
                tt('pool', rstd[:, :TT], mean[:, :TT], mean[:, :TT], ALU.mult, [Bmean], [Brstd])
                stt(rstd[:, :TT], bk2[:, :TT], 1.0 / D, rstd[:, :TT], ALU.mult, ALU.subtract, [Bb2, Brstd], [Brstd])
                act(rstd[:, :TT], rstd[:, :TT], AF.Sqrt, [Brstd, Bconst], [Brstd], bias=epsc[:, 1:2])
                recip(rstd[:, :TT], rstd[:, :TT], [Brstd], [Brstd])
                vnb = []
                for c in range(8):
                    vc = big[:, 8 + c, :TT]
                    tt('dve', vc, vc, mean[:, :TT], ALU.subtract, [Bbig[8 + c], Bmean], [Bbig[8 + c]])
                    tt('pool', vc, vc, rstd[:, :TT], ALU.mult, [Bbig[8 + c], Brstd], [Bbig[8 + c]])
                    ts('dve', vc, vc, vcol(l, V_SLG, c), vcol(l, V_SLB, c), ALU.mult, ALU.add, [Bbig[8 + c], Bvec], [Bbig[8 + c]])
                    cp('pool', bfp[:, 16 + c, :TT], vc, [Bbig[8 + c]], [Bbfp[16 + c]])
                    vnb.append((bfp[:, 16 + c, :], Bbfp[16 + c]))
                if grp == 1:
                    for half in range(2):
                        bk, Bbk = bank()
                        for j in range(4):
                            tr(bk[0:TT, j * 128:(j + 1) * 128], big[:, 8 + half * 4 + j, :TT], identf[:], [Bbig[8 + half * 4 + j], Bconst], [Bbk] if j == 3 else None)
                        stg, Bstg = tmpf()
                        cp('act', stg[0:TT, 0:512], bk[0:TT, :], [Bbk], [Bstg])
                        dma('pool', 'R', osgu[l][:, half * 512:(half + 1) * 512], stg[0:TT, 0:512], reads=[Bstg])
                vtm = []
                for tb_ in range(NTB_):
                    halves = []
                    for half in range(2):
                        pt, Bpt = ptbank()
                        for j in range(4):
                            cc = half * 4 + j
                            tr(pt[0:TB, j * 128:(j + 1) * 128], vnb[cc][0][:, tb_ * 128: tb_ * 128 + TB], identb[:], [vnb[cc][1], Bconst], [Bpt] if j == 3 else None)
                        vt_, Bvt_ = bfp[:, 24 + tb_ * 2 + half, :], Bbfp[24 + tb_ * 2 + half]
                        cp('act', vt_[0:TB, :], pt[0:TB, :], [Bpt], [Bvt_])
                        halves.append((vt_, Bvt_))
                    vtm.append(halves)
                wsT = smallw[:, 0:1024]
                for c in range(8):
                    bk, Bbk = bank()
                    for tb_ in range(NTB_):
                        vt_, Bvt_ = vtm[tb_][c // 4]
                        mm(bk[:, tb_ * 128: tb_ * 128 + TB], vt_[0:TB, (c % 4) * 128:(c % 4 + 1) * 128], wsT[0:TB, c * 128: c * 128 + TB], True, True,
                           [Bvt_, Bsmall], [Bbk] if tb_ == NTB_ - 1 else None)
                    sv, Bsv = tmpf()
                    if NTB_ > 1:
                        tt('dve', sv[:, :TT].rearrange("p (a b) -> p a b", b=128), bk[:, :TT].rearrange("p (a b) -> p a b", b=128),
                           bsb[:, c * 128:(c + 1) * 128].unsqueeze(1).to_broadcast([128, NTB_, 128]), ALU.add, [Bbk, Bbsb], [Bsv])
                    else:
                        tt('dve', sv[:, :TT], bk[:, :TT], bsb[:, c * 128: c * 128 + TT], ALU.add, [Bbk, Bbsb], [Bsv])
                    tt('pool', bfp[:, 8 + c, :TT], sv[:, :TT], big[:, c, :TT], ALU.mult, [Bsv, Bbig[c]], [Bbfp[8 + c]])
                checkpoint('sgu')
                for i in range(2):
                    wt, Bw = wpiece(l, PI_Q + i)
                    for o4 in range(4):
                        oc = i * 4 + o4
                        bk, Bbk = big_mm(wt, Bw, o4, hrhs(TT), Bh, TT)
                        cp('act', bfp[:, 24 + oc, :TT], bk[:, :TT], [Bbk], [Bbfp[24 + oc]])
                kvt, Bkvp = piece('kv', l, grp)
                KT = kvt[:, 0:2048].rearrange("p (j m) -> p j m", j=8)
                VV = kvt[:, 2048:4096].rearrange("p (mb n) -> p mb n", mb=2)
                for hh in range(4):
                    pTs = []
                    for mc in range(2):
                        bk, Bbk = bank()
                        for dc in range(2):
                            mm(bk[:, :TT], KT[:, 2 * hh + dc, mc * 128:(mc + 1) * 128], bfp[:, 24 + 2 * hh + dc, :TT], dc == 0, dc == 1,
                               [Bkvp, Bbfp[24 + 2 * hh + dc]], [Bbk] if dc == 1 else None)
                        p_, Bp_ = tmpb()
                        act(p_[:, :TT], bk[:, :TT], AF.Exp, [Bbk], [Bp_], scale=1.0 / 16)
                        pTs.append((p_, Bp_))
                    bk, Bbk = bank()
                    for mc in range(2):
                        mm(bk[:, :TT], onesb[:], pTs[mc][0][:, :TT], mc == 0, mc == 1, [pTs[mc][1], Bconst], [Bbk] if mc == 1 else None)
                    rd, Brd = tmpf()
                    recip(rd[:, :TT], bk[:, :TT], [Bbk], [Brd])
                    for dvc in range(2):
                        bk, Bbk = bank()
                        for mc in range(2):
                            mm(bk[:, :TT], VV[:, mc, hh * 256 + dvc * 128: hh * 256 + (dvc + 1) * 128], pTs[mc][0][:, :TT], mc == 0, mc == 1,
                               [Bkvp, pTs[mc][1]], [Bbk] if mc == 1 else None)
                        tt('dve', bfp[:, 16 + 2 * hh + dvc, :TT], bk[:, :TT], rd[:, :TT], ALU.mult, [Bbk, Brd], [Bbfp[16 + 2 * hh + dvc]])
                if DBG and grp == 0 and l == 0:
                    dma('pool', 'R', dbg_y, bfp[:, 0:24, :], reads=[Bbfp[i] for i in range(24)])
                checkpoint('attn')
                for b in range(3):
                    for hf in range(2):
                        wg, Bwg = wpiece(l, PI_GATE + b * 2 + hf)
                        wb_, Bwb = wpiece(l, PI_BR + b * 2 + hf)
                        for o4 in range(4):
                            oc = hf * 4 + o4
                            bk, Bbk = big_mm(wg, Bwg, o4, hrhs(TT), Bh, TT)
                            gt, Bgt = tmpb()
                            act(gt[:, :TT], bk[:, :TT], AF.Sigmoid, [Bbk], [Bgt])
                            rh = [bfp[:, b * 8 + kc, :TT] for kc in range(8)]
                            bk, Bbk = big_mm(wb_, Bwb, o4, rh, [Bbfp[b * 8 + kc] for kc in range(8)], TT)
                            if b == 0:
                                tt('dve', big[:, oc, :TT], bk[:, :TT], gt[:, :TT], ALU.mult, [Bbk, Bgt], [Bbig[oc]])
                            else:
                                tm_, Btm = tmpf()
                                tt('dve', tm_[:, :TT], bk[:, :TT], gt[:, :TT], ALU.mult, [Bbk, Bgt], [Btm])
                                if b == 1:
                                    tt('pool', big[:, oc, :TT], big[:, oc, :TT], tm_[:, :TT], ALU.add, [Bbig[oc], Btm], [Bbig[oc]])
                                else:
                                    tt('pool', bfp[:, 24 + oc, :TT], big[:, oc, :TT], tm_[:, :TT], ALU.add, [Bbig[oc], Btm], [Bbfp[24 + oc]])
                if DBG and grp == 0 and l == 0:
                    dma('pool', 'R', dbg_m, bfp[:, 24:32, :], reads=[Bbfp[24 + i] for i in range(8)])
                for hf in range(2):
                    wt, Bw = wpiece(l, PI_OUT + hf)
                    for o4 in range(4):
                        oc = hf * 4 + o4
                        rh = [bfp[:, 24 + kc, :TT] for kc in range(8)]
                        bk, Bbk = big_mm(wt, Bw, o4, rh, [Bbfp[24 + kc] for kc in range(8)], TT)
                        tt('dve', x_t[:, oc, :TT], x_t[:, oc, :TT], bk[:, :TT], ALU.add, [Bx[oc], Bbk], [Bx[oc]])
                if DBG and grp == 0 and l == 0:
                    dma('pool', 'R', dbg_x, x_t[:], reads=Bx)
                checkpoint('merge')
                rmsnorm_to_h(l, V_FFNG, TT)
                for i in range(8):
                    wt, Bw = wpiece(l, PI_UP + i)
                    for o4 in range(4):
                        oc = i * 4 + o4
                        bk, Bbk = big_mm(wt, Bw, o4, hrhs(TT), Bh, TT)
                        rl, Brl = tmpf()
                        act(rl[:, :TT], bk[:, :TT], AF.Relu, [Bbk], [Brl])
                        tt('pool', bfp[:, oc, :TT], rl[:, :TT], rl[:, :TT], ALU.mult, [Brl], [Bbfp[oc]])
                for g in range(2):
                    banks = [bank() for _ in range(4)]
                    for q in range(4):
                        wt, Bw = wpiece(l, PI_DN + g * 4 + q)
                        for o4 in range(4):
                            rh = [bfp[:, q * 8 + kc, :TT] for kc in range(8)]
                            big_mm(wt, Bw, o4, rh, [Bbfp[q * 8 + kc] for kc in range(8)], TT, out_bk=banks[o4], start=(q == 0), stop=(q == 3))
                    for o4 in range(4):
                        oc = g * 4 + o4
                        tt('dve', x_t[:, oc, :TT], x_t[:, oc, :TT], banks[o4][0][:, :TT], ALU.add, [Bx[oc], banks[o4][1]], [Bx[oc]])

            def dbg_x2_dump():
                if DBG:
                    dma('pool', 'R', dbg_x2, x_t[:], reads=Bx)

            def load_x(src, TT):
                NTB_ = max(1, TT // 128); TB = min(TT, 128)
                for tb_ in range(NTB_):
                    dma('pool', 'W', big[0:TB, 2 * tb_:2 * tb_ + 2, :].rearrange("p a n -> p (a n)"), src[tb_ * 128: tb_ * 128 + TB, :],
                        writes=[Bbig[2 * tb_], Bbig[2 * tb_ + 1]])
                for c in range(8):
                    bk, Bbk = bank()
                    for tb_ in range(NTB_):
                        tr(bk[:, tb_ * 128: tb_ * 128 + TB], big[0:TB, 2 * tb_ + c // 4, (c % 4) * 128:(c % 4 + 1) * 128], identf[0:TB, 0:TB],
                           [Bbig[2 * tb_ + c // 4], Bconst], [Bbk] if tb_ == NTB_ - 1 else None)
                    cp('act', x_t[:, c, :TT], bk[:, :TT], [Bbk], [Bx[c]])

            def store_y(dst, TT):
                NTB_ = max(1, TT // 128); TB = min(TT, 128)
                bk, Bbk = bank()
                for c in range(8):
                    sq, Bsq = tmpb()
                    act(sq[:, :TT], x_t[:, c, :TT], AF.Square, [Bx[c]], [Bsq])
                    mm(bk[:, :TT], onesb[:], sq[:, :TT], c == 0, c == 7, [Bsq, Bconst], [Bbk] if c == 7 else None)
                rs, Brs = tmpf()
                act(rs[:, :TT], bk[:, :TT], AF.Sqrt, [Bbk, Bconst], [Brs], scale=1.0 / D, bias=epsc[:, 0:1])
                recip(rs[:, :TT], rs[:, :TT], [Brs], [Brs])
                for c in range(8):
                    stt(big[:, 8 + c, :TT], x_t[:, c, :TT], fing_t[:, c:c + 1], rs[:, :TT], ALU.mult, ALU.mult, [Bx[c], Brs, Bfing], [Bbig[8 + c]])
                for tb_ in range(NTB_):
                    for half in range(2):
                        bk, Bbk = bank()
                        for j in range(4):
                            cc = half * 4 + j
                            tr(bk[0:TB, j * 128:(j + 1) * 128], big[:, 8 + cc, tb_ * 128: tb_ * 128 + TB], identf[:], [Bbig[8 + cc], Bconst], [Bbk] if j == 3 else None)
                        cp('act', big[0:TB, 2 * tb_ + half, :], bk[0:TB, :], [Bbk], [Bbig[2 * tb_ + half]])
                        dma('pool', 'R', dst[tb_ * 128: tb_ * 128 + TB, half * 512:(half + 1) * 512], big[0:TB, 2 * tb_ + half, :], reads=[Bbig[2 * tb_ + half]])

            def store_state(l, owkv, oshift):
                for c in range(8):
                    bk, Bbk = bank()
                    tr(bk[0:64, 0:128], S32[:, l, c, :], identf[:], [BS32[l][c], Bconst], [Bbk])
                    stg, Bstg = tmpf()
                    cp('act', stg[0:64, 0:128], bk[0:64, 0:128], [Bbk], [Bstg])
                    dma('pool', 'R', owkv[l, 2 * c:2 * c + 2].rearrange("h v k -> v h k"), stg[0:64, 0:128].rearrange("p (h k) -> p h k", h=2), reads=[Bstg])
                bk, Bbk = bank()
                tr(bk[0:26, 0:128], prev[:, l, :], identf[:], [Bprev[l], Bconst], [Bbk])
                stg, Bstg = tmpf()
                cp('act', stg[0:26, 0:128], bk[0:26, 0:128], [Bbk], [Bstg])
                dma('pool', 'R', oshift[l].rearrange("(j p) -> j p", p=128), stg[0:26, 0:128], reads=[Bstg])

            memset('pool', cmask[:], 1.0, [Bconst])
            memset('pool', cmask[:].rearrange("p (a b) -> p a b", b=128)[:, :, 0:1], 0.0, [Bconst])

            Bm_ = [Bbig[i] for i in range(4)]
            dma('pool', 'W', big[:, 0:4, :].rearrange("p (mb a) n -> p mb (a n)", mb=2), memp.rearrange("(mb p) n -> p mb n", p=128), writes=Bm_)
            memf = big[:, 4:8, :].rearrange("p a (b n) -> p (a b) n", b=2)
            Bmemf = [Bbig[i] for i in range(4, 8)]
            for mb in range(2):
                ssq, Bssq = tmpf()
                for a in range(2):
                    jk, Bjk = tmpf()
                    rec.op('act', lambda e, a=a, mb=mb, jk=jk, ssq=ssq: e.activation(out=jk[:, 0:512], in_=big[:, mb * 2 + a, :], func=AF.Square, accum_out=ssq[:, a:a + 1]),
                           reads=[Bbig[mb * 2 + a]], writes=[Bjk, Bssq])
                tt('dve', ssq[:, 2:3], ssq[:, 0:1], ssq[:, 1:2], ALU.add, [Bssq], [Bssq])
                act(ssq[:, 3:4], ssq[:, 2:3], AF.Sqrt, [Bssq, Bconst], [Bssq], scale=1.0 / D, bias=epsc[:, 0:1])
                recip(ssq[:, 4:5], ssq[:, 3:4], [Bssq], [Bssq])
                for a in range(2):
                    ts('dve', big[:, mb * 2 + a, :], big[:, mb * 2 + a, :], ssq[:, 4:5], None, ALU.mult, None, [Bbig[mb * 2 + a], Bssq], [Bbig[mb * 2 + a]])
            for c in range(8):
                bk, Bbk = bank()
                for mb in range(2):
                    tr(bk[:, mb * 128:(mb + 1) * 128], big[:, mb * 2 + c // 4, (c % 4) * 128:(c % 4 + 1) * 128], identf[:], Bm_ + [Bconst], [Bbk] if mb == 1 else None)
                cp('act', memf[:, c, :], bk[:, 0:256], [Bbk], Bmemf)
            for l in range(L):
                mnt = [(bfp[:, 8 + i, :], Bbfp[8 + i]) for i in range(4)]
                for c in range(8):
                    t_, B_ = mnt[c // 2]
                    ts('dve', t_[:, (c % 2) * 256:(c % 2 + 1) * 256], memf[:, c, :], vcol(l, V_MEMG, c), None, ALU.mult, None, Bmemf + [Bvec], [B_])
                mrhs = lambda c: mnt[c // 2][0][:, (c % 2) * 256:(c % 2 + 1) * 256]
                kvst = [(bfp[:, i, :], Bbfp[i]) for i in range(8)]
                for i in range(4):
                    wt, Bw = wpiece(l, PI_MKV + i)
                    for mb in range(2):
                        bk, Bbk = bank()
                        for kc in range(8):
                            mm(bk[:, :], mrhs(kc)[:, mb * 128:(mb + 1) * 128], wt[:, kc, :], kc == 0, kc == 7, [mnt[kc // 2][1], Bw], [Bbk] if kc == 7 else None)
                        stg, Bstg = tmpf()
                        cp('act', stg[:, 0:512], bk[:, :], [Bbk], [Bstg])
                        dst = (omk if i < 2 else omv)[l][mb * 128:(mb + 1) * 128, (i % 2) * 512:(i % 2 + 1) * 512]
                        dma('pool', 'R', dst, stg[:, 0:512], reads=[Bstg])
                        if i >= 2:
                            s_, Bs_ = kvst[4 + mb * 2 + (i - 2)]
                            cp('pool', s_, stg[:, 0:512], [Bstg], [Bs_])
                    if i < 2:
                        for o4 in range(4):
                            j = i * 4 + o4
                            bk, Bbk = bank()
                            for kc in range(8):
                                mm(bk[:, 0:256], wt[:, kc, o4 * 128:(o4 + 1) * 128], mrhs(kc), kc == 0, kc == 7, [mnt[kc // 2][1], Bw], [Bbk] if kc == 7 else None)
                            s_, Bs_ = kvst[j // 2]
                            cp('act', s_[:, (j % 2) * 256:(j % 2 + 1) * 256], bk[:, 0:256], [Bbk], [Bs_])
                for i in range(8):
                    Bd = Buf(); Bkv[l][0].append(Bd)
                    dma('pool', 'R', kvsc[l, 0][:, i * 512:(i + 1) * 512], kvst[i][0], reads=[kvst[i][1]], writes=[Bd])

            checkpoint('memkv')
            for l in range(L):
                for c in range(8):
                    stg, Bstg = tmpf()
                    dma('pool', 'W', stg[0:64, 0:128].rearrange("p (h k) -> p h k", h=2), swkv[l, 2 * c:2 * c + 2].rearrange("h v k -> v h k"), writes=[Bstg])
                    bk, Bbk = bank()
                    tr(bk[:, 0:64], stg[0:64, 0:128], identf[0:64, 0:64], [Bstg, Bconst], [Bbk])
                    cp('act', S32[:, l, c, :], bk[:, 0:64], [Bbk], [BS32[l][c]])
                stg, Bstg = tmpf()
                dma('pool', 'W', stg[0:26, 0:128], sshift[l].rearrange("(j p) -> j p", p=128), writes=[Bstg])
                bk, Bbk = bank()
                tr(bk[:, 0:26], stg[0:26, 0:128], identf[0:26, 0:26], [Bstg, Bconst], [Bbk])
                cp('act', prev[:, l, :], bk[:, 0:26], [Bbk], [Bprev[l]])
            load_x(xs, TS)
            for l in range(L):
                tile_layer(l, 1, TS, TS, 5)
            store_y(ys, TS)
            for l in range(L):
                store_state(l, owkv_s, oshift_s)

            checkpoint('sample')
            for l in range(L):
                for c in range(8):
                    memset('pool', S32[:, l, c, :], 0.0, [BS32[l][c]])
                memset('pool', prev[:, l, :], 0.0, [Bprev[l]])
            for t_ in range(NT):
                load_x(xp[t_ * T:(t_ + 1) * T], T)
                for l in range(L):
                    tile_layer(l, 0, T, 128, 7)
                    if l == 0 and t_ == 0:
                        dbg_x2_dump()
                store_y(yp[t_ * T:(t_ + 1) * T], T)
            for l in range(L):
                store_state(l, owkv_p, oshift_p)

        except _Stop:
            pass
        finish()
    return nc


def _vec_table(inp, L):
    def fm(a, n):
        return np.ascontiguousarray(a.reshape(L, n, 128).transpose(2, 0, 1))
    mu = inp['rwkv_mu'][:L]
    ka = inp['rwkv_k_a'][:L]
    parts = [fm(inp['norm_mix_g'][:L], 8), fm(inp['norm_ffn_g'][:L], 8), fm(inp['norm_mem_g'][:L], 8),
             fm(mu, 26), np.zeros((128, L, 26), np.float32), fm(inp['rwkv_w0'][:L], 8), fm(inp['rwkv_a0'][:L], 8),
             fm(inp['rwkv_k_k'][:L], 8), fm(ka, 8), np.zeros((128, L, 8), np.float32), fm(inp['rwkv_r_k'][:L].reshape(L, 1024), 8),
             fm(inp['sgu_ln_g'][:L], 8), fm(inp['sgu_ln_b'][:L], 8), fm(inp['rwkv_lnx_g'][:L], 8), fm(inp['rwkv_lnx_b'][:L], 8)]
    return np.ascontiguousarray(np.concatenate(parts, axis=2).astype(np.float32))


def run(inp, SEQ, L, n_cores=8, trace=False, DBG=False, STOP=None):
    nc = build(SEQ, L, DBG=DBG, STOP=STOP)
    f = lambda a: np.ascontiguousarray(a, dtype=np.float32)
    vecs = _vec_table(inp, L)
    fing = np.ascontiguousarray(inp['norm_final_g'].reshape(8, 128).T.astype(np.float32))
    shared = {
        'vecs': vecs, 'fing': fing, 'w_in': f(inp['w_in'][:L]), 'w_mkv': f(inp['w_mem_kv'][:L]),
        'w2': f(inp['rwkv_w2'][:L]), 'a2': f(inp['rwkv_a2'][:L]), 'g2': f(inp['rwkv_g2'][:L]),
        'w_s': f(inp['sgu_w_s'][:L]), 'b_s': f(inp['sgu_b_s'][:L].reshape(L, 1024)),
        'w_br': f(inp['w_branch'][:L]), 'w_out': f(inp['w_out'][:L]), 'w_up': f(inp['w_ffn_up'][:L]), 'w_dn': f(inp['w_ffn_down'][:L]),
    }
    in_maps = []
    for b in range(n_cores):
        m = dict(shared)
        m['xp'] = f(inp['x_prompt'][b, :SEQ]); m['xs'] = f(inp['x_sample'][b])
        m['ck'] = f(inp['cache_mem_k'][:L, b].reshape(L, 256, 1024)); m['cv'] = f(inp['cache_mem_v'][:L, b].reshape(L, 256, 1024))
        m['swkv'] = f(inp['state_wkv'][:L, b]); m['sshift'] = f(inp['state_shift'][:L, b, 0])
        m['memp'] = f(inp['mem_prompt'][b])
        in_maps.append(m)
    res = run_bass_kernel_spmd(nc, in_maps, core_ids=list(range(n_cores)), trace=trace)
    R = res.results
    st = lambda k: np.stack([np.asarray(r[k]) for r in R])
    y_p = st('yp'); y_s = st('ys')
    mk = st('omk').transpose(1, 0, 2, 3).reshape(L, n_cores, 256, 4, 256)
    mv = st('omv').transpose(1, 0, 2, 3).reshape(L, n_cores, 256, 4, 256)
    wkv_p = st('owkv_p').transpose(1, 0, 2, 3, 4)
    sh_p = st('oshift_p').transpose(1, 0, 2)[:, :, None, :]
    wkv_s = st('owkv_s').transpose(1, 0, 2, 3, 4)
    sh_s = st('oshift_s').transpose(1, 0, 2)[:, :, None, :]
    sgu = st('osgu').transpose(1, 0, 2, 3)
    outs = (y_p, y_s, mk, mv, wkv_p, sh_p, wkv_s, sh_s, sgu)
    return tuple(np.ascontiguousarray(o.astype(np.float32)) for o in outs), res


def kernel(**inputs):
    inp = {k: np.asarray(v) for k, v in inputs.items()}
    outs, _ = run(inp, 8192, 4, 8)
    return outs
```

```python
import contextlib
import numpy as np
import concourse.bass as bass
import concourse.mybir as mybir
from concourse.bass_utils import run_bass_kernel_spmd

F32 = mybir.dt.float32
BF16 = mybir.dt.bfloat16
AF = mybir.ActivationFunctionType
ALU = mybir.AluOpType
AX = mybir.AxisListType

ENGS = ('pe', 'act', 'dve', 'pool', 'sp')


class Buf:
    __slots__ = ('w', 'r', 'id')
    _n = [0]

    def __init__(self):
        self.w = None
        self.r = {}
        Buf._n[0] += 1
        self.id = Buf._n[0]


class Rec:
    def __init__(self, nc):
        self.nc = nc
        self.streams = {e: [] for e in ENGS}
        self.cnt = {e: 0 for e in ENGS}
        self.dcnt = {}
        self.waited = {e: {} for e in ENGS}
        self.pending = {e: [] for e in ENGS}

    def _wait(self, eng, tok):
        if tok is None:
            return
        key, val = tok
        if self.waited[eng].get(key, 0) >= val:
            return
        self.waited[eng][key] = val
        self.streams[eng].append(('w', key, val))

    def _deps(self, eng, reads, writes, extra):
        for t in extra:
            self._wait(eng, t)
        for b in reads:
            self._wait(eng, b.w)
        for b in writes:
            self._wait(eng, b.w)
            for t in b.r.values():
                self._wait(eng, t)
            for e2 in ENGS:
                if e2 != eng and any(pb_ is b for pb_ in self.pending[e2]):
                    raise RuntimeError("write to buffer %d with pending unmarked reads on %s" % (b.id, e2))

    def op(self, eng, fn, reads=(), writes=(), mark=True, extra=(), wdeps=()):
        self._deps(eng, reads, tuple(writes) + tuple(wdeps), extra)
        if mark:
            self.cnt[eng] += 1
            tok = (eng, self.cnt[eng])
            self.streams[eng].append(('i', fn, True))
            for b in self.pending[eng]:
                b.r[eng] = tok
            self.pending[eng] = []
            for b in reads:
                b.r[eng] = tok
            for b in writes:
                b.w = tok
                b.r = {}
            return tok
        assert not writes
        self.streams[eng].append(('i', fn, False))
        self.pending[eng].extend(reads)
        return None

    def dma(self, q, sem, fn, reads=(), writes=(), extra=()):
        self._deps(q, reads, writes, extra)
        self.dcnt[sem] = self.dcnt.get(sem, 0) + 16
        tok = (sem, self.dcnt[sem])
        self.streams[q].append(('d', fn, sem))
        for b in reads:
            b.r[sem] = tok
        for b in writes:
            b.w = tok
            b.r = {}
        return tok

    def emit(self, stack):
        nc = self.nc
        sems = {}
        for e in ENGS:
            sems[e] = stack.enter_context(nc.semaphore('s_' + e))
        for d in self.dcnt:
            sems[d] = stack.enter_context(nc.semaphore('d_' + d))
        block = stack.enter_context(nc.Block())
        dec = {'pe': block.tensor, 'act': block.scalar, 'dve': block.vector, 'pool': block.gpsimd, 'sp': block.sync}
        for e in ENGS:
            stream = self.streams[e]
            own = sems[e]

            def body(engine, stream=stream, own=own):
                for item in stream:
                    if item[0] == 'w':
                        engine.wait_ge(sems[item[1]], item[2])
                    elif item[0] == 'i':
                        ins = item[1](engine)
                        if item[2]:
                            ins.then_inc(own, 1)
                    else:
                        item[1](engine).then_inc(sems[item[2]], 16)
            dec[e](body)


D = 1024
NCH = 8
RW = 3328
IN_COLS = 9472
NP_ = 49
PI_LORA, PI_RKV, PI_SGU, PI_Q, PI_GATE, PI_BR, PI_OUT, PI_UP, PI_DN, PI_MKV = 0, 1, 9, 13, 15, 21, 27, 29, 37, 45
C0 = -0.6065306597126334
V_MIXG, V_FFNG, V_MEMG, V_MU, V_OMM, V_W0, V_A0, V_KK, V_KA, V_OMKA, V_RK, V_SLG, V_SLB, V_LXG, V_LXB = (
    0, 8, 16, 24, 50, 76, 84, 92, 100, 108, 116, 124, 132, 140, 148)
VW = 156


class _Stop(Exception):
    pass


def build(SEQ, DEPTH, MEMN=256, TS=32, DBG=False, STOP=None):
    nc = bass.Bass("TRN2", target_bir_lowering=False)
    L = DEPTH
    T = 512
    NT = SEQ // T
    din = lambda name, shape, dt=F32: nc.dram_tensor(name, shape, dt, kind="ExternalInput").ap()
    dout = lambda name, shape, dt=F32: nc.dram_tensor(name, shape, dt, kind="ExternalOutput").ap()
    dscr = lambda name, shape, dt: nc.dram_tensor(name, shape, dt, kind="Internal").ap()
    xp = din("xp", [SEQ, D]); xs = din("xs", [TS, D])
    ck = din("ck", [L, MEMN, D]); cv = din("cv", [L, MEMN, D])
    swkv = din("swkv", [L, 16, 64, 64]); sshift = din("sshift", [L, RW]); memp = din("memp", [MEMN, D])
    vecs = din("vecs", [128, L, VW]); fing = din("fing", [128, 8])
    w_in = din("w_in", [L, D, IN_COLS]); w_mkv = din("w_mkv", [L, D, 2 * D])
    w2 = din("w2", [L, 64, D]); a2 = din("a2", [L, 64, D]); g2 = din("g2", [L, 128, D])
    w_s = din("w_s", [L, 8, 128, 128]); b_s = din("b_s", [L, 8 * 128])
    w_br = din("w_br", [L, 3, D, D]); w_out = din("w_out", [L, D, D])
    w_up = din("w_up", [L, D, 4 * D]); w_dn = din("w_dn", [L, 4 * D, D])
    yp = dout("yp", [SEQ, D]); ys = dout("ys", [TS, D])
    omk = dout("omk", [L, MEMN, D]); omv = dout("omv", [L, MEMN, D])
    owkv_p = dout("owkv_p", [L, 16, 64, 64]); oshift_p = dout("oshift_p", [L, RW])
    owkv_s = dout("owkv_s", [L, 16, 64, 64]); oshift_s = dout("oshift_s", [L, RW])
    osgu = dout("osgu", [L, TS, D])
    if DBG:
        dbg_y = dout("dbg_y", [128, 24, 512], BF16); dbg_x = dout("dbg_x", [128, 8, 512]); dbg_x2 = dout("dbg_x2", [128, 8, 512]); dbg_m = dout("dbg_m", [128, 8, 512], BF16)
    wsc = dscr("wsc", [L, NP_, 128, 4096], BF16)
    kvsc = dscr("kvsc", [L, 2, 128, 4096], BF16)
    smsc = dscr("smsc", [L, 128, 3072], BF16)

    rec = Rec(nc)
    st = contextlib.ExitStack()
    with st:
        def sb(name, shape, dt):
            return st.enter_context(nc.sbuf_tensor(name, shape, dt))

        def ps(name, shape, dt):
            return st.enter_context(nc.psum_tensor(name, shape, dt))

        def mm(out, lhsT, rhs, start, stop, reads, writes=None):
            rec.op('pe', lambda e: e.matmul(out, lhsT=lhsT, rhs=rhs, start=start, stop=stop),
                   reads=reads, writes=writes or (), mark=writes is not None, wdeps=() if writes is not None else psum_bufs[out.name])

        def tr(out, in_, ident, reads, writes=None):
            rec.op('pe', lambda e: e.transpose(out=out, in_=in_, identity=ident),
                   reads=reads, writes=writes or (), mark=writes is not None, wdeps=() if writes is not None else psum_bufs[out.name])

        def act(out, in_, func, reads, writes, scale=1.0, bias=None):
            if bias is None:
                rec.op('act', lambda e: e.activation(out=out, in_=in_, func=func, scale=scale), reads=reads, writes=writes)
            else:
                rec.op('act', lambda e: e.activation(out=out, in_=in_, func=func, scale=scale, bias=bias), reads=reads, writes=writes)

        def tt(eng, out, in0, in1, op, reads, writes):
            rec.op(eng, lambda e: e.tensor_tensor(out=out, in0=in0, in1=in1, op=op), reads=reads, writes=writes)

        def ts(eng, out, in0, s1, s2, op0, op1, reads, writes):
            if op1 is None:
                rec.op(eng, lambda e: e.tensor_scalar(out=out, in0=in0, scalar1=s1, scalar2=None, op0=op0), reads=reads, writes=writes)
            else:
                rec.op(eng, lambda e: e.tensor_scalar(out=out, in0=in0, scalar1=s1, scalar2=s2, op0=op0, op1=op1), reads=reads, writes=writes)

        def stt(out, in0, scalar, in1, op0, op1, reads, writes):
            rec.op('dve', lambda e: e.scalar_tensor_tensor(out=out, in0=in0, scalar=scalar, in1=in1, op0=op0, op1=op1), reads=reads, writes=writes)

        def cp(eng, out, in_, reads, writes):
            if eng == 'act':
                act(out, in_, AF.Copy, reads, writes)
            else:
                rec.op(eng, lambda e: e.tensor_copy(out=out, in_=in_), reads=reads, writes=writes)

        def recip(out, in_, reads, writes):
            rec.op('dve', lambda e: e.reciprocal(out=out, in_=in_), reads=reads, writes=writes)

        def memset(eng, ap, val, writes):
            rec.op(eng, lambda e: e.memset(ap, val), writes=writes)

        def dma(q, sem, out, in_, reads=(), writes=(), slow=False):
            if sem == 'W':
                sem = 'b%d' % writes[0].id
            elif sem == 'R':
                sem = 'b%d' % reads[0].id
            if slow:
                return rec.dma(q, sem, lambda e: e.dma_start(out=out, in_=in_, allow_slow_non_contiguous=True), reads=reads, writes=writes)
            return rec.dma(q, sem, lambda e: e.dma_start(out=out, in_=in_), reads=reads, writes=writes)

        x_t = sb("x_t", [128, 8, T], F32); Bx = [Buf() for _ in range(8)]
        h_t = sb("h_t", [128, 8, T], BF16); Bh = [Buf() for _ in range(8)]
        NS = 3
        ws_t = [sb(f"ws{i}", [128, 4096], BF16) for i in range(NS)]; Bws = [Buf() for _ in range(NS)]
        big = sb("big", [128, 16, T], F32); Bbig = [Buf() for _ in range(16)]
        bfp = sb("bfp", [128, 32, T], BF16); Bbfp = [Buf() for _ in range(32)]
        NTF = 12
        tf_t = sb("tf_t", [128, NTF, T + 1], F32); Btf = [Buf() for _ in range(NTF)]
        NTB = 10
        tb_t = sb("tb_t", [128, NTB, T], BF16); Btb = [Buf() for _ in range(NTB)]
        tfi = [0]; tbi = [0]

        def tmpf():
            i = tfi[0] % NTF; tfi[0] += 1
            return tf_t[:, i, :], Btf[i]

        def tmpb():
            i = tbi[0] % NTB; tbi[0] += 1
            return tb_t[:, i, :], Btb[i]

        S32 = sb("S32", [128, L, 8, 64], F32); BS32 = [[Buf() for _ in range(8)] for _ in range(L)]
        Sbd = sb("Sbd", [128, 8, 128], BF16); BSbd = [Buf() for _ in range(8)]
        prev = sb("prev", [128, L, 26], F32); Bprev = [Buf() for _ in range(L)]
        vec_t = sb("vec_t", [128, L, VW], F32); Bvec = Buf()
        fing_t = sb("fing_t", [128, 8], F32); Bfing = Buf()
        lw = sb("lw", [128, T], BF16); Blw = Buf(); sgl = sb("sgl", [128, T], BF16); Bsgl = Buf()
        smallw = sb("smallw", [128, 3072], BF16); Bsmall = Buf()
        bsb = sb("bsb", [128, 1024], F32); Bbsb = Buf()
        identf = sb("identf", [128, 128], F32); identb = sb("identb", [128, 128], BF16)
        onesb = sb("onesb", [128, 128], BF16); blkb = sb("blkb", [128, 128], BF16)
        onesf = sb("onesf", [128, 128], F32)
        msu = sb("msu", [128, 128], F32); miu = sb("miu", [128, 128], F32); msl = sb("msl", [128, 128], F32)
        cmask = sb("cmask", [128, T], F32)
        epsc = sb("epsc", [128, 4], F32)
        Bconst = Buf()
        vT = sb("vT", [128, 4, 128], BF16); kT = sb("kT", [128, 4, 128], BF16); bT = sb("bT", [128, 4, 128], BF16)
        BvT, BkT, BbT = Buf(), Buf(), Buf()
        NQt = [sb(f"NQt{i}", [128, 2, 2, 128], BF16) for i in range(4)]; BNQ = [Buf() for _ in range(4)]
        MTt = [sb(f"MTt{i}", [128, 2, 2, 128], BF16) for i in range(4)]; BMT = [Buf() for _ in range(4)]
        Xbt = [sb(f"Xbt{i}", [128, 2, 2, 128], BF16) for i in range(4)]; BXb = [Buf() for _ in range(4)]
        m2d = sb("m2d", [128, 7, 2, 128], BF16)
        AKt = [sb(f"AKt{i}", [128, 2, 128], BF16) for i in range(4)]; BAK = [Buf() for _ in range(4)]
        RRt = [sb(f"RRt{i}", [128, 4, 128], BF16) for i in range(4)]; BRR = [Buf() for _ in range(4)]
        Zb = [sb(f"Zb{i}", [128, 128], BF16) for i in range(2)]; BZb = [Buf(), Buf()]
        Ub = [sb(f"Ub{i}", [128, 128], BF16) for i in range(2)]; BUb = [Buf(), Buf()]
        Swt = [sb(f"Swt{i}", [128, 64], F32) for i in range(2)]; BSw = [Buf(), Buf()]
        wc_t = sb("wc_t", [128, 4], F32); Bwc = Buf()
        yTM = sb("yTM", [128, 4, 128], F32); ByTM = Buf()
        ysq = sb("ysq", [128, 4, 128], F32); Bysq = Buf()
        ynb = sb("ynb", [128, 4, 128], BF16); Bynb = Buf()
        gst = sb("gst", [128, 4, 8], F32); Bgst = Buf()

        NB = 7
        pb = [ps(f"pb{i}", [128, 512], F32) for i in range(NB)]; Bpb = [Buf() for _ in range(NB)]
        pT = ps("pT", [128, 1024], BF16); _bpt = Buf(); BpT = [_bpt, _bpt]
        psum_bufs = {f"pb{i}": [Bpb[i]] for i in range(NB)}
        psum_bufs["pT"] = [_bpt]
        bki = [0]

        def bank():
            i = bki[0] % NB; bki[0] += 1
            return pb[i], Bpb[i]

        pti = [0]

        def ptbank():
            i = pti[0] % 2; pti[0] += 1
            return pT[:, i * 512:(i + 1) * 512], BpT[i]

        gcur = ['']

        def checkpoint(name):
            if STOP == name or STOP == gcur[0] + ':' + name:
                raise _Stop()

        def finish():
            for k_, v_ in list(rec.dcnt.items()):
                rec._wait('sp', (k_, v_))
            rec.emit(st)

        try:
            memset('pool', onesf[:], 1.0, [Bconst])
            memset('pool', identf[:], 0.0, [Bconst])
            rec.op('pool', lambda e: e.affine_select(out=identf[:], in_=identf[:], pattern=[[-1, 128]], compare_op=ALU.not_equal, fill=1.0, base=0, channel_multiplier=1), reads=[Bconst], writes=[Bconst])
            cp('pool', identb[:], identf[:], [Bconst], [Bconst])
            cp('pool', onesb[:], onesf[:], [Bconst], [Bconst])
            memset('pool', blkb[:], 0.0, [Bconst])
            memset('pool', blkb[0:64, 0:64], 1.0, [Bconst])
            memset('pool', blkb[64:128, 64:128], 1.0, [Bconst])
            rec.op('pool', lambda e: e.affine_select(out=msu[:], in_=onesf[:], pattern=[[1, 128]], compare_op=ALU.is_gt, fill=0.0, base=0, channel_multiplier=-1), reads=[Bconst], writes=[Bconst])
            rec.op('pool', lambda e: e.affine_select(out=miu[:], in_=onesf[:], pattern=[[1, 128]], compare_op=ALU.is_ge, fill=0.0, base=0, channel_multiplier=-1), reads=[Bconst], writes=[Bconst])
            rec.op('pool', lambda e: e.affine_select(out=msl[:], in_=onesf[:], pattern=[[-1, 128]], compare_op=ALU.is_gt, fill=0.0, base=0, channel_multiplier=1), reads=[Bconst], writes=[Bconst])
            Ea, Eb, Ec = tf_t[:, 0, 0:128], tf_t[:, 1, 0:128], tf_t[:, 2, 0:128]
            BE_ = [Btf[0], Btf[1], Btf[2]]
            cp('pool', Ea, identf[:], [Bconst], [BE_[0]])
            Ecur, Bcur, Enxt, Bnxt = Ea, BE_[0], Eb, BE_[1]
            for j in range(7):
                s_ = 2 ** (j + 1); nb_ = 128 // s_
                if j == 6:
                    cp('pool', Enxt, onesf[:], [Bconst], [Bnxt])
                else:
                    def sel1(e, out=Enxt, s_=s_, nb_=nb_):
                        return e.affine_select(out=out.rearrange("p (a b) -> p a b", b=s_), in_=onesf[:].rearrange("p (a b) -> p a b", b=s_),
                                               pattern=[[-s_, nb_], [0, s_]], compare_op=ALU.is_ge, fill=0.0, base=0, channel_multiplier=1)

                    def sel2(e, out=Enxt, s_=s_, nb_=nb_):
                        return e.affine_select(out=out.rearrange("p (a b) -> p a b", b=s_), in_=out.rearrange("p (a b) -> p a b", b=s_),
                                               pattern=[[s_, nb_], [0, s_]], compare_op=ALU.is_ge, fill=0.0, base=s_ - 1, channel_multiplier=-1)
                    rec.op('pool', sel1, reads=[Bconst], writes=[Bnxt])
                    rec.op('pool', sel2, reads=[Bnxt], writes=[Bnxt])
                tt('pool', Ec, Enxt, Ecur, ALU.subtract, [Bnxt, Bcur], [BE_[2]])
                tt('pool', m2d[:, j, 0, :], Ec, msu[:], ALU.mult, [BE_[2], Bconst], [Bconst])
                tt('pool', m2d[:, j, 1, :], Ec, msl[:], ALU.mult, [BE_[2], Bconst], [Bconst])
                Ecur, Bcur, Enxt, Bnxt = Enxt, Bnxt, Ecur, Bcur
            memset('pool', epsc[:, 0:1], 1e-6, [Bconst])
            memset('pool', epsc[:, 1:2], 1e-5, [Bconst])
            memset('pool', epsc[:, 2:3], 64e-5, [Bconst])
            dma('pool', 'W', vec_t[:], vecs, writes=[Bvec])
            dma('pool', 'W', fing_t[:], fing, writes=[Bfing])
            ts('dve', vec_t[:, :, V_OMM:V_OMM + 26], vec_t[:, :, V_MU:V_MU + 26], -1.0, 1.0, ALU.mult, ALU.add, [Bvec], [Bvec])
            ts('dve', vec_t[:, :, V_OMKA:V_OMKA + 8], vec_t[:, :, V_KA:V_KA + 8], -1.0, 1.0, ALU.mult, ALU.add, [Bvec], [Bvec])

            def vcol(l, base, c):
                return vec_t[:, l, base + c:base + c + 1]

            checkpoint('const')
            Bwsc = Buf()
            Bsm = [[] for _ in range(L)]
            Bkv = [[[], []] for _ in range(L)]

            def conv(l, pi, src2d, ncols, col0=0):
                dst = wsc[l, pi].rearrange("p (kc n) -> p kc n", kc=8)[:, :, col0:col0 + ncols]
                dma('pool', 'cv', dst, src2d.rearrange("(kc p) n -> p kc n", p=128), writes=[Bwsc])

            for l in range(L):
                conv(l, PI_LORA, w_in[l][:, 3072:3328], 256)
                for c in range(8):
                    for j in range(3):
                        conv(l, PI_RKV + c, w_in[l][:, j * 1024 + c * 128: j * 1024 + (c + 1) * 128], 128, col0=j * 128)
                for i in range(4):
                    conv(l, PI_SGU + i, w_in[l][:, 3328 + i * 512: 3328 + (i + 1) * 512], 512)
                for i in range(2):
                    conv(l, PI_Q + i, w_in[l][:, 5376 + i * 512: 5376 + (i + 1) * 512], 512)
                for i in range(6):
                    conv(l, PI_GATE + i, w_in[l][:, 6400 + i * 512: 6400 + (i + 1) * 512], 512)
                for b in range(3):
                    for hf in range(2):
                        conv(l, PI_BR + b * 2 + hf, w_br[l, b][:, hf * 512:(hf + 1) * 512], 512)
                for hf in range(2):
                    conv(l, PI_OUT + hf, w_out[l][:, hf * 512:(hf + 1) * 512], 512)
                for i in range(8):
                    conv(l, PI_UP + i, w_up[l][:, i * 512:(i + 1) * 512], 512)
                for g in range(2):
                    for q in range(4):
                        conv(l, PI_DN + g * 4 + q, w_dn[l][q * 1024:(q + 1) * 1024, g * 512:(g + 1) * 512], 512)
                for i in range(4):
                    conv(l, PI_MKV + i, w_mkv[l][:, i * 512:(i + 1) * 512], 512)
                dma('pool', 'cv', smsc[l][0:64, 1024:2048], w2[l], writes=[Bwsc])
                dma('pool', 'cv', smsc[l][64:128, 1024:2048], a2[l], writes=[Bwsc])
                dma('pool', 'cv', smsc[l][:, 2048:3072], g2[l], writes=[Bwsc])
                dma('pool', 'cv', kvsc[l, 1][:, 2048:4096].rearrange("p (mb n) -> p mb n", mb=2),
                    cv[l].rearrange("(mb p) n -> p mb n", p=128), writes=[Bwsc])
            checkpoint('conv')
            for l in range(L):
                wsf, Bw_ = big[:, 0:2, :], [Bbig[0], Bbig[1]]
                dma('pool', 'W', big[:, 0:2, :].rearrange("p a (g j) -> p (a g) j", j=128), w_s[l].rearrange("g i j -> i g j"), writes=Bw_)
                for g in range(8):
                    if g % 4 == 0:
                        stg, Bstg = tmpb()
                    bk, Bbk = bank()
                    tr(bk[:, 0:128], big[:, g // 4, (g % 4) * 128:(g % 4 + 1) * 128], identf[:], reads=Bw_ + [Bconst], writes=[Bbk])
                    tt('dve', stg[:, (g % 4) * 128:(g % 4 + 1) * 128], bk[:, 0:128], miu[:], ALU.mult, [Bbk, Bconst], [Bstg])
                    if g % 4 == 3:
                        Bd = Buf(); Bsm[l].append(Bd)
                        dma('pool', 'R', smsc[l][:, (g // 4) * 512:(g // 4 + 1) * 512], stg, reads=[Bstg], writes=[Bd])
            for l in range(L):
                Bk_ = [Bbig[i] for i in range(4)]
                dma('pool', 'W', big[:, 0:4, :].rearrange("p (mb a) n -> p mb (a n)", mb=2), ck[l].rearrange("(mb p) n -> p mb n", p=128), writes=Bk_)
                for j in range(8):
                    if j % 2 == 0:
                        stg, Bstg = tmpb()
                    bk, Bbk = bank()
                    for mb in range(2):
                        tr(bk[:, mb * 128:(mb + 1) * 128], big[:, mb * 2 + j // 4, (j % 4) * 128:(j % 4 + 1) * 128], identf[:],
                           reads=Bk_ + [Bconst], writes=[Bbk] if mb == 1 else None)
                    cp('act', stg[:, (j % 2) * 256:(j % 2 + 1) * 256], bk[:, 0:256], [Bbk], [Bstg])
                    if j % 2 == 1:
                        Bd = Buf(); Bkv[l][1].append(Bd)
                        dma('pool', 'R', kvsc[l, 1][:, (j - 1) * 256:(j + 1) * 256], stg, reads=[Bstg], writes=[Bd])

            checkpoint('prep')
            uses = []
            state = {'k': 0, 'loaded': 0}

            def plan_tile_layer(l, grp):
                seq = [('w', l, PI_LORA)] + [('w', l, PI_RKV + c) for c in range(8)] + [('w', l, PI_SGU + i) for i in range(4)]
                seq += [('w', l, PI_Q), ('w', l, PI_Q + 1), ('kv', l, grp)]
                for b in range(3):
                    for hf in range(2):
                        seq += [('w', l, PI_GATE + b * 2 + hf), ('w', l, PI_BR + b * 2 + hf)]
                seq += [('w', l, PI_OUT), ('w', l, PI_OUT + 1)] + [('w', l, PI_UP + i) for i in range(8)]
                seq += [('w', l, PI_DN + i) for i in range(8)]
                return seq

            for l in range(L):
                uses += [('w', l, PI_MKV + i) for i in range(4)]
            for l in range(L):
                uses += plan_tile_layer(l, 1)
            for t_ in range(NT):
                for l in range(L):
                    uses += plan_tile_layer(l, 0)
            def _issue_load(k):
                kind, l, idx = uses[k]
                slot = k % NS
                if kind == 'w':
                    ncols = 256 if idx == PI_LORA else (384 if PI_RKV <= idx < PI_RKV + 8 else 512)
                    if ncols == 512:
                        dma('sp', f'w{slot}', ws_t[slot][:], wsc[l, idx], reads=[Bwsc], writes=[Bws[slot]])
                    else:
                        dma('sp', f'w{slot}', ws_t[slot][:].rearrange("p (kc n) -> p kc n", kc=8)[:, :, 0:ncols],
                            wsc[l, idx].rearrange("p (kc n) -> p kc n", kc=8)[:, :, 0:ncols], reads=[Bwsc], writes=[Bws[slot]])
                else:
                    dma('sp', f'w{slot}', ws_t[slot][:], kvsc[l, idx], reads=[Bwsc] + Bkv[l][idx], writes=[Bws[slot]])

            def piece(kind, l, idx):
                k = state['k']
                assert uses[k] == (kind, l, idx), (uses[k], kind, l, idx)
                while state['loaded'] < min(len(uses), k + NS - 1):
                    _issue_load(state['loaded']); state['loaded'] += 1
                state['k'] += 1
                slot = k % NS
                return ws_t[slot], Bws[slot]

            def wpiece(l, idx):
                t_, B_ = piece('w', l, idx)
                return t_[:].rearrange("p (kc n) -> p kc n", kc=8), B_

            def rmsnorm_to_h(l, gbase, TT):
                bk, Bbk = bank()
                for c in range(8):
                    sq, Bsq = tmpb()
                    act(sq[:, :TT], x_t[:, c, :TT], AF.Square, [Bx[c]], [Bsq])
                    mm(bk[:, :TT], onesb[:], sq[:, :TT], c == 0, c == 7, [Bsq, Bconst], [Bbk] if c == 7 else None)
                rs, Brs = tmpf()
                act(rs[:, :TT], bk[:, :TT], AF.Sqrt, [Bbk, Bconst], [Brs], scale=1.0 / D, bias=epsc[:, 0:1])
                recip(rs[:, :TT], rs[:, :TT], [Brs], [Brs])
                for c in range(8):
                    gcol = vcol(l, gbase, c) if l is not None else fing_t[:, c:c + 1]
                    yield_out = h_t[:, c, :TT]
                    stt(yield_out, x_t[:, c, :TT], gcol, rs[:, :TT], ALU.mult, ALU.mult, [Bx[c], Brs, Bvec], [Bh[c]])

            def big_mm(wt, Bw, oc4, rhs_list, Brhs, TT, out_bk=None, start=True, stop=True, kc0=0):
                if out_bk is None:
                    bk, Bbk = bank()
                else:
                    bk, Bbk = out_bk
                n = len(rhs_list)
                for kc in range(n):
                    last = (kc == n - 1)
                    mm(bk[:, :TT], wt[:, kc, oc4 * 128:(oc4 + 1) * 128], rhs_list[kc], start and kc == 0, stop and last,
                       [Bw, Brhs[kc]], [Bbk] if last else None)
                return bk, Bbk

            def hrhs(TT):
                return [h_t[:, kc, :TT] for kc in range(8)]

            def rwkv_pair(l, c, TT, C, NSTEP):
                NC_ = TT // C
                wt, Bw = wpiece(l, PI_RKV + c)
                w2a2 = smallw[:, 1024:2048]; g2t = smallw[:, 2048:3072]
                F = lambda i: (tf_t[:, i, :], Btf[i])
                Bq = lambda i: (tb_t[:, i, :], Btb[i])
                CB = [F(0), F(1)]
                (r_, Br), (k_, Bk), (v_, Bv) = F(2), F(3), F(4)
                (sig, Bsig), (a_, Ba), (kk, Bkk), (Lc, BL), (E, BE), (E2, BE2), (t1, Bt1) = F(5), F(6), F(7), F(8), F(9), F(10), F(11)
                (g_, Bg), (sq, Bsq), (rt, Brt), (kt, Bkt), (bt, Bbt), (at, Bat), (pr, Bpr), (vb, Bvb) = [Bq(i) for i in range(8)]
                mixed = [(r_, Br), (k_, Bk), (v_, Bv)]
                for j in range(3):
                    col = j * 8 + c
                    bk, Bbk = big_mm(wt, Bw, j, hrhs(TT), Bh, TT)
                    cb, Bcb = CB[j % 2]
                    mx, Bmx = mixed[j]
                    cp('act', cb[:, 1:TT + 1], bk[:, :TT], [Bbk], [Bcb])
                    cp('pool', cb[:, 0:1], prev[:, l, col:col + 1], [Bprev[l]], [Bcb])
                    ts('dve', t1[:, :TT], cb[:, 0:TT], vcol(l, V_MU, col), None, ALU.mult, None, [Bcb, Bvec], [Bt1])
                    stt(mx[:, :TT], cb[:, 1:TT + 1], vcol(l, V_OMM, col), t1[:, :TT], ALU.mult, ALU.add, [Bcb, Bt1, Bvec], [Bmx])
                    cp('pool', prev[:, l, col:col + 1], cb[:, TT:TT + 1], [Bcb], [Bprev[l]])
                bk, Bbk = bank()
                mm(bk[:, :TT], w2a2[0:64, c * 128:(c + 1) * 128], lw[0:64, :TT], True, True, [Bsmall, Blw], [Bbk])
                act(sig[:, :TT], bk[:, :TT], AF.Sigmoid, [Bbk, Bvec], [Bsig], bias=vcol(l, V_W0, c))
                bk, Bbk = bank()
                mm(bk[:, :TT], w2a2[64:128, c * 128:(c + 1) * 128], lw[64:128, :TT], True, True, [Bsmall, Blw], [Bbk])
                act(a_[:, :TT], bk[:, :TT], AF.Sigmoid, [Bbk, Bvec], [Ba], bias=vcol(l, V_A0, c))
                bk, Bbk = bank()
                mm(bk[:, :TT], g2t[:, c * 128:(c + 1) * 128], sgl[:, :TT], True, True, [Bsmall, Bsgl], [Bbk])
                cp('act', g_[:, :TT], bk[:, :TT], [Bbk], [Bg])
                ts('pool', kk[:, :TT], k_[:, :TT], vcol(l, V_KK, c), None, ALU.mult, None, [Bk, Bvec], [Bkk])
                act(sq[:, :TT], kk[:, :TT], AF.Square, [Bkk], [Bsq])
                bk, Bbk = bank()
                mm(bk[:, :TT], blkb[:], sq[:, :TT], True, True, [Bconst, Bsq], [Bbk])
                rn, Brn = E2, BE2
                act(rn[:, :TT], bk[:, :TT], AF.Sqrt, [Bbk], [Brn])
                ts('dve', rn[:, :TT], rn[:, :TT], 1e-12, None, ALU.max, None, [Brn], [Brn])
                recip(rn[:, :TT], rn[:, :TT], [Brn], [Brn])
                tt('pool', kk[:, :TT], kk[:, :TT], rn[:, :TT], ALU.mult, [Bkk, Brn], [Bkk])
                ts('dve', t1[:, :TT], a_[:, :TT], vcol(l, V_KA, c), vcol(l, V_OMKA, c), ALU.mult, ALU.add, [Ba, Bvec], [Bt1])
                tt('pool', k_[:, :TT], k_[:, :TT], t1[:, :TT], ALU.mult, [Bk, Bt1], [Bk])
                scan_mask = cmask[:, :TT] if C == 128 else onesf[:, :TT]
                rec.op('dve', lambda e: e.tensor_tensor_scan(out=Lc[:, :TT], data0=scan_mask, data1=sig[:, :TT],
                                                             initial=0.0, op0=ALU.mult, op1=ALU.add), reads=[Bsig, Bconst], writes=[BL])
                act(E[:, :TT], Lc[:, :TT], AF.Exp, [BL], [BE], scale=C0)
                tt('dve', rt[:, :TT], r_[:, :TT], E[:, :TT], ALU.mult, [Br, BE], [Brt])
                cp('pool', wc_t[:, 0:NC_], E[:, :TT].rearrange("p (a b) -> p a b", b=C)[:, :, C - 1], [BE], [Bwc])
                act(E2[:, :TT], Lc[:, :TT], AF.Exp, [BL], [BE2], scale=-C0)
                tt('dve', kt[:, :TT], k_[:, :TT], E2[:, :TT], ALU.mult, [Bk, BE2], [Bkt])
                tt('pool', t1[:, :TT], kk[:, :TT], a_[:, :TT], ALU.mult, [Bkk, Ba], [Bt1])
                tt('dve', bt[:, :TT], t1[:, :TT], E2[:, :TT], ALU.mult, [Bt1, BE2], [Bbt])
                tt('pool', Lc[:, :TT], Lc[:, :TT], sig[:, :TT], ALU.subtract, [BL, Bsig], [BL])
                act(E[:, :TT], Lc[:, :TT], AF.Exp, [BL], [BE], scale=C0)
                stt(at[:, :TT], kk[:, :TT], -1.0, E[:, :TT], ALU.mult, ALU.mult, [Bkk, BE], [Bat])
                stt(pr[:, :TT], r_[:, :TT], vcol(l, V_RK, c), k_[:, :TT], ALU.mult, ALU.mult, [Br, Bk, Bvec], [Bpr])
                bk, Bbk = bank()
                mm(bk[:, :TT], blkb[:], pr[:, :TT], True, True, [Bconst, Bpr], [Bbk])
                bv, Bbv = sig, Bsig
                tt('dve', bv[:, :TT], bk[:, :TT], v_[:, :TT], ALU.mult, [Bbk, Bv], [Bbv])
                cp('pool', vb[:, :TT], v_[:, :TT], [Bv], [Bvb])
                checkpoint('pelem')
                for (src, Bsrc, dst, Bdst) in ((vb, Bvb, vT, BvT), (kt, Bkt, kT, BkT), (bt, Bbt, bT, BbT)):
                    pt, Bpt = ptbank()
                    for ci in range(NC_):
                        tr(pt[0:C, ci * 128:(ci + 1) * 128], src[:, ci * C:(ci + 1) * C], identb[:], [Bsrc, Bconst], [Bpt] if ci == NC_ - 1 else None)
                    cp('act', dst[0:C, 0:NC_, :], pt[0:C, 0:NC_ * 128].rearrange("p (a b) -> p a b", b=128), [Bpt], [Bdst])
                checkpoint('ptm')
                memset('pool', Sbd[:, c, :], 0.0, [BSbd[c]])
                for hd in range(2):
                    cp('pool', Sbd[hd * 64:(hd + 1) * 64, c, hd * 64:(hd + 1) * 64], S32[hd * 64:(hd + 1) * 64, l, c, :], [BS32[l][c]], [BSbd[c]])
                checkpoint('psbd')
                hr = lambda hd: slice(hd * 64, (hd + 1) * 64)
                NL = C.bit_length() - 1
                for ci in range(NC_):
                    cs = slice(ci * C, (ci + 1) * C)
                    NQ, MT, AK, RR = NQt[ci], MTt[ci], AKt[ci], RRt[ci]
                    for hd in range(2):
                        bkX, BX = bank()
                        X3 = bkX[0:C, :].rearrange("p (a b) -> p a b", b=128)
                        mm(X3[:, 0, 0:C], bt[hr(hd), cs], at[hr(hd), cs], True, True, [Bbt, Bat])
                        mm(X3[:, 1, 0:C], kt[hr(hd), cs], at[hr(hd), cs], True, True, [Bkt, Bat])
                        mm(X3[:, 2, 0:C], bt[hr(hd), cs], rt[hr(hd), cs], True, True, [Bbt, Brt])
                        mm(X3[:, 3, 0:C], kt[hr(hd), cs], rt[hr(hd), cs], True, True, [Bkt, Brt], [BX])
                        bkY, BY = bank()
                        mm(bkY[0:C, 0:C], at[hr(hd), cs], bt[hr(hd), cs], True, True, [Bbt, Bat], [BY])
                        tt('dve', NQ[0:C, 0, hd, 0:C], X3[:, 0, 0:C], msu[0:C, 0:C], ALU.mult, [BX, Bconst], [BNQ[ci]])
                        tt('dve', AK[0:C, hd, 0:C], X3[:, 1, 0:C], msu[0:C, 0:C], ALU.mult, [BX, Bconst], [BAK[ci]])
                        tt('dve', RR[0:C, hd:4:2, 0:C], X3[:, 2:4, 0:C], miu[0:C, 0:C].unsqueeze(1).to_broadcast([C, 2, C]), ALU.mult, [BX, Bconst], [BRR[ci]])
                        tt('dve', NQ[0:C, 1, hd, 0:C], bkY[0:C, 0:C], msl[0:C, 0:C], ALU.mult, [BY, Bconst], [BNQ[ci]])
                    cp('pool', MT[0:C].rearrange("p f h q -> p (f h) q")[:, :, 0:C], identb[0:C, 0:C].unsqueeze(1).to_broadcast([C, 4, C]), [Bconst], [BMT[ci]])
                checkpoint('patype')
                for j in range(NL):
                    last = j == NL - 1
                    nf = 1 if last else 2
                    xs_ = []
                    for ci in range(NC_):
                        NQ, MT = NQt[ci], MTt[ci]
                        bkX, BX = bank()
                        Xv = bkX[0:C, :].rearrange("p (f h q) -> p f h q", f=2, h=2)
                        for hd in range(2):
                            mm(Xv[:, 0, hd, 0:C], NQ[0:C, 1, hd, 0:C], MT[0:C, 0, hd, 0:C], True, True, [BNQ[ci], BMT[ci]], [BX] if (last and hd == 1) else None)
                            if not last:
                                mm(Xv[:, 1, hd, 0:C], NQ[0:C, 0, hd, 0:C], MT[0:C, 1, hd, 0:C], True, True, [BNQ[ci], BMT[ci]], [BX] if hd == 1 else None)
                        xs_.append((Xv, BX))
                    ys_ = []
                    for ci in range(NC_):
                        MT, XB_ = MTt[ci], Xbt[ci]
                        Xv, BX = xs_[ci]
                        tt('dve', XB_[0:C, 0:nf, :, 0:C], Xv[:, 0:nf, :, 0:C], m2d[0:C, j, 0:nf, 0:C].unsqueeze(2).to_broadcast([C, nf, 2, C]), ALU.mult,
                           [BX, Bconst], [BXb[ci]])
                        bkY, BY = bank()
                        Yv = bkY[0:C, :].rearrange("p (f h q) -> p f h q", f=2, h=2)
                        for hd in range(2):
                            mm(Yv[:, 0, hd, 0:C], MT[0:C, 1, hd, 0:C], XB_[0:C, 0, hd, 0:C], True, True, [BMT[ci], BXb[ci]], [BY] if (last and hd == 1) else None)
                            if not last:
                                mm(Yv[:, 1, hd, 0:C], MT[0:C, 0, hd, 0:C], XB_[0:C, 1, hd, 0:C], True, True, [BMT[ci], BXb[ci]], [BY] if hd == 1 else None)
                        ys_.append((Yv, BY))
                    for ci in range(NC_):
                        MT = MTt[ci]
                        Yv, BY = ys_[ci]
                        tt('dve', MT[0:C, 0:nf, :, 0:C], Yv[:, 0:nf, :, 0:C], MT[0:C, 0:nf, :, 0:C], ALU.add, [BY, BMT[ci]], [BMT[ci]])
                for ci in range(NC_):
                    par = ci % 2
                    cs = slice(ci * C, (ci + 1) * C)
                    AK, RR, MT = AKt[ci], RRt[ci], MTt[ci]
                    checkpoint('pdbl')
                    ts('pool', Swt[par][:, :], S32[:, l, c, :], wc_t[:, ci:ci + 1], None, ALU.mult, None, [BS32[l][c], Bwc], [BSw[par]])
                    bkW, BW = bank()
                    mm(bkW[0:C, 0:128], at[:, cs], Sbd[:, c, :], True, False, [Bat, BSbd[c]])
                    for hd in range(2):
                        mm(bkW[0:C, hr(hd)], AK[0:C, hd, 0:C], vT[0:C, ci, hr(hd)], False, hd == 1, [BAK[ci], BvT], [BW] if hd == 1 else None)
                    cp('act', Zb[par][0:C, :], bkW[0:C, 0:128], [BW], [BZb[par]])
                    bkU, BU = bank()
                    for hd in range(2):
                        mm(bkU[0:C, hr(hd)], MT[0:C, 0, hd, 0:C], Zb[par][0:C, hr(hd)], True, True, [BMT[ci], BZb[par]], [BU] if hd == 1 else None)
                    cp('act', Ub[par][0:C, :], bkU[0:C, 0:128], [BU], [BUb[par]])
                    bkY, BY = bank()
                    mm(bkY[0:C, 0:128], rt[:, cs], Sbd[:, c, :], True, False, [Brt, BSbd[c]])
                    for hd in range(2):
                        mm(bkY[0:C, hr(hd)], RR[0:C, hd, 0:C], Ub[par][0:C, hr(hd)], False, False, [BRR[ci], BUb[par]])
                        mm(bkY[0:C, hr(hd)], RR[0:C, 2 + hd, 0:C], vT[0:C, ci, hr(hd)], False, hd == 1, [BRR[ci], BvT], [BY] if hd == 1 else None)
                    cp('act', yTM[0:C, ci, :], bkY[0:C, 0:128], [BY], [ByTM])
                    bkS, BS = bank()
                    mm(bkS[:, 0:128], bT[0:C, ci, :], Ub[par][0:C, :], True, False, [BbT, BUb[par]])
                    mm(bkS[:, 0:128], kT[0:C, ci, :], vT[0:C, ci, :], False, True, [BkT, BvT], [BS])
                    for hd in range(2):
                        stt(S32[hr(hd), l, c, :], bkS[hr(hd), hr(hd)], wc_t[hr(hd), ci:ci + 1], Swt[par][hr(hd), :], ALU.mult, ALU.add,
                            [BS, Bwc, BSw[par]], [BS32[l][c]])
                    if ci < NC_ - 1:
                        for hd in range(2):
                            cp('pool', Sbd[hr(hd), c, hr(hd)], S32[hr(hd), l, c, :], [BS32[l][c]], [BSbd[c]])
                checkpoint('pseq')
                NG = NC_ * 2
                y3 = yTM[0:C, 0:NC_, :].rearrange("p a (h v) -> p (a h) v", v=64)
                rec.op('dve', lambda e: e.tensor_reduce(out=gst[0:C, 0, 0:NG], in_=y3, axis=AX.X, op=ALU.add), reads=[ByTM], writes=[Bgst])
                tt('pool', ysq[0:C, 0:NC_, :], yTM[0:C, 0:NC_, :], yTM[0:C, 0:NC_, :], ALU.mult, [ByTM], [Bysq])
                s3 = ysq[0:C, 0:NC_, :].rearrange("p a (h v) -> p (a h) v", v=64)
                rec.op('dve', lambda e: e.tensor_reduce(out=gst[0:C, 1, 0:NG], in_=s3, axis=AX.X, op=ALU.add), reads=[Bysq], writes=[Bgst])
                ts('dve', gst[0:C, 2, 0:NG], gst[0:C, 0, 0:NG], 1.0 / 64, None, ALU.mult, None, [Bgst], [Bgst])
                tt('dve', gst[0:C, 3, 0:NG], gst[0:C, 2, 0:NG], gst[0:C, 2, 0:NG], ALU.mult, [Bgst], [Bgst])
                stt(gst[0:C, 3, 0:NG], gst[0:C, 1, 0:NG], 1.0 / 64, gst[0:C, 3, 0:NG], ALU.mult, ALU.subtract, [Bgst], [Bgst])
                act(gst[0:C, 3, 0:NG], gst[0:C, 3, 0:NG], AF.Sqrt, [Bgst, Bconst], [Bgst], bias=epsc[0:C, 2:3])
                recip(gst[0:C, 3, 0:NG], gst[0:C, 3, 0:NG], [Bgst], [Bgst])
                tt('dve', ysq[0:C, 0:NC_, :].rearrange("p a (h v) -> p (a h) v", v=64), y3,
                   gst[0:C, 2, 0:NG].unsqueeze(2).to_broadcast([C, NG, 64]), ALU.subtract, [ByTM, Bgst], [Bysq])
                tt('dve', ynb[0:C, 0:NC_, :].rearrange("p a (h v) -> p (a h) v", v=64), s3,
                   gst[0:C, 3, 0:NG].unsqueeze(2).to_broadcast([C, NG, 64]), ALU.mult, [Bysq, Bgst], [Bynb])
                pt, Bpt = ptbank()
                for ci in range(NC_):
                    tr(pt[:, ci * C:(ci + 1) * C], ynb[0:C, ci, :], identb[0:C, 0:C], [Bynb, Bconst], [Bpt] if ci == NC_ - 1 else None)
                y1, By1 = E2, BE2
                act(y1[:, :TT], pt[:, :TT], AF.Identity, [Bpt, Bvec], [By1], scale=vcol(l, V_LXG, c), bias=vcol(l, V_LXB, c))
                tt('pool', y1[:, :TT], y1[:, :TT], bv[:, :TT], ALU.add, [By1, Bbv], [By1])
                tt('dve', bfp[:, c, :TT], y1[:, :TT], g_[:, :TT], ALU.mult, [By1, Bg], [Bbfp[c]])

            def tile_layer(l, grp, TT, C, NSTEP):
                gcur[0] = 'sp'[1 - grp] if False else ('p' if grp == 0 else 's')
                NTB_ = max(1, TT // 128)
                TB = min(TT, 128)
                dma('pool', 'W', smallw[:], smsc[l], reads=[Bwsc] + Bsm[l], writes=[Bsmall])
                dma('pool', 'W', bsb[:], b_s[l].unsqueeze(0).partition_broadcast(128).rearrange("p o n -> p (o n)"), writes=[Bbsb])
                rmsnorm_to_h(l, V_MIXG, TT)
                wt, Bw = wpiece(l, PI_LORA)
                for j in range(2):
                    col = 24 + j
                    bk, Bbk = big_mm(wt, Bw, j, hrhs(TT), Bh, TT)
                    cb, Bcb = tmpf()
                    cp('act', cb[:, 1:TT + 1], bk[:, :TT], [Bbk], [Bcb])
                    cp('pool', cb[:, 0:1], prev[:, l, col:col + 1], [Bprev[l]], [Bcb])
                    t1, Bt1 = tmpf()
                    ts('dve', t1[:, :TT], cb[:, 0:TT], vcol(l, V_MU, col), None, ALU.mult, None, [Bcb, Bvec], [Bt1])
                    mx, Bmx = tmpf()
                    stt(mx[:, :TT], cb[:, 1:TT + 1], vcol(l, V_OMM, col), t1[:, :TT], ALU.mult, ALU.add, [Bcb, Bt1, Bvec], [Bmx])
                    cp('pool', prev[:, l, col:col + 1], cb[:, TT:TT + 1], [Bcb], [Bprev[l]])
                    if j == 0:
                        act(lw[0:64, :TT], mx[0:64, :TT], AF.Tanh, [Bmx], [Blw])
                        cp('act', lw[64:128, :TT], mx[64:128, :TT], [Bmx], [Blw])
                    else:
                        act(sgl[:, :TT], mx[:, :TT], AF.Sigmoid, [Bmx], [Bsgl])
                checkpoint('lora')
                for c in range(8):
                    rwkv_pair(l, c, TT, C, NSTEP)
                    checkpoint('pair0')
                checkpoint('rwkv')
                for i in range(4):
                    wt, Bw = wpiece(l, PI_SGU + i)
                    for o4 in range(4):
                        oc = i * 4 + o4
                        bk, Bbk = big_mm(wt, Bw, o4, hrhs(TT), Bh, TT)
                        act(big[:, oc, :TT], bk[:, :TT], AF.Gelu_apprx_tanh, [Bbk], [Bbig[oc]])
                bk1, Bb1 = bank(); bk2, Bb2 = bank()
                for c in range(8):
                    vb, Bvb = tmpb(); vs, Bvs = tmpb()
                    cp('pool', vb[:, :TT], big[:, 8 + c, :TT], [Bbig[8 + c]], [Bvb])
                    act(vs[:, :TT], big[:, 8 + c, :TT], AF.Square, [Bbig[8 + c]], [Bvs])
                    mm(bk1[:, :TT], onesb[:], vb[:, :TT], c == 0, c == 7, [Bvb, Bconst], [Bb1])
                    mm(bk2[:, :TT], onesb[:], vs[:, :TT], c == 0, c == 7, [Bvs, Bconst], [Bb2])
                mean, Bmean = tmpf(); rstd, Brstd = tmpf()
                act(mean[:, :TT], bk1[:, :TT], AF.Copy, [Bb1], [Bmean], scale=1.0 / D)
                tt('pool', rstd[:, :TT], mean[:, :TT], mean[:, :TT], ALU.mult, [Bmean], [Brstd])
                stt(rstd[:, :TT], bk2[:, :TT], 1.0 / D, rstd[:, :TT], ALU.mult, ALU.subtract, [Bb2, Brstd], [Brstd])
                act(rstd[:, :TT], rstd[:, :TT], AF.Sqrt, [Brstd, Bconst], [Brstd], bias=epsc[:, 1:2])
                recip(rstd[:, :TT], rstd[:, :TT], [Brstd], [Brstd])
                vnb = []
                for c in range(8):
                    vc = big[:, 8 + c, :TT]
                    tt('dve', vc, vc, mean[:, :TT], ALU.subtract, [Bbig[8 + c], Bmean], [Bbig[8 + c]])
                    tt('pool', vc, vc, rstd[:, :TT], ALU.mult, [Bbig[8 + c], Brstd], [Bbig[8 + c]])
                    ts('dve', vc, vc, vcol(l, V_SLG, c), vcol(l, V_SLB, c), ALU.mult, ALU.add, [Bbig[8 + c], Bvec], [Bbig[8 + c]])
                    cp('pool', bfp[:, 16 + c, :TT], vc, [Bbig[8 + c]], [Bbfp[16 + c]])
                    vnb.append((bfp[:, 16 + c, :], Bbfp[16 + c]))
                if grp == 1:
                    for half in range(2):
                        bk, Bbk = bank()
                        for j in range(4):
                            tr(bk[0:TT, j * 128:(j + 1) * 128], big[:, 8 + half * 4 + j, :TT], identf[:], [Bbig[8 + half * 4 + j], Bconst], [Bbk] if j == 3 else None)
                        stg, Bstg = tmpf()
                        cp('act', stg[0:TT, 0:512], bk[0:TT, :], [Bbk], [Bstg])
                        dma('pool', 'R', osgu[l][:, half * 512:(half + 1) * 512], stg[0:TT, 0:512], reads=[Bstg])
                vtm = []
                for tb_ in range(NTB_):
                    halves = []
                    for half in range(2):
                        pt, Bpt = ptbank()
                        for j in range(4):
                            cc = half * 4 + j
                            tr(pt[0:TB, j * 128:(j + 1) * 128], vnb[cc][0][:, tb_ * 128: tb_ * 128 + TB], identb[:], [vnb[cc][1], Bconst], [Bpt] if j == 3 else None)
                        vt_, Bvt_ = bfp[:, 24 + tb_ * 2 + half, :], Bbfp[24 + tb_ * 2 + half]
                        cp('act', vt_[0:TB, :], pt[0:TB, :], [Bpt], [Bvt_])
                        halves.append((vt_, Bvt_))
                    vtm.append(halves)
                wsT = smallw[:, 0:1024]
                for c in range(8):
                    bk, Bbk = bank()
                    for tb_ in range(NTB_):
                        vt_, Bvt_ = vtm[tb_][c // 4]
                        mm(bk[:, tb_ * 128: tb_ * 128 + TB], vt_[0:TB, (c % 4) * 128:(c % 4 + 1) * 128], wsT[0:TB, c * 128: c * 128 + TB], True, True,
                           [Bvt_, Bsmall], [Bbk] if tb_ == NTB_ - 1 else None)
                    sv, Bsv = tmpf()
                    if NTB_ > 1:
                        tt('dve', sv[:, :TT].rearrange("p (a b) -> p a b", b=128), bk[:, :TT].rearrange("p (a b) -> p a b", b=128),
                           bsb[:, c * 128:(c + 1) * 128].unsqueeze(1).to_broadcast([128, NTB_, 128]), ALU.add, [Bbk, Bbsb], [Bsv])
                    else:
                        tt('dve', sv[:, :TT], bk[:, :TT], bsb[:, c * 128: c * 128 + TT], ALU.add, [Bbk, Bbsb], [Bsv])
                    tt('pool', bfp[:, 8 + c, :TT], sv[:, :TT], big[:, c, :TT], ALU.mult, [Bsv, Bbig[c]], [Bbfp[8 + c]])
                checkpoint('sgu')
                for i in range(2):
                    wt, Bw = wpiece(l, PI_Q + i)
                    for o4 in range(4):
                        oc = i * 4 + o4
                        bk, Bbk = big_mm(wt, Bw, o4, hrhs(TT), Bh, TT)
                        cp('act', bfp[:, 24 + oc, :TT], bk[:, :TT], [Bbk], [Bbfp[24 + oc]])
                kvt, Bkvp = piece('kv', l, grp)
                KT = kvt[:, 0:2048].rearrange("p (j m) -> p j m", j=8)
                VV = kvt[:, 2048:4096].rearrange("p (mb n) -> p mb n", mb=2)
                for hh in range(4):
                    pTs = []
                    for mc in range(2):
                        bk, Bbk = bank()
                        for dc in range(2):
                            mm(bk[:, :TT], KT[:, 2 * hh + dc, mc * 128:(mc + 1) * 128], bfp[:, 24 + 2 * hh + dc, :TT], dc == 0, dc == 1,
                               [Bkvp, Bbfp[24 + 2 * hh + dc]], [Bbk] if dc == 1 else None)
                        p_, Bp_ = tmpb()
                        act(p_[:, :TT], bk[:, :TT], AF.Exp, [Bbk], [Bp_], scale=1.0 / 16)
                        pTs.append((p_, Bp_))
                    bk, Bbk = bank()
                    for mc in range(2):
                        mm(bk[:, :TT], onesb[:], pTs[mc][0][:, :TT], mc == 0, mc == 1, [pTs[mc][1], Bconst], [Bbk] if mc == 1 else None)
                    rd, Brd = tmpf()
                    recip(rd[:, :TT], bk[:, :TT], [Bbk], [Brd])
                    for dvc in range(2):
                        bk, Bbk = bank()
                        for mc in range(2):
                            mm(bk[:, :TT], VV[:, mc, hh * 256 + dvc * 128: hh * 256 + (dvc + 1) * 128], pTs[mc][0][:, :TT], mc == 0, mc == 1,
                               [Bkvp, pTs[mc][1]], [Bbk] if mc == 1 else None)
                        tt('dve', bfp[:, 16 + 2 * hh + dvc, :TT], bk[:, :TT], rd[:, :TT], ALU.mult, [Bbk, Brd], [Bbfp[16 + 2 * hh + dvc]])
                if DBG and grp == 0 and l == 0:
                    dma('pool', 'R', dbg_y, bfp[:, 0:24, :], reads=[Bbfp[i] for i in range(24)])
                checkpoint('attn')
                for b in range(3):
                    for hf in range(2):
                        wg, Bwg = wpiece(l, PI_GATE + b * 2 + hf)
                        wb_, Bwb = wpiece(l, PI_BR + b * 2 + hf)
                        for o4 in range(4):
                            oc = hf * 4 + o4
                            bk, Bbk = big_mm(wg, Bwg, o4, hrhs(TT), Bh, TT)
                            gt, Bgt = tmpb()
                            act(gt[:, :TT], bk[:, :TT], AF.Sigmoid, [Bbk], [Bgt])
                            rh = [bfp[:, b * 8 + kc, :TT] for kc in range(8)]
                            bk, Bbk = big_mm(wb_, Bwb, o4, rh, [Bbfp[b * 8 + kc] for kc in range(8)], TT)
                            if b == 0:
                                tt('dve', big[:, oc, :TT], bk[:, :TT], gt[:, :TT], ALU.mult, [Bbk, Bgt], [Bbig[oc]])
                            else:
                                tm_, Btm = tmpf()
                                tt('dve', tm_[:, :TT], bk[:, :TT], gt[:, :TT], ALU.mult, [Bbk, Bgt], [Btm])
                                if b == 1:
                                    tt('pool', big[:, oc, :TT], big[:, oc, :TT], tm_[:, :TT], ALU.add, [Bbig[oc], Btm], [Bbig[oc]])
                                else:
                                    tt('pool', bfp[:, 24 + oc, :TT], big[:, oc, :TT], tm_[:, :TT], ALU.add, [Bbig[oc], Btm], [Bbfp[24 + oc]])
                if DBG and grp == 0 and l == 0:
                    dma('pool', 'R', dbg_m, bfp[:, 24:32, :], reads=[Bbfp[24 + i] for i in range(8)])
                for hf in range(2):
                    wt, Bw = wpiece(l, PI_OUT + hf)
                    for o4 in range(4):
                        oc = hf * 4 + o4
                        rh = [bfp[:, 24 + kc, :TT] for kc in range(8)]
                        bk, Bbk = big_mm(wt, Bw, o4, rh, [Bbfp[24 + kc] for kc in range(8)], TT)
                        tt('dve', x_t[:, oc, :TT], x_t[:, oc, :TT], bk[:, :TT], ALU.add, [Bx[oc], Bbk], [Bx[oc]])
                if DBG and grp == 0 and l == 0:
                    dma('pool', 'R', dbg_x, x_t[:], reads=Bx)
                checkpoint('merge')
                rmsnorm_to_h(l, V_FFNG, TT)
                for i in range(8):
                    wt, Bw = wpiece(l, PI_UP + i)
                    for o4 in range(4):
                        oc = i * 4 + o4
                        bk, Bbk = big_mm(wt, Bw, o4, hrhs(TT), Bh, TT)
                        rl, Brl = tmpf()
                        act(rl[:, :TT], bk[:, :TT], AF.Relu, [Bbk], [Brl])
                        tt('pool', bfp[:, oc, :TT], rl[:, :TT], rl[:, :TT], ALU.mult, [Brl], [Bbfp[oc]])
                for g in range(2):
                    banks = [bank() for _ in range(4)]
                    for q in range(4):
                        wt, Bw = wpiece(l, PI_DN + g * 4 + q)
                        for o4 in range(4):
                            rh = [bfp[:, q * 8 + kc, :TT] for kc in range(8)]
                            big_mm(wt, Bw, o4, rh, [Bbfp[q * 8 + kc] for kc in range(8)], TT, out_bk=banks[o4], start=(q == 0), stop=(q == 3))
                    for o4 in range(4):
                        oc = g * 4 + o4
                        tt('dve', x_t[:, oc, :TT], x_t[:, oc, :TT], banks[o4][0][:, :TT], ALU.add, [Bx[oc], banks[o4][1]], [Bx[oc]])

            def dbg_x2_dump():
                if DBG:
                    dma('pool', 'R', dbg_x2, x_t[:], reads=Bx)

            def load_x(src, TT):
                NTB_ = max(1, TT // 128); TB = min(TT, 128)
                for tb_ in range(NTB_):
                    dma('pool', 'W', big[0:TB, 2 * tb_:2 * tb_ + 2, :].rearrange("p a n -> p (a n)"), src[tb_ * 128: tb_ * 128 + TB, :],
                        writes=[Bbig[2 * tb_], Bbig[2 * tb_ + 1]])
                for c in range(8):
                    bk, Bbk = bank()
                    for tb_ in range(NTB_):
                        tr(bk[:, tb_ * 128: tb_ * 128 + TB], big[0:TB, 2 * tb_ + c // 4, (c % 4) * 128:(c % 4 + 1) * 128], identf[0:TB, 0:TB],
                           [Bbig[2 * tb_ + c // 4], Bconst], [Bbk] if tb_ == NTB_ - 1 else None)
                    cp('act', x_t[:, c, :TT], bk[:, :TT], [Bbk], [Bx[c]])

            def store_y(dst, TT):
                NTB_ = max(1, TT // 128); TB = min(TT, 128)
                bk, Bbk = bank()
                for c in range(8):
                    sq, Bsq = tmpb()
                    act(sq[:, :TT], x_t[:, c, :TT], AF.Square, [Bx[c]], [Bsq])
                    mm(bk[:, :TT], onesb[:], sq[:, :TT], c == 0, c == 7, [Bsq, Bconst], [Bbk] if c == 7 else None)
                rs, Brs = tmpf()
                act(rs[:, :TT], bk[:, :TT], AF.Sqrt, [Bbk, Bconst], [Brs], scale=1.0 / D, bias=epsc[:, 0:1])
                recip(rs[:, :TT], rs[:, :TT], [Brs], [Brs])
                for c in range(8):
                    stt(big[:, 8 + c, :TT], x_t[:, c, :TT], fing_t[:, c:c + 1], rs[:, :TT], ALU.mult, ALU.mult, [Bx[c], Brs, Bfing], [Bbig[8 + c]])
                for tb_ in range(NTB_):
                    for half in range(2):
                        bk, Bbk = bank()
                        for j in range(4):
                            cc = half * 4 + j
                            tr(bk[0:TB, j * 128:(j + 1) * 128], big[:, 8 + cc, tb_ * 128: tb_ * 128 + TB], identf[:], [Bbig[8 + cc], Bconst], [Bbk] if j == 3 else None)
                        cp('act', big[0:TB, 2 * tb_ + half, :], bk[0:TB, :], [Bbk], [Bbig[2 * tb_ + half]])
                        dma('pool', 'R', dst[tb_ * 128: tb_ * 128 + TB, half * 512:(half + 1) * 512], big[0:TB, 2 * tb_ + half, :], reads=[Bbig[2 * tb_ + half]])

            def store_state(l, owkv, oshift):
                for c in range(8):
                    bk, Bbk = bank()
                    tr(bk[0:64, 0:128], S32[:, l, c, :], identf[:], [BS32[l][c], Bconst], [Bbk])
                    stg, Bstg = tmpf()
                    cp('act', stg[0:64, 0:128], bk[0:64, 0:128], [Bbk], [Bstg])
                    dma('pool', 'R', owkv[l, 2 * c:2 * c + 2].rearrange("h v k -> v h k"), stg[0:64, 0:128].rearrange("p (h k) -> p h k", h=2), reads=[Bstg])
                bk, Bbk = bank()
                tr(bk[0:26, 0:128], prev[:, l, :], identf[:], [Bprev[l], Bconst], [Bbk])
                stg, Bstg = tmpf()
                cp('act', stg[0:26, 0:128], bk[0:26, 0:128], [Bbk], [Bstg])
                dma('pool', 'R', oshift[l].rearrange("(j p) -> j p", p=128), stg[0:26, 0:128], reads=[Bstg])

            memset('pool', cmask[:], 1.0, [Bconst])
            memset('pool', cmask[:].rearrange("p (a b) -> p a b", b=128)[:, :, 0:1], 0.0, [Bconst])

            Bm_ = [Bbig[i] for i in range(4)]
            dma('pool', 'W', big[:, 0:4, :].rearrange("p (mb a) n -> p mb (a n)", mb=2), memp.rearrange("(mb p) n -> p mb n", p=128), writes=Bm_)
            memf = big[:, 4:8, :].rearrange("p a (b n) -> p (a b) n", b=2)
            Bmemf = [Bbig[i] for i in range(4, 8)]
            for mb in range(2):
                ssq, Bssq = tmpf()
                for a in range(2):
                    jk, Bjk = tmpf()
                    rec.op('act', lambda e, a=a, mb=mb, jk=jk, ssq=ssq: e.activation(out=jk[:, 0:512], in_=big[:, mb * 2 + a, :], func=AF.Square, accum_out=ssq[:, a:a + 1]),
                           reads=[Bbig[mb * 2 + a]], writes=[Bjk, Bssq])
                tt('dve', ssq[:, 2:3], ssq[:, 0:1], ssq[:, 1:2], ALU.add, [Bssq], [Bssq])
                act(ssq[:, 3:4], ssq[:, 2:3], AF.Sqrt, [Bssq, Bconst], [Bssq], scale=1.0 / D, bias=epsc[:, 0:1])
                recip(ssq[:, 4:5], ssq[:, 3:4], [Bssq], [Bssq])
                for a in range(2):
                    ts('dve', big[:, mb * 2 + a, :], big[:, mb * 2 + a, :], ssq[:, 4:5], None, ALU.mult, None, [Bbig[mb * 2 + a], Bssq], [Bbig[mb * 2 + a]])
            for c in range(8):
                bk, Bbk = bank()
                for mb in range(2):
                    tr(bk[:, mb * 128:(mb + 1) * 128], big[:, mb * 2 + c // 4, (c % 4) * 128:(c % 4 + 1) * 128], identf[:], Bm_ + [Bconst], [Bbk] if mb == 1 else None)
                cp('act', memf[:, c, :], bk[:, 0:256], [Bbk], Bmemf)
            for l in range(L):
                mnt = [(bfp[:, 8 + i, :], Bbfp[8 + i]) for i in range(4)]
                for c in range(8):
                    t_, B_ = mnt[c // 2]
                    ts('dve', t_[:, (c % 2) * 256:(c % 2 + 1) * 256], memf[:, c, :], vcol(l, V_MEMG, c), None, ALU.mult, None, Bmemf + [Bvec], [B_])
                mrhs = lambda c: mnt[c // 2][0][:, (c % 2) * 256:(c % 2 + 1) * 256]
                kvst = [(bfp[:, i, :], Bbfp[i]) for i in range(8)]
                for i in range(4):
                    wt, Bw = wpiece(l, PI_MKV + i)
                    for mb in range(2):
                        bk, Bbk = bank()
                        for kc in range(8):
                            mm(bk[:, :], mrhs(kc)[:, mb * 128:(mb + 1) * 128], wt[:, kc, :], kc == 0, kc == 7, [mnt[kc // 2][1], Bw], [Bbk] if kc == 7 else None)
                        stg, Bstg = tmpf()
                        cp('act', stg[:, 0:512], bk[:, :], [Bbk], [Bstg])
                        dst = (omk if i < 2 else omv)[l][mb * 128:(mb + 1) * 128, (i % 2) * 512:(i % 2 + 1) * 512]
                        dma('pool', 'R', dst, stg[:, 0:512], reads=[Bstg])
                        if i >= 2:
                            s_, Bs_ = kvst[4 + mb * 2 + (i - 2)]
                            cp('pool', s_, stg[:, 0:512], [Bstg], [Bs_])
                    if i < 2:
                        for o4 in range(4):
                            j = i * 4 + o4
                            bk, Bbk = bank()
                            for kc in range(8):
                                mm(bk[:, 0:256], wt[:, kc, o4 * 128:(o4 + 1) * 128], mrhs(kc), kc == 0, kc == 7, [mnt[kc // 2][1], Bw], [Bbk] if kc == 7 else None)
                            s_, Bs_ = kvst[j // 2]
                            cp('act', s_[:, (j % 2) * 256:(j % 2 + 1) * 256], bk[:, 0:256], [Bbk], [Bs_])
                for i in range(8):
                    Bd = Buf(); Bkv[l][0].append(Bd)
                    dma('pool', 'R', kvsc[l, 0][:, i * 512:(i + 1) * 512], kvst[i][0], reads=[kvst[i][1]], writes=[Bd])

            checkpoint('memkv')
            for l in range(L):
                for c in range(8):
                    stg, Bstg = tmpf()
                    dma('pool', 'W', stg[0:64, 0:128].rearrange("p (h k) -> p h k", h=2), swkv[l, 2 * c:2 * c + 2].rearrange("h v k -> v h k"), writes=[Bstg])
                    bk, Bbk = bank()
                    tr(bk[:, 0:64], stg[0:64, 0:128], identf[0:64, 0:64], [Bstg, Bconst], [Bbk])
                    cp('act', S32[:, l, c, :], bk[:, 0:64], [Bbk], [BS32[l][c]])
                stg, Bstg = tmpf()
                dma('pool', 'W', stg[0:26, 0:128], sshift[l].rearrange("(j p) -> j p", p=128), writes=[Bstg])
                bk, Bbk = bank()
                tr(bk[:, 0:26], stg[0:26, 0:128], identf[0:26, 0:26], [Bstg, Bconst], [Bbk])
                cp('act', prev[:, l, :], bk[:, 0:26], [Bbk], [Bprev[l]])
            load_x(xs, TS)
            for l in range(L):
                tile_layer(l, 1, TS, TS, 5)
            store_y(ys, TS)
            for l in range(L):
                store_state(l, owkv_s, oshift_s)

            checkpoint('sample')
            for l in range(L):
                for c in range(8):
                    memset('pool', S32[:, l, c, :], 0.0, [BS32[l][c]])
                memset('pool', prev[:, l, :], 0.0, [Bprev[l]])
            for t_ in range(NT):
                load_x(xp[t_ * T:(t_ + 1) * T], T)
                for l in range(L):
                    tile_layer(l, 0, T, 128, 7)
                    if l == 0 and t_ == 0:
                        dbg_x2_dump()
                store_y(yp[t_ * T:(t_ + 1) * T], T)
            for l in range(L):
                store_state(l, owkv_p, oshift_p)

        except _Stop:
            pass
        finish()
    return nc


def _vec_table(inp, L):
    def fm(a, n):
        return np.ascontiguousarray(a.reshape(L, n, 128).transpose(2, 0, 1))
    mu = inp['rwkv_mu'][:L]
    ka = inp['rwkv_k_a'][:L]
    parts = [fm(inp['norm_mix_g'][:L], 8), fm(inp['norm_ffn_g'][:L], 8), fm(inp['norm_mem_g'][:L], 8),
             fm(mu, 26), np.zeros((128, L, 26), np.float32), fm(inp['rwkv_w0'][:L], 8), fm(inp['rwkv_a0'][:L], 8),
             fm(inp['rwkv_k_k'][:L], 8), fm(ka, 8), np.zeros((128, L, 8), np.float32), fm(inp['rwkv_r_k'][:L].reshape(L, 1024), 8),
             fm(inp['sgu_ln_g'][:L], 8), fm(inp['sgu_ln_b'][:L], 8), fm(inp['rwkv_lnx_g'][:L], 8), fm(inp['rwkv_lnx_b'][:L], 8)]
    return np.ascontiguousarray(np.concatenate(parts, axis=2).astype(np.float32))


def run(inp, SEQ, L, n_cores=8, trace=False, DBG=False, STOP=None):
    nc = build(SEQ, L, DBG=DBG, STOP=STOP)
    f = lambda a: np.ascontiguousarray(a, dtype=np.float32)
    vecs = _vec_table(inp, L)
    fing = np.ascontiguousarray(inp['norm_final_g'].reshape(8, 128).T.astype(np.float32))
    shared = {
        'vecs': vecs, 'fing': fing, 'w_in': f(inp['w_in'][:L]), 'w_mkv': f(inp['w_mem_kv'][:L]),
        'w2': f(inp['rwkv_w2'][:L]), 'a2': f(inp['rwkv_a2'][:L]), 'g2': f(inp['rwkv_g2'][:L]),
        'w_s': f(inp['sgu_w_s'][:L]), 'b_s': f(inp['sgu_b_s'][:L].reshape(L, 1024)),
        'w_br': f(inp['w_branch'][:L]), 'w_out': f(inp['w_out'][:L]), 'w_up': f(inp['w_ffn_up'][:L]), 'w_dn': f(inp['w_ffn_down'][:L]),
    }
    in_maps = []
    for b in range(n_cores):
        m = dict(shared)
        m['xp'] = f(inp['x_prompt'][b, :SEQ]); m['xs'] = f(inp['x_sample'][b])
        m['ck'] = f(inp['cache_mem_k'][:L, b].reshape(L, 256, 1024)); m['cv'] = f(inp['cache_mem_v'][:L, b].reshape(L, 256, 1024))
        m['swkv'] = f(inp['state_wkv'][:L, b]); m['sshift'] = f(inp['state_shift'][:L, b, 0])
        m['memp'] = f(inp['mem_prompt'][b])
        in_maps.append(m)
    res = run_bass_kernel_spmd(nc, in_maps, core_ids=list(range(n_cores)), trace=trace)
    R = res.results
    st = lambda k: np.stack([np.asarray(r[k]) for r in R])
    y_p = st('yp'); y_s = st('ys')
    mk = st('omk').transpose(1, 0, 2, 3).reshape(L, n_cores, 256, 4, 256)
    mv = st('omv').transpose(1, 0, 2, 3).reshape(L, n_cores, 256, 4, 256)
    wkv_p = st('owkv_p').transpose(1, 0, 2, 3, 4)
    sh_p = st('oshift_p').transpose(1, 0, 2)[:, :, None, :]
    wkv_s = st('owkv_s').transpose(1, 0, 2, 3, 4)
    sh_s = st('oshift_s').transpose(1, 0, 2)[:, :, None, :]
    sgu = st('osgu').transpose(1, 0, 2, 3)
    outs = (y_p, y_s, mk, mv, wkv_p, sh_p, wkv_s, sh_s, sgu)
    return tuple(np.ascontiguousarray(o.astype(np.float32)) for o in outs), res


def kernel(**inputs):
    inp = {k: np.asarray(v) for k, v in inputs.items()}
    outs, _ = run(inp, 8192, 4, 8)
    return outs
```

```python
import contextlib
import numpy as np
import concourse.bass as bass
import concourse.mybir as mybir
from concourse.bass_utils import run_bass_kernel_spmd

F32 = mybir.dt.float32
BF16 = mybir.dt.bfloat16
AF = mybir.ActivationFunctionType
ALU = mybir.AluOpType
AX = mybir.AxisListType

ENGS = ('pe', 'act', 'dve', 'pool', 'sp')


class Buf:
    __slots__ = ('w', 'r', 'id')
    _n = [0]

    def __init__(self):
        self.w = None
        self.r = {}
        Buf._n[0] += 1
        self.id = Buf._n[0]


class Rec:
    def __init__(self, nc):
        self.nc = nc
        self.streams = {e: [] for e in ENGS}
        self.cnt = {e: 0 for e in ENGS}
        self.dcnt = {}
        self.waited = {e: {} for e in ENGS}
        self.pending = {e: [] for e in ENGS}

    def _wait(self, eng, tok):
        if tok is None:
            return
        key, val = tok
        if self.waited[eng].get(key, 0) >= val:
            return
        self.waited[eng][key] = val
        self.streams[eng].append(('w', key, val))

    def _deps(self, eng, reads, writes, extra):
        for t in extra:
            self._wait(eng, t)
        for b in reads:
            self._wait(eng, b.w)
        for b in writes:
            self._wait(eng, b.w)
            for t in b.r.values():
                self._wait(eng, t)
            for e2 in ENGS:
                if e2 != eng and any(pb_ is b for pb_ in self.pending[e2]):
                    raise RuntimeError("write to buffer %d with pending unmarked reads on %s" % (b.id, e2))

    def op(self, eng, fn, reads=(), writes=(), mark=True, extra=(), wdeps=()):
        self._deps(eng, reads, tuple(writes) + tuple(wdeps), extra)
        if mark:
            self.cnt[eng] += 1
            tok = (eng, self.cnt[eng])
            self.streams[eng].append(('i', fn, True))
            for b in self.pending[eng]:
                b.r[eng] = tok
            self.pending[eng] = []
            for b in reads:
                b.r[eng] = tok
            for b in writes:
                b.w = tok
                b.r = {}
            return tok
        assert not writes
        self.streams[eng].append(('i', fn, False))
        self.pending[eng].extend(reads)
        return None

    def dma(self, q, sem, fn, reads=(), writes=(), extra=()):
        self._deps(q, reads, writes, extra)
        self.dcnt[sem] = self.dcnt.get(sem, 0) + 16
        tok = (sem, self.dcnt[sem])
        self.streams[q].append(('d', fn, sem))
        for b in reads:
            b.r[sem] = tok
        for b in writes:
            b.w = tok
            b.r = {}
        return tok

    def emit(self, stack):
        nc = self.nc
        sems = {}
        for e in ENGS:
            sems[e] = stack.enter_context(nc.semaphore('s_' + e))
        for d in self.dcnt:
            sems[d] = stack.enter_context(nc.semaphore('d_' + d))
        block = stack.enter_context(nc.Block())
        dec = {'pe': block.tensor, 'act': block.scalar, 'dve': block.vector, 'pool': block.gpsimd, 'sp': block.sync}
        for e in ENGS:
            stream = self.streams[e]
            own = sems[e]

            def body(engine, stream=stream, own=own):
                for item in stream:
                    if item[0] == 'w':
                        engine.wait_ge(sems[item[1]], item[2])
                    elif item[0] == 'i':
                        ins = item[1](engine)
                        if item[2]:
                            ins.then_inc(own, 1)
                    else:
                        item[1](engine).then_inc(sems[item[2]], 16)
            dec[e](body)


D = 1024
NCH = 8
RW = 3328
IN_COLS = 9472
NP_ = 49
PI_LORA, PI_RKV, PI_SGU, PI_Q, PI_GATE, PI_BR, PI_OUT, PI_UP, PI_DN, PI_MKV = 0, 1, 9, 13, 15, 21, 27, 29, 37, 45
C0 = -0.6065306597126334
V_MIXG, V_FFNG, V_MEMG, V_MU, V_OMM, V_W0, V_A0, V_KK, V_KA, V_OMKA, V_RK, V_SLG, V_SLB, V_LXG, V_LXB = (
    0, 8, 16, 24, 50, 76, 84, 92, 100, 108, 116, 124, 132, 140, 148)
VW = 156


class _Stop(Exception):
    pass


def build(SEQ, DEPTH, MEMN=256, TS=32, DBG=False, STOP=None):
    nc = bass.Bass("TRN2", target_bir_lowering=False)
    L = DEPTH
    T = 512
    NT = SEQ // T
    din = lambda name, shape, dt=F32: nc.dram_tensor(name, shape, dt, kind="ExternalInput").ap()
    dout = lambda name, shape, dt=F32: nc.dram_tensor(name, shape, dt, kind="ExternalOutput").ap()
    dscr = lambda name, shape, dt: nc.dram_tensor(name, shape, dt, kind="Internal").ap()
    xp = din("xp", [SEQ, D]); xs = din("xs", [TS, D])
    ck = din("ck", [L, MEMN, D]); cv = din("cv", [L, MEMN, D])
    swkv = din("swkv", [L, 16, 64, 64]); sshift = din("sshift", [L, RW]); memp = din("memp", [MEMN, D])
    vecs = din("vecs", [128, L, VW]); fing = din("fing", [128, 8])
    w_in = din("w_in", [L, D, IN_COLS]); w_mkv = din("w_mkv", [L, D, 2 * D])
    w2 = din("w2", [L, 64, D]); a2 = din("a2", [L, 64, D]); g2 = din("g2", [L, 128, D])
    w_s = din("w_s", [L, 8, 128, 128]); b_s = din("b_s", [L, 8 * 128])
    w_br = din("w_br", [L, 3, D, D]); w_out = din("w_out", [L, D, D])
    w_up = din("w_up", [L, D, 4 * D]); w_dn = din("w_dn", [L, 4 * D, D])
    yp = dout("yp", [SEQ, D]); ys = dout("ys", [TS, D])
    omk = dout("omk", [L, MEMN, D]); omv = dout("omv", [L, MEMN, D])
    owkv_p = dout("owkv_p", [L, 16, 64, 64]); oshift_p = dout("oshift_p", [L, RW])
    owkv_s = dout("owkv_s", [L, 16, 64, 64]); oshift_s = dout("oshift_s", [L, RW])
    osgu = dout("osgu", [L, TS, D])
    if DBG:
        dbg_y = dout("dbg_y", [128, 24, 512], BF16); dbg_x = dout("dbg_x", [128, 8, 512]); dbg_x2 = dout("dbg_x2", [128, 8, 512]); dbg_m = dout("dbg_m", [128, 8, 512], BF16)
    wsc = dscr("wsc", [L, NP_, 128, 4096], BF16)
    kvsc = dscr("kvsc", [L, 2, 128, 4096], BF16)
    smsc = dscr("smsc", [L, 128, 3072], BF16)

    rec = Rec(nc)
    st = contextlib.ExitStack()
    with st:
        def sb(name, shape, dt):
            return st.enter_context(nc.sbuf_tensor(name, shape, dt))

        def ps(name, shape, dt):
            return st.enter_context(nc.psum_tensor(name, shape, dt))

        def mm(out, lhsT, rhs, start, stop, reads, writes=None):
            rec.op('pe', lambda e: e.matmul(out, lhsT=lhsT, rhs=rhs, start=start, stop=stop),
                   reads=reads, writes=writes or (), mark=writes is not None, wdeps=() if writes is not None else psum_bufs[out.name])

        def tr(out, in_, ident, reads, writes=None):
            rec.op('pe', lambda e: e.transpose(out=out, in_=in_, identity=ident),
                   reads=reads, writes=writes or (), mark=writes is not None, wdeps=() if writes is not None else psum_bufs[out.name])

        def act(out, in_, func, reads, writes, scale=1.0, bias=None):
            if bias is None:
                rec.op('act', lambda e: e.activation(out=out, in_=in_, func=func, scale=scale), reads=reads, writes=writes)
            else:
                rec.op('act', lambda e: e.activation(out=out, in_=in_, func=func, scale=scale, bias=bias), reads=reads, writes=writes)

        def tt(eng, out, in0, in1, op, reads, writes):
            rec.op(eng, lambda e: e.tensor_tensor(out=out, in0=in0, in1=in1, op=op), reads=reads, writes=writes)

        def ts(eng, out, in0, s1, s2, op0, op1, reads, writes):
            if op1 is None and eng == 'pool' and op0 == ALU.mult:
                s2, op1 = 0.0, ALU.add
            if op1 is None:
                rec.op(eng, lambda e: e.tensor_scalar(out=out, in0=in0, scalar1=s1, scalar2=None, op0=op0), reads=reads, writes=writes)
            else:
                rec.op(eng, lambda e: e.tensor_scalar(out=out, in0=in0, scalar1=s1, scalar2=s2, op0=op0, op1=op1), reads=reads, writes=writes)

        def stt(out, in0, scalar, in1, op0, op1, reads, writes):
            rec.op('dve', lambda e: e.scalar_tensor_tensor(out=out, in0=in0, scalar=scalar, in1=in1, op0=op0, op1=op1), reads=reads, writes=writes)

        def cp(eng, out, in_, reads, writes):
            if eng == 'act':
                act(out, in_, AF.Copy, reads, writes)
            else:
                rec.op(eng, lambda e: e.tensor_copy(out=out, in_=in_), reads=reads, writes=writes)

        def rsqrt(out, in_, reads, writes, scale=1.0, bias=None, power=-0.5):
            act(out, in_, AF.Ln, reads, writes, scale=scale, bias=bias)
            act(out, out, AF.Exp, writes, writes, scale=power)

        def cpred(out, mask, data, reads, writes):
            rec.op('dve', lambda e: e.copy_predicated(out=out, mask=mask, data=data), reads=reads, writes=writes)

        def recip(out, in_, reads, writes):
            rec.op('dve', lambda e: e.reciprocal(out=out, in_=in_), reads=reads, writes=writes)

        def memset(eng, ap, val, writes):
            rec.op(eng, lambda e: e.memset(ap, val), writes=writes)

        def dma(q, sem, out, in_, reads=(), writes=(), slow=False):
            if sem == 'W':
                sem = 'b%d' % writes[0].id
            elif sem == 'R':
                sem = 'b%d' % reads[0].id
            if slow:
                return rec.dma(q, sem, lambda e: e.dma_start(out=out, in_=in_, allow_slow_non_contiguous=True), reads=reads, writes=writes)
            return rec.dma(q, sem, lambda e: e.dma_start(out=out, in_=in_), reads=reads, writes=writes)

        x_t = sb("x_t", [128, 8, T], F32); Bx = [Buf() for _ in range(8)]
        h_t = sb("h_t", [128, 8, T], BF16); Bh = [Buf() for _ in range(8)]
        NS = 3
        ws_t = [sb(f"ws{i}", [128, 4096], BF16) for i in range(NS)]; Bws = [Buf() for _ in range(NS)]
        big = sb("big", [128, 8, T], F32); Bbig = [Buf() for _ in range(8)]
        ub = sb("ub", [128, 8, T], BF16); Bub = [Buf() for _ in range(8)]
        bfp = sb("bfp", [128, 32, T], BF16); Bbfp = [Buf() for _ in range(32)]
        NTF = 11
        tf_t = sb("tf_t", [128, NTF, T + 1], F32); Btf = [Buf() for _ in range(NTF)]
        NTB = 13
        tb_t = sb("tb_t", [128, NTB, T], BF16); Btb = [Buf() for _ in range(NTB)]
        tfi = [0]; tbi = [0]

        def tmpf():
            i = tfi[0] % NTF; tfi[0] += 1
            return tf_t[:, i, :], Btf[i]

        def tmpb():
            i = tbi[0] % NTB; tbi[0] += 1
            return tb_t[:, i, :], Btb[i]

        S32 = sb("S32", [128, L, 8, 64], F32); BS32 = [[Buf() for _ in range(8)] for _ in range(L)]
        Sbd = sb("Sbd", [128, 8, 128], BF16); BSbd = [Buf() for _ in range(8)]
        prev = sb("prev", [128, L, 26], F32); Bprev = [Buf() for _ in range(L)]
        vec_t = sb("vec_t", [128, L, VW], F32); Bvec = Buf()
        fing_t = sb("fing_t", [128, 8], F32); Bfing = Buf()
        lw = sb("lw", [128, T], BF16); Blw = Buf(); sgl = sb("sgl", [128, T], BF16); Bsgl = Buf()
        smallw = sb("smallw", [128, 3072], BF16); Bsmall = Buf()
        bsb = sb("bsb", [128, 1024], F32); Bbsb = Buf()
        identf = sb("identf", [128, 128], F32); identb = sb("identb", [128, 128], BF16)
        onesb = sb("onesb", [128, 128], BF16); blkb = sb("blkb", [128, 128], BF16)
        onesf = sb("onesf", [128, 128], F32)
        msu = sb("msu", [128, 128], F32); miu = sb("miu", [128, 128], F32); msl = sb("msl", [128, 128], F32)
        cmask = sb("cmask", [128, T], F32)
        epsc = sb("epsc", [128, 4], F32)
        Bconst = Buf()
        vTp = [sb(f"vT{i}", [128, 4, 128], BF16) for i in range(2)]; kTp = [sb(f"kT{i}", [128, 4, 128], BF16) for i in range(2)]
        bTp = [sb(f"bT{i}", [128, 4, 128], BF16) for i in range(2)]
        BvTp, BkTp, BbTp = [Buf(), Buf()], [Buf(), Buf()], [Buf(), Buf()]
        bvt = [sb(f"bvt{i}", [128, T], BF16) for i in range(2)]; Bbvt = [Buf(), Buf()]
        y1t = sb("y1t", [128, T], F32); By1t = Buf()
        NQt = [sb(f"NQt{i}", [128, 2, 2, 128], BF16) for i in range(4)]; BNQ = [Buf() for _ in range(4)]
        MTt = [sb(f"MTt{i}", [128, 2, 2, 128], BF16) for i in range(4)]; BMT = [Buf() for _ in range(4)]
        Xbt = [sb(f"Xbt{i}", [128, 2, 2, 128], BF16) for i in range(4)]; BXb = [Buf() for _ in range(4)]
        m2d = sb("m2d", [128, 7, 2, 128], BF16)
        m2d_u16 = m2d[:].bitcast(mybir.dt.uint16)
        AKt = [sb(f"AKt{i}", [128, 2, 128], BF16) for i in range(4)]; BAK = [Buf() for _ in range(4)]
        RRt = [sb(f"RRt{i}", [128, 4, 128], BF16) for i in range(4)]; BRR = [Buf() for _ in range(4)]
        Zb = [sb(f"Zb{i}", [128, 128], BF16) for i in range(2)]; BZb = [Buf(), Buf()]
        Ub = [sb(f"Ub{i}", [128, 128], BF16) for i in range(2)]; BUb = [Buf(), Buf()]
        Swt = [sb(f"Swt{i}", [128, 64], F32) for i in range(2)]; BSw = [Buf(), Buf()]
        wc_p = [sb(f"wc{i}", [128, 4], F32) for i in range(2)]; Bwcp = [Buf(), Buf()]
        yTM = sb("yTM", [128, 4, 128], F32); ByTM = Buf()
        ysq = sb("ysq", [128, 4, 128], F32); Bysq = Buf()
        ynb = sb("ynb", [128, 4, 128], BF16); Bynb = Buf()
        gst = sb("gst", [128, 4, 8], F32); Bgst = Buf()

        NB = 7
        pb = [ps(f"pb{i}", [128, 512], F32) for i in range(NB)]; Bpb = [Buf() for _ in range(NB)]
        pT = ps("pT", [128, 1024], BF16); _bpt = Buf(); BpT = [_bpt, _bpt]
        psum_bufs = {f"pb{i}": [Bpb[i]] for i in range(NB)}
        psum_bufs["pT"] = [_bpt]
        bki = [0]

        def bank():
            i = bki[0] % NB; bki[0] += 1
            return pb[i], Bpb[i]

        pti = [0]

        def ptbank():
            i = pti[0] % 2; pti[0] += 1
            return pT[:, i * 512:(i + 1) * 512], BpT[i]

        gcur = ['']

        def checkpoint(name):
            if STOP == name or STOP == gcur[0] + ':' + name:
                raise _Stop()

        def finish():
            for k_, v_ in list(rec.dcnt.items()):
                rec._wait('sp', (k_, v_))
            rec.emit(st)

        try:
            memset('pool', onesf[:], 1.0, [Bconst])
            memset('pool', identf[:], 0.0, [Bconst])
            rec.op('pool', lambda e: e.affine_select(out=identf[:], in_=identf[:], pattern=[[-1, 128]], compare_op=ALU.not_equal, fill=1.0, base=0, channel_multiplier=1), reads=[Bconst], writes=[Bconst])
            cp('pool', identb[:], identf[:], [Bconst], [Bconst])
            cp('pool', onesb[:], onesf[:], [Bconst], [Bconst])
            memset('pool', blkb[:], 0.0, [Bconst])
            memset('pool', blkb[0:64, 0:64], 1.0, [Bconst])
            memset('pool', blkb[64:128, 64:128], 1.0, [Bconst])
            rec.op('pool', lambda e: e.affine_select(out=msu[:], in_=onesf[:], pattern=[[1, 128]], compare_op=ALU.is_gt, fill=0.0, base=0, channel_multiplier=-1), reads=[Bconst], writes=[Bconst])
            rec.op('pool', lambda e: e.affine_select(out=miu[:], in_=onesf[:], pattern=[[1, 128]], compare_op=ALU.is_ge, fill=0.0, base=0, channel_multiplier=-1), reads=[Bconst], writes=[Bconst])
            rec.op('pool', lambda e: e.affine_select(out=msl[:], in_=onesf[:], pattern=[[-1, 128]], compare_op=ALU.is_gt, fill=0.0, base=0, channel_multiplier=1), reads=[Bconst], writes=[Bconst])
            Ea, Eb, Ec = tf_t[:, 0, 0:128], tf_t[:, 1, 0:128], tf_t[:, 2, 0:128]
            BE_ = [Btf[0], Btf[1], Btf[2]]
            cp('pool', Ea, identf[:], [Bconst], [BE_[0]])
            Ecur, Bcur, Enxt, Bnxt = Ea, BE_[0], Eb, BE_[1]
            for j in range(7):
                s_ = 2 ** (j + 1); nb_ = 128 // s_
                if j == 6:
                    cp('pool', Enxt, onesf[:], [Bconst], [Bnxt])
                else:
                    def sel1(e, out=Enxt, s_=s_, nb_=nb_):
                        return e.affine_select(out=out.rearrange("p (a b) -> p a b", b=s_), in_=onesf[:].rearrange("p (a b) -> p a b", b=s_),
                                               pattern=[[-s_, nb_], [0, s_]], compare_op=ALU.is_ge, fill=0.0, base=0, channel_multiplier=1)

                    def sel2(e, out=Enxt, s_=s_, nb_=nb_):
                        return e.affine_select(out=out.rearrange("p (a b) -> p a b", b=s_), in_=out.rearrange("p (a b) -> p a b", b=s_),
                                               pattern=[[s_, nb_], [0, s_]], compare_op=ALU.is_ge, fill=0.0, base=s_ - 1, channel_multiplier=-1)
                    rec.op('pool', sel1, reads=[Bconst], writes=[Bnxt])
                    rec.op('pool', sel2, reads=[Bnxt], writes=[Bnxt])
                tt('pool', Ec, Enxt, Ecur, ALU.subtract, [Bnxt, Bcur], [BE_[2]])
                tt('pool', m2d[:, j, 0, :], Ec, msu[:], ALU.mult, [BE_[2], Bconst], [Bconst])
                tt('pool', m2d[:, j, 1, :], Ec, msl[:], ALU.mult, [BE_[2], Bconst], [Bconst])
                Ecur, Bcur, Enxt, Bnxt = Enxt, Bnxt, Ecur, Bcur
            memset('pool', epsc[:, 0:1], 1e-6, [Bconst])
            memset('pool', epsc[:, 1:2], 1e-5, [Bconst])
            memset('pool', epsc[:, 2:3], 64e-5, [Bconst])
            memset('pool', epsc[:, 3:4], 1e-24, [Bconst])
            dma('pool', 'W', vec_t[:], vecs, writes=[Bvec])
            dma('pool', 'W', fing_t[:], fing, writes=[Bfing])
            ts('dve', vec_t[:, :, V_OMM:V_OMM + 26], vec_t[:, :, V_MU:V_MU + 26], -1.0, 1.0, ALU.mult, ALU.add, [Bvec], [Bvec])
            ts('dve', vec_t[:, :, V_OMKA:V_OMKA + 8], vec_t[:, :, V_KA:V_KA + 8], -1.0, 1.0, ALU.mult, ALU.add, [Bvec], [Bvec])

            def vcol(l, base, c):
                return vec_t[:, l, base + c:base + c + 1]

            checkpoint('const')
            Bwsc = Buf()
            Bsm = [[] for _ in range(L)]
            Bkv = [[[], []] for _ in range(L)]

            def conv(l, pi, src2d, ncols, col0=0):
                dst = wsc[l, pi].rearrange("p (kc n) -> p kc n", kc=8)[:, :, col0:col0 + ncols]
                dma('pool', 'cv', dst, src2d.rearrange("(kc p) n -> p kc n", p=128), writes=[Bwsc])

            for l in range(L):
                conv(l, PI_LORA, w_in[l][:, 3072:3328], 256)
                for c in range(8):
                    for j in range(3):
                        conv(l, PI_RKV + c, w_in[l][:, j * 1024 + c * 128: j * 1024 + (c + 1) * 128], 128, col0=j * 128)
                for i in range(4):
                    conv(l, PI_SGU + i, w_in[l][:, 3328 + i * 512: 3328 + (i + 1) * 512], 512)
                for i in range(2):
                    conv(l, PI_Q + i, w_in[l][:, 5376 + i * 512: 5376 + (i + 1) * 512], 512)
                for i in range(6):
                    conv(l, PI_GATE + i, w_in[l][:, 6400 + i * 512: 6400 + (i + 1) * 512], 512)
                for b in range(3):
                    for hf in range(2):
                        conv(l, PI_BR + b * 2 + hf, w_br[l, b][:, hf * 512:(hf + 1) * 512], 512)
                for hf in range(2):
                    conv(l, PI_OUT + hf, w_out[l][:, hf * 512:(hf + 1) * 512], 512)
                for i in range(8):
                    conv(l, PI_UP + i, w_up[l][:, i * 512:(i + 1) * 512], 512)
                for g in range(2):
                    for q in range(4):
                        conv(l, PI_DN + g * 4 + q, w_dn[l][q * 1024:(q + 1) * 1024, g * 512:(g + 1) * 512], 512)
                for i in range(4):
                    conv(l, PI_MKV + i, w_mkv[l][:, i * 512:(i + 1) * 512], 512)
                dma('pool', 'cv', smsc[l][0:64, 1024:2048], w2[l], writes=[Bwsc])
                dma('pool', 'cv', smsc[l][64:128, 1024:2048], a2[l], writes=[Bwsc])
                dma('pool', 'cv', smsc[l][:, 2048:3072], g2[l], writes=[Bwsc])
                dma('pool', 'cv', kvsc[l, 1][:, 2048:4096].rearrange("p (mb n) -> p mb n", mb=2),
                    cv[l].rearrange("(mb p) n -> p mb n", p=128), writes=[Bwsc])
            checkpoint('conv')
            for l in range(L):
                wsf, Bw_ = big[:, 0:2, :], [Bbig[0], Bbig[1]]
                dma('pool', 'W', big[:, 0:2, :].rearrange("p a (g j) -> p (a g) j", j=128), w_s[l].rearrange("g i j -> i g j"), writes=Bw_)
                for g in range(8):
                    if g % 4 == 0:
                        stg, Bstg = tmpb()
                    bk, Bbk = bank()
                    tr(bk[:, 0:128], big[:, g // 4, (g % 4) * 128:(g % 4 + 1) * 128], identf[:], reads=Bw_ + [Bconst], writes=[Bbk])
                    tt('dve', stg[:, (g % 4) * 128:(g % 4 + 1) * 128], bk[:, 0:128], miu[:], ALU.mult, [Bbk, Bconst], [Bstg])
                    if g % 4 == 3:
                        Bd = Buf(); Bsm[l].append(Bd)
                        dma('pool', 'R', smsc[l][:, (g // 4) * 512:(g // 4 + 1) * 512], stg, reads=[Bstg], writes=[Bd])
            for l in range(L):
                Bk_ = [Bbig[i] for i in range(4)]
                dma('pool', 'W', big[:, 0:4, :].rearrange("p (mb a) n -> p mb (a n)", mb=2), ck[l].rearrange("(mb p) n -> p mb n", p=128), writes=Bk_)
                for j in range(8):
                    if j % 2 == 0:
                        stg, Bstg = tmpb()
                    bk, Bbk = bank()
                    for mb in range(2):
                        tr(bk[:, mb * 128:(mb + 1) * 128], big[:, mb * 2 + j // 4, (j % 4) * 128:(j % 4 + 1) * 128], identf[:],
                           reads=Bk_ + [Bconst], writes=[Bbk] if mb == 1 else None)
                    cp('act', stg[:, (j % 2) * 256:(j % 2 + 1) * 256], bk[:, 0:256], [Bbk], [Bstg])
                    if j % 2 == 1:
                        Bd = Buf(); Bkv[l][1].append(Bd)
                        dma('pool', 'R', kvsc[l, 1][:, (j - 1) * 256:(j + 1) * 256], stg, reads=[Bstg], writes=[Bd])

            checkpoint('prep')
            uses = []
            state = {'k': 0, 'loaded': 0}

            def plan_tile_layer(l, grp):
                seq = [('w', l, PI_LORA)] + [('w', l, PI_RKV + c) for c in range(8)] + [('w', l, PI_SGU + i) for i in range(4)]
                seq += [('w', l, PI_Q), ('w', l, PI_Q + 1), ('kv', l, grp)]
                for b in range(3):
                    for hf in range(2):
                        seq += [('w', l, PI_GATE + b * 2 + hf), ('w', l, PI_BR + b * 2 + hf)]
                seq += [('w', l, PI_OUT), ('w', l, PI_OUT + 1)] + [('w', l, PI_UP + i) for i in range(8)]
                seq += [('w', l, PI_DN + i) for i in range(8)]
                return seq

            for l in range(L):
                uses += [('w', l, PI_MKV + i) for i in range(4)]
            for l in range(L):
                uses += plan_tile_layer(l, 1)
            for t_ in range(NT):
                for l in range(L):
                    uses += plan_tile_layer(l, 0)
            def _issue_load(k):
                kind, l, idx = uses[k]
                slot = k % NS
                if kind == 'w':
                    ncols = 256 if idx == PI_LORA else (384 if PI_RKV <= idx < PI_RKV + 8 else 512)
                    if ncols == 512:
                        dma('sp', f'w{slot}', ws_t[slot][:], wsc[l, idx], reads=[Bwsc], writes=[Bws[slot]])
                    else:
                        dma('sp', f'w{slot}', ws_t[slot][:].rearrange("p (kc n) -> p kc n", kc=8)[:, :, 0:ncols],
                            wsc[l, idx].rearrange("p (kc n) -> p kc n", kc=8)[:, :, 0:ncols], reads=[Bwsc], writes=[Bws[slot]])
                else:
                    dma('sp', f'w{slot}', ws_t[slot][:], kvsc[l, idx], reads=[Bwsc] + Bkv[l][idx], writes=[Bws[slot]])

            def piece(kind, l, idx):
                k = state['k']
                assert uses[k] == (kind, l, idx), (uses[k], kind, l, idx)
                while state['loaded'] < min(len(uses), k + NS - 1):
                    _issue_load(state['loaded']); state['loaded'] += 1
                state['k'] += 1
                slot = k % NS
                return ws_t[slot], Bws[slot]

            def wpiece(l, idx):
                t_, B_ = piece('w', l, idx)
                return t_[:].rearrange("p (kc n) -> p kc n", kc=8), B_

            def rmsnorm_to_h(l, gbase, TT):
                bk, Bbk = bank()
                for c in range(8):
                    sq, Bsq = tmpb()
                    act(sq[:, :TT], x_t[:, c, :TT], AF.Square, [Bx[c]], [Bsq])
                    mm(bk[:, :TT], onesb[:], sq[:, :TT], c == 0, c == 7, [Bsq, Bconst], [Bbk] if c == 7 else None)
                rs, Brs = tmpf()
                rsqrt(rs[:, :TT], bk[:, :TT], [Bbk, Bconst], [Brs], scale=1.0 / D, bias=epsc[:, 0:1])
                for c in range(8):
                    gcol = vcol(l, gbase, c) if l is not None else fing_t[:, c:c + 1]
                    yield_out = h_t[:, c, :TT]
                    stt(yield_out, x_t[:, c, :TT], gcol, rs[:, :TT], ALU.mult, ALU.mult, [Bx[c], Brs, Bvec], [Bh[c]])

            def big_mm(wt, Bw, oc4, rhs_list, Brhs, TT, out_bk=None, start=True, stop=True, kc0=0):
                if out_bk is None:
                    bk, Bbk = bank()
                else:
                    bk, Bbk = out_bk
                n = len(rhs_list)
                for kc in range(n):
                    last = (kc == n - 1)
                    mm(bk[:, :TT], wt[:, kc, oc4 * 128:(oc4 + 1) * 128], rhs_list[kc], start and kc == 0, stop and last,
                       [Bw, Brhs[kc]], [Bbk] if last else None)
                return bk, Bbk

            def hrhs(TT):
                return [h_t[:, kc, :TT] for kc in range(8)]

            def rwkv_gen(l, c, TT, C):
                NC_ = TT // C
                wt, Bw = wpiece(l, PI_RKV + c)
                w2a2 = smallw[:, 1024:2048]; g2t = smallw[:, 2048:3072]
                F = lambda i: (tf_t[:, i, :], Btf[i])
                Bq = lambda i: (tb_t[:, i, :], Btb[i])
                pp = c % 2
                CB = [F(0), F(0)]
                (r_, Br), (k_, Bk), (v_, Bv) = F(1), F(2), F(3)
                (sig, Bsig), (a_, Ba), (kk, Bkk), (Lc, BL), (E, BE), (E2, BE2), (t1, Bt1) = F(4), F(5), F(6), F(7), F(8), F(9), F(10)
                (g_, Bg), (rt, Brt), (kt, Bkt), (bt, Bbt), (at, Bat) = [Bq(pp * 5 + i) for i in range(5)]
                (sq, Bsq), (pr, Bpr), (vb, Bvb) = Bq(10), Bq(11), Bq(12)
                vT, kT, bT, BvT, BkT, BbT = vTp[pp], kTp[pp], bTp[pp], BvTp[pp], BkTp[pp], BbTp[pp]
                wc_t, Bwc = wc_p[pp], Bwcp[pp]
                mixed = [(r_, Br), (k_, Bk), (v_, Bv)]
                for j in range(3):
                    col = j * 8 + c
                    bk, Bbk = big_mm(wt, Bw, j, hrhs(TT), Bh, TT)
                    cb, Bcb = CB[j % 2]
                    mx, Bmx = mixed[j]
                    cp('act', cb[:, 1:TT + 1], bk[:, :TT], [Bbk], [Bcb])
                    cp('pool', cb[:, 0:1], prev[:, l, col:col + 1], [Bprev[l]], [Bcb])
                    ts('dve', t1[:, :TT], cb[:, 0:TT], vcol(l, V_MU, col), None, ALU.mult, None, [Bcb, Bvec], [Bt1])
                    stt(mx[:, :TT], cb[:, 1:TT + 1], vcol(l, V_OMM, col), t1[:, :TT], ALU.mult, ALU.add, [Bcb, Bt1, Bvec], [Bmx])
                    cp('pool', prev[:, l, col:col + 1], cb[:, TT:TT + 1], [Bcb], [Bprev[l]])
                    yield 's'
                bk, Bbk = bank()
                mm(bk[:, :TT], w2a2[0:64, c * 128:(c + 1) * 128], lw[0:64, :TT], True, True, [Bsmall, Blw], [Bbk])
                act(sig[:, :TT], bk[:, :TT], AF.Sigmoid, [Bbk, Bvec], [Bsig], bias=vcol(l, V_W0, c))
                yield 's'
                bk, Bbk = bank()
                mm(bk[:, :TT], w2a2[64:128, c * 128:(c + 1) * 128], lw[64:128, :TT], True, True, [Bsmall, Blw], [Bbk])
                act(a_[:, :TT], bk[:, :TT], AF.Sigmoid, [Bbk, Bvec], [Ba], bias=vcol(l, V_A0, c))
                yield 's'
                bk, Bbk = bank()
                mm(bk[:, :TT], g2t[:, c * 128:(c + 1) * 128], sgl[:, :TT], True, True, [Bsmall, Bsgl], [Bbk])
                cp('act', g_[:, :TT], bk[:, :TT], [Bbk], [Bg])
                yield 's'
                ts('pool', kk[:, :TT], k_[:, :TT], vcol(l, V_KK, c), None, ALU.mult, None, [Bk, Bvec], [Bkk])
                act(sq[:, :TT], kk[:, :TT], AF.Square, [Bkk], [Bsq])
                bk, Bbk = bank()
                mm(bk[:, :TT], blkb[:], sq[:, :TT], True, True, [Bconst, Bsq], [Bbk])
                rn, Brn = E2, BE2
                rsqrt(rn[:, :TT], bk[:, :TT], [Bbk, Bconst], [Brn], bias=epsc[:, 3:4])
                yield 's'
                yield 's'
                tt('pool', kk[:, :TT], kk[:, :TT], rn[:, :TT], ALU.mult, [Bkk, Brn], [Bkk])
                ts('dve', t1[:, :TT], a_[:, :TT], vcol(l, V_KA, c), vcol(l, V_OMKA, c), ALU.mult, ALU.add, [Ba, Bvec], [Bt1])
                tt('pool', k_[:, :TT], k_[:, :TT], t1[:, :TT], ALU.mult, [Bk, Bt1], [Bk])
                yield 's'
                scan_mask = cmask[:, :TT] if C == 128 else onesf[:, :TT]
                rec.op('dve', lambda e: e.tensor_tensor_scan(out=Lc[:, :TT], data0=scan_mask, data1=sig[:, :TT],
                                                             initial=0.0, op0=ALU.mult, op1=ALU.add), reads=[Bsig, Bconst], writes=[BL])
                act(E[:, :TT], Lc[:, :TT], AF.Exp, [BL], [BE], scale=C0)
                tt('dve', rt[:, :TT], r_[:, :TT], E[:, :TT], ALU.mult, [Br, BE], [Brt])
                cp('pool', wc_t[:, 0:NC_], E[:, :TT].rearrange("p (a b) -> p a b", b=C)[:, :, C - 1], [BE], [Bwc])
                yield 's'
                act(E2[:, :TT], Lc[:, :TT], AF.Exp, [BL], [BE2], scale=-C0)
                tt('dve', kt[:, :TT], k_[:, :TT], E2[:, :TT], ALU.mult, [Bk, BE2], [Bkt])
                tt('pool', t1[:, :TT], kk[:, :TT], a_[:, :TT], ALU.mult, [Bkk, Ba], [Bt1])
                tt('dve', bt[:, :TT], t1[:, :TT], E2[:, :TT], ALU.mult, [Bt1, BE2], [Bbt])
                yield 's'
                tt('pool', Lc[:, :TT], Lc[:, :TT], sig[:, :TT], ALU.subtract, [BL, Bsig], [BL])
                act(E[:, :TT], Lc[:, :TT], AF.Exp, [BL], [BE], scale=C0)
                stt(at[:, :TT], kk[:, :TT], -1.0, E[:, :TT], ALU.mult, ALU.mult, [Bkk, BE], [Bat])
                yield 's'
                stt(pr[:, :TT], r_[:, :TT], vcol(l, V_RK, c), k_[:, :TT], ALU.mult, ALU.mult, [Br, Bk, Bvec], [Bpr])
                bk, Bbk = bank()
                mm(bk[:, :TT], blkb[:], pr[:, :TT], True, True, [Bconst, Bpr], [Bbk])
                bv, Bbv = bvt[pp], Bbvt[pp]
                tt('dve', bv[:, :TT], bk[:, :TT], v_[:, :TT], ALU.mult, [Bbk, Bv], [Bbv])
                cp('pool', vb[:, :TT], v_[:, :TT], [Bv], [Bvb])
                yield 's'
                checkpoint('pelem')
                for (src, Bsrc, dst, Bdst) in ((vb, Bvb, vT, BvT), (kt, Bkt, kT, BkT), (bt, Bbt, bT, BbT)):
                    pt, Bpt = ptbank()
                    for ci in range(NC_):
                        tr(pt[0:C, ci * 128:(ci + 1) * 128], src[:, ci * C:(ci + 1) * C], identb[:], [Bsrc, Bconst], [Bpt] if ci == NC_ - 1 else None)
                    cp('act', dst[0:C, 0:NC_, :], pt[0:C, 0:NC_ * 128].rearrange("p (a b) -> p a b", b=128), [Bpt], [Bdst])
                    yield 's'
                checkpoint('ptm')
                memset('pool', Sbd[:, c, :], 0.0, [BSbd[c]])
                for hd in range(2):
                    cp('pool', Sbd[hd * 64:(hd + 1) * 64, c, hd * 64:(hd + 1) * 64], S32[hd * 64:(hd + 1) * 64, l, c, :], [BS32[l][c]], [BSbd[c]])
                checkpoint('psbd')
                yield 'frontdone'
                hr = lambda hd: slice(hd * 64, (hd + 1) * 64)
                NL = C.bit_length() - 1
                for ci in range(NC_):
                    cs = slice(ci * C, (ci + 1) * C)
                    NQ, MT, AK, RR = NQt[ci], MTt[ci], AKt[ci], RRt[ci]
                    for hd in range(2):
                        bkX, BX = bank()
                        X3 = bkX[0:C, :].rearrange("p (a b) -> p a b", b=128)
                        mm(X3[:, 0, 0:C], bt[hr(hd), cs], at[hr(hd), cs], True, True, [Bbt, Bat])
                        mm(X3[:, 1, 0:C], kt[hr(hd), cs], at[hr(hd), cs], True, True, [Bkt, Bat])
                        mm(X3[:, 2, 0:C], bt[hr(hd), cs], rt[hr(hd), cs], True, True, [Bbt, Brt])
                        mm(X3[:, 3, 0:C], kt[hr(hd), cs], rt[hr(hd), cs], True, True, [Bkt, Brt], [BX])
                        bkY, BY = bank()
                        mm(bkY[0:C, 0:C], at[hr(hd), cs], bt[hr(hd), cs], True, True, [Bbt, Bat], [BY])
                        tt('dve', NQ[0:C, 0, hd, 0:C], X3[:, 0, 0:C], msu[0:C, 0:C], ALU.mult, [BX, Bconst], [BNQ[ci]])
                        tt('dve', AK[0:C, hd, 0:C], X3[:, 1, 0:C], msu[0:C, 0:C], ALU.mult, [BX, Bconst], [BAK[ci]])
                        tt('dve', RR[0:C, hd:4:2, 0:C], X3[:, 2:4, 0:C], miu[0:C, 0:C].unsqueeze(1).to_broadcast([C, 2, C]), ALU.mult, [BX, Bconst], [BRR[ci]])
                        tt('dve', NQ[0:C, 1, hd, 0:C], bkY[0:C, 0:C], msl[0:C, 0:C], ALU.mult, [BY, Bconst], [BNQ[ci]])
                    cp('act', MT[0:C].rearrange("p f h q -> p (f h) q")[:, :, 0:C], identb[0:C, 0:C].unsqueeze(1).to_broadcast([C, 4, C]), [Bconst], [BMT[ci]])
                    yield 's'
                checkpoint('patype')
                for j in range(NL):
                    last = j == NL - 1
                    nf = 1 if last else 2
                    xs_ = []
                    for ci in range(NC_):
                        NQ, MT = NQt[ci], MTt[ci]
                        bkX, BX = bank()
                        Xv = bkX[0:C, :].rearrange("p (f h q) -> p f h q", f=2, h=2)
                        for hd in range(2):
                            mm(Xv[:, 0, hd, 0:C], NQ[0:C, 1, hd, 0:C], MT[0:C, 0, hd, 0:C], True, True, [BNQ[ci], BMT[ci]], [BX] if (last and hd == 1) else None)
                            if not last:
                                mm(Xv[:, 1, hd, 0:C], NQ[0:C, 0, hd, 0:C], MT[0:C, 1, hd, 0:C], True, True, [BNQ[ci], BMT[ci]], [BX] if hd == 1 else None)
                        xs_.append((Xv, BX))
                    ys_ = []
                    for ci in range(NC_):
                        MT, XB_ = MTt[ci], Xbt[ci]
                        Xv, BX = xs_[ci]
                        cp('act', XB_[0:C, 0:nf, :, 0:C], Xv[:, 0:nf, :, 0:C], [BX], [BXb[ci]])
                        bkY, BY = bank()
                        Yv = bkY[0:C, :].rearrange("p (f h q) -> p f h q", f=2, h=2)
                        for hd in range(2):
                            mm(Yv[:, 0, hd, 0:C], MT[0:C, 1, hd, 0:C], XB_[0:C, 0, hd, 0:C], True, True, [BMT[ci], BXb[ci]], [BY] if (last and hd == 1) else None)
                            if not last:
                                mm(Yv[:, 1, hd, 0:C], MT[0:C, 0, hd, 0:C], XB_[0:C, 1, hd, 0:C], True, True, [BMT[ci], BXb[ci]], [BY] if hd == 1 else None)
                        ys_.append((Yv, BY))
                    for ci in range(NC_):
                        MT = MTt[ci]
                        Yv, BY = ys_[ci]
                        cpred(MT[0:C, 0:nf, :, 0:C], m2d_u16[0:C, j, 0:nf, 0:C].unsqueeze(2).to_broadcast([C, nf, 2, C]), Yv[:, 0:nf, :, 0:C],
                              [BY, Bconst, BMT[ci]], [BMT[ci]])
                    yield 's'
                for ci in range(NC_):
                    par = ci % 2
                    cs = slice(ci * C, (ci + 1) * C)
                    AK, RR, MT = AKt[ci], RRt[ci], MTt[ci]
                    checkpoint('pdbl')
                    ts('pool', Swt[par][:, :], S32[:, l, c, :], wc_t[:, ci:ci + 1], None, ALU.mult, None, [BS32[l][c], Bwc], [BSw[par]])
                    bkW, BW = bank()
                    mm(bkW[0:C, 0:128], at[:, cs], Sbd[:, c, :], True, False, [Bat, BSbd[c]])
                    for hd in range(2):
                        mm(bkW[0:C, hr(hd)], AK[0:C, hd, 0:C], vT[0:C, ci, hr(hd)], False, hd == 1, [BAK[ci], BvT], [BW] if hd == 1 else None)
                    cp('act', Zb[par][0:C, :], bkW[0:C, 0:128], [BW], [BZb[par]])
                    yield 's'
                    bkU, BU = bank()
                    for hd in range(2):
                        mm(bkU[0:C, hr(hd)], MT[0:C, 0, hd, 0:C], Zb[par][0:C, hr(hd)], True, True, [BMT[ci], BZb[par]], [BU] if hd == 1 else None)
                    cp('act', Ub[par][0:C, :], bkU[0:C, 0:128], [BU], [BUb[par]])
                    yield 's'
                    bkY, BY = bank()
                    mm(bkY[0:C, 0:128], rt[:, cs], Sbd[:, c, :], True, False, [Brt, BSbd[c]])
                    for hd in range(2):
                        mm(bkY[0:C, hr(hd)], RR[0:C, hd, 0:C], Ub[par][0:C, hr(hd)], False, False, [BRR[ci], BUb[par]])
                        mm(bkY[0:C, hr(hd)], RR[0:C, 2 + hd, 0:C], vT[0:C, ci, hr(hd)], False, hd == 1, [BRR[ci], BvT], [BY] if hd == 1 else None)
                    cp('act', yTM[0:C, ci, :], bkY[0:C, 0:128], [BY], [ByTM])
                    yield 's'
                    bkS, BS = bank()
                    mm(bkS[:, 0:128], bT[0:C, ci, :], Ub[par][0:C, :], True, False, [BbT, BUb[par]])
                    mm(bkS[:, 0:128], kT[0:C, ci, :], vT[0:C, ci, :], False, True, [BkT, BvT], [BS])
                    for hd in range(2):
                        stt(S32[hr(hd), l, c, :], bkS[hr(hd), hr(hd)], wc_t[hr(hd), ci:ci + 1], Swt[par][hr(hd), :], ALU.mult, ALU.add,
                            [BS, Bwc, BSw[par]], [BS32[l][c]])
                    if ci < NC_ - 1:
                        for hd in range(2):
                            cp('pool', Sbd[hr(hd), c, hr(hd)], S32[hr(hd), l, c, :], [BS32[l][c]], [BSbd[c]])
                    yield 's'
                checkpoint('pseq')
                NG = NC_ * 2
                y3 = yTM[0:C, 0:NC_, :].rearrange("p a (h v) -> p (a h) v", v=64)
                rec.op('dve', lambda e: e.tensor_reduce(out=gst[0:C, 0, 0:NG], in_=y3, axis=AX.X, op=ALU.add), reads=[ByTM], writes=[Bgst])
                tt('pool', ysq[0:C, 0:NC_, :], yTM[0:C, 0:NC_, :], yTM[0:C, 0:NC_, :], ALU.mult, [ByTM], [Bysq])
                s3 = ysq[0:C, 0:NC_, :].rearrange("p a (h v) -> p (a h) v", v=64)
                rec.op('dve', lambda e: e.tensor_reduce(out=gst[0:C, 1, 0:NG], in_=s3, axis=AX.X, op=ALU.add), reads=[Bysq], writes=[Bgst])
                ts('dve', gst[0:C, 2, 0:NG], gst[0:C, 0, 0:NG], 1.0 / 64, None, ALU.mult, None, [Bgst], [Bgst])
                tt('dve', gst[0:C, 3, 0:NG], gst[0:C, 2, 0:NG], gst[0:C, 2, 0:NG], ALU.mult, [Bgst], [Bgst])
                stt(gst[0:C, 3, 0:NG], gst[0:C, 1, 0:NG], 1.0 / 64, gst[0:C, 3, 0:NG], ALU.mult, ALU.subtract, [Bgst], [Bgst])
                rsqrt(gst[0:C, 3, 0:NG], gst[0:C, 3, 0:NG], [Bgst, Bconst], [Bgst], bias=epsc[0:C, 2:3])
                yield 's'
                tt('dve', ysq[0:C, 0:NC_, :].rearrange("p a (h v) -> p (a h) v", v=64), y3,
                   gst[0:C, 2, 0:NG].unsqueeze(2).to_broadcast([C, NG, 64]), ALU.subtract, [ByTM, Bgst], [Bysq])
                tt('dve', ynb[0:C, 0:NC_, :].rearrange("p a (h v) -> p (a h) v", v=64), s3,
                   gst[0:C, 3, 0:NG].unsqueeze(2).to_broadcast([C, NG, 64]), ALU.mult, [Bysq, Bgst], [Bynb])
                pt, Bpt = ptbank()
                for ci in range(NC_):
                    tr(pt[:, ci * C:(ci + 1) * C], ynb[0:C, ci, :], identb[0:C, 0:C], [Bynb, Bconst], [Bpt] if ci == NC_ - 1 else None)
                y1, By1 = y1t, By1t
                act(y1[:, :TT], pt[:, :TT], AF.Identity, [Bpt, Bvec], [By1], scale=vcol(l, V_LXG, c), bias=vcol(l, V_LXB, c))
                tt('pool', y1[:, :TT], y1[:, :TT], bv[:, :TT], ALU.add, [By1, Bbv], [By1])
                tt('dve', bfp[:, c, :TT], y1[:, :TT], g_[:, :TT], ALU.mult, [By1, Bg], [Bbfp[c]])

            def tile_layer(l, grp, TT, C, NSTEP):
                gcur[0] = 'sp'[1 - grp] if False else ('p' if grp == 0 else 's')
                NTB_ = max(1, TT // 128)
                TB = min(TT, 128)
                dma('pool', 'W', smallw[:], smsc[l], reads=[Bwsc] + Bsm[l], writes=[Bsmall])
                dma('pool', 'W', bsb[:], b_s[l].unsqueeze(0).partition_broadcast(128).rearrange("p o n -> p (o n)"), writes=[Bbsb])
                rmsnorm_to_h(l, V_MIXG, TT)
                wt, Bw = wpiece(l, PI_LORA)
                for j in range(2):
                    col = 24 + j
                    bk, Bbk = big_mm(wt, Bw, j, hrhs(TT), Bh, TT)
                    cb, Bcb = tmpf()
                    cp('act', cb[:, 1:TT + 1], bk[:, :TT], [Bbk], [Bcb])
                    cp('pool', cb[:, 0:1], prev[:, l, col:col + 1], [Bprev[l]], [Bcb])
                    t1, Bt1 = tmpf()
                    ts('dve', t1[:, :TT], cb[:, 0:TT], vcol(l, V_MU, col), None, ALU.mult, None, [Bcb, Bvec], [Bt1])
                    mx, Bmx = tmpf()
                    stt(mx[:, :TT], cb[:, 1:TT + 1], vcol(l, V_OMM, col), t1[:, :TT], ALU.mult, ALU.add, [Bcb, Bt1, Bvec], [Bmx])
                    cp('pool', prev[:, l, col:col + 1], cb[:, TT:TT + 1], [Bcb], [Bprev[l]])
                    if j == 0:
                        act(lw[0:64, :TT], mx[0:64, :TT], AF.Tanh, [Bmx], [Blw])
                        cp('act', lw[64:128, :TT], mx[64:128, :TT], [Bmx], [Blw])
                    else:
                        act(sgl[:, :TT], mx[:, :TT], AF.Sigmoid, [Bmx], [Bsgl])
                checkpoint('lora')
                def run_front(g):
                    for tok in g:
                        if tok == 'frontdone':
                            return True
                    return False

                def step(g):
                    try:
                        return next(g)
                    except StopIteration:
                        return None
                cur = rwkv_gen(l, 0, TT, C)
                run_front(cur)
                for c in range(8):
                    nxt = rwkv_gen(l, c + 1, TT, C) if c < 7 else None
                    cur_alive, nxt_front = True, nxt is not None
                    while cur_alive or nxt_front:
                        if cur_alive:
                            cur_alive = step(cur) is not None
                        if nxt_front:
                            r_ = step(nxt)
                            if r_ == 'frontdone' or r_ is None:
                                nxt_front = False
                    cur = nxt
                    checkpoint('pair0')
                checkpoint('rwkv')
                for i in range(4):
                    wt, Bw = wpiece(l, PI_SGU + i)
                    for o4 in range(4):
                        oc = i * 4 + o4
                        bk, Bbk = big_mm(wt, Bw, o4, hrhs(TT), Bh, TT)
                        if oc < 8:
                            act(ub[:, oc, :TT], bk[:, :TT], AF.Gelu_apprx_tanh, [Bbk], [Bub[oc]])
                        else:
                            act(big[:, oc - 8, :TT], bk[:, :TT], AF.Gelu_apprx_tanh, [Bbk], [Bbig[oc - 8]])
                bk1, Bb1 = bank(); bk2, Bb2 = bank()
                for c in range(8):
                    vb, Bvb = tmpb(); vs, Bvs = tmpb()
                    cp('pool', vb[:, :TT], big[:, c, :TT], [Bbig[c]], [Bvb])
                    act(vs[:, :TT], big[:, c, :TT], AF.Square, [Bbig[c]], [Bvs])
                    mm(bk1[:, :TT], onesb[:], vb[:, :TT], c == 0, c == 7, [Bvb, Bconst], [Bb1])
                    mm(bk2[:, :TT], onesb[:], vs[:, :TT], c == 0, c == 7, [Bvs, Bconst], [Bb2])
                mean, Bmean = tmpf(); rstd, Brstd = tmpf()
                act(mean[:, :TT], bk1[:, :TT], AF.Copy, [Bb1], [Bmean], scale=1.0 / D)
                tt('pool', rstd[:, :TT], mean[:, :TT], mean[:, :TT], ALU.mult, [Bmean], [Brstd])
                stt(rstd[:, :TT], bk2[:, :TT], 1.0 / D, rstd[:, :TT], ALU.mult, ALU.subtract, [Bb2, Brstd], [Brstd])
                rsqrt(rstd[:, :TT], rstd[:, :TT], [Brstd, Bconst], [Brstd], bias=epsc[:, 1:2])
                vnb = []
                for c in range(8):
                    vc = big[:, c, :TT]
                    tt('dve', vc, vc, mean[:, :TT], ALU.subtract, [Bbig[c], Bmean], [Bbig[c]])
                    tt('pool', vc, vc, rstd[:, :TT], ALU.mult, [Bbig[c], Brstd], [Bbig[c]])
                    ts('dve', vc, vc, vcol(l, V_SLG, c), vcol(l, V_SLB, c), ALU.mult, ALU.add, [Bbig[c], Bvec], [Bbig[c]])
                    cp('pool', bfp[:, 16 + c, :TT], vc, [Bbig[c]], [Bbfp[16 + c]])
                    vnb.append((bfp[:, 16 + c, :], Bbfp[16 + c]))
                if grp == 1:
                    for half in range(2):
                        bk, Bbk = bank()
                        for j in range(4):
                            tr(bk[0:TT, j * 128:(j + 1) * 128], big[:, half * 4 + j, :TT], identf[:], [Bbig[half * 4 + j], Bconst], [Bbk] if j == 3 else None)
                        stg, Bstg = tmpf()
                        cp('act', stg[0:TT, 0:512], bk[0:TT, :], [Bbk], [Bstg])
                        dma('pool', 'R', osgu[l][:, half * 512:(half + 1) * 512], stg[0:TT, 0:512], reads=[Bstg])
                vtm = []
                for tb_ in range(NTB_):
                    halves = []
                    for half in range(2):
                        pt, Bpt = ptbank()
                        for j in range(4):
                            cc = half * 4 + j
                            tr(pt[0:TB, j * 128:(j + 1) * 128], vnb[cc][0][:, tb_ * 128: tb_ * 128 + TB], identb[:], [vnb[cc][1], Bconst], [Bpt] if j == 3 else None)
                        vt_, Bvt_ = bfp[:, 24 + tb_ * 2 + half, :], Bbfp[24 + tb_ * 2 + half]
                        cp('act', vt_[0:TB, :], pt[0:TB, :], [Bpt], [Bvt_])
                        halves.append((vt_, Bvt_))
                    vtm.append(halves)
                wsT = smallw[:, 0:1024]
                for c in range(8):
                    bk, Bbk = bank()
                    for tb_ in range(NTB_):
                        vt_, Bvt_ = vtm[tb_][c // 4]
                        mm(bk[:, tb_ * 128: tb_ * 128 + TB], vt_[0:TB, (c % 4) * 128:(c % 4 + 1) * 128], wsT[0:TB, c * 128: c * 128 + TB], True, True,
                           [Bvt_, Bsmall], [Bbk] if tb_ == NTB_ - 1 else None)
                    sv, Bsv = tmpf()
                    if NTB_ > 1:
                        tt('dve', sv[:, :TT].rearrange("p (a b) -> p a b", b=128), bk[:, :TT].rearrange("p (a b) -> p a b", b=128),
                           bsb[:, c * 128:(c + 1) * 128].unsqueeze(1).to_broadcast([128, NTB_, 128]), ALU.add, [Bbk, Bbsb], [Bsv])
                    else:
                        tt('dve', sv[:, :TT], bk[:, :TT], bsb[:, c * 128: c * 128 + TT], ALU.add, [Bbk, Bbsb], [Bsv])
                    tt('pool', bfp[:, 8 + c, :TT], sv[:, :TT], ub[:, c, :TT], ALU.mult, [Bsv, Bub[c]], [Bbfp[8 + c]])
                checkpoint('sgu')
                for i in range(2):
                    wt, Bw = wpiece(l, PI_Q + i)
                    for o4 in range(4):
                        oc = i * 4 + o4
                        bk, Bbk = big_mm(wt, Bw, o4, hrhs(TT), Bh, TT)
                        cp('act', bfp[:, 24 + oc, :TT], bk[:, :TT], [Bbk], [Bbfp[24 + oc]])
                kvt, Bkvp = piece('kv', l, grp)
                KT = kvt[:, 0:2048].rearrange("p (j m) -> p j m", j=8)
                VV = kvt[:, 2048:4096].rearrange("p (mb n) -> p mb n", mb=2)
                for hh in range(4):
                    pTs = []
                    for mc in range(2):
                        bk, Bbk = bank()
                        for dc in range(2):
                            mm(bk[:, :TT], KT[:, 2 * hh + dc, mc * 128:(mc + 1) * 128], bfp[:, 24 + 2 * hh + dc, :TT], dc == 0, dc == 1,
                               [Bkvp, Bbfp[24 + 2 * hh + dc]], [Bbk] if dc == 1 else None)
                        p_, Bp_ = tmpb()
                        act(p_[:, :TT], bk[:, :TT], AF.Exp, [Bbk], [Bp_], scale=1.0 / 16)
                        pTs.append((p_, Bp_))
                    bk, Bbk = bank()
                    for mc in range(2):
                        mm(bk[:, :TT], onesb[:], pTs[mc][0][:, :TT], mc == 0, mc == 1, [pTs[mc][1], Bconst], [Bbk] if mc == 1 else None)
                    rd, Brd = tmpf()
                    rsqrt(rd[:, :TT], bk[:, :TT], [Bbk], [Brd], power=-1.0)
                    for dvc in range(2):
                        bk, Bbk = bank()
                        for mc in range(2):
                            mm(bk[:, :TT], VV[:, mc, hh * 256 + dvc * 128: hh * 256 + (dvc + 1) * 128], pTs[mc][0][:, :TT], mc == 0, mc == 1,
                               [Bkvp, pTs[mc][1]], [Bbk] if mc == 1 else None)
                        tt('dve', bfp[:, 16 + 2 * hh + dvc, :TT], bk[:, :TT], rd[:, :TT], ALU.mult, [Bbk, Brd], [Bbfp[16 + 2 * hh + dvc]])
                if DBG and grp == 0 and l == 0:
                    dma('pool', 'R', dbg_y, bfp[:, 0:24, :], reads=[Bbfp[i] for i in range(24)])
                checkpoint('attn')
                for b in range(3):
                    for hf in range(2):
                        wg, Bwg = wpiece(l, PI_GATE + b * 2 + hf)
                        wb_, Bwb = wpiece(l, PI_BR + b * 2 + hf)
                        for o4 in range(4):
                            oc = hf * 4 + o4
                            bk, Bbk = big_mm(wg, Bwg, o4, hrhs(TT), Bh, TT)
                            gt, Bgt = tmpb()
                            act(gt[:, :TT], bk[:, :TT], AF.Sigmoid, [Bbk], [Bgt])
                            rh = [bfp[:, b * 8 + kc, :TT] for kc in range(8)]
                            bk, Bbk = big_mm(wb_, Bwb, o4, rh, [Bbfp[b * 8 + kc] for kc in range(8)], TT)
                            if b == 0:
                                tt('dve', big[:, oc, :TT], bk[:, :TT], gt[:, :TT], ALU.mult, [Bbk, Bgt], [Bbig[oc]])
                            else:
                                tm_, Btm = tmpf()
                                tt('dve', tm_[:, :TT], bk[:, :TT], gt[:, :TT], ALU.mult, [Bbk, Bgt], [Btm])
                                if b == 1:
                                    tt('pool', big[:, oc, :TT], big[:, oc, :TT], tm_[:, :TT], ALU.add, [Bbig[oc], Btm], [Bbig[oc]])
                                else:
                                    tt('pool', bfp[:, 24 + oc, :TT], big[:, oc, :TT], tm_[:, :TT], ALU.add, [Bbig[oc], Btm], [Bbfp[24 + oc]])
                if DBG and grp == 0 and l == 0:
                    dma('pool', 'R', dbg_m, bfp[:, 24:32, :], reads=[Bbfp[24 + i] for i in range(8)])
                for hf in range(2):
                    wt, Bw = wpiece(l, PI_OUT + hf)
                    for o4 in range(4):
                        oc = hf * 4 + o4
                        rh = [bfp[:, 24 + kc, :TT] for kc in range(8)]
                        bk, Bbk = big_mm(wt, Bw, o4, rh, [Bbfp[24 + kc] for kc in range(8)], TT)
                        tt('dve', x_t[:, oc, :TT], x_t[:, oc, :TT], bk[:, :TT], ALU.add, [Bx[oc], Bbk], [Bx[oc]])
                if DBG and grp == 0 and l == 0:
                    dma('pool', 'R', dbg_x, x_t[:], reads=Bx)
                checkpoint('merge')
                rmsnorm_to_h(l, V_FFNG, TT)
                for i in range(8):
                    wt, Bw = wpiece(l, PI_UP + i)
                    for o4 in range(4):
                        oc = i * 4 + o4
                        bk, Bbk = big_mm(wt, Bw, o4, hrhs(TT), Bh, TT)
                        rl, Brl = tmpf()
                        act(rl[:, :TT], bk[:, :TT], AF.Relu, [Bbk], [Brl])
                        tt('pool', bfp[:, oc, :TT], rl[:, :TT], rl[:, :TT], ALU.mult, [Brl], [Bbfp[oc]])
                for g in range(2):
                    banks = [bank() for _ in range(4)]
                    for q in range(4):
                        wt, Bw = wpiece(l, PI_DN + g * 4 + q)
                        for o4 in range(4):
                            rh = [bfp[:, q * 8 + kc, :TT] for kc in range(8)]
                            big_mm(wt, Bw, o4, rh, [Bbfp[q * 8 + kc] for kc in range(8)], TT, out_bk=banks[o4], start=(q == 0), stop=(q == 3))
                    for o4 in range(4):
                        oc = g * 4 + o4
                        tt('dve', x_t[:, oc, :TT], x_t[:, oc, :TT], banks[o4][0][:, :TT], ALU.add, [Bx[oc], banks[o4][1]], [Bx[oc]])

            def dbg_x2_dump():
                if DBG:
                    dma('pool', 'R', dbg_x2, x_t[:], reads=Bx)

            def load_x(src, TT):
                NTB_ = max(1, TT // 128); TB = min(TT, 128)
                for tb_ in range(NTB_):
                    dma('pool', 'W', big[0:TB, 2 * tb_:2 * tb_ + 2, :].rearrange("p a n -> p (a n)"), src[tb_ * 128: tb_ * 128 + TB, :],
                        writes=[Bbig[2 * tb_], Bbig[2 * tb_ + 1]])
                for c in range(8):
                    bk, Bbk = bank()
                    for tb_ in range(NTB_):
                        tr(bk[:, tb_ * 128: tb_ * 128 + TB], big[0:TB, 2 * tb_ + c // 4, (c % 4) * 128:(c % 4 + 1) * 128], identf[0:TB, 0:TB],
                           [Bbig[2 * tb_ + c // 4], Bconst], [Bbk] if tb_ == NTB_ - 1 else None)
                    cp('act', x_t[:, c, :TT], bk[:, :TT], [Bbk], [Bx[c]])

            def store_y(dst, TT):
                NTB_ = max(1, TT // 128); TB = min(TT, 128)
                bk, Bbk = bank()
                for c in range(8):
                    sq, Bsq = tmpb()
                    act(sq[:, :TT], x_t[:, c, :TT], AF.Square, [Bx[c]], [Bsq])
                    mm(bk[:, :TT], onesb[:], sq[:, :TT], c == 0, c == 7, [Bsq, Bconst], [Bbk] if c == 7 else None)
                rs, Brs = tmpf()
                rsqrt(rs[:, :TT], bk[:, :TT], [Bbk, Bconst], [Brs], scale=1.0 / D, bias=epsc[:, 0:1])
                for c in range(8):
                    stt(big[:, c, :TT], x_t[:, c, :TT], fing_t[:, c:c + 1], rs[:, :TT], ALU.mult, ALU.mult, [Bx[c], Brs, Bfing], [Bbig[c]])
                for tb_ in range(NTB_):
                    for half in range(2):
                        bk, Bbk = bank()
                        for j in range(4):
                            cc = half * 4 + j
                            tr(bk[0:TB, j * 128:(j + 1) * 128], big[:, cc, tb_ * 128: tb_ * 128 + TB], identf[:], [Bbig[cc], Bconst], [Bbk] if j == 3 else None)
                        stg, Bstg = tmpf()
                        cp('act', stg[0:TB, 0:512], bk[0:TB, :], [Bbk], [Bstg])
                        dma('pool', 'R', dst[tb_ * 128: tb_ * 128 + TB, half * 512:(half + 1) * 512], stg[0:TB, 0:512], reads=[Bstg])

            def store_state(l, owkv, oshift):
                for c in range(8):
                    bk, Bbk = bank()
                    tr(bk[0:64, 0:128], S32[:, l, c, :], identf[:], [BS32[l][c], Bconst], [Bbk])
                    stg, Bstg = tmpf()
                    cp('act', stg[0:64, 0:128], bk[0:64, 0:128], [Bbk], [Bstg])
                    dma('pool', 'R', owkv[l, 2 * c:2 * c + 2].rearrange("h v k -> v h k"), stg[0:64, 0:128].rearrange("p (h k) -> p h k", h=2), reads=[Bstg])
                bk, Bbk = bank()
                tr(bk[0:26, 0:128], prev[:, l, :], identf[:], [Bprev[l], Bconst], [Bbk])
                stg, Bstg = tmpf()
                cp('act', stg[0:26, 0:128], bk[0:26, 0:128], [Bbk], [Bstg])
                dma('pool', 'R', oshift[l].rearrange("(j p) -> j p", p=128), stg[0:26, 0:128], reads=[Bstg])

            memset('pool', cmask[:], 1.0, [Bconst])
            memset('pool', cmask[:].rearrange("p (a b) -> p a b", b=128)[:, :, 0:1], 0.0, [Bconst])

            Bm_ = [Bbig[i] for i in range(4)]
            dma('pool', 'W', big[:, 0:4, :].rearrange("p (mb a) n -> p mb (a n)", mb=2), memp.rearrange("(mb p) n -> p mb n", p=128), writes=Bm_)
            memf = big[:, 4:8, :].rearrange("p a (b n) -> p (a b) n", b=2)
            Bmemf = [Bbig[i] for i in range(4, 8)]
            for mb in range(2):
                ssq, Bssq = tmpf()
                for a in range(2):
                    jk, Bjk = tmpf()
                    rec.op('act', lambda e, a=a, mb=mb, jk=jk, ssq=ssq: e.activation(out=jk[:, 0:512], in_=big[:, mb * 2 + a, :], func=AF.Square, accum_out=ssq[:, a:a + 1]),
                           reads=[Bbig[mb * 2 + a]], writes=[Bjk, Bssq])
                tt('dve', ssq[:, 2:3], ssq[:, 0:1], ssq[:, 1:2], ALU.add, [Bssq], [Bssq])
                act(ssq[:, 3:4], ssq[:, 2:3], AF.Sqrt, [Bssq, Bconst], [Bssq], scale=1.0 / D, bias=epsc[:, 0:1])
                recip(ssq[:, 4:5], ssq[:, 3:4], [Bssq], [Bssq])
                for a in range(2):
                    ts('dve', big[:, mb * 2 + a, :], big[:, mb * 2 + a, :], ssq[:, 4:5], None, ALU.mult, None, [Bbig[mb * 2 + a], Bssq], [Bbig[mb * 2 + a]])
            for c in range(8):
                bk, Bbk = bank()
                for mb in range(2):
                    tr(bk[:, mb * 128:(mb + 1) * 128], big[:, mb * 2 + c // 4, (c % 4) * 128:(c % 4 + 1) * 128], identf[:], Bm_ + [Bconst], [Bbk] if mb == 1 else None)
                cp('act', memf[:, c, :], bk[:, 0:256], [Bbk], Bmemf)
            for l in range(L):
                mnt = [(bfp[:, 8 + i, :], Bbfp[8 + i]) for i in range(4)]
                for c in range(8):
                    t_, B_ = mnt[c // 2]
                    ts('dve', t_[:, (c % 2) * 256:(c % 2 + 1) * 256], memf[:, c, :], vcol(l, V_MEMG, c), None, ALU.mult, None, Bmemf + [Bvec], [B_])
                mrhs = lambda c: mnt[c // 2][0][:, (c % 2) * 256:(c % 2 + 1) * 256]
                kvst = [(bfp[:, i, :], Bbfp[i]) for i in range(8)]
                for i in range(4):
                    wt, Bw = wpiece(l, PI_MKV + i)
                    for mb in range(2):
                        bk, Bbk = bank()
                        for kc in range(8):
                            mm(bk[:, :], mrhs(kc)[:, mb * 128:(mb + 1) * 128], wt[:, kc, :], kc == 0, kc == 7, [mnt[kc // 2][1], Bw], [Bbk] if kc == 7 else None)
                        stg, Bstg = tmpf()
                        cp('act', stg[:, 0:512], bk[:, :], [Bbk], [Bstg])
                        dst = (omk if i < 2 else omv)[l][mb * 128:(mb + 1) * 128, (i % 2) * 512:(i % 2 + 1) * 512]
                        dma('pool', 'R', dst, stg[:, 0:512], reads=[Bstg])
                        if i >= 2:
                            s_, Bs_ = kvst[4 + mb * 2 + (i - 2)]
                            cp('pool', s_, stg[:, 0:512], [Bstg], [Bs_])
                    if i < 2:
                        for o4 in range(4):
                            j = i * 4 + o4
                            bk, Bbk = bank()
                            for kc in range(8):
                                mm(bk[:, 0:256], wt[:, kc, o4 * 128:(o4 + 1) * 128], mrhs(kc), kc == 0, kc == 7, [mnt[kc // 2][1], Bw], [Bbk] if kc == 7 else None)
                            s_, Bs_ = kvst[j // 2]
                            cp('act', s_[:, (j % 2) * 256:(j % 2 + 1) * 256], bk[:, 0:256], [Bbk], [Bs_])
                for i in range(8):
                    Bd = Buf(); Bkv[l][0].append(Bd)
                    dma('pool', 'R', kvsc[l, 0][:, i * 512:(i + 1) * 512], kvst[i][0], reads=[kvst[i][1]], writes=[Bd])

            checkpoint('memkv')
            for l in range(L):
                for c in range(8):
                    stg, Bstg = tmpf()
                    dma('pool', 'W', stg[0:64, 0:128].rearrange("p (h k) -> p h k", h=2), swkv[l, 2 * c:2 * c + 2].rearrange("h v k -> v h k"), writes=[Bstg])
                    bk, Bbk = bank()
                    tr(bk[:, 0:64], stg[0:64, 0:128], identf[0:64, 0:64], [Bstg, Bconst], [Bbk])
                    cp('act', S32[:, l, c, :], bk[:, 0:64], [Bbk], [BS32[l][c]])
                stg, Bstg = tmpf()
                dma('pool', 'W', stg[0:26, 0:128], sshift[l].rearrange("(j p) -> j p", p=128), writes=[Bstg])
                bk, Bbk = bank()
                tr(bk[:, 0:26], stg[0:26, 0:128], identf[0:26, 0:26], [Bstg, Bconst], [Bbk])
                cp('act', prev[:, l, :], bk[:, 0:26], [Bbk], [Bprev[l]])
            load_x(xs, TS)
            for l in range(L):
                tile_layer(l, 1, TS, TS, 5)
            store_y(ys, TS)
            for l in range(L):
                store_state(l, owkv_s, oshift_s)

            checkpoint('sample')
            for l in range(L):
                for c in range(8):
                    memset('pool', S32[:, l, c, :], 0.0, [BS32[l][c]])
                memset('pool', prev[:, l, :], 0.0, [Bprev[l]])
            for t_ in range(NT):
                load_x(xp[t_ * T:(t_ + 1) * T], T)
                for l in range(L):
                    tile_layer(l, 0, T, 128, 7)
                    if l == 0 and t_ == 0:
                        dbg_x2_dump()
                store_y(yp[t_ * T:(t_ + 1) * T], T)
            for l in range(L):
                store_state(l, owkv_p, oshift_p)

        except _Stop:
            pass
        finish()
    return nc


def _vec_table(inp, L):
    def fm(a, n):
        return np.ascontiguousarray(a.reshape(L, n, 128).transpose(2, 0, 1))
    mu = inp['rwkv_mu'][:L]
    ka = inp['rwkv_k_a'][:L]
    parts = [fm(inp['norm_mix_g'][:L], 8), fm(inp['norm_ffn_g'][:L], 8), fm(inp['norm_mem_g'][:L], 8),
             fm(mu, 26), np.zeros((128, L, 26), np.float32), fm(inp['rwkv_w0'][:L], 8), fm(inp['rwkv_a0'][:L], 8),
             fm(inp['rwkv_k_k'][:L], 8), fm(ka, 8), np.zeros((128, L, 8), np.float32), fm(inp['rwkv_r_k'][:L].reshape(L, 1024), 8),
             fm(inp['sgu_ln_g'][:L], 8), fm(inp['sgu_ln_b'][:L], 8), fm(inp['rwkv_lnx_g'][:L], 8), fm(inp['rwkv_lnx_b'][:L], 8)]
    return np.ascontiguousarray(np.concatenate(parts, axis=2).astype(np.float32))


def run(inp, SEQ, L, n_cores=8, trace=False, DBG=False, STOP=None):
    nc = build(SEQ, L, DBG=DBG, STOP=STOP)
    f = lambda a: np.ascontiguousarray(a, dtype=np.float32)
    vecs = _vec_table(inp, L)
    fing = np.ascontiguousarray(inp['norm_final_g'].reshape(8, 128).T.astype(np.float32))
    shared = {
        'vecs': vecs, 'fing': fing, 'w_in': f(inp['w_in'][:L]), 'w_mkv': f(inp['w_mem_kv'][:L]),
        'w2': f(inp['rwkv_w2'][:L]), 'a2': f(inp['rwkv_a2'][:L]), 'g2': f(inp['rwkv_g2'][:L]),
        'w_s': f(inp['sgu_w_s'][:L]), 'b_s': f(inp['sgu_b_s'][:L].reshape(L, 1024)),
        'w_br': f(inp['w_branch'][:L]), 'w_out': f(inp['w_out'][:L]), 'w_up': f(inp['w_ffn_up'][:L]), 'w_dn': f(inp['w_ffn_down'][:L]),
    }
    in_maps = []
    for b in range(n_cores):
        m = dict(shared)
        m['xp'] = f(inp['x_prompt'][b, :SEQ]); m['xs'] = f(inp['x_sample'][b])
        m['ck'] = f(inp['cache_mem_k'][:L, b].reshape(L, 256, 1024)); m['cv'] = f(inp['cache_mem_v'][:L, b].reshape(L, 256, 1024))
        m['swkv'] = f(inp['state_wkv'][:L, b]); m['sshift'] = f(inp['state_shift'][:L, b, 0])
        m['memp'] = f(inp['mem_prompt'][b])
        in_maps.append(m)
    res = run_bass_kernel_spmd(nc, in_maps, core_ids=list(range(n_cores)), trace=trace)
    R = res.results
    st = lambda k: np.stack([np.asarray(r[k]) for r in R])
    y_p = st('yp'); y_s = st('ys')
    mk = st('omk').transpose(1, 0, 2, 3).reshape(L, n_cores, 256, 4, 256)
    mv = st('omv').transpose(1, 0, 2, 3).reshape(L, n_cores, 256, 4, 256)
    wkv_p = st('owkv_p').transpose(1, 0, 2, 3, 4)
    sh_p = st('oshift_p').transpose(1, 0, 2)[:, :, None, :]
    wkv_s = st('owkv_s').transpose(1, 0, 2, 3, 4)
    sh_s = st('oshift_s').transpose(1, 0, 2)[:, :, None, :]
    sgu = st('osgu').transpose(1, 0, 2, 3)
    outs = (y_p, y_s, mk, mv, wkv_p, sh_p, wkv_s, sh_s, sgu)
    return tuple(np.ascontiguousarray(o.astype(np.float32)) for o in outs), res


def kernel(**inputs):
    inp = {k: np.asarray(v) for k, v in inputs.items()}
    outs, _ = run(inp, 8192, 4, 8)
    return outs
```

```python
import contextlib
import numpy as np
import concourse.bass as bass
import concourse.mybir as mybir
from concourse.bass_utils import run_bass_kernel_spmd

F32 = mybir.dt.float32
BF16 = mybir.dt.bfloat16
AF = mybir.ActivationFunctionType
ALU = mybir.AluOpType
AX = mybir.AxisListType

ENGS = ('pe', 'act', 'dve', 'pool', 'sp')


class Buf:
    __slots__ = ('w', 'r', 'id')
    _n = [0]

    def __init__(self):
        self.w = None
        self.r = {}
        Buf._n[0] += 1
        self.id = Buf._n[0]


class Rec:
    def __init__(self, nc):
        self.nc = nc
        self.streams = {e: [] for e in ENGS}
        self.cnt = {e: 0 for e in ENGS}
        self.dcnt = {}
        self.waited = {e: {} for e in ENGS}
        self.pending = {e: [] for e in ENGS}

    def _wait(self, eng, tok):
        if tok is None:
            return
        key, val = tok
        if self.waited[eng].get(key, 0) >= val:
            return
        self.waited[eng][key] = val
        self.streams[eng].append(('w', key, val))

    def _deps(self, eng, reads, writes, extra):
        for t in extra:
            self._wait(eng, t)
        for b in reads:
            self._wait(eng, b.w)
        for b in writes:
            self._wait(eng, b.w)
            for t in b.r.values():
                self._wait(eng, t)
            for e2 in ENGS:
                if e2 != eng and any(pb_ is b for pb_ in self.pending[e2]):
                    raise RuntimeError("write to buffer %d with pending unmarked reads on %s" % (b.id, e2))

    def op(self, eng, fn, reads=(), writes=(), mark=True, extra=(), wdeps=()):
        self._deps(eng, reads, tuple(writes) + tuple(wdeps), extra)
        if mark:
            self.cnt[eng] += 1
            tok = (eng, self.cnt[eng])
            self.streams[eng].append(('i', fn, True))
            for b in self.pending[eng]:
                b.r[eng] = tok
            self.pending[eng] = []
            for b in reads:
                b.r[eng] = tok
            for b in writes:
                b.w = tok
                b.r = {}
            return tok
        assert not writes
        self.streams[eng].append(('i', fn, False))
        self.pending[eng].extend(reads)
        return None

    def dma(self, q, sem, fn, reads=(), writes=(), extra=()):
        self._deps(q, reads, writes, extra)
        self.dcnt[sem] = self.dcnt.get(sem, 0) + 16
        tok = (sem, self.dcnt[sem])
        self.streams[q].append(('d', fn, sem))
        for b in reads:
            b.r[sem] = tok
        for b in writes:
            b.w = tok
            b.r = {}
        return tok

    def emit(self, stack):
        nc = self.nc
        sems = {}
        for e in ENGS:
            sems[e] = stack.enter_context(nc.semaphore('s_' + e))
        for d in self.dcnt:
            sems[d] = stack.enter_context(nc.semaphore('d_' + d))
        block = stack.enter_context(nc.Block())
        dec = {'pe': block.tensor, 'act': block.scalar, 'dve': block.vector, 'pool': block.gpsimd, 'sp': block.sync}
        for e in ENGS:
            stream = self.streams[e]
            own = sems[e]

            def body(engine, stream=stream, own=own):
                for item in stream:
                    if item[0] == 'w':
                        engine.wait_ge(sems[item[1]], item[2])
                    elif item[0] == 'i':
                        ins = item[1](engine)
                        if item[2]:
                            ins.then_inc(own, 1)
                    else:
                        item[1](engine).then_inc(sems[item[2]], 16)
            dec[e](body)


D = 1024
NCH = 8
RW = 3328
IN_COLS = 9472
NP_ = 49
PI_LORA, PI_RKV, PI_SGU, PI_Q, PI_GATE, PI_BR, PI_OUT, PI_UP, PI_DN, PI_MKV = 0, 1, 9, 13, 15, 21, 27, 29, 37, 45
C0 = -0.6065306597126334
V_MIXG, V_FFNG, V_MEMG, V_MU, V_OMM, V_W0, V_A0, V_KK, V_KA, V_OMKA, V_RK, V_SLG, V_SLB, V_LXG, V_LXB = (
    0, 8, 16, 24, 50, 76, 84, 92, 100, 108, 116, 124, 132, 140, 148)
VW = 156


class _Stop(Exception):
    pass


def build(SEQ, DEPTH, MEMN=256, TS=32, DBG=False, STOP=None):
    nc = bass.Bass("TRN2", target_bir_lowering=False)
    L = DEPTH
    T = 512
    NT = SEQ // T
    din = lambda name, shape, dt=F32: nc.dram_tensor(name, shape, dt, kind="ExternalInput").ap()
    dout = lambda name, shape, dt=F32: nc.dram_tensor(name, shape, dt, kind="ExternalOutput").ap()
    dscr = lambda name, shape, dt: nc.dram_tensor(name, shape, dt, kind="Internal").ap()
    xp = din("xp", [SEQ, D]); xs = din("xs", [TS, D])
    ck = din("ck", [L, MEMN, D]); cv = din("cv", [L, MEMN, D])
    swkv = din("swkv", [L, 16, 64, 64]); sshift = din("sshift", [L, RW]); memp = din("memp", [MEMN, D])
    vecs = din("vecs", [128, L, VW]); fing = din("fing", [128, 8])
    w_in = din("w_in", [L, D, IN_COLS]); w_mkv = din("w_mkv", [L, D, 2 * D])
    w2 = din("w2", [L, 64, D]); a2 = din("a2", [L, 64, D]); g2 = din("g2", [L, 128, D])
    w_s = din("w_s", [L, 8, 128, 128]); b_s = din("b_s", [L, 8 * 128])
    w_br = din("w_br", [L, 3, D, D]); w_out = din("w_out", [L, D, D])
    w_up = din("w_up", [L, D, 4 * D]); w_dn = din("w_dn", [L, 4 * D, D])
    yp = dout("yp", [SEQ, D]); ys = dout("ys", [TS, D])
    omk = dout("omk", [L, MEMN, D]); omv = dout("omv", [L, MEMN, D])
    owkv_p = dout("owkv_p", [L, 16, 64, 64]); oshift_p = dout("oshift_p", [L, RW])
    owkv_s = dout("owkv_s", [L, 16, 64, 64]); oshift_s = dout("oshift_s", [L, RW])
    osgu = dout("osgu", [L, TS, D])
    if DBG:
        dbg_y = dout("dbg_y", [128, 24, 512], BF16); dbg_x = dout("dbg_x", [128, 8, 512]); dbg_x2 = dout("dbg_x2", [128, 8, 512]); dbg_m = dout("dbg_m", [128, 8, 512], BF16)
    wsc = dscr("wsc", [L, NP_, 128, 4096], BF16)
    kvsc = dscr("kvsc", [L, 2, 128, 4096], BF16)
    smsc = dscr("smsc", [L, 128, 3072], BF16)

    rec = Rec(nc)
    st = contextlib.ExitStack()
    with st:
        def sb(name, shape, dt):
            return st.enter_context(nc.sbuf_tensor(name, shape, dt))

        def ps(name, shape, dt):
            return st.enter_context(nc.psum_tensor(name, shape, dt))

        def mm(out, lhsT, rhs, start, stop, reads, writes=None):
            rec.op('pe', lambda e: e.matmul(out, lhsT=lhsT, rhs=rhs, start=start, stop=stop),
                   reads=reads, writes=writes or (), mark=writes is not None, wdeps=() if writes is not None else psum_bufs[out.name])

        def tr(out, in_, ident, reads, writes=None):
            rec.op('pe', lambda e: e.transpose(out=out, in_=in_, identity=ident),
                   reads=reads, writes=writes or (), mark=writes is not None, wdeps=() if writes is not None else psum_bufs[out.name])

        def act(out, in_, func, reads, writes, scale=1.0, bias=None):
            if bias is None:
                rec.op('act', lambda e: e.activation(out=out, in_=in_, func=func, scale=scale), reads=reads, writes=writes)
            else:
                rec.op('act', lambda e: e.activation(out=out, in_=in_, func=func, scale=scale, bias=bias), reads=reads, writes=writes)

        def tt(eng, out, in0, in1, op, reads, writes):
            rec.op(eng, lambda e: e.tensor_tensor(out=out, in0=in0, in1=in1, op=op), reads=reads, writes=writes)

        def ts(eng, out, in0, s1, s2, op0, op1, reads, writes):
            if op1 is None and eng == 'pool' and op0 == ALU.mult:
                s2, op1 = 0.0, ALU.add
            if op1 is None:
                rec.op(eng, lambda e: e.tensor_scalar(out=out, in0=in0, scalar1=s1, scalar2=None, op0=op0), reads=reads, writes=writes)
            else:
                rec.op(eng, lambda e: e.tensor_scalar(out=out, in0=in0, scalar1=s1, scalar2=s2, op0=op0, op1=op1), reads=reads, writes=writes)

        def stt(out, in0, scalar, in1, op0, op1, reads, writes):
            rec.op('dve', lambda e: e.scalar_tensor_tensor(out=out, in0=in0, scalar=scalar, in1=in1, op0=op0, op1=op1), reads=reads, writes=writes)

        def cp(eng, out, in_, reads, writes):
            if eng == 'act':
                act(out, in_, AF.Copy, reads, writes)
            else:
                rec.op(eng, lambda e: e.tensor_copy(out=out, in_=in_), reads=reads, writes=writes)

        def rsqrt(out, in_, reads, writes, scale=1.0, bias=None, power=-0.5):
            act(out, in_, AF.Ln, reads, writes, scale=scale, bias=bias)
            act(out, out, AF.Exp, writes, writes, scale=power)

        def cpred(out, mask, data, reads, writes):
            rec.op('dve', lambda e: e.copy_predicated(out=out, mask=mask, data=data), reads=reads, writes=writes)

        def recip(out, in_, reads, writes):
            rec.op('dve', lambda e: e.reciprocal(out=out, in_=in_), reads=reads, writes=writes)

        def memset(eng, ap, val, writes):
            rec.op(eng, lambda e: e.memset(ap, val), writes=writes)

        def dma(q, sem, out, in_, reads=(), writes=(), slow=False):
            if sem == 'W':
                sem = 'b%d' % writes[0].id
            elif sem == 'R':
                sem = 'b%d' % reads[0].id
            if slow:
                return rec.dma(q, sem, lambda e: e.dma_start(out=out, in_=in_, allow_slow_non_contiguous=True), reads=reads, writes=writes)
            return rec.dma(q, sem, lambda e: e.dma_start(out=out, in_=in_), reads=reads, writes=writes)

        x_t = sb("x_t", [128, 8, T], F32); Bx = [Buf() for _ in range(8)]
        h_t = sb("h_t", [128, 8, T], BF16); Bh = [Buf() for _ in range(8)]
        NS = 3
        ws_t = [sb(f"ws{i}", [128, 4096], BF16) for i in range(NS)]; Bws = [Buf() for _ in range(NS)]
        big = sb("big", [128, 8, T], F32); Bbig = [Buf() for _ in range(8)]
        ub = sb("ub", [128, 8, T], BF16); Bub = [Buf() for _ in range(8)]
        bfp = sb("bfp", [128, 32, T], BF16); Bbfp = [Buf() for _ in range(32)]
        NTF = 11
        tf_t = sb("tf_t", [128, NTF, T + 1], F32); Btf = [Buf() for _ in range(NTF)]
        NTB = 13
        tb_t = sb("tb_t", [128, NTB, T], BF16); Btb = [Buf() for _ in range(NTB)]
        tfi = [0]; tbi = [0]

        def tmpf():
            i = tfi[0] % NTF; tfi[0] += 1
            return tf_t[:, i, :], Btf[i]

        def tmpb():
            i = tbi[0] % NTB; tbi[0] += 1
            return tb_t[:, i, :], Btb[i]

        S32 = sb("S32", [128, L, 8, 64], F32); BS32 = [[Buf() for _ in range(8)] for _ in range(L)]
        Sbd = sb("Sbd", [128, 8, 128], BF16); BSbd = [Buf() for _ in range(8)]
        prev = sb("prev", [128, L, 26], F32); Bprev = [Buf() for _ in range(L)]
        vec_t = sb("vec_t", [128, L, VW], F32); Bvec = Buf()
        fing_t = sb("fing_t", [128, 8], F32); Bfing = Buf()
        lw = sb("lw", [128, T], BF16); Blw = Buf(); sgl = sb("sgl", [128, T], BF16); Bsgl = Buf()
        smallw = sb("smallw", [128, 3072], BF16); Bsmall = Buf()
        bsb = sb("bsb", [128, 1024], F32); Bbsb = Buf()
        identf = sb("identf", [128, 128], F32); identb = sb("identb", [128, 128], BF16)
        onesb = sb("onesb", [128, 128], BF16); blkb = sb("blkb", [128, 128], BF16)
        onesf = sb("onesf", [128, 128], F32)
        msu = sb("msu", [128, 128], F32); miu = sb("miu", [128, 128], F32); msl = sb("msl", [128, 128], F32)
        cmask = sb("cmask", [128, T], F32)
        epsc = sb("epsc", [128, 4], F32)
        Bconst = Buf()
        vTp = [sb(f"vT{i}", [128, 4, 128], BF16) for i in range(2)]; kTp = [sb(f"kT{i}", [128, 4, 128], BF16) for i in range(2)]
        bTp = [sb(f"bT{i}", [128, 4, 128], BF16) for i in range(2)]
        BvTp, BkTp, BbTp = [Buf(), Buf()], [Buf(), Buf()], [Buf(), Buf()]
        bvt = [sb(f"bvt{i}", [128, T], BF16) for i in range(2)]; Bbvt = [Buf(), Buf()]
        y1t = sb("y1t", [128, T], F32); By1t = Buf()
        NQt = [sb(f"NQt{i}", [128, 2, 2, 128], BF16) for i in range(4)]; BNQ = [Buf() for _ in range(4)]
        MTt = [sb(f"MTt{i}", [128, 2, 2, 128], BF16) for i in range(4)]; BMT = [Buf() for _ in range(4)]
        Xbt = [sb(f"Xbt{i}", [128, 2, 2, 128], BF16) for i in range(4)]; BXb = [Buf() for _ in range(4)]
        m2d = sb("m2d", [128, 7, 2, 128], BF16)
        m2d_u16 = m2d[:].bitcast(mybir.dt.uint16)
        AKt = [sb(f"AKt{i}", [128, 2, 128], BF16) for i in range(4)]; BAK = [Buf() for _ in range(4)]
        RRt = [sb(f"RRt{i}", [128, 4, 128], BF16) for i in range(4)]; BRR = [Buf() for _ in range(4)]
        Zb = [sb(f"Zb{i}", [128, 128], BF16) for i in range(2)]; BZb = [Buf(), Buf()]
        Ub = [sb(f"Ub{i}", [128, 128], BF16) for i in range(2)]; BUb = [Buf(), Buf()]
        Swt = [sb(f"Swt{i}", [128, 64], F32) for i in range(2)]; BSw = [Buf(), Buf()]
        wc_p = [sb(f"wc{i}", [128, 4], F32) for i in range(2)]; Bwcp = [Buf(), Buf()]
        yTM = sb("yTM", [128, 4, 128], F32); ByTM = Buf()
        ysq = sb("ysq", [128, 4, 128], F32); Bysq = Buf()
        ynb = sb("ynb", [128, 4, 128], BF16); Bynb = Buf()
        gst = sb("gst", [128, 4, 8], F32); Bgst = Buf()

        NB = 7
        pb = [ps(f"pb{i}", [128, 512], F32) for i in range(NB)]; Bpb = [Buf() for _ in range(NB)]
        pT = ps("pT", [128, 1024], BF16); _bpt = Buf(); BpT = [_bpt, _bpt]
        psum_bufs = {f"pb{i}": [Bpb[i]] for i in range(NB)}
        psum_bufs["pT"] = [_bpt]
        bki = [0]

        def bank():
            i = bki[0] % NB; bki[0] += 1
            return pb[i], Bpb[i]

        pti = [0]

        def ptbank():
            i = pti[0] % 2; pti[0] += 1
            return pT[:, i * 512:(i + 1) * 512], BpT[i]

        gcur = ['']

        def checkpoint(name):
            if STOP == name or STOP == gcur[0] + ':' + name:
                raise _Stop()

        def finish():
            for k_, v_ in list(rec.dcnt.items()):
                rec._wait('sp', (k_, v_))
            rec.emit(st)

        try:
            memset('pool', onesf[:], 1.0, [Bconst])
            memset('pool', identf[:], 0.0, [Bconst])
            rec.op('pool', lambda e: e.affine_select(out=identf[:], in_=identf[:], pattern=[[-1, 128]], compare_op=ALU.not_equal, fill=1.0, base=0, channel_multiplier=1), reads=[Bconst], writes=[Bconst])
            cp('pool', identb[:], identf[:], [Bconst], [Bconst])
            cp('pool', onesb[:], onesf[:], [Bconst], [Bconst])
            memset('pool', blkb[:], 0.0, [Bconst])
            memset('pool', blkb[0:64, 0:64], 1.0, [Bconst])
            memset('pool', blkb[64:128, 64:128], 1.0, [Bconst])
            rec.op('pool', lambda e: e.affine_select(out=msu[:], in_=onesf[:], pattern=[[1, 128]], compare_op=ALU.is_gt, fill=0.0, base=0, channel_multiplier=-1), reads=[Bconst], writes=[Bconst])
            rec.op('pool', lambda e: e.affine_select(out=miu[:], in_=onesf[:], pattern=[[1, 128]], compare_op=ALU.is_ge, fill=0.0, base=0, channel_multiplier=-1), reads=[Bconst], writes=[Bconst])
            rec.op('pool', lambda e: e.affine_select(out=msl[:], in_=onesf[:], pattern=[[-1, 128]], compare_op=ALU.is_gt, fill=0.0, base=0, channel_multiplier=1), reads=[Bconst], writes=[Bconst])
            Ea, Eb, Ec = tf_t[:, 0, 0:128], tf_t[:, 1, 0:128], tf_t[:, 2, 0:128]
            BE_ = [Btf[0], Btf[1], Btf[2]]
            cp('pool', Ea, identf[:], [Bconst], [BE_[0]])
            Ecur, Bcur, Enxt, Bnxt = Ea, BE_[0], Eb, BE_[1]
            for j in range(7):
                s_ = 2 ** (j + 1); nb_ = 128 // s_
                if j == 6:
                    cp('pool', Enxt, onesf[:], [Bconst], [Bnxt])
                else:
                    def sel1(e, out=Enxt, s_=s_, nb_=nb_):
                        return e.affine_select(out=out.rearrange("p (a b) -> p a b", b=s_), in_=onesf[:].rearrange("p (a b) -> p a b", b=s_),
                                               pattern=[[-s_, nb_], [0, s_]], compare_op=ALU.is_ge, fill=0.0, base=0, channel_multiplier=1)

                    def sel2(e, out=Enxt, s_=s_, nb_=nb_):
                        return e.affine_select(out=out.rearrange("p (a b) -> p a b", b=s_), in_=out.rearrange("p (a b) -> p a b", b=s_),
                                               pattern=[[s_, nb_], [0, s_]], compare_op=ALU.is_ge, fill=0.0, base=s_ - 1, channel_multiplier=-1)
                    rec.op('pool', sel1, reads=[Bconst], writes=[Bnxt])
                    rec.op('pool', sel2, reads=[Bnxt], writes=[Bnxt])
                tt('pool', Ec, Enxt, Ecur, ALU.subtract, [Bnxt, Bcur], [BE_[2]])
                tt('pool', m2d[:, j, 0, :], Ec, msu[:], ALU.mult, [BE_[2], Bconst], [Bconst])
                tt('pool', m2d[:, j, 1, :], Ec, msl[:], ALU.mult, [BE_[2], Bconst], [Bconst])
                Ecur, Bcur, Enxt, Bnxt = Enxt, Bnxt, Ecur, Bcur
            memset('pool', epsc[:, 0:1], 1e-6, [Bconst])
            memset('pool', epsc[:, 1:2], 1e-5, [Bconst])
            memset('pool', epsc[:, 2:3], 64e-5, [Bconst])
            memset('pool', epsc[:, 3:4], 1e-24, [Bconst])
            dma('pool', 'W', vec_t[:], vecs, writes=[Bvec])
            dma('pool', 'W', fing_t[:], fing, writes=[Bfing])
            ts('dve', vec_t[:, :, V_OMM:V_OMM + 26], vec_t[:, :, V_MU:V_MU + 26], -1.0, 1.0, ALU.mult, ALU.add, [Bvec], [Bvec])
            ts('dve', vec_t[:, :, V_OMKA:V_OMKA + 8], vec_t[:, :, V_KA:V_KA + 8], -1.0, 1.0, ALU.mult, ALU.add, [Bvec], [Bvec])

            def vcol(l, base, c):
                return vec_t[:, l, base + c:base + c + 1]

            checkpoint('const')
            _bw = Buf()
            Bwsc = [_bw] * L
            BwscM = [_bw] * L
            Bsm = [[] for _ in range(L)]
            Bkv = [[[], []] for _ in range(L)]

            def conv(l, pi, src2d, ncols, col0=0):
                dst = wsc[l, pi].rearrange("p (kc n) -> p kc n", kc=8)[:, :, col0:col0 + ncols]
                if pi >= PI_MKV:
                    dma('pool', 'cv', dst, src2d.rearrange("(kc p) n -> p kc n", p=128), writes=[BwscM[l]])
                else:
                    dma('pool', 'cv', dst, src2d.rearrange("(kc p) n -> p kc n", p=128), writes=[Bwsc[l]])

            for l in range(L):
                for i in range(4):
                    conv(l, PI_MKV + i, w_mkv[l][:, i * 512:(i + 1) * 512], 512)
            for l in range(L):
                conv(l, PI_LORA, w_in[l][:, 3072:3328], 256)
                for c in range(8):
                    for j in range(3):
                        conv(l, PI_RKV + c, w_in[l][:, j * 1024 + c * 128: j * 1024 + (c + 1) * 128], 128, col0=j * 128)
                for i in range(4):
                    conv(l, PI_SGU + i, w_in[l][:, 3328 + i * 512: 3328 + (i + 1) * 512], 512)
                for i in range(2):
                    conv(l, PI_Q + i, w_in[l][:, 5376 + i * 512: 5376 + (i + 1) * 512], 512)
                for i in range(6):
                    conv(l, PI_GATE + i, w_in[l][:, 6400 + i * 512: 6400 + (i + 1) * 512], 512)
                for b in range(3):
                    for hf in range(2):
                        conv(l, PI_BR + b * 2 + hf, w_br[l, b][:, hf * 512:(hf + 1) * 512], 512)
                for hf in range(2):
                    conv(l, PI_OUT + hf, w_out[l][:, hf * 512:(hf + 1) * 512], 512)
                for i in range(8):
                    conv(l, PI_UP + i, w_up[l][:, i * 512:(i + 1) * 512], 512)
                for g in range(2):
                    for q in range(4):
                        conv(l, PI_DN + g * 4 + q, w_dn[l][q * 1024:(q + 1) * 1024, g * 512:(g + 1) * 512], 512)
                dma('pool', 'cv', smsc[l][0:64, 1024:2048], w2[l], writes=[Bwsc[l]])
                dma('pool', 'cv', smsc[l][64:128, 1024:2048], a2[l], writes=[Bwsc[l]])
                dma('pool', 'cv', smsc[l][:, 2048:3072], g2[l], writes=[Bwsc[l]])
                dma('pool', 'cv', kvsc[l, 1][:, 2048:4096].rearrange("p (mb n) -> p mb n", mb=2),
                    cv[l].rearrange("(mb p) n -> p mb n", p=128), writes=[Bwsc[l]])
            checkpoint('conv')
            for l in range(L):
                wsf, Bw_ = big[:, 0:2, :], [Bbig[0], Bbig[1]]
                dma('pool', 'W', big[:, 0:2, :].rearrange("p a (g j) -> p (a g) j", j=128), w_s[l].rearrange("g i j -> i g j"), writes=Bw_)
                for g in range(8):
                    if g % 4 == 0:
                        stg, Bstg = tmpb()
                    bk, Bbk = bank()
                    tr(bk[:, 0:128], big[:, g // 4, (g % 4) * 128:(g % 4 + 1) * 128], identf[:], reads=Bw_ + [Bconst], writes=[Bbk])
                    tt('dve', stg[:, (g % 4) * 128:(g % 4 + 1) * 128], bk[:, 0:128], miu[:], ALU.mult, [Bbk, Bconst], [Bstg])
                    if g % 4 == 3:
                        Bd = Buf(); Bsm[l].append(Bd)
                        dma('pool', 'R', smsc[l][:, (g // 4) * 512:(g // 4 + 1) * 512], stg, reads=[Bstg], writes=[Bd])
            for l in range(L):
                Bk_ = [Bbig[i] for i in range(4)]
                dma('pool', 'W', big[:, 0:4, :].rearrange("p (mb a) n -> p mb (a n)", mb=2), ck[l].rearrange("(mb p) n -> p mb n", p=128), writes=Bk_)
                for j in range(8):
                    if j % 2 == 0:
                        stg, Bstg = tmpb()
                    bk, Bbk = bank()
                    for mb in range(2):
                        tr(bk[:, mb * 128:(mb + 1) * 128], big[:, mb * 2 + j // 4, (j % 4) * 128:(j % 4 + 1) * 128], identf[:],
                           reads=Bk_ + [Bconst], writes=[Bbk] if mb == 1 else None)
                    cp('act', stg[:, (j % 2) * 256:(j % 2 + 1) * 256], bk[:, 0:256], [Bbk], [Bstg])
                    if j % 2 == 1:
                        Bd = Buf(); Bkv[l][1].append(Bd)
                        dma('pool', 'R', kvsc[l, 1][:, (j - 1) * 256:(j + 1) * 256], stg, reads=[Bstg], writes=[Bd])

            checkpoint('prep')
            uses = []
            state = {'k': 0, 'loaded': 0}

            def plan_tile_layer(l, grp):
                seq = [('w', l, PI_LORA)] + [('w', l, PI_RKV + c) for c in range(8)] + [('w', l, PI_SGU + i) for i in range(4)]
                seq += [('w', l, PI_Q), ('w', l, PI_Q + 1), ('kv', l, grp)]
                for b in range(3):
                    for hf in range(2):
                        seq += [('w', l, PI_GATE + b * 2 + hf), ('w', l, PI_BR + b * 2 + hf)]
                seq += [('w', l, PI_OUT), ('w', l, PI_OUT + 1)] + [('w', l, PI_UP + i) for i in range(8)]
                seq += [('w', l, PI_DN + i) for i in range(8)]
                return seq

            for l in range(L):
                uses += [('w', l, PI_MKV + i) for i in range(4)]
            for l in range(L):
                uses += plan_tile_layer(l, 1)
            for t_ in range(NT):
                for l in range(L):
                    uses += plan_tile_layer(l, 0)
            def _issue_load(k):
                kind, l, idx = uses[k]
                slot = k % NS
                if kind == 'w':
                    ncols = 256 if idx == PI_LORA else (384 if PI_RKV <= idx < PI_RKV + 8 else 512)
                    if ncols == 512:
                        dma('sp', f'w{slot}', ws_t[slot][:], wsc[l, idx], reads=[BwscM[l] if idx >= PI_MKV else Bwsc[l]], writes=[Bws[slot]])
                    else:
                        dma('sp', f'w{slot}', ws_t[slot][:].rearrange("p (kc n) -> p kc n", kc=8)[:, :, 0:ncols],
                            wsc[l, idx].rearrange("p (kc n) -> p kc n", kc=8)[:, :, 0:ncols], reads=[Bwsc[l]], writes=[Bws[slot]])
                else:
                    dma('sp', f'w{slot}', ws_t[slot][:], kvsc[l, idx], reads=[Bwsc[l]] + Bkv[l][idx], writes=[Bws[slot]])

            def piece(kind, l, idx):
                k = state['k']
                assert uses[k] == (kind, l, idx), (uses[k], kind, l, idx)
                while state['loaded'] < min(len(uses), k + NS - 1):
                    _issue_load(state['loaded']); state['loaded'] += 1
                state['k'] += 1
                slot = k % NS
                return ws_t[slot], Bws[slot]

            def wpiece(l, idx):
                t_, B_ = piece('w', l, idx)
                return t_[:].rearrange("p (kc n) -> p kc n", kc=8), B_

            def rmsnorm_to_h(l, gbase, TT):
                bk, Bbk = bank()
                for c in range(8):
                    sq, Bsq = tmpb()
                    act(sq[:, :TT], x_t[:, c, :TT], AF.Square, [Bx[c]], [Bsq])
                    mm(bk[:, :TT], onesb[:], sq[:, :TT], c == 0, c == 7, [Bsq, Bconst], [Bbk] if c == 7 else None)
                rs, Brs = tmpf()
                rsqrt(rs[:, :TT], bk[:, :TT], [Bbk, Bconst], [Brs], scale=1.0 / D, bias=epsc[:, 0:1])
                for c in range(8):
                    gcol = vcol(l, gbase, c) if l is not None else fing_t[:, c:c + 1]
                    yield_out = h_t[:, c, :TT]
                    stt(yield_out, x_t[:, c, :TT], gcol, rs[:, :TT], ALU.mult, ALU.mult, [Bx[c], Brs, Bvec], [Bh[c]])

            def big_mm(wt, Bw, oc4, rhs_list, Brhs, TT, out_bk=None, start=True, stop=True, kc0=0):
                if out_bk is None:
                    bk, Bbk = bank()
                else:
                    bk, Bbk = out_bk
                n = len(rhs_list)
                for kc in range(n):
                    last = (kc == n - 1)
                    mm(bk[:, :TT], wt[:, kc, oc4 * 128:(oc4 + 1) * 128], rhs_list[kc], start and kc == 0, stop and last,
                       [Bw, Brhs[kc]], [Bbk] if last else None)
                return bk, Bbk

            def hrhs(TT):
                return [h_t[:, kc, :TT] for kc in range(8)]

            def rwkv_gen(l, c, TT, C):
                NC_ = TT // C
                wt, Bw = wpiece(l, PI_RKV + c)
                w2a2 = smallw[:, 1024:2048]; g2t = smallw[:, 2048:3072]
                F = lambda i: (tf_t[:, i, :], Btf[i])
                Bq = lambda i: (tb_t[:, i, :], Btb[i])
                pp = c % 2
                CB = [F(0), F(0)]
                (r_, Br), (k_, Bk), (v_, Bv) = F(1), F(2), F(3)
                (sig, Bsig), (a_, Ba), (kk, Bkk), (Lc, BL), (E, BE), (E2, BE2), (t1, Bt1) = F(4), F(5), F(6), F(7), F(8), F(9), F(10)
                (g_, Bg), (rt, Brt), (kt, Bkt), (bt, Bbt), (at, Bat) = [Bq(pp * 5 + i) for i in range(5)]
                (sq, Bsq), (pr, Bpr), (vb, Bvb) = Bq(10), Bq(11), Bq(12)
                vT, kT, bT, BvT, BkT, BbT = vTp[pp], kTp[pp], bTp[pp], BvTp[pp], BkTp[pp], BbTp[pp]
                wc_t, Bwc = wc_p[pp], Bwcp[pp]
                mixed = [(r_, Br), (k_, Bk), (v_, Bv)]
                for j in range(3):
                    col = j * 8 + c
                    bk, Bbk = big_mm(wt, Bw, j, hrhs(TT), Bh, TT)
                    cb, Bcb = CB[j % 2]
                    mx, Bmx = mixed[j]
                    cp('act', cb[:, 1:TT + 1], bk[:, :TT], [Bbk], [Bcb])
                    cp('pool', cb[:, 0:1], prev[:, l, col:col + 1], [Bprev[l]], [Bcb])
                    ts('dve', t1[:, :TT], cb[:, 0:TT], vcol(l, V_MU, col), None, ALU.mult, None, [Bcb, Bvec], [Bt1])
                    stt(mx[:, :TT], cb[:, 1:TT + 1], vcol(l, V_OMM, col), t1[:, :TT], ALU.mult, ALU.add, [Bcb, Bt1, Bvec], [Bmx])
                    cp('pool', prev[:, l, col:col + 1], cb[:, TT:TT + 1], [Bcb], [Bprev[l]])
                    yield 's'
                bk, Bbk = bank()
                mm(bk[:, :TT], w2a2[0:64, c * 128:(c + 1) * 128], lw[0:64, :TT], True, True, [Bsmall, Blw], [Bbk])
                act(sig[:, :TT], bk[:, :TT], AF.Sigmoid, [Bbk, Bvec], [Bsig], bias=vcol(l, V_W0, c))
                yield 's'
                bk, Bbk = bank()
                mm(bk[:, :TT], w2a2[64:128, c * 128:(c + 1) * 128], lw[64:128, :TT], True, True, [Bsmall, Blw], [Bbk])
                act(a_[:, :TT], bk[:, :TT], AF.Sigmoid, [Bbk, Bvec], [Ba], bias=vcol(l, V_A0, c))
                yield 's'
                bk, Bbk = bank()
                mm(bk[:, :TT], g2t[:, c * 128:(c + 1) * 128], sgl[:, :TT], True, True, [Bsmall, Bsgl], [Bbk])
                cp('act', g_[:, :TT], bk[:, :TT], [Bbk], [Bg])
                yield 's'
                ts('pool', kk[:, :TT], k_[:, :TT], vcol(l, V_KK, c), None, ALU.mult, None, [Bk, Bvec], [Bkk])
                act(sq[:, :TT], kk[:, :TT], AF.Square, [Bkk], [Bsq])
                bk, Bbk = bank()
                mm(bk[:, :TT], blkb[:], sq[:, :TT], True, True, [Bconst, Bsq], [Bbk])
                rn, Brn = E2, BE2
                rsqrt(rn[:, :TT], bk[:, :TT], [Bbk, Bconst], [Brn], bias=epsc[:, 3:4])
                yield 's'
                yield 's'
                tt('pool', kk[:, :TT], kk[:, :TT], rn[:, :TT], ALU.mult, [Bkk, Brn], [Bkk])
                ts('dve', t1[:, :TT], a_[:, :TT], vcol(l, V_KA, c), vcol(l, V_OMKA, c), ALU.mult, ALU.add, [Ba, Bvec], [Bt1])
                tt('pool', k_[:, :TT], k_[:, :TT], t1[:, :TT], ALU.mult, [Bk, Bt1], [Bk])
                yield 's'
                scan_mask = cmask[:, :TT] if C == 128 else onesf[:, :TT]
                rec.op('dve', lambda e: e.tensor_tensor_scan(out=Lc[:, :TT], data0=scan_mask, data1=sig[:, :TT],
                                                             initial=0.0, op0=ALU.mult, op1=ALU.add), reads=[Bsig, Bconst], writes=[BL])
                act(E[:, :TT], Lc[:, :TT], AF.Exp, [BL], [BE], scale=C0)
                tt('dve', rt[:, :TT], r_[:, :TT], E[:, :TT], ALU.mult, [Br, BE], [Brt])
                cp('pool', wc_t[:, 0:NC_], E[:, :TT].rearrange("p (a b) -> p a b", b=C)[:, :, C - 1], [BE], [Bwc])
                yield 's'
                act(E2[:, :TT], Lc[:, :TT], AF.Exp, [BL], [BE2], scale=-C0)
                tt('dve', kt[:, :TT], k_[:, :TT], E2[:, :TT], ALU.mult, [Bk, BE2], [Bkt])
                tt('pool', t1[:, :TT], kk[:, :TT], a_[:, :TT], ALU.mult, [Bkk, Ba], [Bt1])
                tt('dve', bt[:, :TT], t1[:, :TT], E2[:, :TT], ALU.mult, [Bt1, BE2], [Bbt])
                yield 's'
                tt('pool', Lc[:, :TT], Lc[:, :TT], sig[:, :TT], ALU.subtract, [BL, Bsig], [BL])
                act(E[:, :TT], Lc[:, :TT], AF.Exp, [BL], [BE], scale=C0)
                stt(at[:, :TT], kk[:, :TT], -1.0, E[:, :TT], ALU.mult, ALU.mult, [Bkk, BE], [Bat])
                yield 's'
                stt(pr[:, :TT], r_[:, :TT], vcol(l, V_RK, c), k_[:, :TT], ALU.mult, ALU.mult, [Br, Bk, Bvec], [Bpr])
                bk, Bbk = bank()
                mm(bk[:, :TT], blkb[:], pr[:, :TT], True, True, [Bconst, Bpr], [Bbk])
                bv, Bbv = bvt[pp], Bbvt[pp]
                tt('dve', bv[:, :TT], bk[:, :TT], v_[:, :TT], ALU.mult, [Bbk, Bv], [Bbv])
                cp('pool', vb[:, :TT], v_[:, :TT], [Bv], [Bvb])
                yield 's'
                checkpoint('pelem')
                for (src, Bsrc, dst, Bdst) in ((vb, Bvb, vT, BvT), (kt, Bkt, kT, BkT), (bt, Bbt, bT, BbT)):
                    pt, Bpt = ptbank()
                    for ci in range(NC_):
                        tr(pt[0:C, ci * 128:(ci + 1) * 128], src[:, ci * C:(ci + 1) * C], identb[:], [Bsrc, Bconst], [Bpt] if ci == NC_ - 1 else None)
                    cp('act', dst[0:C, 0:NC_, :], pt[0:C, 0:NC_ * 128].rearrange("p (a b) -> p a b", b=128), [Bpt], [Bdst])
                    yield 's'
                checkpoint('ptm')
                memset('pool', Sbd[:, c, :], 0.0, [BSbd[c]])
                for hd in range(2):
                    cp('pool', Sbd[hd * 64:(hd + 1) * 64, c, hd * 64:(hd + 1) * 64], S32[hd * 64:(hd + 1) * 64, l, c, :], [BS32[l][c]], [BSbd[c]])
                checkpoint('psbd')
                yield 'frontdone'
                hr = lambda hd: slice(hd * 64, (hd + 1) * 64)
                NL = C.bit_length() - 1
                for ci in range(NC_):
                    cs = slice(ci * C, (ci + 1) * C)
                    NQ, MT, AK, RR = NQt[ci], MTt[ci], AKt[ci], RRt[ci]
                    for hd in range(2):
                        bkX, BX = bank()
                        X3 = bkX[0:C, :].rearrange("p (a b) -> p a b", b=128)
                        mm(X3[:, 0, 0:C], bt[hr(hd), cs], at[hr(hd), cs], True, True, [Bbt, Bat])
                        mm(X3[:, 1, 0:C], kt[hr(hd), cs], at[hr(hd), cs], True, True, [Bkt, Bat])
                        mm(X3[:, 2, 0:C], bt[hr(hd), cs], rt[hr(hd), cs], True, True, [Bbt, Brt])
                        mm(X3[:, 3, 0:C], kt[hr(hd), cs], rt[hr(hd), cs], True, True, [Bkt, Brt], [BX])
                        bkY, BY = bank()
                        mm(bkY[0:C, 0:C], at[hr(hd), cs], bt[hr(hd), cs], True, True, [Bbt, Bat], [BY])
                        tt('dve', NQ[0:C, 0, hd, 0:C], X3[:, 0, 0:C], msu[0:C, 0:C], ALU.mult, [BX, Bconst], [BNQ[ci]])
                        tt('dve', AK[0:C, hd, 0:C], X3[:, 1, 0:C], msu[0:C, 0:C], ALU.mult, [BX, Bconst], [BAK[ci]])
                        tt('dve', RR[0:C, hd:4:2, 0:C], X3[:, 2:4, 0:C], miu[0:C, 0:C].unsqueeze(1).to_broadcast([C, 2, C]), ALU.mult, [BX, Bconst], [BRR[ci]])
                        tt('dve', NQ[0:C, 1, hd, 0:C], bkY[0:C, 0:C], msl[0:C, 0:C], ALU.mult, [BY, Bconst], [BNQ[ci]])
                    cp('act', MT[0:C].rearrange("p f h q -> p (f h) q")[:, :, 0:C], identb[0:C, 0:C].unsqueeze(1).to_broadcast([C, 4, C]), [Bconst], [BMT[ci]])
                    yield 's'
                checkpoint('patype')
                for j in range(NL):
                    last = j == NL - 1
                    nf = 1 if last else 2
                    xs_ = []
                    for ci in range(NC_):
                        NQ, MT = NQt[ci], MTt[ci]
                        bkX, BX = bank()
                        Xv = bkX[0:C, :].rearrange("p (f h q) -> p f h q", f=2, h=2)
                        for hd in range(2):
                            mm(Xv[:, 0, hd, 0:C], NQ[0:C, 1, hd, 0:C], MT[0:C, 0, hd, 0:C], True, True, [BNQ[ci], BMT[ci]], [BX] if hd == 1 else None)
                        xs_.append((Xv, BX))
                    ys_ = []
                    for ci in range(NC_):
                        MT, XB_ = MTt[ci], Xbt[ci]
                        Xv, BX = xs_[ci]
                        cp('act', XB_[0:C, 0, :, 0:C], Xv[:, 0, :, 0:C], [BX], [BXb[ci]])
                        bkY, BY = bank()
                        Yv = bkY[0:C, :].rearrange("p (f h q) -> p f h q", f=2, h=2)
                        for hd in range(2):
                            mm(Yv[:, 0, hd, 0:C], MT[0:C, 1, hd, 0:C], XB_[0:C, 0, hd, 0:C], True, True, [BMT[ci], BXb[ci]], [BY] if (last and hd == 1) else None)
                            if not last:
                                mm(Yv[:, 1, hd, 0:C], XB_[0:C, 0, hd, 0:C], MT[0:C, 1, hd, 0:C], True, True, [BMT[ci], BXb[ci]], [BY] if hd == 1 else None)
                        ys_.append((Yv, BY))
                    for ci in range(NC_):
                        MT = MTt[ci]
                        Yv, BY = ys_[ci]
                        cpred(MT[0:C, 0:nf, :, 0:C], m2d_u16[0:C, j, 0:nf, 0:C].unsqueeze(2).to_broadcast([C, nf, 2, C]), Yv[:, 0:nf, :, 0:C],
                              [BY, Bconst, BMT[ci]], [BMT[ci]])
                    yield 's'
                yield 'p2done'
                for ci in range(NC_):
                    par = ci % 2
                    cs = slice(ci * C, (ci + 1) * C)
                    AK, RR, MT = AKt[ci], RRt[ci], MTt[ci]
                    checkpoint('pdbl')
                    Rw, BRw = Swt[par], BSw[par]
                    for hd in range(2):
                        tt('dve', Rw[hr(hd), :], S32[hr(hd), l, c, :], Sbd[hr(hd), c, hr(hd)], ALU.subtract, [BS32[l][c], BSbd[c]], [BRw])
                    ts('dve', Rw[:, :], Rw[:, :], wc_t[:, ci:ci + 1], None, ALU.mult, None, [BRw, Bwc], [BRw])
                    bkW, BW = bank()
                    mm(bkW[0:C, 0:128], at[:, cs], Sbd[:, c, :], True, False, [Bat, BSbd[c]])
                    for hd in range(2):
                        mm(bkW[0:C, hr(hd)], AK[0:C, hd, 0:C], vT[0:C, ci, hr(hd)], False, hd == 1, [BAK[ci], BvT], [BW] if hd == 1 else None)
                    cp('act', Zb[par][0:C, :], bkW[0:C, 0:128], [BW], [BZb[par]])
                    yield 's'
                    bkU, BU = bank()
                    for hd in range(2):
                        mm(bkU[0:C, hr(hd)], MT[0:C, 0, hd, 0:C], Zb[par][0:C, hr(hd)], True, True, [BMT[ci], BZb[par]], [BU] if hd == 1 else None)
                    cp('act', Ub[par][0:C, :], bkU[0:C, 0:128], [BU], [BUb[par]])
                    yield 's'
                    bkS, BS = bank()
                    mm(bkS[:, 0:128], bT[0:C, ci, :], Ub[par][0:C, :], True, False, [BbT, BUb[par]])
                    mm(bkS[:, 0:128], kT[0:C, ci, :], vT[0:C, ci, :], False, False, [BkT, BvT])
                    mm(bkS[:, 0:128], identb[:, :], Sbd[:, c, :], False, True, [Bconst, BSbd[c]], [BS])
                    bkY, BY = bank()
                    mm(bkY[0:C, 0:128], rt[:, cs], Sbd[:, c, :], True, False, [Brt, BSbd[c]])
                    for hd in range(2):
                        mm(bkY[0:C, hr(hd)], RR[0:C, hd, 0:C], Ub[par][0:C, hr(hd)], False, False, [BRR[ci], BUb[par]])
                        mm(bkY[0:C, hr(hd)], RR[0:C, 2 + hd, 0:C], vT[0:C, ci, hr(hd)], False, hd == 1, [BRR[ci], BvT], [BY] if hd == 1 else None)
                    if ci < NC_ - 1:
                        for hd in range(2):
                            act(Sbd[hr(hd), c, hr(hd)], bkS[hr(hd), hr(hd)], AF.Identity, [BS, Bwc], [BSbd[c]], scale=wc_t[hr(hd), ci:ci + 1])
                    for hd in range(2):
                        stt(S32[hr(hd), l, c, :], bkS[hr(hd), hr(hd)], wc_t[hr(hd), ci:ci + 1], Rw[hr(hd), :], ALU.mult, ALU.add,
                            [BS, Bwc, BRw, BSbd[c]], [BS32[l][c]])
                    cp('act', yTM[0:C, ci, :], bkY[0:C, 0:128], [BY], [ByTM])
                    yield 's'
                checkpoint('pseq')
                yield 'p3done'
                NG = NC_ * 2
                y3 = yTM[0:C, 0:NC_, :].rearrange("p a (h v) -> p (a h) v", v=64)
                rec.op('dve', lambda e: e.tensor_reduce(out=gst[0:C, 0, 0:NG], in_=y3, axis=AX.X, op=ALU.add), reads=[ByTM], writes=[Bgst])
                tt('pool', ysq[0:C, 0:NC_, :], yTM[0:C, 0:NC_, :], yTM[0:C, 0:NC_, :], ALU.mult, [ByTM], [Bysq])
                s3 = ysq[0:C, 0:NC_, :].rearrange("p a (h v) -> p (a h) v", v=64)
                rec.op('dve', lambda e: e.tensor_reduce(out=gst[0:C, 1, 0:NG], in_=s3, axis=AX.X, op=ALU.add), reads=[Bysq], writes=[Bgst])
                ts('dve', gst[0:C, 2, 0:NG], gst[0:C, 0, 0:NG], 1.0 / 64, None, ALU.mult, None, [Bgst], [Bgst])
                tt('dve', gst[0:C, 3, 0:NG], gst[0:C, 2, 0:NG], gst[0:C, 2, 0:NG], ALU.mult, [Bgst], [Bgst])
                stt(gst[0:C, 3, 0:NG], gst[0:C, 1, 0:NG], 1.0 / 64, gst[0:C, 3, 0:NG], ALU.mult, ALU.subtract, [Bgst], [Bgst])
                rsqrt(gst[0:C, 3, 0:NG], gst[0:C, 3, 0:NG], [Bgst, Bconst], [Bgst], bias=epsc[0:C, 2:3])
                yield 's'
                tt('dve', ysq[0:C, 0:NC_, :].rearrange("p a (h v) -> p (a h) v", v=64), y3,
                   gst[0:C, 2, 0:NG].unsqueeze(2).to_broadcast([C, NG, 64]), ALU.subtract, [ByTM, Bgst], [Bysq])
                tt('dve', ynb[0:C, 0:NC_, :].rearrange("p a (h v) -> p (a h) v", v=64), s3,
                   gst[0:C, 3, 0:NG].unsqueeze(2).to_broadcast([C, NG, 64]), ALU.mult, [Bysq, Bgst], [Bynb])
                pt, Bpt = ptbank()
                for ci in range(NC_):
                    tr(pt[:, ci * C:(ci + 1) * C], ynb[0:C, ci, :], identb[0:C, 0:C], [Bynb, Bconst], [Bpt] if ci == NC_ - 1 else None)
                y1, By1 = y1t, By1t
                act(y1[:, :TT], pt[:, :TT], AF.Identity, [Bpt, Bvec], [By1], scale=vcol(l, V_LXG, c), bias=vcol(l, V_LXB, c))
                tt('pool', y1[:, :TT], y1[:, :TT], bv[:, :TT], ALU.add, [By1, Bbv], [By1])
                tt('dve', bfp[:, c, :TT], y1[:, :TT], g_[:, :TT], ALU.mult, [By1, Bg], [Bbfp[c]])

            def tile_layer(l, grp, TT, C, NSTEP):
                gcur[0] = 'sp'[1 - grp] if False else ('p' if grp == 0 else 's')
                NTB_ = max(1, TT // 128)
                TB = min(TT, 128)
                dma('pool', 'W', smallw[:], smsc[l], reads=[Bwsc[l]] + Bsm[l], writes=[Bsmall])
                dma('pool', 'W', bsb[:], b_s[l].unsqueeze(0).partition_broadcast(128).rearrange("p o n -> p (o n)"), writes=[Bbsb])
                rmsnorm_to_h(l, V_MIXG, TT)
                wt, Bw = wpiece(l, PI_LORA)
                for j in range(2):
                    col = 24 + j
                    bk, Bbk = big_mm(wt, Bw, j, hrhs(TT), Bh, TT)
                    cb, Bcb = tmpf()
                    cp('act', cb[:, 1:TT + 1], bk[:, :TT], [Bbk], [Bcb])
                    cp('pool', cb[:, 0:1], prev[:, l, col:col + 1], [Bprev[l]], [Bcb])
                    t1, Bt1 = tmpf()
                    ts('dve', t1[:, :TT], cb[:, 0:TT], vcol(l, V_MU, col), None, ALU.mult, None, [Bcb, Bvec], [Bt1])
                    mx, Bmx = tmpf()
                    stt(mx[:, :TT], cb[:, 1:TT + 1], vcol(l, V_OMM, col), t1[:, :TT], ALU.mult, ALU.add, [Bcb, Bt1, Bvec], [Bmx])
                    cp('pool', prev[:, l, col:col + 1], cb[:, TT:TT + 1], [Bcb], [Bprev[l]])
                    if j == 0:
                        act(lw[0:64, :TT], mx[0:64, :TT], AF.Tanh, [Bmx], [Blw])
                        cp('act', lw[64:128, :TT], mx[64:128, :TT], [Bmx], [Blw])
                    else:
                        act(sgl[:, :TT], mx[:, :TT], AF.Sigmoid, [Bmx], [Bsgl])
                checkpoint('lora')
                gens, stt_ = {}, {}
                n_started = 0
                while n_started < 8 or any(v != 'done' for v in stt_.values()):
                    if n_started < 8:
                        i_ = n_started
                        ok = (i_ == 0 or stt_[i_ - 1] != 'front') and (i_ < 2 or stt_[i_ - 2] == 'done')
                        if ok:
                            gens[i_] = rwkv_gen(l, i_, TT, C); stt_[i_] = 'front'; n_started += 1
                    in_p3 = any(v == 'chunk3' for v in stt_.values())
                    in_p12 = any(v == 'chunk12' for v in stt_.values())
                    progressed = False
                    for i_ in sorted(gens):
                        st_ = stt_[i_]
                        if st_ == 'done':
                            continue
                        if st_ == 'wait':
                            if i_ == 0 or stt_[i_ - 1] in ('final', 'done'):
                                stt_[i_] = st_ = 'chunk12'
                            else:
                                continue
                        reps = 1
                        for _r in range(reps):
                            try:
                                tok = next(gens[i_])
                            except StopIteration:
                                stt_[i_] = 'done'; progressed = True
                                break
                            progressed = True
                            if tok == 'frontdone':
                                stt_[i_] = 'wait'; break
                            elif tok == 'p2done':
                                stt_[i_] = 'chunk3'; break
                            elif tok == 'p3done':
                                stt_[i_] = 'final'; break
                    assert progressed or n_started < 8
                checkpoint('rwkv')
                for i in range(4):
                    wt, Bw = wpiece(l, PI_SGU + i)
                    for o4 in range(4):
                        oc = i * 4 + o4
                        bk, Bbk = big_mm(wt, Bw, o4, hrhs(TT), Bh, TT)
                        if oc < 8:
                            act(ub[:, oc, :TT], bk[:, :TT], AF.Gelu_apprx_tanh, [Bbk], [Bub[oc]])
                        else:
                            act(big[:, oc - 8, :TT], bk[:, :TT], AF.Gelu_apprx_tanh, [Bbk], [Bbig[oc - 8]])
                bk1, Bb1 = bank(); bk2, Bb2 = bank()
                for c in range(8):
                    vb, Bvb = tmpb(); vs, Bvs = tmpb()
                    cp('pool', vb[:, :TT], big[:, c, :TT], [Bbig[c]], [Bvb])
                    act(vs[:, :TT], big[:, c, :TT], AF.Square, [Bbig[c]], [Bvs])
                    mm(bk1[:, :TT], onesb[:], vb[:, :TT], c == 0, c == 7, [Bvb, Bconst], [Bb1])
                    mm(bk2[:, :TT], onesb[:], vs[:, :TT], c == 0, c == 7, [Bvs, Bconst], [Bb2])
                mean, Bmean = tmpf(); rstd, Brstd = tmpf()
                act(mean[:, :TT], bk1[:, :TT], AF.Copy, [Bb1], [Bmean], scale=1.0 / D)
                tt('pool', rstd[:, :TT], mean[:, :TT], mean[:, :TT], ALU.mult, [Bmean], [Brstd])
                stt(rstd[:, :TT], bk2[:, :TT], 1.0 / D, rstd[:, :TT], ALU.mult, ALU.subtract, [Bb2, Brstd], [Brstd])
                rsqrt(rstd[:, :TT], rstd[:, :TT], [Brstd, Bconst], [Brstd], bias=epsc[:, 1:2])
                vnb = []
                for c in range(8):
                    vc = big[:, c, :TT]
                    tt('dve', vc, vc, mean[:, :TT], ALU.subtract, [Bbig[c], Bmean], [Bbig[c]])
                    tt('pool', vc, vc, rstd[:, :TT], ALU.mult, [Bbig[c], Brstd], [Bbig[c]])
                    ts('dve', vc, vc, vcol(l, V_SLG, c), vcol(l, V_SLB, c), ALU.mult, ALU.add, [Bbig[c], Bvec], [Bbig[c]])
                    cp('pool', bfp[:, 16 + c, :TT], vc, [Bbig[c]], [Bbfp[16 + c]])
                    vnb.append((bfp[:, 16 + c, :], Bbfp[16 + c]))
                if grp == 1:
                    for half in range(2):
                        bk, Bbk = bank()
                        for j in range(4):
                            tr(bk[0:TT, j * 128:(j + 1) * 128], big[:, half * 4 + j, :TT], identf[:], [Bbig[half * 4 + j], Bconst], [Bbk] if j == 3 else None)
                        stg, Bstg = tmpf()
                        cp('act', stg[0:TT, 0:512], bk[0:TT, :], [Bbk], [Bstg])
                        dma('pool', 'R', osgu[l][:, half * 512:(half + 1) * 512], stg[0:TT, 0:512], reads=[Bstg])
                vtm = []
                for tb_ in range(NTB_):
                    halves = []
                    for half in range(2):
                        pt, Bpt = ptbank()
                        for j in range(4):
                            cc = half * 4 + j
                            tr(pt[0:TB, j * 128:(j + 1) * 128], vnb[cc][0][:, tb_ * 128: tb_ * 128 + TB], identb[:], [vnb[cc][1], Bconst], [Bpt] if j == 3 else None)
                        vt_, Bvt_ = bfp[:, 24 + tb_ * 2 + half, :], Bbfp[24 + tb_ * 2 + half]
                        cp('act', vt_[0:TB, :], pt[0:TB, :], [Bpt], [Bvt_])
                        halves.append((vt_, Bvt_))
                    vtm.append(halves)
                wsT = smallw[:, 0:1024]
                for c in range(8):
                    bk, Bbk = bank()
                    for tb_ in range(NTB_):
                        vt_, Bvt_ = vtm[tb_][c // 4]
                        mm(bk[:, tb_ * 128: tb_ * 128 + TB], vt_[0:TB, (c % 4) * 128:(c % 4 + 1) * 128], wsT[0:TB, c * 128: c * 128 + TB], True, True,
                           [Bvt_, Bsmall], [Bbk] if tb_ == NTB_ - 1 else None)
                    sv, Bsv = tmpf()
                    if NTB_ > 1:
                        tt('dve', sv[:, :TT].rearrange("p (a b) -> p a b", b=128), bk[:, :TT].rearrange("p (a b) -> p a b", b=128),
                           bsb[:, c * 128:(c + 1) * 128].unsqueeze(1).to_broadcast([128, NTB_, 128]), ALU.add, [Bbk, Bbsb], [Bsv])
                    else:
                        tt('dve', sv[:, :TT], bk[:, :TT], bsb[:, c * 128: c * 128 + TT], ALU.add, [Bbk, Bbsb], [Bsv])
                    tt('pool', bfp[:, 8 + c, :TT], sv[:, :TT], ub[:, c, :TT], ALU.mult, [Bsv, Bub[c]], [Bbfp[8 + c]])
                checkpoint('sgu')
                for i in range(2):
                    wt, Bw = wpiece(l, PI_Q + i)
                    for o4 in range(4):
                        oc = i * 4 + o4
                        bk, Bbk = big_mm(wt, Bw, o4, hrhs(TT), Bh, TT)
                        cp('act', bfp[:, 24 + oc, :TT], bk[:, :TT], [Bbk], [Bbfp[24 + oc]])
                kvt, Bkvp = piece('kv', l, grp)
                KT = kvt[:, 0:2048].rearrange("p (j m) -> p j m", j=8)
                VV = kvt[:, 2048:4096].rearrange("p (mb n) -> p mb n", mb=2)
                for hh in range(4):
                    pTs = []
                    for mc in range(2):
                        bk, Bbk = bank()
                        for dc in range(2):
                            mm(bk[:, :TT], KT[:, 2 * hh + dc, mc * 128:(mc + 1) * 128], bfp[:, 24 + 2 * hh + dc, :TT], dc == 0, dc == 1,
                               [Bkvp, Bbfp[24 + 2 * hh + dc]], [Bbk] if dc == 1 else None)
                        p_, Bp_ = tmpb()
                        act(p_[:, :TT], bk[:, :TT], AF.Exp, [Bbk], [Bp_], scale=1.0 / 16)
                        pTs.append((p_, Bp_))
                    bk, Bbk = bank()
                    for mc in range(2):
                        mm(bk[:, :TT], onesb[:], pTs[mc][0][:, :TT], mc == 0, mc == 1, [pTs[mc][1], Bconst], [Bbk] if mc == 1 else None)
                    rd, Brd = tmpf()
                    rsqrt(rd[:, :TT], bk[:, :TT], [Bbk], [Brd], power=-1.0)
                    for dvc in range(2):
                        bk, Bbk = bank()
                        for mc in range(2):
                            mm(bk[:, :TT], VV[:, mc, hh * 256 + dvc * 128: hh * 256 + (dvc + 1) * 128], pTs[mc][0][:, :TT], mc == 0, mc == 1,
                               [Bkvp, pTs[mc][1]], [Bbk] if mc == 1 else None)
                        tt('dve', bfp[:, 16 + 2 * hh + dvc, :TT], bk[:, :TT], rd[:, :TT], ALU.mult, [Bbk, Brd], [Bbfp[16 + 2 * hh + dvc]])
                if DBG and grp == 0 and l == 0:
                    dma('pool', 'R', dbg_y, bfp[:, 0:24, :], reads=[Bbfp[i] for i in range(24)])
                checkpoint('attn')
                for b in range(3):
                    for hf in range(2):
                        wg, Bwg = wpiece(l, PI_GATE + b * 2 + hf)
                        wb_, Bwb = wpiece(l, PI_BR + b * 2 + hf)
                        for o4 in range(4):
                            oc = hf * 4 + o4
                            bk, Bbk = big_mm(wg, Bwg, o4, hrhs(TT), Bh, TT)
                            gt, Bgt = tmpb()
                            act(gt[:, :TT], bk[:, :TT], AF.Sigmoid, [Bbk], [Bgt])
                            rh = [bfp[:, b * 8 + kc, :TT] for kc in range(8)]
                            bk, Bbk = big_mm(wb_, Bwb, o4, rh, [Bbfp[b * 8 + kc] for kc in range(8)], TT)
                            if b == 0:
                                tt('dve', big[:, oc, :TT], bk[:, :TT], gt[:, :TT], ALU.mult, [Bbk, Bgt], [Bbig[oc]])
                            else:
                                tm_, Btm = tmpf()
                                tt('dve', tm_[:, :TT], bk[:, :TT], gt[:, :TT], ALU.mult, [Bbk, Bgt], [Btm])
                                if b == 1:
                                    tt('pool', big[:, oc, :TT], big[:, oc, :TT], tm_[:, :TT], ALU.add, [Bbig[oc], Btm], [Bbig[oc]])
                                else:
                                    tt('pool', bfp[:, 24 + oc, :TT], big[:, oc, :TT], tm_[:, :TT], ALU.add, [Bbig[oc], Btm], [Bbfp[24 + oc]])
                if DBG and grp == 0 and l == 0:
                    dma('pool', 'R', dbg_m, bfp[:, 24:32, :], reads=[Bbfp[24 + i] for i in range(8)])
                for hf in range(2):
                    wt, Bw = wpiece(l, PI_OUT + hf)
                    for o4 in range(4):
                        oc = hf * 4 + o4
                        rh = [bfp[:, 24 + kc, :TT] for kc in range(8)]
                        bk, Bbk = big_mm(wt, Bw, o4, rh, [Bbfp[24 + kc] for kc in range(8)], TT)
                        tt('dve', x_t[:, oc, :TT], x_t[:, oc, :TT], bk[:, :TT], ALU.add, [Bx[oc], Bbk], [Bx[oc]])
                if DBG and grp == 0 and l == 0:
                    dma('pool', 'R', dbg_x, x_t[:], reads=Bx)
                checkpoint('merge')
                rmsnorm_to_h(l, V_FFNG, TT)
                for i in range(8):
                    wt, Bw = wpiece(l, PI_UP + i)
                    for o4 in range(4):
                        oc = i * 4 + o4
                        bk, Bbk = big_mm(wt, Bw, o4, hrhs(TT), Bh, TT)
                        rl, Brl = tmpf()
                        act(rl[:, :TT], bk[:, :TT], AF.Relu, [Bbk], [Brl])
                        tt('pool', bfp[:, oc, :TT], rl[:, :TT], rl[:, :TT], ALU.mult, [Brl], [Bbfp[oc]])
                for g in range(2):
                    banks = [bank() for _ in range(4)]
                    for q in range(4):
                        wt, Bw = wpiece(l, PI_DN + g * 4 + q)
                        for o4 in range(4):
                            rh = [bfp[:, q * 8 + kc, :TT] for kc in range(8)]
                            big_mm(wt, Bw, o4, rh, [Bbfp[q * 8 + kc] for kc in range(8)], TT, out_bk=banks[o4], start=(q == 0), stop=(q == 3))
                    for o4 in range(4):
                        oc = g * 4 + o4
                        tt('dve', x_t[:, oc, :TT], x_t[:, oc, :TT], banks[o4][0][:, :TT], ALU.add, [Bx[oc], banks[o4][1]], [Bx[oc]])

            def dbg_x2_dump():
                if DBG:
                    dma('pool', 'R', dbg_x2, x_t[:], reads=Bx)

            def load_x(src, TT):
                NTB_ = max(1, TT // 128); TB = min(TT, 128)
                for tb_ in range(NTB_):
                    dma('pool', 'W', big[0:TB, 2 * tb_:2 * tb_ + 2, :].rearrange("p a n -> p (a n)"), src[tb_ * 128: tb_ * 128 + TB, :],
                        writes=[Bbig[2 * tb_], Bbig[2 * tb_ + 1]])
                for c in range(8):
                    bk, Bbk = bank()
                    for tb_ in range(NTB_):
                        tr(bk[:, tb_ * 128: tb_ * 128 + TB], big[0:TB, 2 * tb_ + c // 4, (c % 4) * 128:(c % 4 + 1) * 128], identf[0:TB, 0:TB],
                           [Bbig[2 * tb_ + c // 4], Bconst], [Bbk] if tb_ == NTB_ - 1 else None)
                    cp('act', x_t[:, c, :TT], bk[:, :TT], [Bbk], [Bx[c]])

            def store_y(dst, TT):
                NTB_ = max(1, TT // 128); TB = min(TT, 128)
                bk, Bbk = bank()
                for c in range(8):
                    sq, Bsq = tmpb()
                    act(sq[:, :TT], x_t[:, c, :TT], AF.Square, [Bx[c]], [Bsq])
                    mm(bk[:, :TT], onesb[:], sq[:, :TT], c == 0, c == 7, [Bsq, Bconst], [Bbk] if c == 7 else None)
                rs, Brs = tmpf()
                rsqrt(rs[:, :TT], bk[:, :TT], [Bbk, Bconst], [Brs], scale=1.0 / D, bias=epsc[:, 0:1])
                for c in range(8):
                    stt(big[:, c, :TT], x_t[:, c, :TT], fing_t[:, c:c + 1], rs[:, :TT], ALU.mult, ALU.mult, [Bx[c], Brs, Bfing], [Bbig[c]])
                for tb_ in range(NTB_):
                    for half in range(2):
                        bk, Bbk = bank()
                        for j in range(4):
                            cc = half * 4 + j
                            tr(bk[0:TB, j * 128:(j + 1) * 128], big[:, cc, tb_ * 128: tb_ * 128 + TB], identf[:], [Bbig[cc], Bconst], [Bbk] if j == 3 else None)
                        stg, Bstg = tmpf()
                        cp('act', stg[0:TB, 0:512], bk[0:TB, :], [Bbk], [Bstg])
                        dma('pool', 'R', dst[tb_ * 128: tb_ * 128 + TB, half * 512:(half + 1) * 512], stg[0:TB, 0:512], reads=[Bstg])

            def store_state(l, owkv, oshift):
                for c in range(8):
                    bk, Bbk = bank()
                    tr(bk[0:64, 0:128], S32[:, l, c, :], identf[:], [BS32[l][c], Bconst], [Bbk])
                    stg, Bstg = tmpf()
                    cp('act', stg[0:64, 0:128], bk[0:64, 0:128], [Bbk], [Bstg])
                    dma('pool', 'R', owkv[l, 2 * c:2 * c + 2].rearrange("h v k -> v h k"), stg[0:64, 0:128].rearrange("p (h k) -> p h k", h=2), reads=[Bstg])
                bk, Bbk = bank()
                tr(bk[0:26, 0:128], prev[:, l, :], identf[:], [Bprev[l], Bconst], [Bbk])
                stg, Bstg = tmpf()
                cp('act', stg[0:26, 0:128], bk[0:26, 0:128], [Bbk], [Bstg])
                dma('pool', 'R', oshift[l].rearrange("(j p) -> j p", p=128), stg[0:26, 0:128], reads=[Bstg])

            memset('pool', cmask[:], 1.0, [Bconst])
            memset('pool', cmask[:].rearrange("p (a b) -> p a b", b=128)[:, :, 0:1], 0.0, [Bconst])

            Bm_ = [Bbig[i] for i in range(4)]
            dma('pool', 'W', big[:, 0:4, :].rearrange("p (mb a) n -> p mb (a n)", mb=2), memp.rearrange("(mb p) n -> p mb n", p=128), writes=Bm_)
            memf = big[:, 4:8, :].rearrange("p a (b n) -> p (a b) n", b=2)
            Bmemf = [Bbig[i] for i in range(4, 8)]
            for mb in range(2):
                ssq, Bssq = tmpf()
                for a in range(2):
                    jk, Bjk = tmpf()
                    rec.op('act', lambda e, a=a, mb=mb, jk=jk, ssq=ssq: e.activation(out=jk[:, 0:512], in_=big[:, mb * 2 + a, :], func=AF.Square, accum_out=ssq[:, a:a + 1]),
                           reads=[Bbig[mb * 2 + a]], writes=[Bjk, Bssq])
                tt('dve', ssq[:, 2:3], ssq[:, 0:1], ssq[:, 1:2], ALU.add, [Bssq], [Bssq])
                act(ssq[:, 3:4], ssq[:, 2:3], AF.Sqrt, [Bssq, Bconst], [Bssq], scale=1.0 / D, bias=epsc[:, 0:1])
                recip(ssq[:, 4:5], ssq[:, 3:4], [Bssq], [Bssq])
                for a in range(2):
                    ts('dve', big[:, mb * 2 + a, :], big[:, mb * 2 + a, :], ssq[:, 4:5], None, ALU.mult, None, [Bbig[mb * 2 + a], Bssq], [Bbig[mb * 2 + a]])
            for c in range(8):
                bk, Bbk = bank()
                for mb in range(2):
                    tr(bk[:, mb * 128:(mb + 1) * 128], big[:, mb * 2 + c // 4, (c % 4) * 128:(c % 4 + 1) * 128], identf[:], Bm_ + [Bconst], [Bbk] if mb == 1 else None)
                cp('act', memf[:, c, :], bk[:, 0:256], [Bbk], Bmemf)
            for l in range(L):
                mnt = [(bfp[:, 8 + i, :], Bbfp[8 + i]) for i in range(4)]
                for c in range(8):
                    t_, B_ = mnt[c // 2]
                    ts('dve', t_[:, (c % 2) * 256:(c % 2 + 1) * 256], memf[:, c, :], vcol(l, V_MEMG, c), None, ALU.mult, None, Bmemf + [Bvec], [B_])
                mrhs = lambda c: mnt[c // 2][0][:, (c % 2) * 256:(c % 2 + 1) * 256]
                kvst = [(bfp[:, i, :], Bbfp[i]) for i in range(8)]
                for i in range(4):
                    wt, Bw = wpiece(l, PI_MKV + i)
                    for mb in range(2):
                        bk, Bbk = bank()
                        for kc in range(8):
                            mm(bk[:, :], mrhs(kc)[:, mb * 128:(mb + 1) * 128], wt[:, kc, :], kc == 0, kc == 7, [mnt[kc // 2][1], Bw], [Bbk] if kc == 7 else None)
                        stg, Bstg = tmpf()
                        cp('act', stg[:, 0:512], bk[:, :], [Bbk], [Bstg])
                        dst = (omk if i < 2 else omv)[l][mb * 128:(mb + 1) * 128, (i % 2) * 512:(i % 2 + 1) * 512]
                        dma('pool', 'R', dst, stg[:, 0:512], reads=[Bstg])
                        if i >= 2:
                            s_, Bs_ = kvst[4 + mb * 2 + (i - 2)]
                            cp('pool', s_, stg[:, 0:512], [Bstg], [Bs_])
                    if i < 2:
                        for o4 in range(4):
                            j = i * 4 + o4
                            bk, Bbk = bank()
                            for kc in range(8):
                                mm(bk[:, 0:256], wt[:, kc, o4 * 128:(o4 + 1) * 128], mrhs(kc), kc == 0, kc == 7, [mnt[kc // 2][1], Bw], [Bbk] if kc == 7 else None)
                            s_, Bs_ = kvst[j // 2]
                            cp('act', s_[:, (j % 2) * 256:(j % 2 + 1) * 256], bk[:, 0:256], [Bbk], [Bs_])
                for i in range(8):
                    Bd = Buf(); Bkv[l][0].append(Bd)
                    dma('pool', 'R', kvsc[l, 0][:, i * 512:(i + 1) * 512], kvst[i][0], reads=[kvst[i][1]], writes=[Bd])

            checkpoint('memkv')
            for l in range(L):
                for c in range(8):
                    stg, Bstg = tmpf()
                    dma('pool', 'W', stg[0:64, 0:128].rearrange("p (h k) -> p h k", h=2), swkv[l, 2 * c:2 * c + 2].rearrange("h v k -> v h k"), writes=[Bstg])
                    bk, Bbk = bank()
                    tr(bk[:, 0:64], stg[0:64, 0:128], identf[0:64, 0:64], [Bstg, Bconst], [Bbk])
                    cp('act', S32[:, l, c, :], bk[:, 0:64], [Bbk], [BS32[l][c]])
                stg, Bstg = tmpf()
                dma('pool', 'W', stg[0:26, 0:128], sshift[l].rearrange("(j p) -> j p", p=128), writes=[Bstg])
                bk, Bbk = bank()
                tr(bk[:, 0:26], stg[0:26, 0:128], identf[0:26, 0:26], [Bstg, Bconst], [Bbk])
                cp('act', prev[:, l, :], bk[:, 0:26], [Bbk], [Bprev[l]])
            load_x(xs, TS)
            for l in range(L):
                tile_layer(l, 1, TS, TS, 5)
            store_y(ys, TS)
            for l in range(L):
                store_state(l, owkv_s, oshift_s)

            checkpoint('sample')
            for l in range(L):
                for c in range(8):
                    memset('pool', S32[:, l, c, :], 0.0, [BS32[l][c]])
                memset('pool', prev[:, l, :], 0.0, [Bprev[l]])
            for t_ in range(NT):
                load_x(xp[t_ * T:(t_ + 1) * T], T)
                for l in range(L):
                    tile_layer(l, 0, T, 128, 7)
                    if l == 0 and t_ == 0:
                        dbg_x2_dump()
                store_y(yp[t_ * T:(t_ + 1) * T], T)
            for l in range(L):
                store_state(l, owkv_p, oshift_p)

        except _Stop:
            pass
        finish()
    return nc


def _vec_table(inp, L):
    def fm(a, n):
        return np.ascontiguousarray(a.reshape(L, n, 128).transpose(2, 0, 1))
    mu = inp['rwkv_mu'][:L]
    ka = inp['rwkv_k_a'][:L]
    parts = [fm(inp['norm_mix_g'][:L], 8), fm(inp['norm_ffn_g'][:L], 8), fm(inp['norm_mem_g'][:L], 8),
             fm(mu, 26), np.zeros((128, L, 26), np.float32), fm(inp['rwkv_w0'][:L], 8), fm(inp['rwkv_a0'][:L], 8),
             fm(inp['rwkv_k_k'][:L], 8), fm(ka, 8), np.zeros((128, L, 8), np.float32), fm(inp['rwkv_r_k'][:L].reshape(L, 1024), 8),
             fm(inp['sgu_ln_g'][:L], 8), fm(inp['sgu_ln_b'][:L], 8), fm(inp['rwkv_lnx_g'][:L], 8), fm(inp['rwkv_lnx_b'][:L], 8)]
    return np.ascontiguousarray(np.concatenate(parts, axis=2).astype(np.float32))


def run(inp, SEQ, L, n_cores=8, trace=False, DBG=False, STOP=None):
    nc = build(SEQ, L, DBG=DBG, STOP=STOP)
    f = lambda a: np.ascontiguousarray(a, dtype=np.float32)
    vecs = _vec_table(inp, L)
    fing = np.ascontiguousarray(inp['norm_final_g'].reshape(8, 128).T.astype(np.float32))
    shared = {
        'vecs': vecs, 'fing': fing, 'w_in': f(inp['w_in'][:L]), 'w_mkv': f(inp['w_mem_kv'][:L]),
        'w2': f(inp['rwkv_w2'][:L]), 'a2': f(inp['rwkv_a2'][:L]), 'g2': f(inp['rwkv_g2'][:L]),
        'w_s': f(inp['sgu_w_s'][:L]), 'b_s': f(inp['sgu_b_s'][:L].reshape(L, 1024)),
        'w_br': f(inp['w_branch'][:L]), 'w_out': f(inp['w_out'][:L]), 'w_up': f(inp['w_ffn_up'][:L]), 'w_dn': f(inp['w_ffn_down'][:L]),
    }
    in_maps = []
    for b in range(n_cores):
        m = dict(shared)
        m['xp'] = f(inp['x_prompt'][b, :SEQ]); m['xs'] = f(inp['x_sample'][b])
        m['ck'] = f(inp['cache_mem_k'][:L, b].reshape(L, 256, 1024)); m['cv'] = f(inp['cache_mem_v'][:L, b].reshape(L, 256, 1024))
        m['swkv'] = f(inp['state_wkv'][:L, b]); m['sshift'] = f(inp['state_shift'][:L, b, 0])
        m['memp'] = f(inp['mem_prompt'][b])
        in_maps.append(m)
    res = run_bass_kernel_spmd(nc, in_maps, core_ids=list(range(n_cores)), trace=trace)
    R = res.results
    st = lambda k: np.stack([np.asarray(r[k]) for r in R])
    y_p = st('yp'); y_s = st('ys')
    mk = st('omk').transpose(1, 0, 2, 3).reshape(L, n_cores, 256, 4, 256)
    mv = st('omv').transpose(1, 0, 2, 3).reshape(L, n_cores, 256, 4, 256)
    wkv_p = st('owkv_p').transpose(1, 0, 2, 3, 4)
    sh_p = st('oshift_p').transpose(1, 0, 2)[:, :, None, :]
    wkv_s = st('owkv_s').transpose(1, 0, 2, 3, 4)
    sh_s = st('oshift_s').transpose(1, 0, 2)[:, :, None, :]
    sgu = st('osgu').transpose(1, 0, 2, 3)
    outs = (y_p, y_s, mk, mv, wkv_p, sh_p, wkv_s, sh_s, sgu)
    return tuple(np.ascontiguousarray(o.astype(np.float32)) for o in outs), res


def kernel(**inputs):
    inp = {k: np.asarray(v) for k, v in inputs.items()}
    outs, _ = run(inp, 8192, 4, 8)
    return outs
```

```python
import contextlib
import numpy as np
import concourse.bass as bass
import concourse.mybir as mybir
from concourse.bass_utils import run_bass_kernel_spmd

F32 = mybir.dt.float32
BF16 = mybir.dt.bfloat16
AF = mybir.ActivationFunctionType
ALU = mybir.AluOpType
AX = mybir.AxisListType

ENGS = ('pe', 'act', 'dve', 'pool', 'sp')


class Buf:
    __slots__ = ('w', 'r', 'id')
    _n = [0]

    def __init__(self):
        self.w = None
        self.r = {}
        Buf._n[0] += 1
        self.id = Buf._n[0]


class Rec:
    def __init__(self, nc):
        self.nc = nc
        self.streams = {e: [] for e in ENGS}
        self.cnt = {e: 0 for e in ENGS}
        self.dcnt = {}
        self.waited = {e: {} for e in ENGS}
        self.pending = {e: [] for e in ENGS}

    def _wait(self, eng, tok):
        if tok is None:
            return
        key, val = tok
        if self.waited[eng].get(key, 0) >= val:
            return
        self.waited[eng][key] = val
        self.streams[eng].append(('w', key, val))

    def _deps(self, eng, reads, writes, extra):
        for t in extra:
            self._wait(eng, t)
        for b in reads:
            self._wait(eng, b.w)
        for b in writes:
            self._wait(eng, b.w)
            for t in b.r.values():
                self._wait(eng, t)
            for e2 in ENGS:
                if e2 != eng and any(pb_ is b for pb_ in self.pending[e2]):
                    raise RuntimeError("write to buffer %d with pending unmarked reads on %s" % (b.id, e2))

    def op(self, eng, fn, reads=(), writes=(), mark=True, extra=(), wdeps=()):
        self._deps(eng, reads, tuple(writes) + tuple(wdeps), extra)
        if mark:
            self.cnt[eng] += 1
            tok = (eng, self.cnt[eng])
            self.streams[eng].append(('i', fn, True))
            for b in self.pending[eng]:
                b.r[eng] = tok
            self.pending[eng] = []
            for b in reads:
                b.r[eng] = tok
            for b in writes:
                b.w = tok
                b.r = {}
            return tok
        assert not writes
        self.streams[eng].append(('i', fn, False))
        self.pending[eng].extend(reads)
        return None

    def dma(self, q, sem, fn, reads=(), writes=(), extra=()):
        self._deps(q, reads, writes, extra)
        self.dcnt[sem] = self.dcnt.get(sem, 0) + 16
        tok = (sem, self.dcnt[sem])
        self.streams[q].append(('d', fn, sem))
        for b in reads:
            b.r[sem] = tok
        for b in writes:
            b.w = tok
            b.r = {}
        return tok

    def emit(self, stack):
        nc = self.nc
        sems = {}
        for e in ENGS:
            sems[e] = stack.enter_context(nc.semaphore('s_' + e))
        for d in self.dcnt:
            sems[d] = stack.enter_context(nc.semaphore('d_' + d))
        block = stack.enter_context(nc.Block())
        dec = {'pe': block.tensor, 'act': block.scalar, 'dve': block.vector, 'pool': block.gpsimd, 'sp': block.sync}
        for e in ENGS:
            stream = self.streams[e]
            own = sems[e]

            def body(engine, stream=stream, own=own):
                for item in stream:
                    if item[0] == 'w':
                        engine.wait_ge(sems[item[1]], item[2])
                    elif item[0] == 'i':
                        ins = item[1](engine)
                        if item[2]:
                            ins.then_inc(own, 1)
                    else:
                        item[1](engine).then_inc(sems[item[2]], 16)
            dec[e](body)


D = 1024
NCH = 8
RW = 3328
IN_COLS = 9472
NP_ = 49
PI_LORA, PI_RKV, PI_SGU, PI_Q, PI_GATE, PI_BR, PI_OUT, PI_UP, PI_DN, PI_MKV = 0, 1, 9, 13, 15, 21, 27, 29, 37, 45
C0 = -0.6065306597126334
V_MIXG, V_FFNG, V_MEMG, V_MU, V_OMM, V_W0, V_A0, V_KK, V_KA, V_OMKA, V_RK, V_SLG, V_SLB, V_LXG, V_LXB = (
    0, 8, 16, 24, 50, 76, 84, 92, 100, 108, 116, 124, 132, 140, 148)
VW = 156


class _Stop(Exception):
    pass


def build(SEQ, DEPTH, MEMN=256, TS=32, DBG=False, STOP=None):
    nc = bass.Bass("TRN2", target_bir_lowering=False)
    L = DEPTH
    T = 512
    NT = SEQ // T
    din = lambda name, shape, dt=F32: nc.dram_tensor(name, shape, dt, kind="ExternalInput").ap()
    dout = lambda name, shape, dt=F32: nc.dram_tensor(name, shape, dt, kind="ExternalOutput").ap()
    dscr = lambda name, shape, dt: nc.dram_tensor(name, shape, dt, kind="Internal").ap()
    xp = din("xp", [SEQ, D]); xs = din("xs", [TS, D])
    ck = din("ck", [L, MEMN, D]); cv = din("cv", [L, MEMN, D])
    swkv = din("swkv", [L, 16, 64, 64]); sshift = din("sshift", [L, RW]); memp = din("memp", [MEMN, D])
    vecs = din("vecs", [128, L, VW]); fing = din("fing", [128, 8])
    w_in = din("w_in", [L, D, IN_COLS]); w_mkv = din("w_mkv", [L, D, 2 * D])
    w2 = din("w2", [L, 64, D]); a2 = din("a2", [L, 64, D]); g2 = din("g2", [L, 128, D])
    w_s = din("w_s", [L, 8, 128, 128]); b_s = din("b_s", [L, 8 * 128])
    w_br = din("w_br", [L, 3, D, D]); w_out = din("w_out", [L, D, D])
    w_up = din("w_up", [L, D, 4 * D]); w_dn = din("w_dn", [L, 4 * D, D])
    yp = dout("yp", [SEQ, D]); ys = dout("ys", [TS, D])
    omk = dout("omk", [L, MEMN, D]); omv = dout("omv", [L, MEMN, D])
    owkv_p = dout("owkv_p", [L, 16, 64, 64]); oshift_p = dout("oshift_p", [L, RW])
    owkv_s = dout("owkv_s", [L, 16, 64, 64]); oshift_s = dout("oshift_s", [L, RW])
    osgu = dout("osgu", [L, TS, D])
    if DBG:
        dbg_y = dout("dbg_y", [128, 24, 512], BF16); dbg_x = dout("dbg_x", [128, 8, 512]); dbg_x2 = dout("dbg_x2", [128, 8, 512]); dbg_m = dout("dbg_m", [128, 8, 512], BF16)
    wsc = dscr("wsc", [L, NP_, 128, 4096], BF16)
    kvsc = dscr("kvsc", [L, 2, 128, 4096], BF16)
    smsc = dscr("smsc", [L, 128, 3072], BF16)

    rec = Rec(nc)
    st = contextlib.ExitStack()
    with st:
        def sb(name, shape, dt):
            return st.enter_context(nc.sbuf_tensor(name, shape, dt))

        def ps(name, shape, dt):
            return st.enter_context(nc.psum_tensor(name, shape, dt))

        def mm(out, lhsT, rhs, start, stop, reads, writes=None):
            rec.op('pe', lambda e: e.matmul(out, lhsT=lhsT, rhs=rhs, start=start, stop=stop),
                   reads=reads, writes=writes or (), mark=writes is not None, wdeps=() if writes is not None else psum_bufs[out.name])

        def tr(out, in_, ident, reads, writes=None):
            rec.op('pe', lambda e: e.transpose(out=out, in_=in_, identity=ident),
                   reads=reads, writes=writes or (), mark=writes is not None, wdeps=() if writes is not None else psum_bufs[out.name])

        def act(out, in_, func, reads, writes, scale=1.0, bias=None):
            if bias is None:
                rec.op('act', lambda e: e.activation(out=out, in_=in_, func=func, scale=scale), reads=reads, writes=writes)
            else:
                rec.op('act', lambda e: e.activation(out=out, in_=in_, func=func, scale=scale, bias=bias), reads=reads, writes=writes)

        def tt(eng, out, in0, in1, op, reads, writes):
            rec.op(eng, lambda e: e.tensor_tensor(out=out, in0=in0, in1=in1, op=op), reads=reads, writes=writes)

        def ts(eng, out, in0, s1, s2, op0, op1, reads, writes):
            if op1 is None and eng == 'pool' and op0 == ALU.mult:
                s2, op1 = 0.0, ALU.add
            if op1 is None:
                rec.op(eng, lambda e: e.tensor_scalar(out=out, in0=in0, scalar1=s1, scalar2=None, op0=op0), reads=reads, writes=writes)
            else:
                rec.op(eng, lambda e: e.tensor_scalar(out=out, in0=in0, scalar1=s1, scalar2=s2, op0=op0, op1=op1), reads=reads, writes=writes)

        def stt(out, in0, scalar, in1, op0, op1, reads, writes):
            rec.op('dve', lambda e: e.scalar_tensor_tensor(out=out, in0=in0, scalar=scalar, in1=in1, op0=op0, op1=op1), reads=reads, writes=writes)

        def cp(eng, out, in_, reads, writes):
            if eng == 'act':
                act(out, in_, AF.Copy, reads, writes)
            else:
                rec.op(eng, lambda e: e.tensor_copy(out=out, in_=in_), reads=reads, writes=writes)

        def rsqrt(out, in_, reads, writes, scale=1.0, bias=None, power=-0.5):
            act(out, in_, AF.Ln, reads, writes, scale=scale, bias=bias)
            act(out, out, AF.Exp, writes, writes, scale=power)

        def cpred(out, mask, data, reads, writes):
            rec.op('dve', lambda e: e.copy_predicated(out=out, mask=mask, data=data), reads=reads, writes=writes)

        def recip(out, in_, reads, writes):
            rec.op('dve', lambda e: e.reciprocal(out=out, in_=in_), reads=reads, writes=writes)

        def memset(eng, ap, val, writes):
            rec.op(eng, lambda e: e.memset(ap, val), writes=writes)

        def dma(q, sem, out, in_, reads=(), writes=(), slow=False):
            if sem == 'W':
                sem = 'b%d' % writes[0].id
            elif sem == 'R':
                sem = 'b%d' % reads[0].id
            if slow:
                return rec.dma(q, sem, lambda e: e.dma_start(out=out, in_=in_, allow_slow_non_contiguous=True), reads=reads, writes=writes)
            return rec.dma(q, sem, lambda e: e.dma_start(out=out, in_=in_), reads=reads, writes=writes)

        x_t = sb("x_t", [128, 8, T], F32); Bx = [Buf() for _ in range(8)]
        h_t = sb("h_t", [128, 8, T], BF16); Bh = [Buf() for _ in range(8)]
        NS = 3
        ws_t = [sb(f"ws{i}", [128, 4096], BF16) for i in range(NS)]; Bws = [Buf() for _ in range(NS)]
        big = sb("big", [128, 8, T], F32); Bbig = [Buf() for _ in range(8)]
        ub = sb("ub", [128, 8, T], BF16); Bub = [Buf() for _ in range(8)]
        bfp = sb("bfp", [128, 32, T], BF16); Bbfp = [Buf() for _ in range(32)]
        NTF = 11
        tf_t = sb("tf_t", [128, NTF, T + 1], F32); Btf = [Buf() for _ in range(NTF)]
        NTB = 13
        tb_t = sb("tb_t", [128, NTB, T], BF16); Btb = [Buf() for _ in range(NTB)]
        tfi = [0]; tbi = [0]

        def tmpf():
            i = tfi[0] % NTF; tfi[0] += 1
            return tf_t[:, i, :], Btf[i]

        def tmpb():
            i = tbi[0] % NTB; tbi[0] += 1
            return tb_t[:, i, :], Btb[i]

        S32 = sb("S32", [128, L, 8, 64], F32); BS32 = [[Buf() for _ in range(8)] for _ in range(L)]
        Sbd = sb("Sbd", [128, 8, 128], BF16); BSbd = [Buf() for _ in range(8)]
        prev = sb("prev", [128, L, 26], F32); Bprev = [Buf() for _ in range(L)]
        vec_t = sb("vec_t", [128, L, VW], F32); Bvec = Buf()
        fing_t = sb("fing_t", [128, 8], F32); Bfing = Buf()
        lw = sb("lw", [128, T], BF16); Blw = Buf(); sgl = sb("sgl", [128, T], BF16); Bsgl = Buf()
        smallw = sb("smallw", [128, 3072], BF16); Bsmall = Buf()
        bsb = sb("bsb", [128, 1024], F32); Bbsb = Buf()
        identf = sb("identf", [128, 128], F32); identb = sb("identb", [128, 128], BF16)
        onesb = sb("onesb", [128, 128], BF16); blkb = sb("blkb", [128, 128], BF16)
        onesf = sb("onesf", [128, 128], F32)
        msu = sb("msu", [128, 128], F32); miu = sb("miu", [128, 128], F32); msl = sb("msl", [128, 128], F32)
        cmask = sb("cmask", [128, T], F32)
        epsc = sb("epsc", [128, 4], F32)
        Bconst = Buf()
        vTp = [sb(f"vT{i}", [128, 4, 128], BF16) for i in range(2)]; kTp = [sb(f"kT{i}", [128, 4, 128], BF16) for i in range(2)]
        bTp = [sb(f"bT{i}", [128, 4, 128], BF16) for i in range(2)]
        BvTp, BkTp, BbTp = [Buf(), Buf()], [Buf(), Buf()], [Buf(), Buf()]
        bvt = [sb(f"bvt{i}", [128, T], BF16) for i in range(2)]; Bbvt = [Buf(), Buf()]
        y1t = sb("y1t", [128, T], F32); By1t = Buf()
        NQt = [sb(f"NQt{i}", [128, 2, 2, 128], BF16) for i in range(4)]; BNQ = [Buf() for _ in range(4)]
        MTt = [sb(f"MTt{i}", [128, 2, 2, 128], BF16) for i in range(4)]; BMT = [Buf() for _ in range(4)]
        Xbt = [sb(f"Xbt{i}", [128, 2, 2, 128], BF16) for i in range(4)]; BXb = [Buf() for _ in range(4)]
        m2d = sb("m2d", [128, 7, 2, 128], BF16)
        m2d_u16 = m2d[:].bitcast(mybir.dt.uint16)
        AKt = [sb(f"AKt{i}", [128, 2, 128], BF16) for i in range(4)]; BAK = [Buf() for _ in range(4)]
        RRt = [sb(f"RRt{i}", [128, 4, 128], BF16) for i in range(4)]; BRR = [Buf() for _ in range(4)]
        Zb = [sb(f"Zb{i}", [128, 128], BF16) for i in range(2)]; BZb = [Buf(), Buf()]
        Ub = [sb(f"Ub{i}", [128, 128], BF16) for i in range(2)]; BUb = [Buf(), Buf()]
        Swt = [sb(f"Swt{i}", [128, 64], F32) for i in range(2)]; BSw = [Buf(), Buf()]
        wc_p = [sb(f"wc{i}", [128, 4], F32) for i in range(2)]; Bwcp = [Buf(), Buf()]
        yTM = sb("yTM", [128, 4, 128], F32); ByTM = Buf()
        ysq = sb("ysq", [128, 4, 128], F32); Bysq = Buf()
        ynb = sb("ynb", [128, 4, 128], BF16); Bynb = Buf()
        gst = sb("gst", [128, 4, 8], F32); Bgst = Buf()

        NB = 7
        pb = [ps(f"pb{i}", [128, 512], F32) for i in range(NB)]; Bpb = [Buf() for _ in range(NB)]
        pT = ps("pT", [128, 1024], BF16); _bpt = Buf(); BpT = [_bpt, _bpt]
        psum_bufs = {f"pb{i}": [Bpb[i]] for i in range(NB)}
        psum_bufs["pT"] = [_bpt]
        bki = [0]

        def bank():
            i = bki[0] % NB; bki[0] += 1
            return pb[i], Bpb[i]

        pti = [0]

        def ptbank():
            i = pti[0] % 2; pti[0] += 1
            return pT[:, i * 512:(i + 1) * 512], BpT[i]

        gcur = ['']

        def checkpoint(name):
            if STOP == name or STOP == gcur[0] + ':' + name:
                raise _Stop()

        def finish():
            for k_, v_ in list(rec.dcnt.items()):
                rec._wait('sp', (k_, v_))
            rec.emit(st)

        try:
            memset('pool', onesf[:], 1.0, [Bconst])
            memset('pool', identf[:], 0.0, [Bconst])
            rec.op('pool', lambda e: e.affine_select(out=identf[:], in_=identf[:], pattern=[[-1, 128]], compare_op=ALU.not_equal, fill=1.0, base=0, channel_multiplier=1), reads=[Bconst], writes=[Bconst])
            cp('pool', identb[:], identf[:], [Bconst], [Bconst])
            cp('pool', onesb[:], onesf[:], [Bconst], [Bconst])
            memset('pool', blkb[:], 0.0, [Bconst])
            memset('pool', blkb[0:64, 0:64], 1.0, [Bconst])
            memset('pool', blkb[64:128, 64:128], 1.0, [Bconst])
            rec.op('pool', lambda e: e.affine_select(out=msu[:], in_=onesf[:], pattern=[[1, 128]], compare_op=ALU.is_gt, fill=0.0, base=0, channel_multiplier=-1), reads=[Bconst], writes=[Bconst])
            rec.op('pool', lambda e: e.affine_select(out=miu[:], in_=onesf[:], pattern=[[1, 128]], compare_op=ALU.is_ge, fill=0.0, base=0, channel_multiplier=-1), reads=[Bconst], writes=[Bconst])
            rec.op('pool', lambda e: e.affine_select(out=msl[:], in_=onesf[:], pattern=[[-1, 128]], compare_op=ALU.is_gt, fill=0.0, base=0, channel_multiplier=1), reads=[Bconst], writes=[Bconst])
            Ea, Eb, Ec = tf_t[:, 0, 0:128], tf_t[:, 1, 0:128], tf_t[:, 2, 0:128]
            BE_ = [Btf[0], Btf[1], Btf[2]]
            cp('pool', Ea, identf[:], [Bconst], [BE_[0]])
            Ecur, Bcur, Enxt, Bnxt = Ea, BE_[0], Eb, BE_[1]
            for j in range(7):
                s_ = 2 ** (j + 1); nb_ = 128 // s_
                if j == 6:
                    cp('pool', Enxt, onesf[:], [Bconst], [Bnxt])
                else:
                    def sel1(e, out=Enxt, s_=s_, nb_=nb_):
                        return e.affine_select(out=out.rearrange("p (a b) -> p a b", b=s_), in_=onesf[:].rearrange("p (a b) -> p a b", b=s_),
                                               pattern=[[-s_, nb_], [0, s_]], compare_op=ALU.is_ge, fill=0.0, base=0, channel_multiplier=1)

                    def sel2(e, out=Enxt, s_=s_, nb_=nb_):
                        return e.affine_select(out=out.rearrange("p (a b) -> p a b", b=s_), in_=out.rearrange("p (a b) -> p a b", b=s_),
                                               pattern=[[s_, nb_], [0, s_]], compare_op=ALU.is_ge, fill=0.0, base=s_ - 1, channel_multiplier=-1)
                    rec.op('pool', sel1, reads=[Bconst], writes=[Bnxt])
                    rec.op('pool', sel2, reads=[Bnxt], writes=[Bnxt])
                tt('pool', Ec, Enxt, Ecur, ALU.subtract, [Bnxt, Bcur], [BE_[2]])
                tt('pool', m2d[:, j, 0, :], Ec, msu[:], ALU.mult, [BE_[2], Bconst], [Bconst])
                tt('pool', m2d[:, j, 1, :], Ec, msl[:], ALU.mult, [BE_[2], Bconst], [Bconst])
                Ecur, Bcur, Enxt, Bnxt = Enxt, Bnxt, Ecur, Bcur
            memset('pool', epsc[:, 0:1], 1e-6, [Bconst])
            memset('pool', epsc[:, 1:2], 1e-5, [Bconst])
            memset('pool', epsc[:, 2:3], 64e-5, [Bconst])
            memset('pool', epsc[:, 3:4], 1e-24, [Bconst])
            dma('pool', 'W', vec_t[:], vecs, writes=[Bvec])
            dma('pool', 'W', fing_t[:], fing, writes=[Bfing])
            ts('dve', vec_t[:, :, V_OMM:V_OMM + 26], vec_t[:, :, V_MU:V_MU + 26], -1.0, 1.0, ALU.mult, ALU.add, [Bvec], [Bvec])
            ts('dve', vec_t[:, :, V_OMKA:V_OMKA + 8], vec_t[:, :, V_KA:V_KA + 8], -1.0, 1.0, ALU.mult, ALU.add, [Bvec], [Bvec])

            def vcol(l, base, c):
                return vec_t[:, l, base + c:base + c + 1]

            checkpoint('const')
            _bw = Buf()
            Bwsc = [_bw] * L
            BwscM = [_bw] * L
            Bsm = [[] for _ in range(L)]
            Bkv = [[[], []] for _ in range(L)]

            def conv(l, pi, src2d, ncols, col0=0):
                dst = wsc[l, pi].rearrange("p (kc n) -> p kc n", kc=8)[:, :, col0:col0 + ncols]
                if pi >= PI_MKV:
                    dma('pool', 'cv', dst, src2d.rearrange("(kc p) n -> p kc n", p=128), writes=[BwscM[l]])
                else:
                    dma('pool', 'cv', dst, src2d.rearrange("(kc p) n -> p kc n", p=128), writes=[Bwsc[l]])

            for l in range(L):
                for i in range(4):
                    conv(l, PI_MKV + i, w_mkv[l][:, i * 512:(i + 1) * 512], 512)
            for l in range(L):
                conv(l, PI_LORA, w_in[l][:, 3072:3328], 256)
                for c in range(8):
                    for j in range(3):
                        conv(l, PI_RKV + c, w_in[l][:, j * 1024 + c * 128: j * 1024 + (c + 1) * 128], 128, col0=j * 128)
                for i in range(4):
                    conv(l, PI_SGU + i, w_in[l][:, 3328 + i * 512: 3328 + (i + 1) * 512], 512)
                for i in range(2):
                    conv(l, PI_Q + i, w_in[l][:, 5376 + i * 512: 5376 + (i + 1) * 512], 512)
                for i in range(6):
                    conv(l, PI_GATE + i, w_in[l][:, 6400 + i * 512: 6400 + (i + 1) * 512], 512)
                for b in range(3):
                    for hf in range(2):
                        conv(l, PI_BR + b * 2 + hf, w_br[l, b][:, hf * 512:(hf + 1) * 512], 512)
                for hf in range(2):
                    conv(l, PI_OUT + hf, w_out[l][:, hf * 512:(hf + 1) * 512], 512)
                for i in range(8):
                    conv(l, PI_UP + i, w_up[l][:, i * 512:(i + 1) * 512], 512)
                for g in range(2):
                    for q in range(4):
                        conv(l, PI_DN + g * 4 + q, w_dn[l][q * 1024:(q + 1) * 1024, g * 512:(g + 1) * 512], 512)
                dma('pool', 'cv', smsc[l][0:64, 1024:2048], w2[l], writes=[Bwsc[l]])
                dma('pool', 'cv', smsc[l][64:128, 1024:2048], a2[l], writes=[Bwsc[l]])
                dma('pool', 'cv', smsc[l][:, 2048:3072], g2[l], writes=[Bwsc[l]])
                dma('pool', 'cv', kvsc[l, 1][:, 2048:4096].rearrange("p (mb n) -> p mb n", mb=2),
                    cv[l].rearrange("(mb p) n -> p mb n", p=128), writes=[Bwsc[l]])
            checkpoint('conv')
            for l in range(L):
                wsf, Bw_ = big[:, 0:2, :], [Bbig[0], Bbig[1]]
                dma('pool', 'W', big[:, 0:2, :].rearrange("p a (g j) -> p (a g) j", j=128), w_s[l].rearrange("g i j -> i g j"), writes=Bw_)
                for g in range(8):
                    if g % 4 == 0:
                        stg, Bstg = tmpb()
                    bk, Bbk = bank()
                    tr(bk[:, 0:128], big[:, g // 4, (g % 4) * 128:(g % 4 + 1) * 128], identf[:], reads=Bw_ + [Bconst], writes=[Bbk])
                    tt('dve', stg[:, (g % 4) * 128:(g % 4 + 1) * 128], bk[:, 0:128], miu[:], ALU.mult, [Bbk, Bconst], [Bstg])
                    if g % 4 == 3:
                        Bd = Buf(); Bsm[l].append(Bd)
                        dma('pool', 'R', smsc[l][:, (g // 4) * 512:(g // 4 + 1) * 512], stg, reads=[Bstg], writes=[Bd])
            for l in range(L):
                Bk_ = [Bbig[i] for i in range(4)]
                dma('pool', 'W', big[:, 0:4, :].rearrange("p (mb a) n -> p mb (a n)", mb=2), ck[l].rearrange("(mb p) n -> p mb n", p=128), writes=Bk_)
                for j in range(8):
                    if j % 2 == 0:
                        stg, Bstg = tmpb()
                    bk, Bbk = bank()
                    for mb in range(2):
                        tr(bk[:, mb * 128:(mb + 1) * 128], big[:, mb * 2 + j // 4, (j % 4) * 128:(j % 4 + 1) * 128], identf[:],
                           reads=Bk_ + [Bconst], writes=[Bbk] if mb == 1 else None)
                    cp('act', stg[:, (j % 2) * 256:(j % 2 + 1) * 256], bk[:, 0:256], [Bbk], [Bstg])
                    if j % 2 == 1:
                        Bd = Buf(); Bkv[l][1].append(Bd)
                        dma('pool', 'R', kvsc[l, 1][:, (j - 1) * 256:(j + 1) * 256], stg, reads=[Bstg], writes=[Bd])

            checkpoint('prep')
            uses = []
            state = {'k': 0, 'loaded': 0}

            def plan_tile_layer(l, grp):
                seq = [('w', l, PI_LORA)] + [('w', l, PI_RKV + c) for c in range(8)] + [('w', l, PI_SGU + i) for i in range(4)]
                seq += [('w', l, PI_Q), ('w', l, PI_Q + 1), ('kv', l, grp)]
                for b in range(3):
                    for hf in range(2):
                        seq += [('w', l, PI_GATE + b * 2 + hf), ('w', l, PI_BR + b * 2 + hf)]
                seq += [('w', l, PI_OUT), ('w', l, PI_OUT + 1)] + [('w', l, PI_UP + i) for i in range(8)]
                seq += [('w', l, PI_DN + i) for i in range(8)]
                return seq

            for l in range(L):
                uses += [('w', l, PI_MKV + i) for i in range(4)]
            for l in range(L):
                uses += plan_tile_layer(l, 1)
            for t_ in range(NT):
                for l in range(L):
                    uses += plan_tile_layer(l, 0)
            def _issue_load(k):
                kind, l, idx = uses[k]
                slot = k % NS
                if kind == 'w':
                    ncols = 256 if idx == PI_LORA else (384 if PI_RKV <= idx < PI_RKV + 8 else 512)
                    if ncols == 512:
                        dma('sp', f'w{slot}', ws_t[slot][:], wsc[l, idx], reads=[BwscM[l] if idx >= PI_MKV else Bwsc[l]], writes=[Bws[slot]])
                    else:
                        dma('sp', f'w{slot}', ws_t[slot][:].rearrange("p (kc n) -> p kc n", kc=8)[:, :, 0:ncols],
                            wsc[l, idx].rearrange("p (kc n) -> p kc n", kc=8)[:, :, 0:ncols], reads=[Bwsc[l]], writes=[Bws[slot]])
                else:
                    dma('sp', f'w{slot}', ws_t[slot][:], kvsc[l, idx], reads=[Bwsc[l]] + Bkv[l][idx], writes=[Bws[slot]])

            def piece(kind, l, idx):
                k = state['k']
                assert uses[k] == (kind, l, idx), (uses[k], kind, l, idx)
                while state['loaded'] < min(len(uses), k + NS - 1):
                    _issue_load(state['loaded']); state['loaded'] += 1
                state['k'] += 1
                slot = k % NS
                return ws_t[slot], Bws[slot]

            def wpiece(l, idx):
                t_, B_ = piece('w', l, idx)
                return t_[:].rearrange("p (kc n) -> p kc n", kc=8), B_

            def rmsnorm_to_h(l, gbase, TT):
                bk, Bbk = bank()
                for c in range(8):
                    sq, Bsq = tmpb()
                    act(sq[:, :TT], x_t[:, c, :TT], AF.Square, [Bx[c]], [Bsq])
                    mm(bk[:, :TT], onesb[:], sq[:, :TT], c == 0, c == 7, [Bsq, Bconst], [Bbk] if c == 7 else None)
                rs, Brs = tmpf()
                rsqrt(rs[:, :TT], bk[:, :TT], [Bbk, Bconst], [Brs], scale=1.0 / D, bias=epsc[:, 0:1])
                for c in range(8):
                    gcol = vcol(l, gbase, c) if l is not None else fing_t[:, c:c + 1]
                    yield_out = h_t[:, c, :TT]
                    stt(yield_out, x_t[:, c, :TT], gcol, rs[:, :TT], ALU.mult, ALU.mult, [Bx[c], Brs, Bvec], [Bh[c]])

            def big_mm(wt, Bw, oc4, rhs_list, Brhs, TT, out_bk=None, start=True, stop=True, kc0=0):
                if out_bk is None:
                    bk, Bbk = bank()
                else:
                    bk, Bbk = out_bk
                n = len(rhs_list)
                for kc in range(n):
                    last = (kc == n - 1)
                    mm(bk[:, :TT], wt[:, kc, oc4 * 128:(oc4 + 1) * 128], rhs_list[kc], start and kc == 0, stop and last,
                       [Bw, Brhs[kc]], [Bbk] if last else None)
                return bk, Bbk

            def hrhs(TT):
                return [h_t[:, kc, :TT] for kc in range(8)]

            def rwkv_gen(l, c, TT, C):
                NC_ = TT // C
                wt, Bw = wpiece(l, PI_RKV + c)
                w2a2 = smallw[:, 1024:2048]; g2t = smallw[:, 2048:3072]
                F = lambda i: (tf_t[:, i, :], Btf[i])
                Bq = lambda i: (tb_t[:, i, :], Btb[i])
                pp = c % 2
                CB = [F(0), F(0)]
                (r_, Br), (k_, Bk), (v_, Bv) = F(1), F(2), F(3)
                (sig, Bsig), (a_, Ba), (kk, Bkk), (Lc, BL), (E, BE), (E2, BE2), (t1, Bt1) = F(4), F(5), F(6), F(7), F(8), F(9), F(10)
                (g_, Bg), (rt, Brt), (kt, Bkt), (bt, Bbt), (at, Bat) = [Bq(pp * 5 + i) for i in range(5)]
                (sq, Bsq), (pr, Bpr), (vb, Bvb) = Bq(10), Bq(11), Bq(12)
                vT, kT, bT, BvT, BkT, BbT = vTp[pp], kTp[pp], bTp[pp], BvTp[pp], BkTp[pp], BbTp[pp]
                wc_t, Bwc = wc_p[pp], Bwcp[pp]
                mixed = [(r_, Br), (k_, Bk), (v_, Bv)]
                for j in range(3):
                    col = j * 8 + c
                    bk, Bbk = big_mm(wt, Bw, j, hrhs(TT), Bh, TT)
                    cb, Bcb = CB[j % 2]
                    mx, Bmx = mixed[j]
                    cp('act', cb[:, 1:TT + 1], bk[:, :TT], [Bbk], [Bcb])
                    cp('pool', cb[:, 0:1], prev[:, l, col:col + 1], [Bprev[l]], [Bcb])
                    ts('dve', t1[:, :TT], cb[:, 0:TT], vcol(l, V_MU, col), None, ALU.mult, None, [Bcb, Bvec], [Bt1])
                    stt(mx[:, :TT], cb[:, 1:TT + 1], vcol(l, V_OMM, col), t1[:, :TT], ALU.mult, ALU.add, [Bcb, Bt1, Bvec], [Bmx])
                    cp('pool', prev[:, l, col:col + 1], cb[:, TT:TT + 1], [Bcb], [Bprev[l]])
                    yield 's'
                bk, Bbk = bank()
                mm(bk[:, :TT], w2a2[0:64, c * 128:(c + 1) * 128], lw[0:64, :TT], True, True, [Bsmall, Blw], [Bbk])
                act(sig[:, :TT], bk[:, :TT], AF.Sigmoid, [Bbk, Bvec], [Bsig], bias=vcol(l, V_W0, c))
                yield 's'
                bk, Bbk = bank()
                mm(bk[:, :TT], w2a2[64:128, c * 128:(c + 1) * 128], lw[64:128, :TT], True, True, [Bsmall, Blw], [Bbk])
                act(a_[:, :TT], bk[:, :TT], AF.Sigmoid, [Bbk, Bvec], [Ba], bias=vcol(l, V_A0, c))
                yield 's'
                bk, Bbk = bank()
                mm(bk[:, :TT], g2t[:, c * 128:(c + 1) * 128], sgl[:, :TT], True, True, [Bsmall, Bsgl], [Bbk])
                cp('act', g_[:, :TT], bk[:, :TT], [Bbk], [Bg])
                yield 's'
                ts('pool', kk[:, :TT], k_[:, :TT], vcol(l, V_KK, c), None, ALU.mult, None, [Bk, Bvec], [Bkk])
                act(sq[:, :TT], kk[:, :TT], AF.Square, [Bkk], [Bsq])
                bk, Bbk = bank()
                mm(bk[:, :TT], blkb[:], sq[:, :TT], True, True, [Bconst, Bsq], [Bbk])
                rn, Brn = E2, BE2
                rsqrt(rn[:, :TT], bk[:, :TT], [Bbk, Bconst], [Brn], bias=epsc[:, 3:4])
                yield 's'
                yield 's'
                tt('pool', kk[:, :TT], kk[:, :TT], rn[:, :TT], ALU.mult, [Bkk, Brn], [Bkk])
                ts('dve', t1[:, :TT], a_[:, :TT], vcol(l, V_KA, c), vcol(l, V_OMKA, c), ALU.mult, ALU.add, [Ba, Bvec], [Bt1])
                tt('pool', k_[:, :TT], k_[:, :TT], t1[:, :TT], ALU.mult, [Bk, Bt1], [Bk])
                yield 's'
                scan_mask = cmask[:, :TT] if C == 128 else onesf[:, :TT]
                rec.op('dve', lambda e: e.tensor_tensor_scan(out=Lc[:, :TT], data0=scan_mask, data1=sig[:, :TT],
                                                             initial=0.0, op0=ALU.mult, op1=ALU.add), reads=[Bsig, Bconst], writes=[BL])
                act(E[:, :TT], Lc[:, :TT], AF.Exp, [BL], [BE], scale=C0)
                tt('dve', rt[:, :TT], r_[:, :TT], E[:, :TT], ALU.mult, [Br, BE], [Brt])
                cp('pool', wc_t[:, 0:NC_], E[:, :TT].rearrange("p (a b) -> p a b", b=C)[:, :, C - 1], [BE], [Bwc])
                yield 's'
                act(E2[:, :TT], Lc[:, :TT], AF.Exp, [BL], [BE2], scale=-C0)
                tt('dve', kt[:, :TT], k_[:, :TT], E2[:, :TT], ALU.mult, [Bk, BE2], [Bkt])
                tt('pool', t1[:, :TT], kk[:, :TT], a_[:, :TT], ALU.mult, [Bkk, Ba], [Bt1])
                tt('dve', bt[:, :TT], t1[:, :TT], E2[:, :TT], ALU.mult, [Bt1, BE2], [Bbt])
                yield 's'
                tt('pool', Lc[:, :TT], Lc[:, :TT], sig[:, :TT], ALU.subtract, [BL, Bsig], [BL])
                act(E[:, :TT], Lc[:, :TT], AF.Exp, [BL], [BE], scale=C0)
                stt(at[:, :TT], kk[:, :TT], -1.0, E[:, :TT], ALU.mult, ALU.mult, [Bkk, BE], [Bat])
                yield 's'
                stt(pr[:, :TT], r_[:, :TT], vcol(l, V_RK, c), k_[:, :TT], ALU.mult, ALU.mult, [Br, Bk, Bvec], [Bpr])
                bk, Bbk = bank()
                mm(bk[:, :TT], blkb[:], pr[:, :TT], True, True, [Bconst, Bpr], [Bbk])
                bv, Bbv = bvt[pp], Bbvt[pp]
                tt('dve', bv[:, :TT], bk[:, :TT], v_[:, :TT], ALU.mult, [Bbk, Bv], [Bbv])
                cp('pool', vb[:, :TT], v_[:, :TT], [Bv], [Bvb])
                yield 's'
                checkpoint('pelem')
                for (src, Bsrc, dst, Bdst) in ((vb, Bvb, vT, BvT), (kt, Bkt, kT, BkT), (bt, Bbt, bT, BbT)):
                    pt, Bpt = ptbank()
                    for ci in range(NC_):
                        tr(pt[0:C, ci * 128:(ci + 1) * 128], src[:, ci * C:(ci + 1) * C], identb[:], [Bsrc, Bconst], [Bpt] if ci == NC_ - 1 else None)
                    cp('act', dst[0:C, 0:NC_, :], pt[0:C, 0:NC_ * 128].rearrange("p (a b) -> p a b", b=128), [Bpt], [Bdst])
                    yield 's'
                checkpoint('ptm')
                memset('pool', Sbd[:, c, :], 0.0, [BSbd[c]])
                for hd in range(2):
                    cp('pool', Sbd[hd * 64:(hd + 1) * 64, c, hd * 64:(hd + 1) * 64], S32[hd * 64:(hd + 1) * 64, l, c, :], [BS32[l][c]], [BSbd[c]])
                checkpoint('psbd')
                yield 'frontdone'
                hr = lambda hd: slice(hd * 64, (hd + 1) * 64)
                NL = C.bit_length() - 1
                for ci in range(NC_):
                    cs = slice(ci * C, (ci + 1) * C)
                    NQ, MT, AK, RR = NQt[ci], MTt[ci], AKt[ci], RRt[ci]
                    for hd in range(2):
                        bkX, BX = bank()
                        X3 = bkX[0:C, :].rearrange("p (a b) -> p a b", b=128)
                        mm(X3[:, 0, 0:C], bt[hr(hd), cs], at[hr(hd), cs], True, True, [Bbt, Bat])
                        mm(X3[:, 1, 0:C], kt[hr(hd), cs], at[hr(hd), cs], True, True, [Bkt, Bat])
                        mm(X3[:, 2, 0:C], bt[hr(hd), cs], rt[hr(hd), cs], True, True, [Bbt, Brt])
                        mm(X3[:, 3, 0:C], kt[hr(hd), cs], rt[hr(hd), cs], True, True, [Bkt, Brt], [BX])
                        bkY, BY = bank()
                        mm(bkY[0:C, 0:C], at[hr(hd), cs], bt[hr(hd), cs], True, True, [Bbt, Bat], [BY])
                        tt('dve', NQ[0:C, 0, hd, 0:C], X3[:, 0, 0:C], msu[0:C, 0:C], ALU.mult, [BX, Bconst], [BNQ[ci]])
                        tt('dve', AK[0:C, hd, 0:C], X3[:, 1, 0:C], msu[0:C, 0:C], ALU.mult, [BX, Bconst], [BAK[ci]])
                        tt('dve', RR[0:C, hd:4:2, 0:C], X3[:, 2:4, 0:C], miu[0:C, 0:C].unsqueeze(1).to_broadcast([C, 2, C]), ALU.mult, [BX, Bconst], [BRR[ci]])
                        tt('dve', NQ[0:C, 1, hd, 0:C], bkY[0:C, 0:C], msl[0:C, 0:C], ALU.mult, [BY, Bconst], [BNQ[ci]])
                    for f_ in range(2):
                        tt('dve', MT[0:C, f_, :, 0:C], NQ[0:C, f_, :, 0:C], m2d[0:C, 0, f_, 0:C].unsqueeze(1).to_broadcast([C, 2, C]), ALU.mult,
                           [BNQ[ci], Bconst], [BMT[ci]])
                    tt('dve', MT[0:C].rearrange("p f h q -> p (f h) q")[:, :, 0:C], MT[0:C].rearrange("p f h q -> p (f h) q")[:, :, 0:C],
                       identb[0:C, 0:C].unsqueeze(1).to_broadcast([C, 4, C]), ALU.add, [BMT[ci], Bconst], [BMT[ci]])
                    yield 's'
                checkpoint('patype')
                for j in range(1, NL):
                    last = j == NL - 1
                    nf = 1 if last else 2
                    xs_ = []
                    for ci in range(NC_):
                        NQ, MT = NQt[ci], MTt[ci]
                        bkX, BX = bank()
                        Xv = bkX[0:C, :].rearrange("p (f h q) -> p f h q", f=2, h=2)
                        for hd in range(2):
                            mm(Xv[:, 0, hd, 0:C], NQ[0:C, 1, hd, 0:C], MT[0:C, 0, hd, 0:C], True, True, [BNQ[ci], BMT[ci]], [BX] if hd == 1 else None)
                        xs_.append((Xv, BX))
                    ys_ = []
                    for ci in range(NC_):
                        MT, XB_ = MTt[ci], Xbt[ci]
                        Xv, BX = xs_[ci]
                        cp('act', XB_[0:C, 0, :, 0:C], Xv[:, 0, :, 0:C], [BX], [BXb[ci]])
                        bkY, BY = bank()
                        Yv = bkY[0:C, :].rearrange("p (f h q) -> p f h q", f=2, h=2)
                        for hd in range(2):
                            mm(Yv[:, 0, hd, 0:C], MT[0:C, 1, hd, 0:C], XB_[0:C, 0, hd, 0:C], True, True, [BMT[ci], BXb[ci]], [BY] if (last and hd == 1) else None)
                            if not last:
                                mm(Yv[:, 1, hd, 0:C], XB_[0:C, 0, hd, 0:C], MT[0:C, 1, hd, 0:C], True, True, [BMT[ci], BXb[ci]], [BY] if hd == 1 else None)
                        ys_.append((Yv, BY))
                    for ci in range(NC_):
                        MT = MTt[ci]
                        Yv, BY = ys_[ci]
                        cpred(MT[0:C, 0:nf, :, 0:C], m2d_u16[0:C, j, 0:nf, 0:C].unsqueeze(2).to_broadcast([C, nf, 2, C]), Yv[:, 0:nf, :, 0:C],
                              [BY, Bconst, BMT[ci]], [BMT[ci]])
                    yield 's'
                yield 'p2done'
                for ci in range(NC_):
                    par = ci % 2
                    cs = slice(ci * C, (ci + 1) * C)
                    AK, RR, MT = AKt[ci], RRt[ci], MTt[ci]
                    checkpoint('pdbl')
                    Rw, BRw = Swt[par], BSw[par]
                    for hd in range(2):
                        tt('dve', Rw[hr(hd), :], S32[hr(hd), l, c, :], Sbd[hr(hd), c, hr(hd)], ALU.subtract, [BS32[l][c], BSbd[c]], [BRw])
                    ts('dve', Rw[:, :], Rw[:, :], wc_t[:, ci:ci + 1], None, ALU.mult, None, [BRw, Bwc], [BRw])
                    bkW, BW = bank()
                    mm(bkW[0:C, 0:128], at[:, cs], Sbd[:, c, :], True, False, [Bat, BSbd[c]])
                    for hd in range(2):
                        mm(bkW[0:C, hr(hd)], AK[0:C, hd, 0:C], vT[0:C, ci, hr(hd)], False, hd == 1, [BAK[ci], BvT], [BW] if hd == 1 else None)
                    cp('act', Zb[par][0:C, :], bkW[0:C, 0:128], [BW], [BZb[par]])
                    yield 's'
                    bkU, BU = bank()
                    for hd in range(2):
                        mm(bkU[0:C, hr(hd)], MT[0:C, 0, hd, 0:C], Zb[par][0:C, hr(hd)], True, True, [BMT[ci], BZb[par]], [BU] if hd == 1 else None)
                    cp('act', Ub[par][0:C, :], bkU[0:C, 0:128], [BU], [BUb[par]])
                    yield 's'
                    bkS, BS = bank()
                    mm(bkS[:, 0:128], bT[0:C, ci, :], Ub[par][0:C, :], True, False, [BbT, BUb[par]])
                    mm(bkS[:, 0:128], kT[0:C, ci, :], vT[0:C, ci, :], False, False, [BkT, BvT])
                    mm(bkS[:, 0:128], identb[:, :], Sbd[:, c, :], False, True, [Bconst, BSbd[c]], [BS])
                    bkY, BY = bank()
                    mm(bkY[0:C, 0:128], rt[:, cs], Sbd[:, c, :], True, False, [Brt, BSbd[c]])
                    for hd in range(2):
                        mm(bkY[0:C, hr(hd)], RR[0:C, hd, 0:C], Ub[par][0:C, hr(hd)], False, False, [BRR[ci], BUb[par]])
                        mm(bkY[0:C, hr(hd)], RR[0:C, 2 + hd, 0:C], vT[0:C, ci, hr(hd)], False, hd == 1, [BRR[ci], BvT], [BY] if hd == 1 else None)
                    if ci < NC_ - 1:
                        for hd in range(2):
                            act(Sbd[hr(hd), c, hr(hd)], bkS[hr(hd), hr(hd)], AF.Identity, [BS, Bwc], [BSbd[c]], scale=wc_t[hr(hd), ci:ci + 1])
                    for hd in range(2):
                        stt(S32[hr(hd), l, c, :], bkS[hr(hd), hr(hd)], wc_t[hr(hd), ci:ci + 1], Rw[hr(hd), :], ALU.mult, ALU.add,
                            [BS, Bwc, BRw, BSbd[c]], [BS32[l][c]])
                    cp('act', yTM[0:C, ci, :], bkY[0:C, 0:128], [BY], [ByTM])
                    yield 's'
                checkpoint('pseq')
                yield 'p3done'
                NG = NC_ * 2
                y3 = yTM[0:C, 0:NC_, :].rearrange("p a (h v) -> p (a h) v", v=64)
                rec.op('dve', lambda e: e.tensor_reduce(out=gst[0:C, 0, 0:NG], in_=y3, axis=AX.X, op=ALU.add), reads=[ByTM], writes=[Bgst])
                tt('pool', ysq[0:C, 0:NC_, :], yTM[0:C, 0:NC_, :], yTM[0:C, 0:NC_, :], ALU.mult, [ByTM], [Bysq])
                s3 = ysq[0:C, 0:NC_, :].rearrange("p a (h v) -> p (a h) v", v=64)
                rec.op('dve', lambda e: e.tensor_reduce(out=gst[0:C, 1, 0:NG], in_=s3, axis=AX.X, op=ALU.add), reads=[Bysq], writes=[Bgst])
                ts('dve', gst[0:C, 2, 0:NG], gst[0:C, 0, 0:NG], 1.0 / 64, None, ALU.mult, None, [Bgst], [Bgst])
                tt('dve', gst[0:C, 3, 0:NG], gst[0:C, 2, 0:NG], gst[0:C, 2, 0:NG], ALU.mult, [Bgst], [Bgst])
                stt(gst[0:C, 3, 0:NG], gst[0:C, 1, 0:NG], 1.0 / 64, gst[0:C, 3, 0:NG], ALU.mult, ALU.subtract, [Bgst], [Bgst])
                rsqrt(gst[0:C, 3, 0:NG], gst[0:C, 3, 0:NG], [Bgst, Bconst], [Bgst], bias=epsc[0:C, 2:3])
                yield 's'
                tt('dve', ysq[0:C, 0:NC_, :].rearrange("p a (h v) -> p (a h) v", v=64), y3,
                   gst[0:C, 2, 0:NG].unsqueeze(2).to_broadcast([C, NG, 64]), ALU.subtract, [ByTM, Bgst], [Bysq])
                tt('dve', ynb[0:C, 0:NC_, :].rearrange("p a (h v) -> p (a h) v", v=64), s3,
                   gst[0:C, 3, 0:NG].unsqueeze(2).to_broadcast([C, NG, 64]), ALU.mult, [Bysq, Bgst], [Bynb])
                pt, Bpt = ptbank()
                for ci in range(NC_):
                    tr(pt[:, ci * C:(ci + 1) * C], ynb[0:C, ci, :], identb[0:C, 0:C], [Bynb, Bconst], [Bpt] if ci == NC_ - 1 else None)
                y1, By1 = y1t, By1t
                act(y1[:, :TT], pt[:, :TT], AF.Identity, [Bpt, Bvec], [By1], scale=vcol(l, V_LXG, c), bias=vcol(l, V_LXB, c))
                tt('pool', y1[:, :TT], y1[:, :TT], bv[:, :TT], ALU.add, [By1, Bbv], [By1])
                tt('dve', bfp[:, c, :TT], y1[:, :TT], g_[:, :TT], ALU.mult, [By1, Bg], [Bbfp[c]])

            def tile_layer(l, grp, TT, C, NSTEP):
                gcur[0] = 'sp'[1 - grp] if False else ('p' if grp == 0 else 's')
                NTB_ = max(1, TT // 128)
                TB = min(TT, 128)
                dma('pool', 'W', smallw[:], smsc[l], reads=[Bwsc[l]] + Bsm[l], writes=[Bsmall])
                dma('pool', 'W', bsb[:], b_s[l].unsqueeze(0).partition_broadcast(128).rearrange("p o n -> p (o n)"), writes=[Bbsb])
                rmsnorm_to_h(l, V_MIXG, TT)
                wt, Bw = wpiece(l, PI_LORA)
                for j in range(2):
                    col = 24 + j
                    bk, Bbk = big_mm(wt, Bw, j, hrhs(TT), Bh, TT)
                    cb, Bcb = tmpf()
                    cp('act', cb[:, 1:TT + 1], bk[:, :TT], [Bbk], [Bcb])
                    cp('pool', cb[:, 0:1], prev[:, l, col:col + 1], [Bprev[l]], [Bcb])
                    t1, Bt1 = tmpf()
                    ts('dve', t1[:, :TT], cb[:, 0:TT], vcol(l, V_MU, col), None, ALU.mult, None, [Bcb, Bvec], [Bt1])
                    mx, Bmx = tmpf()
                    stt(mx[:, :TT], cb[:, 1:TT + 1], vcol(l, V_OMM, col), t1[:, :TT], ALU.mult, ALU.add, [Bcb, Bt1, Bvec], [Bmx])
                    cp('pool', prev[:, l, col:col + 1], cb[:, TT:TT + 1], [Bcb], [Bprev[l]])
                    if j == 0:
                        act(lw[0:64, :TT], mx[0:64, :TT], AF.Tanh, [Bmx], [Blw])
                        cp('act', lw[64:128, :TT], mx[64:128, :TT], [Bmx], [Blw])
                    else:
                        act(sgl[:, :TT], mx[:, :TT], AF.Sigmoid, [Bmx], [Bsgl])
                checkpoint('lora')
                gens, stt_ = {}, {}
                n_started = 0
                while n_started < 8 or any(v != 'done' for v in stt_.values()):
                    if n_started < 8:
                        i_ = n_started
                        ok = (i_ == 0 or stt_[i_ - 1] != 'front') and (i_ < 2 or stt_[i_ - 2] == 'done')
                        if ok:
                            gens[i_] = rwkv_gen(l, i_, TT, C); stt_[i_] = 'front'; n_started += 1
                    in_p3 = any(v == 'chunk3' for v in stt_.values())
                    in_p12 = any(v == 'chunk12' for v in stt_.values())
                    progressed = False
                    for i_ in sorted(gens):
                        st_ = stt_[i_]
                        if st_ == 'done':
                            continue
                        if st_ == 'wait':
                            if i_ == 0 or stt_[i_ - 1] in ('final', 'done'):
                                stt_[i_] = st_ = 'chunk12'
                            else:
                                continue
                        reps = 1
                        for _r in range(reps):
                            try:
                                tok = next(gens[i_])
                            except StopIteration:
                                stt_[i_] = 'done'; progressed = True
                                break
                            progressed = True
                            if tok == 'frontdone':
                                stt_[i_] = 'wait'; break
                            elif tok == 'p2done':
                                stt_[i_] = 'chunk3'; break
                            elif tok == 'p3done':
                                stt_[i_] = 'final'; break
                    assert progressed or n_started < 8
                checkpoint('rwkv')
                for i in range(4):
                    wt, Bw = wpiece(l, PI_SGU + i)
                    for o4 in range(4):
                        oc = i * 4 + o4
                        bk, Bbk = big_mm(wt, Bw, o4, hrhs(TT), Bh, TT)
                        if oc < 8:
                            act(ub[:, oc, :TT], bk[:, :TT], AF.Gelu_apprx_tanh, [Bbk], [Bub[oc]])
                        else:
                            act(big[:, oc - 8, :TT], bk[:, :TT], AF.Gelu_apprx_tanh, [Bbk], [Bbig[oc - 8]])
                bk1, Bb1 = bank(); bk2, Bb2 = bank()
                for c in range(8):
                    vb, Bvb = tmpb(); vs, Bvs = tmpb()
                    cp('pool', vb[:, :TT], big[:, c, :TT], [Bbig[c]], [Bvb])
                    act(vs[:, :TT], big[:, c, :TT], AF.Square, [Bbig[c]], [Bvs])
                    mm(bk1[:, :TT], onesb[:], vb[:, :TT], c == 0, c == 7, [Bvb, Bconst], [Bb1])
                    mm(bk2[:, :TT], onesb[:], vs[:, :TT], c == 0, c == 7, [Bvs, Bconst], [Bb2])
                mean, Bmean = tmpf(); rstd, Brstd = tmpf()
                act(mean[:, :TT], bk1[:, :TT], AF.Copy, [Bb1], [Bmean], scale=1.0 / D)
                tt('pool', rstd[:, :TT], mean[:, :TT], mean[:, :TT], ALU.mult, [Bmean], [Brstd])
                stt(rstd[:, :TT], bk2[:, :TT], 1.0 / D, rstd[:, :TT], ALU.mult, ALU.subtract, [Bb2, Brstd], [Brstd])
                rsqrt(rstd[:, :TT], rstd[:, :TT], [Brstd, Bconst], [Brstd], bias=epsc[:, 1:2])
                vnb = []
                for c in range(8):
                    vc = big[:, c, :TT]
                    tt('dve', vc, vc, mean[:, :TT], ALU.subtract, [Bbig[c], Bmean], [Bbig[c]])
                    tt('pool', vc, vc, rstd[:, :TT], ALU.mult, [Bbig[c], Brstd], [Bbig[c]])
                    ts('dve', vc, vc, vcol(l, V_SLG, c), vcol(l, V_SLB, c), ALU.mult, ALU.add, [Bbig[c], Bvec], [Bbig[c]])
                    cp('pool', bfp[:, 16 + c, :TT], vc, [Bbig[c]], [Bbfp[16 + c]])
                    vnb.append((bfp[:, 16 + c, :], Bbfp[16 + c]))
                if grp == 1:
                    for half in range(2):
                        bk, Bbk = bank()
                        for j in range(4):
                            tr(bk[0:TT, j * 128:(j + 1) * 128], big[:, half * 4 + j, :TT], identf[:], [Bbig[half * 4 + j], Bconst], [Bbk] if j == 3 else None)
                        stg, Bstg = tmpf()
                        cp('act', stg[0:TT, 0:512], bk[0:TT, :], [Bbk], [Bstg])
                        dma('pool', 'R', osgu[l][:, half * 512:(half + 1) * 512], stg[0:TT, 0:512], reads=[Bstg])
                vtm = []
                for tb_ in range(NTB_):
                    halves = []
                    for half in range(2):
                        pt, Bpt = ptbank()
                        for j in range(4):
                            cc = half * 4 + j
                            tr(pt[0:TB, j * 128:(j + 1) * 128], vnb[cc][0][:, tb_ * 128: tb_ * 128 + TB], identb[:], [vnb[cc][1], Bconst], [Bpt] if j == 3 else None)
                        vt_, Bvt_ = bfp[:, 24 + tb_ * 2 + half, :], Bbfp[24 + tb_ * 2 + half]
                        cp('act', vt_[0:TB, :], pt[0:TB, :], [Bpt], [Bvt_])
                        halves.append((vt_, Bvt_))
                    vtm.append(halves)
                wsT = smallw[:, 0:1024]
                for c in range(8):
                    bk, Bbk = bank()
                    for tb_ in range(NTB_):
                        vt_, Bvt_ = vtm[tb_][c // 4]
                        mm(bk[:, tb_ * 128: tb_ * 128 + TB], vt_[0:TB, (c % 4) * 128:(c % 4 + 1) * 128], wsT[0:TB, c * 128: c * 128 + TB], True, True,
                           [Bvt_, Bsmall], [Bbk] if tb_ == NTB_ - 1 else None)
                    sv, Bsv = tmpf()
                    if NTB_ > 1:
                        tt('dve', sv[:, :TT].rearrange("p (a b) -> p a b", b=128), bk[:, :TT].rearrange("p (a b) -> p a b", b=128),
                           bsb[:, c * 128:(c + 1) * 128].unsqueeze(1).to_broadcast([128, NTB_, 128]), ALU.add, [Bbk, Bbsb], [Bsv])
                    else:
                        tt('dve', sv[:, :TT], bk[:, :TT], bsb[:, c * 128: c * 128 + TT], ALU.add, [Bbk, Bbsb], [Bsv])
                    tt('pool', bfp[:, 8 + c, :TT], sv[:, :TT], ub[:, c, :TT], ALU.mult, [Bsv, Bub[c]], [Bbfp[8 + c]])
                checkpoint('sgu')
                for i in range(2):
                    wt, Bw = wpiece(l, PI_Q + i)
                    for o4 in range(4):
                        oc = i * 4 + o4
                        bk, Bbk = big_mm(wt, Bw, o4, hrhs(TT), Bh, TT)
                        cp('act', bfp[:, 24 + oc, :TT], bk[:, :TT], [Bbk], [Bbfp[24 + oc]])
                kvt, Bkvp = piece('kv', l, grp)
                KT = kvt[:, 0:2048].rearrange("p (j m) -> p j m", j=8)
                VV = kvt[:, 2048:4096].rearrange("p (mb n) -> p mb n", mb=2)
                for hh in range(4):
                    pTs = []
                    for mc in range(2):
                        bk, Bbk = bank()
                        for dc in range(2):
                            mm(bk[:, :TT], KT[:, 2 * hh + dc, mc * 128:(mc + 1) * 128], bfp[:, 24 + 2 * hh + dc, :TT], dc == 0, dc == 1,
                               [Bkvp, Bbfp[24 + 2 * hh + dc]], [Bbk] if dc == 1 else None)
                        p_, Bp_ = tmpb()
                        act(p_[:, :TT], bk[:, :TT], AF.Exp, [Bbk], [Bp_], scale=1.0 / 16)
                        pTs.append((p_, Bp_))
                    bk, Bbk = bank()
                    for mc in range(2):
                        mm(bk[:, :TT], onesb[:], pTs[mc][0][:, :TT], mc == 0, mc == 1, [pTs[mc][1], Bconst], [Bbk] if mc == 1 else None)
                    rd, Brd = tmpf()
                    rsqrt(rd[:, :TT], bk[:, :TT], [Bbk], [Brd], power=-1.0)
                    for dvc in range(2):
                        bk, Bbk = bank()
                        for mc in range(2):
                            mm(bk[:, :TT], VV[:, mc, hh * 256 + dvc * 128: hh * 256 + (dvc + 1) * 128], pTs[mc][0][:, :TT], mc == 0, mc == 1,
                               [Bkvp, pTs[mc][1]], [Bbk] if mc == 1 else None)
                        tt('dve', bfp[:, 16 + 2 * hh + dvc, :TT], bk[:, :TT], rd[:, :TT], ALU.mult, [Bbk, Brd], [Bbfp[16 + 2 * hh + dvc]])
                if DBG and grp == 0 and l == 0:
                    dma('pool', 'R', dbg_y, bfp[:, 0:24, :], reads=[Bbfp[i] for i in range(24)])
                checkpoint('attn')
                for b in range(3):
                    for hf in range(2):
                        wg, Bwg = wpiece(l, PI_GATE + b * 2 + hf)
                        wb_, Bwb = wpiece(l, PI_BR + b * 2 + hf)
                        for o4 in range(4):
                            oc = hf * 4 + o4
                            bk, Bbk = big_mm(wg, Bwg, o4, hrhs(TT), Bh, TT)
                            gt, Bgt = tmpb()
                            act(gt[:, :TT], bk[:, :TT], AF.Sigmoid, [Bbk], [Bgt])
                            rh = [bfp[:, b * 8 + kc, :TT] for kc in range(8)]
                            bk, Bbk = big_mm(wb_, Bwb, o4, rh, [Bbfp[b * 8 + kc] for kc in range(8)], TT)
                            if b == 0:
                                tt('dve', big[:, oc, :TT], bk[:, :TT], gt[:, :TT], ALU.mult, [Bbk, Bgt], [Bbig[oc]])
                            else:
                                tm_, Btm = tmpf()
                                tt('dve', tm_[:, :TT], bk[:, :TT], gt[:, :TT], ALU.mult, [Bbk, Bgt], [Btm])
                                if b == 1:
                                    tt('pool', big[:, oc, :TT], big[:, oc, :TT], tm_[:, :TT], ALU.add, [Bbig[oc], Btm], [Bbig[oc]])
                                else:
                                    tt('pool', bfp[:, 24 + oc, :TT], big[:, oc, :TT], tm_[:, :TT], ALU.add, [Bbig[oc], Btm], [Bbfp[24 + oc]])
                if DBG and grp == 0 and l == 0:
                    dma('pool', 'R', dbg_m, bfp[:, 24:32, :], reads=[Bbfp[24 + i] for i in range(8)])
                for hf in range(2):
                    wt, Bw = wpiece(l, PI_OUT + hf)
                    for o4 in range(4):
                        oc = hf * 4 + o4
                        rh = [bfp[:, 24 + kc, :TT] for kc in range(8)]
                        bk, Bbk = big_mm(wt, Bw, o4, rh, [Bbfp[24 + kc] for kc in range(8)], TT)
                        tt('dve', x_t[:, oc, :TT], x_t[:, oc, :TT], bk[:, :TT], ALU.add, [Bx[oc], Bbk], [Bx[oc]])
                if DBG and grp == 0 and l == 0:
                    dma('pool', 'R', dbg_x, x_t[:], reads=Bx)
                checkpoint('merge')
                rmsnorm_to_h(l, V_FFNG, TT)
                for i in range(8):
                    wt, Bw = wpiece(l, PI_UP + i)
                    for o4 in range(4):
                        oc = i * 4 + o4
                        bk, Bbk = big_mm(wt, Bw, o4, hrhs(TT), Bh, TT)
                        rl, Brl = tmpf()
                        act(rl[:, :TT], bk[:, :TT], AF.Relu, [Bbk], [Brl])
                        tt('pool', bfp[:, oc, :TT], rl[:, :TT], rl[:, :TT], ALU.mult, [Brl], [Bbfp[oc]])
                for g in range(2):
                    banks = [bank() for _ in range(4)]
                    for q in range(4):
                        wt, Bw = wpiece(l, PI_DN + g * 4 + q)
                        for o4 in range(4):
                            rh = [bfp[:, q * 8 + kc, :TT] for kc in range(8)]
                            big_mm(wt, Bw, o4, rh, [Bbfp[q * 8 + kc] for kc in range(8)], TT, out_bk=banks[o4], start=(q == 0), stop=(q == 3))
                    for o4 in range(4):
                        oc = g * 4 + o4
                        tt('dve', x_t[:, oc, :TT], x_t[:, oc, :TT], banks[o4][0][:, :TT], ALU.add, [Bx[oc], banks[o4][1]], [Bx[oc]])

            def dbg_x2_dump():
                if DBG:
                    dma('pool', 'R', dbg_x2, x_t[:], reads=Bx)

            def load_x(src, TT):
                NTB_ = max(1, TT // 128); TB = min(TT, 128)
                for tb_ in range(NTB_):
                    dma('pool', 'W', big[0:TB, 2 * tb_:2 * tb_ + 2, :].rearrange("p a n -> p (a n)"), src[tb_ * 128: tb_ * 128 + TB, :],
                        writes=[Bbig[2 * tb_], Bbig[2 * tb_ + 1]])
                for c in range(8):
                    bk, Bbk = bank()
                    for tb_ in range(NTB_):
                        tr(bk[:, tb_ * 128: tb_ * 128 + TB], big[0:TB, 2 * tb_ + c // 4, (c % 4) * 128:(c % 4 + 1) * 128], identf[0:TB, 0:TB],
                           [Bbig[2 * tb_ + c // 4], Bconst], [Bbk] if tb_ == NTB_ - 1 else None)
                    cp('act', x_t[:, c, :TT], bk[:, :TT], [Bbk], [Bx[c]])

            def store_y(dst, TT):
                NTB_ = max(1, TT // 128); TB = min(TT, 128)
                bk, Bbk = bank()
                for c in range(8):
                    sq, Bsq = tmpb()
                    act(sq[:, :TT], x_t[:, c, :TT], AF.Square, [Bx[c]], [Bsq])
                    mm(bk[:, :TT], onesb[:], sq[:, :TT], c == 0, c == 7, [Bsq, Bconst], [Bbk] if c == 7 else None)
                rs, Brs = tmpf()
                rsqrt(rs[:, :TT], bk[:, :TT], [Bbk, Bconst], [Brs], scale=1.0 / D, bias=epsc[:, 0:1])
                for c in range(8):
                    stt(big[:, c, :TT], x_t[:, c, :TT], fing_t[:, c:c + 1], rs[:, :TT], ALU.mult, ALU.mult, [Bx[c], Brs, Bfing], [Bbig[c]])
                for tb_ in range(NTB_):
                    for half in range(2):
                        bk, Bbk = bank()
                        for j in range(4):
                            cc = half * 4 + j
                            tr(bk[0:TB, j * 128:(j + 1) * 128], big[:, cc, tb_ * 128: tb_ * 128 + TB], identf[:], [Bbig[cc], Bconst], [Bbk] if j == 3 else None)
                        stg, Bstg = tmpf()
                        cp('act', stg[0:TB, 0:512], bk[0:TB, :], [Bbk], [Bstg])
                        dma('pool', 'R', dst[tb_ * 128: tb_ * 128 + TB, half * 512:(half + 1) * 512], stg[0:TB, 0:512], reads=[Bstg])

            def store_state(l, owkv, oshift):
                for c in range(8):
                    bk, Bbk = bank()
                    tr(bk[0:64, 0:128], S32[:, l, c, :], identf[:], [BS32[l][c], Bconst], [Bbk])
                    stg, Bstg = tmpf()
                    cp('act', stg[0:64, 0:128], bk[0:64, 0:128], [Bbk], [Bstg])
                    dma('pool', 'R', owkv[l, 2 * c:2 * c + 2].rearrange("h v k -> v h k"), stg[0:64, 0:128].rearrange("p (h k) -> p h k", h=2), reads=[Bstg])
                bk, Bbk = bank()
                tr(bk[0:26, 0:128], prev[:, l, :], identf[:], [Bprev[l], Bconst], [Bbk])
                stg, Bstg = tmpf()
                cp('act', stg[0:26, 0:128], bk[0:26, 0:128], [Bbk], [Bstg])
                dma('pool', 'R', oshift[l].rearrange("(j p) -> j p", p=128), stg[0:26, 0:128], reads=[Bstg])

            memset('pool', cmask[:], 1.0, [Bconst])
            memset('pool', cmask[:].rearrange("p (a b) -> p a b", b=128)[:, :, 0:1], 0.0, [Bconst])

            Bm_ = [Bbig[i] for i in range(4)]
            dma('pool', 'W', big[:, 0:4, :].rearrange("p (mb a) n -> p mb (a n)", mb=2), memp.rearrange("(mb p) n -> p mb n", p=128), writes=Bm_)
            memf = big[:, 4:8, :].rearrange("p a (b n) -> p (a b) n", b=2)
            Bmemf = [Bbig[i] for i in range(4, 8)]
            for mb in range(2):
                ssq, Bssq = tmpf()
                for a in range(2):
                    jk, Bjk = tmpf()
                    rec.op('act', lambda e, a=a, mb=mb, jk=jk, ssq=ssq: e.activation(out=jk[:, 0:512], in_=big[:, mb * 2 + a, :], func=AF.Square, accum_out=ssq[:, a:a + 1]),
                           reads=[Bbig[mb * 2 + a]], writes=[Bjk, Bssq])
                tt('dve', ssq[:, 2:3], ssq[:, 0:1], ssq[:, 1:2], ALU.add, [Bssq], [Bssq])
                act(ssq[:, 3:4], ssq[:, 2:3], AF.Sqrt, [Bssq, Bconst], [Bssq], scale=1.0 / D, bias=epsc[:, 0:1])
                recip(ssq[:, 4:5], ssq[:, 3:4], [Bssq], [Bssq])
                for a in range(2):
                    ts('dve', big[:, mb * 2 + a, :], big[:, mb * 2 + a, :], ssq[:, 4:5], None, ALU.mult, None, [Bbig[mb * 2 + a], Bssq], [Bbig[mb * 2 + a]])
            for c in range(8):
                bk, Bbk = bank()
                for mb in range(2):
                    tr(bk[:, mb * 128:(mb + 1) * 128], big[:, mb * 2 + c // 4, (c % 4) * 128:(c % 4 + 1) * 128], identf[:], Bm_ + [Bconst], [Bbk] if mb == 1 else None)
                cp('act', memf[:, c, :], bk[:, 0:256], [Bbk], Bmemf)
            for l in range(L):
                mnt = [(bfp[:, 8 + i, :], Bbfp[8 + i]) for i in range(4)]
                for c in range(8):
                    t_, B_ = mnt[c // 2]
                    ts('dve', t_[:, (c % 2) * 256:(c % 2 + 1) * 256], memf[:, c, :], vcol(l, V_MEMG, c), None, ALU.mult, None, Bmemf + [Bvec], [B_])
                mrhs = lambda c: mnt[c // 2][0][:, (c % 2) * 256:(c % 2 + 1) * 256]
                kvst = [(bfp[:, i, :], Bbfp[i]) for i in range(8)]
                for i in range(4):
                    wt, Bw = wpiece(l, PI_MKV + i)
                    for mb in range(2):
                        bk, Bbk = bank()
                        for kc in range(8):
                            mm(bk[:, :], mrhs(kc)[:, mb * 128:(mb + 1) * 128], wt[:, kc, :], kc == 0, kc == 7, [mnt[kc // 2][1], Bw], [Bbk] if kc == 7 else None)
                        stg, Bstg = tmpf()
                        cp('act', stg[:, 0:512], bk[:, :], [Bbk], [Bstg])
                        dst = (omk if i < 2 else omv)[l][mb * 128:(mb + 1) * 128, (i % 2) * 512:(i % 2 + 1) * 512]
                        dma('pool', 'R', dst, stg[:, 0:512], reads=[Bstg])
                        if i >= 2:
                            s_, Bs_ = kvst[4 + mb * 2 + (i - 2)]
                            cp('pool', s_, stg[:, 0:512], [Bstg], [Bs_])
                    if i < 2:
                        for o4 in range(4):
                            j = i * 4 + o4
                            bk, Bbk = bank()
                            for kc in range(8):
                                mm(bk[:, 0:256], wt[:, kc, o4 * 128:(o4 + 1) * 128], mrhs(kc), kc == 0, kc == 7, [mnt[kc // 2][1], Bw], [Bbk] if kc == 7 else None)
                            s_, Bs_ = kvst[j // 2]
                            cp('act', s_[:, (j % 2) * 256:(j % 2 + 1) * 256], bk[:, 0:256], [Bbk], [Bs_])
                for i in range(8):
                    Bd = Buf(); Bkv[l][0].append(Bd)
                    dma('pool', 'R', kvsc[l, 0][:, i * 512:(i + 1) * 512], kvst[i][0], reads=[kvst[i][1]], writes=[Bd])

            checkpoint('memkv')
            for l in range(L):
                for c in range(8):
                    stg, Bstg = tmpf()
                    dma('pool', 'W', stg[0:64, 0:128].rearrange("p (h k) -> p h k", h=2), swkv[l, 2 * c:2 * c + 2].rearrange("h v k -> v h k"), writes=[Bstg])
                    bk, Bbk = bank()
                    tr(bk[:, 0:64], stg[0:64, 0:128], identf[0:64, 0:64], [Bstg, Bconst], [Bbk])
                    cp('act', S32[:, l, c, :], bk[:, 0:64], [Bbk], [BS32[l][c]])
                stg, Bstg = tmpf()
                dma('pool', 'W', stg[0:26, 0:128], sshift[l].rearrange("(j p) -> j p", p=128), writes=[Bstg])
                bk, Bbk = bank()
                tr(bk[:, 0:26], stg[0:26, 0:128], identf[0:26, 0:26], [Bstg, Bconst], [Bbk])
                cp('act', prev[:, l, :], bk[:, 0:26], [Bbk], [Bprev[l]])
            load_x(xs, TS)
            for l in range(L):
                tile_layer(l, 1, TS, TS, 5)
            store_y(ys, TS)
            for l in range(L):
                store_state(l, owkv_s, oshift_s)

            checkpoint('sample')
            for l in range(L):
                for c in range(8):
                    memset('pool', S32[:, l, c, :], 0.0, [BS32[l][c]])
                memset('pool', prev[:, l, :], 0.0, [Bprev[l]])
            for t_ in range(NT):
                load_x(xp[t_ * T:(t_ + 1) * T], T)
                for l in range(L):
                    tile_layer(l, 0, T, 128, 7)
                    if l == 0 and t_ == 0:
                        dbg_x2_dump()
                store_y(yp[t_ * T:(t_ + 1) * T], T)
            for l in range(L):
                store_state(l, owkv_p, oshift_p)

        except _Stop:
            pass
        finish()
    return nc


def _vec_table(inp, L):
    def fm(a, n):
        return np.ascontiguousarray(a.reshape(L, n, 128).transpose(2, 0, 1))
    mu = inp['rwkv_mu'][:L]
    ka = inp['rwkv_k_a'][:L]
    parts = [fm(inp['norm_mix_g'][:L], 8), fm(inp['norm_ffn_g'][:L], 8), fm(inp['norm_mem_g'][:L], 8),
             fm(mu, 26), np.zeros((128, L, 26), np.float32), fm(inp['rwkv_w0'][:L], 8), fm(inp['rwkv_a0'][:L], 8),
             fm(inp['rwkv_k_k'][:L], 8), fm(ka, 8), np.zeros((128, L, 8), np.float32), fm(inp['rwkv_r_k'][:L].reshape(L, 1024), 8),
             fm(inp['sgu_ln_g'][:L], 8), fm(inp['sgu_ln_b'][:L], 8), fm(inp['rwkv_lnx_g'][:L], 8), fm(inp['rwkv_lnx_b'][:L], 8)]
    return np.ascontiguousarray(np.concatenate(parts, axis=2).astype(np.float32))


def run(inp, SEQ, L, n_cores=8, trace=False, DBG=False, STOP=None):
    nc = build(SEQ, L, DBG=DBG, STOP=STOP)
    f = lambda a: np.ascontiguousarray(a, dtype=np.float32)
    vecs = _vec_table(inp, L)
    fing = np.ascontiguousarray(inp['norm_final_g'].reshape(8, 128).T.astype(np.float32))
    shared = {
        'vecs': vecs, 'fing': fing, 'w_in': f(inp['w_in'][:L]), 'w_mkv': f(inp['w_mem_kv'][:L]),
        'w2': f(inp['rwkv_w2'][:L]), 'a2': f(inp['rwkv_a2'][:L]), 'g2': f(inp['rwkv_g2'][:L]),
        'w_s': f(inp['sgu_w_s'][:L]), 'b_s': f(inp['sgu_b_s'][:L].reshape(L, 1024)),
        'w_br': f(inp['w_branch'][:L]), 'w_out': f(inp['w_out'][:L]), 'w_up': f(inp['w_ffn_up'][:L]), 'w_dn': f(inp['w_ffn_down'][:L]),
    }
    in_maps = []
    for b in range(n_cores):
        m = dict(shared)
        m['xp'] = f(inp['x_prompt'][b, :SEQ]); m['xs'] = f(inp['x_sample'][b])
        m['ck'] = f(inp['cache_mem_k'][:L, b].reshape(L, 256, 1024)); m['cv'] = f(inp['cache_mem_v'][:L, b].reshape(L, 256, 1024))
        m['swkv'] = f(inp['state_wkv'][:L, b]); m['sshift'] = f(inp['state_shift'][:L, b, 0])
        m['memp'] = f(inp['mem_prompt'][b])
        in_maps.append(m)
    res = run_bass_kernel_spmd(nc, in_maps, core_ids=list(range(n_cores)), trace=trace)
    R = res.results
    st = lambda k: np.stack([np.asarray(r[k]) for r in R])
    y_p = st('yp'); y_s = st('ys')
    mk = st('omk').transpose(1, 0, 2, 3).reshape(L, n_cores, 256, 4, 256)
    mv = st('omv').transpose(1, 0, 2, 3).reshape(L, n_cores, 256, 4, 256)
    wkv_p = st('owkv_p').transpose(1, 0, 2, 3, 4)
    sh_p = st('oshift_p').transpose(1, 0, 2)[:, :, None, :]
    wkv_s = st('owkv_s').transpose(1, 0, 2, 3, 4)
    sh_s = st('oshift_s').transpose(1, 0, 2)[:, :, None, :]
    sgu = st('osgu').transpose(1, 0, 2, 3)
    outs = (y_p, y_s, mk, mv, wkv_p, sh_p, wkv_s, sh_s, sgu)
    return tuple(np.ascontiguousarray(o.astype(np.float32)) for o in outs), res


def kernel(**inputs):
    inp = {k: np.asarray(v) for k, v in inputs.items()}
    outs, _ = run(inp, 8192, 4, 8)
    return outs
```

```python
import contextlib
import numpy as np
import concourse.bass as bass
import concourse.mybir as mybir
from concourse.bass_utils import run_bass_kernel_spmd

F32 = mybir.dt.float32
BF16 = mybir.dt.bfloat16
AF = mybir.ActivationFunctionType
ALU = mybir.AluOpType
AX = mybir.AxisListType

ENGS = ('pe', 'act', 'dve', 'pool', 'sp')
EMBED_ENGS = ('act', 'dve', 'pool')


class Buf:
    __slots__ = ('w', 'r', 'id')
    _n = [0]

    def __init__(self):
        self.w = None
        self.r = {}
        Buf._n[0] += 1
        self.id = Buf._n[0]


class Rec:
    def __init__(self, nc):
        self.nc = nc
        self.streams = {e: [] for e in ENGS}
        self.cnt = {e: 0 for e in ENGS}
        self.dcnt = {}
        self.waited = {e: {} for e in ENGS}
        self.pending = {e: [] for e in ENGS}

    def _wait(self, eng, tok):
        if tok is None:
            return
        key, val = tok
        if self.waited[eng].get(key, 0) >= val:
            return
        self.waited[eng][key] = val
        self.streams[eng].append(('w', key, val))

    def _deps(self, eng, reads, writes, extra):
        for t in extra:
            self._wait(eng, t)
        for b in reads:
            self._wait(eng, b.w)
        for b in writes:
            self._wait(eng, b.w)
            for t in b.r.values():
                self._wait(eng, t)
            for e2 in ENGS:
                if e2 != eng and any(pb_ is b for pb_ in self.pending[e2]):
                    raise RuntimeError("write to buffer %d with pending unmarked reads on %s" % (b.id, e2))

    def op(self, eng, fn, reads=(), writes=(), mark=True, extra=(), wdeps=()):
        self._deps(eng, reads, tuple(writes) + tuple(wdeps), extra)
        if mark:
            self.cnt[eng] += 1
            tok = (eng, self.cnt[eng])
            self.streams[eng].append(('i', fn, True))
            for b in self.pending[eng]:
                b.r[eng] = tok
            self.pending[eng] = []
            for b in reads:
                b.r[eng] = tok
            for b in writes:
                b.w = tok
                b.r = {}
            return tok
        assert not writes
        self.streams[eng].append(('i', fn, False))
        self.pending[eng].extend(reads)
        return None

    def dma(self, q, sem, fn, reads=(), writes=(), extra=()):
        self._deps(q, reads, writes, extra)
        self.dcnt[sem] = self.dcnt.get(sem, 0) + 16
        tok = (sem, self.dcnt[sem])
        self.streams[q].append(('d', fn, sem))
        for b in reads:
            b.r[sem] = tok
        for b in writes:
            b.w = tok
            b.r = {}
        return tok

    def emit(self, stack):
        nc = self.nc
        sems = {}
        for e in ENGS:
            sems[e] = stack.enter_context(nc.semaphore('s_' + e))
        for d in self.dcnt:
            sems[d] = stack.enter_context(nc.semaphore('d_' + d))
        block = stack.enter_context(nc.Block())
        dec = {'pe': block.tensor, 'act': block.scalar, 'dve': block.vector, 'pool': block.gpsimd, 'sp': block.sync}
        for e in ENGS:
            stream = self.streams[e]
            own = sems[e]

            def body(engine, stream=stream, own=own, embed=(e in EMBED_ENGS)):
                n = len(stream)
                pend = None
                for k, item in enumerate(stream):
                    if item[0] == 'w':
                        if embed and k + 1 < n and stream[k + 1][0] == 'i':
                            pend = item
                        else:
                            engine.wait_ge(sems[item[1]], item[2])
                    elif item[0] == 'i':
                        ins = item[1](engine)
                        if pend is not None:
                            ins._wait_ge(sems[pend[1]], pend[2])
                            pend = None
                        if item[2]:
                            ins.then_inc(own, 1)
                    else:
                        item[1](engine).then_inc(sems[item[2]], 16)
            dec[e](body)


D = 1024
NCH = 8
RW = 3328
IN_COLS = 9472
NP_ = 49
PI_LORA, PI_RKV, PI_SGU, PI_Q, PI_GATE, PI_BR, PI_OUT, PI_UP, PI_DN, PI_MKV = 0, 1, 9, 13, 15, 21, 27, 29, 37, 45
C0 = -0.6065306597126334
V_MIXG, V_FFNG, V_MEMG, V_MU, V_OMM, V_W0, V_A0, V_KK, V_KA, V_OMKA, V_RK, V_SLG, V_SLB, V_LXG, V_LXB = (
    0, 8, 16, 24, 50, 76, 84, 92, 100, 108, 116, 124, 132, 140, 148)
VW = 156


class _Stop(Exception):
    pass


def build(SEQ, DEPTH, MEMN=256, TS=32, DBG=False, STOP=None):
    nc = bass.Bass("TRN2", target_bir_lowering=False)
    L = DEPTH
    T = 512
    NT = SEQ // T
    din = lambda name, shape, dt=F32: nc.dram_tensor(name, shape, dt, kind="ExternalInput").ap()
    dout = lambda name, shape, dt=F32: nc.dram_tensor(name, shape, dt, kind="ExternalOutput").ap()
    dscr = lambda name, shape, dt: nc.dram_tensor(name, shape, dt, kind="Internal").ap()
    xp = din("xp", [SEQ, D]); xs = din("xs", [TS, D])
    ck = din("ck", [L, MEMN, D]); cv = din("cv", [L, MEMN, D])
    swkv = din("swkv", [L, 16, 64, 64]); sshift = din("sshift", [L, RW]); memp = din("memp", [MEMN, D])
    vecs = din("vecs", [128, L, VW]); fing = din("fing", [128, 8])
    w_in = din("w_in", [L, D, IN_COLS]); w_mkv = din("w_mkv", [L, D, 2 * D])
    w2 = din("w2", [L, 64, D]); a2 = din("a2", [L, 64, D]); g2 = din("g2", [L, 128, D])
    w_s = din("w_s", [L, 8, 128, 128]); b_s = din("b_s", [L, 8 * 128])
    w_br = din("w_br", [L, 3, D, D]); w_out = din("w_out", [L, D, D])
    w_up = din("w_up", [L, D, 4 * D]); w_dn = din("w_dn", [L, 4 * D, D])
    yp = dout("yp", [SEQ, D]); ys = dout("ys", [TS, D])
    omk = dout("omk", [L, MEMN, D]); omv = dout("omv", [L, MEMN, D])
    owkv_p = dout("owkv_p", [L, 16, 64, 64]); oshift_p = dout("oshift_p", [L, RW])
    owkv_s = dout("owkv_s", [L, 16, 64, 64]); oshift_s = dout("oshift_s", [L, RW])
    osgu = dout("osgu", [L, TS, D])
    if DBG:
        dbg_y = dout("dbg_y", [128, 24, 512], BF16); dbg_x = dout("dbg_x", [128, 8, 512]); dbg_x2 = dout("dbg_x2", [128, 8, 512]); dbg_m = dout("dbg_m", [128, 8, 512], BF16)
    wsc = dscr("wsc", [L, NP_, 128, 4096], BF16)
    kvsc = dscr("kvsc", [L, 2, 128, 4096], BF16)
    smsc = dscr("smsc", [L, 128, 3072], BF16)

    rec = Rec(nc)
    st = contextlib.ExitStack()
    with st:
        def sb(name, shape, dt):
            return st.enter_context(nc.sbuf_tensor(name, shape, dt))

        def ps(name, shape, dt):
            return st.enter_context(nc.psum_tensor(name, shape, dt))

        def mm(out, lhsT, rhs, start, stop, reads, writes=None):
            rec.op('pe', lambda e: e.matmul(out, lhsT=lhsT, rhs=rhs, start=start, stop=stop),
                   reads=reads, writes=writes or (), mark=writes is not None, wdeps=() if writes is not None else psum_bufs[out.name])

        def tr(out, in_, ident, reads, writes=None):
            rec.op('pe', lambda e: e.transpose(out=out, in_=in_, identity=ident),
                   reads=reads, writes=writes or (), mark=writes is not None, wdeps=() if writes is not None else psum_bufs[out.name])

        def act(out, in_, func, reads, writes, scale=1.0, bias=None):
            if bias is None:
                rec.op('act', lambda e: e.activation(out=out, in_=in_, func=func, scale=scale), reads=reads, writes=writes)
            else:
                rec.op('act', lambda e: e.activation(out=out, in_=in_, func=func, scale=scale, bias=bias), reads=reads, writes=writes)

        def tt(eng, out, in0, in1, op, reads, writes):
            rec.op(eng, lambda e: e.tensor_tensor(out=out, in0=in0, in1=in1, op=op), reads=reads, writes=writes)

        def ts(eng, out, in0, s1, s2, op0, op1, reads, writes):
            if op1 is None and eng == 'pool' and op0 == ALU.mult:
                s2, op1 = 0.0, ALU.add
            if op1 is None:
                rec.op(eng, lambda e: e.tensor_scalar(out=out, in0=in0, scalar1=s1, scalar2=None, op0=op0), reads=reads, writes=writes)
            else:
                rec.op(eng, lambda e: e.tensor_scalar(out=out, in0=in0, scalar1=s1, scalar2=s2, op0=op0, op1=op1), reads=reads, writes=writes)

        def stt(out, in0, scalar, in1, op0, op1, reads, writes):
            rec.op('dve', lambda e: e.scalar_tensor_tensor(out=out, in0=in0, scalar=scalar, in1=in1, op0=op0, op1=op1), reads=reads, writes=writes)

        def cp(eng, out, in_, reads, writes):
            if eng == 'act':
                act(out, in_, AF.Copy, reads, writes)
            else:
                rec.op(eng, lambda e: e.tensor_copy(out=out, in_=in_), reads=reads, writes=writes)

        def rsqrt(out, in_, reads, writes, scale=1.0, bias=None, power=-0.5):
            act(out, in_, AF.Ln, reads, writes, scale=scale, bias=bias)
            act(out, out, AF.Exp, writes, writes, scale=power)

        def cpred(out, mask, data, reads, writes):
            rec.op('dve', lambda e: e.copy_predicated(out=out, mask=mask, data=data), reads=reads, writes=writes)

        def recip(out, in_, reads, writes):
            rec.op('dve', lambda e: e.reciprocal(out=out, in_=in_), reads=reads, writes=writes)

        def memset(eng, ap, val, writes):
            rec.op(eng, lambda e: e.memset(ap, val), writes=writes)

        def dma(q, sem, out, in_, reads=(), writes=(), slow=False):
            if sem == 'W':
                sem = 'b%d' % writes[0].id
            elif sem == 'R':
                sem = 'b%d' % reads[0].id
            if slow:
                return rec.dma(q, sem, lambda e: e.dma_start(out=out, in_=in_, allow_slow_non_contiguous=True), reads=reads, writes=writes)
            return rec.dma(q, sem, lambda e: e.dma_start(out=out, in_=in_), reads=reads, writes=writes)

        x_t = sb("x_t", [128, 8, T], F32); Bx = [Buf() for _ in range(8)]
        h_t = sb("h_t", [128, 8, T], BF16); Bh = [Buf() for _ in range(8)]
        NS = 3
        ws_t = [sb(f"ws{i}", [128, 4096], BF16) for i in range(NS)]; Bws = [Buf() for _ in range(NS)]
        big = sb("big", [128, 8, T], F32); Bbig = [Buf() for _ in range(8)]
        ub = sb("ub", [128, 8, T], BF16); Bub = [Buf() for _ in range(8)]
        bfp = sb("bfp", [128, 32, T], BF16); Bbfp = [Buf() for _ in range(32)]
        NTF = 11
        tf_t = sb("tf_t", [128, NTF, T + 1], F32); Btf = [Buf() for _ in range(NTF)]
        NTB = 13
        tb_t = sb("tb_t", [128, NTB, T], BF16); Btb = [Buf() for _ in range(NTB)]
        tfi = [0]; tbi = [0]

        def tmpf():
            i = tfi[0] % NTF; tfi[0] += 1
            return tf_t[:, i, :], Btf[i]

        def tmpb():
            i = tbi[0] % NTB; tbi[0] += 1
            return tb_t[:, i, :], Btb[i]

        S32 = sb("S32", [128, L, 8, 64], F32); BS32 = [[Buf() for _ in range(8)] for _ in range(L)]
        Sbd = sb("Sbd", [128, 8, 128], BF16); BSbd = [Buf() for _ in range(8)]
        prev = sb("prev", [128, L, 26], F32); Bprev = [Buf() for _ in range(L)]
        vec_t = sb("vec_t", [128, L, VW], F32); Bvec = Buf()
        fing_t = sb("fing_t", [128, 8], F32); Bfing = Buf()
        lw = sb("lw", [128, T], BF16); Blw = Buf(); sgl = sb("sgl", [128, T], BF16); Bsgl = Buf()
        smallw = sb("smallw", [128, 3072], BF16); Bsmall = Buf()
        bsb = sb("bsb", [128, 1024], F32); Bbsb = Buf()
        identf = sb("identf", [128, 128], F32); identb = sb("identb", [128, 128], BF16)
        onesb = sb("onesb", [128, 128], BF16); blkb = sb("blkb", [128, 128], BF16)
        onesf = sb("onesf", [128, 128], F32)
        msu = sb("msu", [128, 128], F32); miu = sb("miu", [128, 128], F32); msl = sb("msl", [128, 128], F32)
        cmask = sb("cmask", [128, T], F32)
        epsc = sb("epsc", [128, 4], F32)
        Bconst = Buf()
        vTp = [sb(f"vT{i}", [128, 4, 128], BF16) for i in range(2)]; kTp = [sb(f"kT{i}", [128, 4, 128], BF16) for i in range(2)]
        bTp = [sb(f"bT{i}", [128, 4, 128], BF16) for i in range(2)]
        BvTp, BkTp, BbTp = [Buf(), Buf()], [Buf(), Buf()], [Buf(), Buf()]
        bvt = [sb(f"bvt{i}", [128, T], BF16) for i in range(2)]; Bbvt = [Buf(), Buf()]
        y1t = sb("y1t", [128, T], F32); By1t = Buf()
        NQt = [sb(f"NQt{i}", [128, 2, 2, 128], BF16) for i in range(4)]; BNQ = [Buf() for _ in range(4)]
        MTt = [sb(f"MTt{i}", [128, 2, 2, 128], BF16) for i in range(4)]; BMT = [Buf() for _ in range(4)]
        Xbt = [sb(f"Xbt{i}", [128, 2, 2, 128], BF16) for i in range(4)]; BXb = [Buf() for _ in range(4)]
        m2d = sb("m2d", [128, 7, 2, 128], BF16)
        m2d_u16 = m2d[:].bitcast(mybir.dt.uint16)
        AKt = [sb(f"AKt{i}", [128, 2, 128], BF16) for i in range(4)]; BAK = [Buf() for _ in range(4)]
        RRt = [sb(f"RRt{i}", [128, 4, 128], BF16) for i in range(4)]; BRR = [Buf() for _ in range(4)]
        Zb = [sb(f"Zb{i}", [128, 128], BF16) for i in range(2)]; BZb = [Buf(), Buf()]
        Ub = [sb(f"Ub{i}", [128, 128], BF16) for i in range(2)]; BUb = [Buf(), Buf()]
        Swt = [sb(f"Swt{i}", [128, 64], F32) for i in range(2)]; BSw = [Buf(), Buf()]
        wc_p = [sb(f"wc{i}", [128, 4], F32) for i in range(2)]; Bwcp = [Buf(), Buf()]
        yTM = sb("yTM", [128, 4, 128], F32); ByTM = Buf()
        ysq = sb("ysq", [128, 4, 128], F32); Bysq = Buf()
        ynb = sb("ynb", [128, 4, 128], BF16); Bynb = Buf()
        gst = sb("gst", [128, 4, 8], F32); Bgst = Buf()

        NB = 7
        pb = [ps(f"pb{i}", [128, 512], F32) for i in range(NB)]; Bpb = [Buf() for _ in range(NB)]
        pT = ps("pT", [128, 1024], BF16); _bpt = Buf(); BpT = [_bpt, _bpt]
        psum_bufs = {f"pb{i}": [Bpb[i]] for i in range(NB)}
        psum_bufs["pT"] = [_bpt]
        bki = [0]

        def bank():
            i = bki[0] % NB; bki[0] += 1
            return pb[i], Bpb[i]

        pti = [0]

        def ptbank():
            i = pti[0] % 2; pti[0] += 1
            return pT[:, i * 512:(i + 1) * 512], BpT[i]

        gcur = ['']

        def checkpoint(name):
            if STOP == name or STOP == gcur[0] + ':' + name:
                raise _Stop()

        def finish():
            for k_, v_ in list(rec.dcnt.items()):
                rec._wait('sp', (k_, v_))
            rec.emit(st)

        try:
            memset('pool', onesf[:], 1.0, [Bconst])
            memset('pool', identf[:], 0.0, [Bconst])
            rec.op('pool', lambda e: e.affine_select(out=identf[:], in_=identf[:], pattern=[[-1, 128]], compare_op=ALU.not_equal, fill=1.0, base=0, channel_multiplier=1), reads=[Bconst], writes=[Bconst])
            cp('pool', identb[:], identf[:], [Bconst], [Bconst])
            cp('pool', onesb[:], onesf[:], [Bconst], [Bconst])
            memset('pool', blkb[:], 0.0, [Bconst])
            memset('pool', blkb[0:64, 0:64], 1.0, [Bconst])
            memset('pool', blkb[64:128, 64:128], 1.0, [Bconst])
            rec.op('pool', lambda e: e.affine_select(out=msu[:], in_=onesf[:], pattern=[[1, 128]], compare_op=ALU.is_gt, fill=0.0, base=0, channel_multiplier=-1), reads=[Bconst], writes=[Bconst])
            rec.op('pool', lambda e: e.affine_select(out=miu[:], in_=onesf[:], pattern=[[1, 128]], compare_op=ALU.is_ge, fill=0.0, base=0, channel_multiplier=-1), reads=[Bconst], writes=[Bconst])
            rec.op('pool', lambda e: e.affine_select(out=msl[:], in_=onesf[:], pattern=[[-1, 128]], compare_op=ALU.is_gt, fill=0.0, base=0, channel_multiplier=1), reads=[Bconst], writes=[Bconst])
            Ea, Eb, Ec = tf_t[:, 0, 0:128], tf_t[:, 1, 0:128], tf_t[:, 2, 0:128]
            BE_ = [Btf[0], Btf[1], Btf[2]]
            cp('pool', Ea, identf[:], [Bconst], [BE_[0]])
            Ecur, Bcur, Enxt, Bnxt = Ea, BE_[0], Eb, BE_[1]
            for j in range(7):
                s_ = 2 ** (j + 1); nb_ = 128 // s_
                if j == 6:
                    cp('pool', Enxt, onesf[:], [Bconst], [Bnxt])
                else:
                    def sel1(e, out=Enxt, s_=s_, nb_=nb_):
                        return e.affine_select(out=out.rearrange("p (a b) -> p a b", b=s_), in_=onesf[:].rearrange("p (a b) -> p a b", b=s_),
                                               pattern=[[-s_, nb_], [0, s_]], compare_op=ALU.is_ge, fill=0.0, base=0, channel_multiplier=1)

                    def sel2(e, out=Enxt, s_=s_, nb_=nb_):
                        return e.affine_select(out=out.rearrange("p (a b) -> p a b", b=s_), in_=out.rearrange("p (a b) -> p a b", b=s_),
                                               pattern=[[s_, nb_], [0, s_]], compare_op=ALU.is_ge, fill=0.0, base=s_ - 1, channel_multiplier=-1)
                    rec.op('pool', sel1, reads=[Bconst], writes=[Bnxt])
                    rec.op('pool', sel2, reads=[Bnxt], writes=[Bnxt])
                tt('pool', Ec, Enxt, Ecur, ALU.subtract, [Bnxt, Bcur], [BE_[2]])
                tt('pool', m2d[:, j, 0, :], Ec, msu[:], ALU.mult, [BE_[2], Bconst], [Bconst])
                tt('pool', m2d[:, j, 1, :], Ec, msl[:], ALU.mult, [BE_[2], Bconst], [Bconst])
                Ecur, Bcur, Enxt, Bnxt = Enxt, Bnxt, Ecur, Bcur
            memset('pool', epsc[:, 0:1], 1e-6, [Bconst])
            memset('pool', epsc[:, 1:2], 1e-5, [Bconst])
            memset('pool', epsc[:, 2:3], 64e-5, [Bconst])
            memset('pool', epsc[:, 3:4], 1e-24, [Bconst])
            dma('pool', 'W', vec_t[:], vecs, writes=[Bvec])
            dma('pool', 'W', fing_t[:], fing, writes=[Bfing])
            ts('dve', vec_t[:, :, V_OMM:V_OMM + 26], vec_t[:, :, V_MU:V_MU + 26], -1.0, 1.0, ALU.mult, ALU.add, [Bvec], [Bvec])
            ts('dve', vec_t[:, :, V_OMKA:V_OMKA + 8], vec_t[:, :, V_KA:V_KA + 8], -1.0, 1.0, ALU.mult, ALU.add, [Bvec], [Bvec])

            def vcol(l, base, c):
                return vec_t[:, l, base + c:base + c + 1]

            checkpoint('const')
            _bw = Buf()
            Bwsc = [_bw] * L
            BwscM = [_bw] * L
            Bsm = [[] for _ in range(L)]
            Bkv = [[[], []] for _ in range(L)]

            def conv(l, pi, src2d, ncols, col0=0):
                dst = wsc[l, pi].rearrange("p (kc n) -> p kc n", kc=8)[:, :, col0:col0 + ncols]
                if pi >= PI_MKV:
                    dma('pool', 'cv', dst, src2d.rearrange("(kc p) n -> p kc n", p=128), writes=[BwscM[l]])
                else:
                    dma('pool', 'cv', dst, src2d.rearrange("(kc p) n -> p kc n", p=128), writes=[Bwsc[l]])

            for l in range(L):
                for i in range(4):
                    conv(l, PI_MKV + i, w_mkv[l][:, i * 512:(i + 1) * 512], 512)
            for l in range(L):
                conv(l, PI_LORA, w_in[l][:, 3072:3328], 256)
                for c in range(8):
                    for j in range(3):
                        conv(l, PI_RKV + c, w_in[l][:, j * 1024 + c * 128: j * 1024 + (c + 1) * 128], 128, col0=j * 128)
                for i in range(4):
                    conv(l, PI_SGU + i, w_in[l][:, 3328 + i * 512: 3328 + (i + 1) * 512], 512)
                for i in range(2):
                    conv(l, PI_Q + i, w_in[l][:, 5376 + i * 512: 5376 + (i + 1) * 512], 512)
                for i in range(6):
                    conv(l, PI_GATE + i, w_in[l][:, 6400 + i * 512: 6400 + (i + 1) * 512], 512)
                for b in range(3):
                    for hf in range(2):
                        conv(l, PI_BR + b * 2 + hf, w_br[l, b][:, hf * 512:(hf + 1) * 512], 512)
                for hf in range(2):
                    conv(l, PI_OUT + hf, w_out[l][:, hf * 512:(hf + 1) * 512], 512)
                for i in range(8):
                    conv(l, PI_UP + i, w_up[l][:, i * 512:(i + 1) * 512], 512)
                for g in range(2):
                    for q in range(4):
                        conv(l, PI_DN + g * 4 + q, w_dn[l][q * 1024:(q + 1) * 1024, g * 512:(g + 1) * 512], 512)
                dma('pool', 'cv', smsc[l][0:64, 1024:2048], w2[l], writes=[Bwsc[l]])
                dma('pool', 'cv', smsc[l][64:128, 1024:2048], a2[l], writes=[Bwsc[l]])
                dma('pool', 'cv', smsc[l][:, 2048:3072], g2[l], writes=[Bwsc[l]])
                dma('pool', 'cv', kvsc[l, 1][:, 2048:4096].rearrange("p (mb n) -> p mb n", mb=2),
                    cv[l].rearrange("(mb p) n -> p mb n", p=128), writes=[Bwsc[l]])
            checkpoint('conv')
            for l in range(L):
                wsf, Bw_ = big[:, 0:2, :], [Bbig[0], Bbig[1]]
                dma('pool', 'W', big[:, 0:2, :].rearrange("p a (g j) -> p (a g) j", j=128), w_s[l].rearrange("g i j -> i g j"), writes=Bw_)
                for g in range(8):
                    if g % 4 == 0:
                        stg, Bstg = tmpb()
                    bk, Bbk = bank()
                    tr(bk[:, 0:128], big[:, g // 4, (g % 4) * 128:(g % 4 + 1) * 128], identf[:], reads=Bw_ + [Bconst], writes=[Bbk])
                    tt('dve', stg[:, (g % 4) * 128:(g % 4 + 1) * 128], bk[:, 0:128], miu[:], ALU.mult, [Bbk, Bconst], [Bstg])
                    if g % 4 == 3:
                        Bd = Buf(); Bsm[l].append(Bd)
                        dma('pool', 'R', smsc[l][:, (g // 4) * 512:(g // 4 + 1) * 512], stg, reads=[Bstg], writes=[Bd])
            for l in range(L):
                Bk_ = [Bbig[i] for i in range(4)]
                dma('pool', 'W', big[:, 0:4, :].rearrange("p (mb a) n -> p mb (a n)", mb=2), ck[l].rearrange("(mb p) n -> p mb n", p=128), writes=Bk_)
                for j in range(8):
                    if j % 2 == 0:
                        stg, Bstg = tmpb()
                    bk, Bbk = bank()
                    for mb in range(2):
                        tr(bk[:, mb * 128:(mb + 1) * 128], big[:, mb * 2 + j // 4, (j % 4) * 128:(j % 4 + 1) * 128], identf[:],
                           reads=Bk_ + [Bconst], writes=[Bbk] if mb == 1 else None)
                    cp('act', stg[:, (j % 2) * 256:(j % 2 + 1) * 256], bk[:, 0:256], [Bbk], [Bstg])
                    if j % 2 == 1:
                        Bd = Buf(); Bkv[l][1].append(Bd)
                        dma('pool', 'R', kvsc[l, 1][:, (j - 1) * 256:(j + 1) * 256], stg, reads=[Bstg], writes=[Bd])

            checkpoint('prep')
            uses = []
            state = {'k': 0, 'loaded': 0}

            def plan_tile_layer(l, grp):
                seq = [('w', l, PI_LORA)] + [('w', l, PI_RKV + c) for c in range(8)] + [('w', l, PI_SGU + i) for i in range(4)]
                seq += [('w', l, PI_Q), ('w', l, PI_Q + 1), ('kv', l, grp)]
                for b in range(3):
                    for hf in range(2):
                        seq += [('w', l, PI_GATE + b * 2 + hf), ('w', l, PI_BR + b * 2 + hf)]
                seq += [('w', l, PI_OUT), ('w', l, PI_OUT + 1)] + [('w', l, PI_UP + i) for i in range(8)]
                seq += [('w', l, PI_DN + i) for i in range(8)]
                return seq

            for l in range(L):
                uses += [('w', l, PI_MKV + i) for i in range(4)]
            for l in range(L):
                uses += plan_tile_layer(l, 1)
            for t_ in range(NT):
                for l in range(L):
                    uses += plan_tile_layer(l, 0)
            def _issue_load(k):
                kind, l, idx = uses[k]
                slot = k % NS
                if kind == 'w':
                    ncols = 256 if idx == PI_LORA else (384 if PI_RKV <= idx < PI_RKV + 8 else 512)
                    if ncols == 512:
                        dma('sp', f'w{slot}', ws_t[slot][:], wsc[l, idx], reads=[BwscM[l] if idx >= PI_MKV else Bwsc[l]], writes=[Bws[slot]])
                    else:
                        dma('sp', f'w{slot}', ws_t[slot][:].rearrange("p (kc n) -> p kc n", kc=8)[:, :, 0:ncols],
                            wsc[l, idx].rearrange("p (kc n) -> p kc n", kc=8)[:, :, 0:ncols], reads=[Bwsc[l]], writes=[Bws[slot]])
                else:
                    dma('sp', f'w{slot}', ws_t[slot][:], kvsc[l, idx], reads=[Bwsc[l]] + Bkv[l][idx], writes=[Bws[slot]])

            def piece(kind, l, idx):
                k = state['k']
                assert uses[k] == (kind, l, idx), (uses[k], kind, l, idx)
                while state['loaded'] < min(len(uses), k + NS - 1):
                    _issue_load(state['loaded']); state['loaded'] += 1
                state['k'] += 1
                slot = k % NS
                return ws_t[slot], Bws[slot]

            def wpiece(l, idx):
                t_, B_ = piece('w', l, idx)
                return t_[:].rearrange("p (kc n) -> p kc n", kc=8), B_

            def rmsnorm_to_h(l, gbase, TT):
                bk, Bbk = bank()
                for c in range(8):
                    sq, Bsq = tmpb()
                    act(sq[:, :TT], x_t[:, c, :TT], AF.Square, [Bx[c]], [Bsq])
                    mm(bk[:, :TT], onesb[:], sq[:, :TT], c == 0, c == 7, [Bsq, Bconst], [Bbk] if c == 7 else None)
                rs, Brs = tmpf()
                rsqrt(rs[:, :TT], bk[:, :TT], [Bbk, Bconst], [Brs], scale=1.0 / D, bias=epsc[:, 0:1])
                for c in range(8):
                    gcol = vcol(l, gbase, c) if l is not None else fing_t[:, c:c + 1]
                    yield_out = h_t[:, c, :TT]
                    stt(yield_out, x_t[:, c, :TT], gcol, rs[:, :TT], ALU.mult, ALU.mult, [Bx[c], Brs, Bvec], [Bh[c]])

            def big_mm(wt, Bw, oc4, rhs_list, Brhs, TT, out_bk=None, start=True, stop=True, kc0=0):
                if out_bk is None:
                    bk, Bbk = bank()
                else:
                    bk, Bbk = out_bk
                n = len(rhs_list)
                for kc in range(n):
                    last = (kc == n - 1)
                    mm(bk[:, :TT], wt[:, kc, oc4 * 128:(oc4 + 1) * 128], rhs_list[kc], start and kc == 0, stop and last,
                       [Bw, Brhs[kc]], [Bbk] if last else None)
                return bk, Bbk

            def hrhs(TT):
                return [h_t[:, kc, :TT] for kc in range(8)]

            def rwkv_gen(l, c, TT, C):
                NC_ = TT // C
                wt, Bw = wpiece(l, PI_RKV + c)
                w2a2 = smallw[:, 1024:2048]; g2t = smallw[:, 2048:3072]
                F = lambda i: (tf_t[:, i, :], Btf[i])
                Bq = lambda i: (tb_t[:, i, :], Btb[i])
                pp = c % 2
                CB = [F(0), F(0)]
                (r_, Br), (k_, Bk), (v_, Bv) = F(1), F(2), F(3)
                (sig, Bsig), (a_, Ba), (kk, Bkk), (Lc, BL), (E, BE), (E2, BE2), (t1, Bt1) = F(4), F(5), F(6), F(7), F(8), F(9), F(10)
                (g_, Bg), (rt, Brt), (kt, Bkt), (bt, Bbt), (at, Bat) = [Bq(pp * 5 + i) for i in range(5)]
                (sq, Bsq), (pr, Bpr), (vb, Bvb) = Bq(10), Bq(11), Bq(12)
                vT, kT, bT, BvT, BkT, BbT = vTp[pp], kTp[pp], bTp[pp], BvTp[pp], BkTp[pp], BbTp[pp]
                wc_t, Bwc = wc_p[pp], Bwcp[pp]
                mixed = [(r_, Br), (k_, Bk), (v_, Bv)]
                for j in range(3):
                    col = j * 8 + c
                    bk, Bbk = big_mm(wt, Bw, j, hrhs(TT), Bh, TT)
                    cb, Bcb = CB[j % 2]
                    mx, Bmx = mixed[j]
                    cp('act', cb[:, 1:TT + 1], bk[:, :TT], [Bbk], [Bcb])
                    cp('pool', cb[:, 0:1], prev[:, l, col:col + 1], [Bprev[l]], [Bcb])
                    ts('dve', t1[:, :TT], cb[:, 0:TT], vcol(l, V_MU, col), None, ALU.mult, None, [Bcb, Bvec], [Bt1])
                    stt(mx[:, :TT], cb[:, 1:TT + 1], vcol(l, V_OMM, col), t1[:, :TT], ALU.mult, ALU.add, [Bcb, Bt1, Bvec], [Bmx])
                    cp('pool', prev[:, l, col:col + 1], cb[:, TT:TT + 1], [Bcb], [Bprev[l]])
                    yield 's'
                bk, Bbk = bank()
                mm(bk[:, :TT], w2a2[0:64, c * 128:(c + 1) * 128], lw[0:64, :TT], True, True, [Bsmall, Blw], [Bbk])
                act(sig[:, :TT], bk[:, :TT], AF.Sigmoid, [Bbk, Bvec], [Bsig], bias=vcol(l, V_W0, c))
                yield 's'
                bk, Bbk = bank()
                mm(bk[:, :TT], w2a2[64:128, c * 128:(c + 1) * 128], lw[64:128, :TT], True, True, [Bsmall, Blw], [Bbk])
                act(a_[:, :TT], bk[:, :TT], AF.Sigmoid, [Bbk, Bvec], [Ba], bias=vcol(l, V_A0, c))
                yield 's'
                bk, Bbk = bank()
                mm(bk[:, :TT], g2t[:, c * 128:(c + 1) * 128], sgl[:, :TT], True, True, [Bsmall, Bsgl], [Bbk])
                cp('act', g_[:, :TT], bk[:, :TT], [Bbk], [Bg])
                yield 's'
                ts('pool', kk[:, :TT], k_[:, :TT], vcol(l, V_KK, c), None, ALU.mult, None, [Bk, Bvec], [Bkk])
                act(sq[:, :TT], kk[:, :TT], AF.Square, [Bkk], [Bsq])
                bk, Bbk = bank()
                mm(bk[:, :TT], blkb[:], sq[:, :TT], True, True, [Bconst, Bsq], [Bbk])
                rn, Brn = E2, BE2
                rsqrt(rn[:, :TT], bk[:, :TT], [Bbk, Bconst], [Brn], bias=epsc[:, 3:4])
                yield 's'
                yield 's'
                tt('pool', kk[:, :TT], kk[:, :TT], rn[:, :TT], ALU.mult, [Bkk, Brn], [Bkk])
                ts('dve', t1[:, :TT], a_[:, :TT], vcol(l, V_KA, c), vcol(l, V_OMKA, c), ALU.mult, ALU.add, [Ba, Bvec], [Bt1])
                tt('pool', k_[:, :TT], k_[:, :TT], t1[:, :TT], ALU.mult, [Bk, Bt1], [Bk])
                yield 's'
                scan_mask = cmask[:, :TT] if C == 128 else onesf[:, :TT]
                rec.op('dve', lambda e: e.tensor_tensor_scan(out=Lc[:, :TT], data0=scan_mask, data1=sig[:, :TT],
                                                             initial=0.0, op0=ALU.mult, op1=ALU.add), reads=[Bsig, Bconst], writes=[BL])
                act(E[:, :TT], Lc[:, :TT], AF.Exp, [BL], [BE], scale=C0)
                tt('dve', rt[:, :TT], r_[:, :TT], E[:, :TT], ALU.mult, [Br, BE], [Brt])
                cp('pool', wc_t[:, 0:NC_], E[:, :TT].rearrange("p (a b) -> p a b", b=C)[:, :, C - 1], [BE], [Bwc])
                yield 's'
                act(E2[:, :TT], Lc[:, :TT], AF.Exp, [BL], [BE2], scale=-C0)
                tt('dve', kt[:, :TT], k_[:, :TT], E2[:, :TT], ALU.mult, [Bk, BE2], [Bkt])
                tt('pool', t1[:, :TT], kk[:, :TT], a_[:, :TT], ALU.mult, [Bkk, Ba], [Bt1])
                tt('dve', bt[:, :TT], t1[:, :TT], E2[:, :TT], ALU.mult, [Bt1, BE2], [Bbt])
                yield 's'
                tt('pool', Lc[:, :TT], Lc[:, :TT], sig[:, :TT], ALU.subtract, [BL, Bsig], [BL])
                act(E[:, :TT], Lc[:, :TT], AF.Exp, [BL], [BE], scale=C0)
                stt(at[:, :TT], kk[:, :TT], -1.0, E[:, :TT], ALU.mult, ALU.mult, [Bkk, BE], [Bat])
                yield 's'
                stt(pr[:, :TT], r_[:, :TT], vcol(l, V_RK, c), k_[:, :TT], ALU.mult, ALU.mult, [Br, Bk, Bvec], [Bpr])
                bk, Bbk = bank()
                mm(bk[:, :TT], blkb[:], pr[:, :TT], True, True, [Bconst, Bpr], [Bbk])
                bv, Bbv = bvt[pp], Bbvt[pp]
                tt('dve', bv[:, :TT], bk[:, :TT], v_[:, :TT], ALU.mult, [Bbk, Bv], [Bbv])
                cp('pool', vb[:, :TT], v_[:, :TT], [Bv], [Bvb])
                yield 's'
                checkpoint('pelem')
                for (src, Bsrc, dst, Bdst) in ((vb, Bvb, vT, BvT), (kt, Bkt, kT, BkT), (bt, Bbt, bT, BbT)):
                    pt, Bpt = ptbank()
                    for ci in range(NC_):
                        tr(pt[0:C, ci * 128:(ci + 1) * 128], src[:, ci * C:(ci + 1) * C], identb[:], [Bsrc, Bconst], [Bpt] if ci == NC_ - 1 else None)
                    cp('act', dst[0:C, 0:NC_, :], pt[0:C, 0:NC_ * 128].rearrange("p (a b) -> p a b", b=128), [Bpt], [Bdst])
                    yield 's'
                checkpoint('ptm')
                memset('pool', Sbd[:, c, :], 0.0, [BSbd[c]])
                for hd in range(2):
                    cp('pool', Sbd[hd * 64:(hd + 1) * 64, c, hd * 64:(hd + 1) * 64], S32[hd * 64:(hd + 1) * 64, l, c, :], [BS32[l][c]], [BSbd[c]])
                checkpoint('psbd')
                yield 'frontdone'
                hr = lambda hd: slice(hd * 64, (hd + 1) * 64)
                NL = C.bit_length() - 1
                for ci in range(NC_):
                    cs = slice(ci * C, (ci + 1) * C)
                    NQ, MT, AK, RR = NQt[ci], MTt[ci], AKt[ci], RRt[ci]
                    for hd in range(2):
                        bkX, BX = bank()
                        X3 = bkX[0:C, :].rearrange("p (a b) -> p a b", b=128)
                        mm(X3[:, 0, 0:C], bt[hr(hd), cs], at[hr(hd), cs], True, True, [Bbt, Bat])
                        mm(X3[:, 1, 0:C], kt[hr(hd), cs], at[hr(hd), cs], True, True, [Bkt, Bat])
                        mm(X3[:, 2, 0:C], bt[hr(hd), cs], rt[hr(hd), cs], True, True, [Bbt, Brt])
                        mm(X3[:, 3, 0:C], kt[hr(hd), cs], rt[hr(hd), cs], True, True, [Bkt, Brt], [BX])
                        bkY, BY = bank()
                        mm(bkY[0:C, 0:C], at[hr(hd), cs], bt[hr(hd), cs], True, True, [Bbt, Bat], [BY])
                        tt('dve', NQ[0:C, 0, hd, 0:C], X3[:, 0, 0:C], msu[0:C, 0:C], ALU.mult, [BX, Bconst], [BNQ[ci]])
                        tt('dve', AK[0:C, hd, 0:C], X3[:, 1, 0:C], msu[0:C, 0:C], ALU.mult, [BX, Bconst], [BAK[ci]])
                        tt('dve', RR[0:C, hd:4:2, 0:C], X3[:, 2:4, 0:C], miu[0:C, 0:C].unsqueeze(1).to_broadcast([C, 2, C]), ALU.mult, [BX, Bconst], [BRR[ci]])
                        tt('dve', NQ[0:C, 1, hd, 0:C], bkY[0:C, 0:C], msl[0:C, 0:C], ALU.mult, [BY, Bconst], [BNQ[ci]])
                    for f_ in range(2):
                        tt('dve', MT[0:C, f_, :, 0:C], NQ[0:C, f_, :, 0:C], m2d[0:C, 0, f_, 0:C].unsqueeze(1).to_broadcast([C, 2, C]), ALU.mult,
                           [BNQ[ci], Bconst], [BMT[ci]])
                    tt('dve', MT[0:C].rearrange("p f h q -> p (f h) q")[:, :, 0:C], MT[0:C].rearrange("p f h q -> p (f h) q")[:, :, 0:C],
                       identb[0:C, 0:C].unsqueeze(1).to_broadcast([C, 4, C]), ALU.add, [BMT[ci], Bconst], [BMT[ci]])
                    yield 's'
                checkpoint('patype')
                for j in range(1, NL):
                    last = j == NL - 1
                    nf = 1 if last else 2
                    xs_ = []
                    for ci in range(NC_):
                        NQ, MT = NQt[ci], MTt[ci]
                        bkX, BX = bank()
                        Xv = bkX[0:C, :].rearrange("p (f h q) -> p f h q", f=2, h=2)
                        for hd in range(2):
                            mm(Xv[:, 0, hd, 0:C], NQ[0:C, 1, hd, 0:C], MT[0:C, 0, hd, 0:C], True, True, [BNQ[ci], BMT[ci]], [BX] if hd == 1 else None)
                        xs_.append((Xv, BX))
                    ys_ = []
                    for ci in range(NC_):
                        MT, XB_ = MTt[ci], Xbt[ci]
                        Xv, BX = xs_[ci]
                        cp('act', XB_[0:C, 0, :, 0:C], Xv[:, 0, :, 0:C], [BX], [BXb[ci]])
                        bkY, BY = bank()
                        Yv = bkY[0:C, :].rearrange("p (f h q) -> p f h q", f=2, h=2)
                        for hd in range(2):
                            mm(Yv[:, 0, hd, 0:C], MT[0:C, 1, hd, 0:C], XB_[0:C, 0, hd, 0:C], True, True, [BMT[ci], BXb[ci]], [BY] if (last and hd == 1) else None)
                            if not last:
                                mm(Yv[:, 1, hd, 0:C], XB_[0:C, 0, hd, 0:C], MT[0:C, 1, hd, 0:C], True, True, [BMT[ci], BXb[ci]], [BY] if hd == 1 else None)
                        ys_.append((Yv, BY))
                    for ci in range(NC_):
                        MT = MTt[ci]
                        Yv, BY = ys_[ci]
                        cpred(MT[0:C, 0:nf, :, 0:C], m2d_u16[0:C, j, 0:nf, 0:C].unsqueeze(2).to_broadcast([C, nf, 2, C]), Yv[:, 0:nf, :, 0:C],
                              [BY, Bconst, BMT[ci]], [BMT[ci]])
                    yield 's'
                yield 'p2done'
                for ci in range(NC_):
                    par = ci % 2
                    cs = slice(ci * C, (ci + 1) * C)
                    AK, RR, MT = AKt[ci], RRt[ci], MTt[ci]
                    checkpoint('pdbl')
                    Rw, BRw = Swt[par], BSw[par]
                    for hd in range(2):
                        tt('dve', Rw[hr(hd), :], S32[hr(hd), l, c, :], Sbd[hr(hd), c, hr(hd)], ALU.subtract, [BS32[l][c], BSbd[c]], [BRw])
                    ts('dve', Rw[:, :], Rw[:, :], wc_t[:, ci:ci + 1], None, ALU.mult, None, [BRw, Bwc], [BRw])
                    bkW, BW = bank()
                    mm(bkW[0:C, 0:128], at[:, cs], Sbd[:, c, :], True, False, [Bat, BSbd[c]])
                    for hd in range(2):
                        mm(bkW[0:C, hr(hd)], AK[0:C, hd, 0:C], vT[0:C, ci, hr(hd)], False, hd == 1, [BAK[ci], BvT], [BW] if hd == 1 else None)
                    cp('act', Zb[par][0:C, :], bkW[0:C, 0:128], [BW], [BZb[par]])
                    yield 's'
                    bkU, BU = bank()
                    for hd in range(2):
                        mm(bkU[0:C, hr(hd)], MT[0:C, 0, hd, 0:C], Zb[par][0:C, hr(hd)], True, True, [BMT[ci], BZb[par]], [BU] if hd == 1 else None)
                    cp('act', Ub[par][0:C, :], bkU[0:C, 0:128], [BU], [BUb[par]])
                    yield 's'
                    bkS, BS = bank()
                    mm(bkS[:, 0:128], bT[0:C, ci, :], Ub[par][0:C, :], True, False, [BbT, BUb[par]])
                    mm(bkS[:, 0:128], kT[0:C, ci, :], vT[0:C, ci, :], False, False, [BkT, BvT])
                    mm(bkS[:, 0:128], identb[:, :], Sbd[:, c, :], False, True, [Bconst, BSbd[c]], [BS])
                    bkY, BY = bank()
                    mm(bkY[0:C, 0:128], rt[:, cs], Sbd[:, c, :], True, False, [Brt, BSbd[c]])
                    for hd in range(2):
                        mm(bkY[0:C, hr(hd)], RR[0:C, hd, 0:C], Ub[par][0:C, hr(hd)], False, False, [BRR[ci], BUb[par]])
                        mm(bkY[0:C, hr(hd)], RR[0:C, 2 + hd, 0:C], vT[0:C, ci, hr(hd)], False, hd == 1, [BRR[ci], BvT], [BY] if hd == 1 else None)
                    if ci < NC_ - 1:
                        for hd in range(2):
                            act(Sbd[hr(hd), c, hr(hd)], bkS[hr(hd), hr(hd)], AF.Identity, [BS, Bwc], [BSbd[c]], scale=wc_t[hr(hd), ci:ci + 1])
                    for hd in range(2):
                        stt(S32[hr(hd), l, c, :], bkS[hr(hd), hr(hd)], wc_t[hr(hd), ci:ci + 1], Rw[hr(hd), :], ALU.mult, ALU.add,
                            [BS, Bwc, BRw, BSbd[c]], [BS32[l][c]])
                    cp('act', yTM[0:C, ci, :], bkY[0:C, 0:128], [BY], [ByTM])
                    yield 's'
                checkpoint('pseq')
                yield 'p3done'
                NG = NC_ * 2
                y3 = yTM[0:C, 0:NC_, :].rearrange("p a (h v) -> p (a h) v", v=64)
                rec.op('dve', lambda e: e.tensor_reduce(out=gst[0:C, 0, 0:NG], in_=y3, axis=AX.X, op=ALU.add), reads=[ByTM], writes=[Bgst])
                tt('pool', ysq[0:C, 0:NC_, :], yTM[0:C, 0:NC_, :], yTM[0:C, 0:NC_, :], ALU.mult, [ByTM], [Bysq])
                s3 = ysq[0:C, 0:NC_, :].rearrange("p a (h v) -> p (a h) v", v=64)
                rec.op('dve', lambda e: e.tensor_reduce(out=gst[0:C, 1, 0:NG], in_=s3, axis=AX.X, op=ALU.add), reads=[Bysq], writes=[Bgst])
                ts('dve', gst[0:C, 2, 0:NG], gst[0:C, 0, 0:NG], 1.0 / 64, None, ALU.mult, None, [Bgst], [Bgst])
                tt('dve', gst[0:C, 3, 0:NG], gst[0:C, 2, 0:NG], gst[0:C, 2, 0:NG], ALU.mult, [Bgst], [Bgst])
                stt(gst[0:C, 3, 0:NG], gst[0:C, 1, 0:NG], 1.0 / 64, gst[0:C, 3, 0:NG], ALU.mult, ALU.subtract, [Bgst], [Bgst])
                rsqrt(gst[0:C, 3, 0:NG], gst[0:C, 3, 0:NG], [Bgst, Bconst], [Bgst], bias=epsc[0:C, 2:3])
                yield 's'
                tt('dve', ysq[0:C, 0:NC_, :].rearrange("p a (h v) -> p (a h) v", v=64), y3,
                   gst[0:C, 2, 0:NG].unsqueeze(2).to_broadcast([C, NG, 64]), ALU.subtract, [ByTM, Bgst], [Bysq])
                tt('dve', ynb[0:C, 0:NC_, :].rearrange("p a (h v) -> p (a h) v", v=64), s3,
                   gst[0:C, 3, 0:NG].unsqueeze(2).to_broadcast([C, NG, 64]), ALU.mult, [Bysq, Bgst], [Bynb])
                pt, Bpt = ptbank()
                for ci in range(NC_):
                    tr(pt[:, ci * C:(ci + 1) * C], ynb[0:C, ci, :], identb[0:C, 0:C], [Bynb, Bconst], [Bpt] if ci == NC_ - 1 else None)
                y1, By1 = y1t, By1t
                act(y1[:, :TT], pt[:, :TT], AF.Identity, [Bpt, Bvec], [By1], scale=vcol(l, V_LXG, c), bias=vcol(l, V_LXB, c))
                tt('pool', y1[:, :TT], y1[:, :TT], bv[:, :TT], ALU.add, [By1, Bbv], [By1])
                tt('dve', bfp[:, c, :TT], y1[:, :TT], g_[:, :TT], ALU.mult, [By1, Bg], [Bbfp[c]])

            def tile_layer(l, grp, TT, C, NSTEP):
                gcur[0] = 'sp'[1 - grp] if False else ('p' if grp == 0 else 's')
                NTB_ = max(1, TT // 128)
                TB = min(TT, 128)
                dma('pool', 'W', smallw[:], smsc[l], reads=[Bwsc[l]] + Bsm[l], writes=[Bsmall])
                dma('pool', 'W', bsb[:], b_s[l].unsqueeze(0).partition_broadcast(128).rearrange("p o n -> p (o n)"), writes=[Bbsb])
                rmsnorm_to_h(l, V_MIXG, TT)
                wt, Bw = wpiece(l, PI_LORA)
                for j in range(2):
                    col = 24 + j
                    bk, Bbk = big_mm(wt, Bw, j, hrhs(TT), Bh, TT)
                    cb, Bcb = tmpf()
                    cp('act', cb[:, 1:TT + 1], bk[:, :TT], [Bbk], [Bcb])
                    cp('pool', cb[:, 0:1], prev[:, l, col:col + 1], [Bprev[l]], [Bcb])
                    t1, Bt1 = tmpf()
                    ts('dve', t1[:, :TT], cb[:, 0:TT], vcol(l, V_MU, col), None, ALU.mult, None, [Bcb, Bvec], [Bt1])
                    mx, Bmx = tmpf()
                    stt(mx[:, :TT], cb[:, 1:TT + 1], vcol(l, V_OMM, col), t1[:, :TT], ALU.mult, ALU.add, [Bcb, Bt1, Bvec], [Bmx])
                    cp('pool', prev[:, l, col:col + 1], cb[:, TT:TT + 1], [Bcb], [Bprev[l]])
                    if j == 0:
                        act(lw[0:64, :TT], mx[0:64, :TT], AF.Tanh, [Bmx], [Blw])
                        cp('act', lw[64:128, :TT], mx[64:128, :TT], [Bmx], [Blw])
                    else:
                        act(sgl[:, :TT], mx[:, :TT], AF.Sigmoid, [Bmx], [Bsgl])
                checkpoint('lora')
                gens, stt_ = {}, {}
                n_started = 0
                while n_started < 8 or any(v != 'done' for v in stt_.values()):
                    if n_started < 8:
                        i_ = n_started
                        ok = (i_ == 0 or stt_[i_ - 1] != 'front') and (i_ < 2 or stt_[i_ - 2] == 'done')
                        if ok:
                            gens[i_] = rwkv_gen(l, i_, TT, C); stt_[i_] = 'front'; n_started += 1
                    in_p3 = any(v == 'chunk3' for v in stt_.values())
                    in_p12 = any(v == 'chunk12' for v in stt_.values())
                    progressed = False
                    for i_ in sorted(gens):
                        st_ = stt_[i_]
                        if st_ == 'done':
                            continue
                        if st_ == 'wait':
                            if i_ == 0 or stt_[i_ - 1] in ('final', 'done'):
                                stt_[i_] = st_ = 'chunk12'
                            else:
                                continue
                        reps = 1
                        for _r in range(reps):
                            try:
                                tok = next(gens[i_])
                            except StopIteration:
                                stt_[i_] = 'done'; progressed = True
                                break
                            progressed = True
                            if tok == 'frontdone':
                                stt_[i_] = 'wait'; break
                            elif tok == 'p2done':
                                stt_[i_] = 'chunk3'; break
                            elif tok == 'p3done':
                                stt_[i_] = 'final'; break
                    assert progressed or n_started < 8
                checkpoint('rwkv')
                for i in range(4):
                    wt, Bw = wpiece(l, PI_SGU + i)
                    for o4 in range(4):
                        oc = i * 4 + o4
                        bk, Bbk = big_mm(wt, Bw, o4, hrhs(TT), Bh, TT)
                        if oc < 8:
                            act(ub[:, oc, :TT], bk[:, :TT], AF.Gelu_apprx_tanh, [Bbk], [Bub[oc]])
                        else:
                            act(big[:, oc - 8, :TT], bk[:, :TT], AF.Gelu_apprx_tanh, [Bbk], [Bbig[oc - 8]])
                bk1, Bb1 = bank(); bk2, Bb2 = bank()
                for c in range(8):
                    vb, Bvb = tmpb(); vs, Bvs = tmpb()
                    cp('pool', vb[:, :TT], big[:, c, :TT], [Bbig[c]], [Bvb])
                    act(vs[:, :TT], big[:, c, :TT], AF.Square, [Bbig[c]], [Bvs])
                    mm(bk1[:, :TT], onesb[:], vb[:, :TT], c == 0, c == 7, [Bvb, Bconst], [Bb1])
                    mm(bk2[:, :TT], onesb[:], vs[:, :TT], c == 0, c == 7, [Bvs, Bconst], [Bb2])
                mean, Bmean = tmpf(); rstd, Brstd = tmpf()
                act(mean[:, :TT], bk1[:, :TT], AF.Copy, [Bb1], [Bmean], scale=1.0 / D)
                tt('pool', rstd[:, :TT], mean[:, :TT], mean[:, :TT], ALU.mult, [Bmean], [Brstd])
                stt(rstd[:, :TT], bk2[:, :TT], 1.0 / D, rstd[:, :TT], ALU.mult, ALU.subtract, [Bb2, Brstd], [Brstd])
                rsqrt(rstd[:, :TT], rstd[:, :TT], [Brstd, Bconst], [Brstd], bias=epsc[:, 1:2])
                vnb = []
                for c in range(8):
                    vc = big[:, c, :TT]
                    tt('dve', vc, vc, mean[:, :TT], ALU.subtract, [Bbig[c], Bmean], [Bbig[c]])
                    tt('pool', vc, vc, rstd[:, :TT], ALU.mult, [Bbig[c], Brstd], [Bbig[c]])
                    ts('dve', vc, vc, vcol(l, V_SLG, c), vcol(l, V_SLB, c), ALU.mult, ALU.add, [Bbig[c], Bvec], [Bbig[c]])
                    cp('pool', bfp[:, 16 + c, :TT], vc, [Bbig[c]], [Bbfp[16 + c]])
                    vnb.append((bfp[:, 16 + c, :], Bbfp[16 + c]))
                if grp == 1:
                    for half in range(2):
                        bk, Bbk = bank()
                        for j in range(4):
                            tr(bk[0:TT, j * 128:(j + 1) * 128], big[:, half * 4 + j, :TT], identf[:], [Bbig[half * 4 + j], Bconst], [Bbk] if j == 3 else None)
                        stg, Bstg = tmpf()
                        cp('act', stg[0:TT, 0:512], bk[0:TT, :], [Bbk], [Bstg])
                        dma('pool', 'R', osgu[l][:, half * 512:(half + 1) * 512], stg[0:TT, 0:512], reads=[Bstg])
                vtm = []
                for tb_ in range(NTB_):
                    halves = []
                    for half in range(2):
                        pt, Bpt = ptbank()
                        for j in range(4):
                            cc = half * 4 + j
                            tr(pt[0:TB, j * 128:(j + 1) * 128], vnb[cc][0][:, tb_ * 128: tb_ * 128 + TB], identb[:], [vnb[cc][1], Bconst], [Bpt] if j == 3 else None)
                        vt_, Bvt_ = bfp[:, 24 + tb_ * 2 + half, :], Bbfp[24 + tb_ * 2 + half]
                        cp('act', vt_[0:TB, :], pt[0:TB, :], [Bpt], [Bvt_])
                        halves.append((vt_, Bvt_))
                    vtm.append(halves)
                wsT = smallw[:, 0:1024]
                for c in range(8):
                    bk, Bbk = bank()
                    for tb_ in range(NTB_):
                        vt_, Bvt_ = vtm[tb_][c // 4]
                        mm(bk[:, tb_ * 128: tb_ * 128 + TB], vt_[0:TB, (c % 4) * 128:(c % 4 + 1) * 128], wsT[0:TB, c * 128: c * 128 + TB], True, True,
                           [Bvt_, Bsmall], [Bbk] if tb_ == NTB_ - 1 else None)
                    sv, Bsv = tmpf()
                    if NTB_ > 1:
                        tt('dve', sv[:, :TT].rearrange("p (a b) -> p a b", b=128), bk[:, :TT].rearrange("p (a b) -> p a b", b=128),
                           bsb[:, c * 128:(c + 1) * 128].unsqueeze(1).to_broadcast([128, NTB_, 128]), ALU.add, [Bbk, Bbsb], [Bsv])
                    else:
                        tt('dve', sv[:, :TT], bk[:, :TT], bsb[:, c * 128: c * 128 + TT], ALU.add, [Bbk, Bbsb], [Bsv])
                    tt('pool', bfp[:, 8 + c, :TT], sv[:, :TT], ub[:, c, :TT], ALU.mult, [Bsv, Bub[c]], [Bbfp[8 + c]])
                checkpoint('sgu')
                for i in range(2):
                    wt, Bw = wpiece(l, PI_Q + i)
                    for o4 in range(4):
                        oc = i * 4 + o4
                        bk, Bbk = big_mm(wt, Bw, o4, hrhs(TT), Bh, TT)
                        cp('act', bfp[:, 24 + oc, :TT], bk[:, :TT], [Bbk], [Bbfp[24 + oc]])
                kvt, Bkvp = piece('kv', l, grp)
                KT = kvt[:, 0:2048].rearrange("p (j m) -> p j m", j=8)
                VV = kvt[:, 2048:4096].rearrange("p (mb n) -> p mb n", mb=2)
                for hh in range(4):
                    pTs = []
                    for mc in range(2):
                        bk, Bbk = bank()
                        for dc in range(2):
                            mm(bk[:, :TT], KT[:, 2 * hh + dc, mc * 128:(mc + 1) * 128], bfp[:, 24 + 2 * hh + dc, :TT], dc == 0, dc == 1,
                               [Bkvp, Bbfp[24 + 2 * hh + dc]], [Bbk] if dc == 1 else None)
                        p_, Bp_ = tmpb()
                        act(p_[:, :TT], bk[:, :TT], AF.Exp, [Bbk], [Bp_], scale=1.0 / 16)
                        pTs.append((p_, Bp_))
                    bk, Bbk = bank()
                    for mc in range(2):
                        mm(bk[:, :TT], onesb[:], pTs[mc][0][:, :TT], mc == 0, mc == 1, [pTs[mc][1], Bconst], [Bbk] if mc == 1 else None)
                    rd, Brd = tmpf()
                    rsqrt(rd[:, :TT], bk[:, :TT], [Bbk], [Brd], power=-1.0)
                    for dvc in range(2):
                        bk, Bbk = bank()
                        for mc in range(2):
                            mm(bk[:, :TT], VV[:, mc, hh * 256 + dvc * 128: hh * 256 + (dvc + 1) * 128], pTs[mc][0][:, :TT], mc == 0, mc == 1,
                               [Bkvp, pTs[mc][1]], [Bbk] if mc == 1 else None)
                        tt('dve', bfp[:, 16 + 2 * hh + dvc, :TT], bk[:, :TT], rd[:, :TT], ALU.mult, [Bbk, Brd], [Bbfp[16 + 2 * hh + dvc]])
                if DBG and grp == 0 and l == 0:
                    dma('pool', 'R', dbg_y, bfp[:, 0:24, :], reads=[Bbfp[i] for i in range(24)])
                checkpoint('attn')
                for b in range(3):
                    for hf in range(2):
                        wg, Bwg = wpiece(l, PI_GATE + b * 2 + hf)
                        wb_, Bwb = wpiece(l, PI_BR + b * 2 + hf)
                        for o4 in range(4):
                            oc = hf * 4 + o4
                            bk, Bbk = big_mm(wg, Bwg, o4, hrhs(TT), Bh, TT)
                            gt, Bgt = tmpb()
                            act(gt[:, :TT], bk[:, :TT], AF.Sigmoid, [Bbk], [Bgt])
                            rh = [bfp[:, b * 8 + kc, :TT] for kc in range(8)]
                            bk, Bbk = big_mm(wb_, Bwb, o4, rh, [Bbfp[b * 8 + kc] for kc in range(8)], TT)
                            if b == 0:
                                tt('dve', big[:, oc, :TT], bk[:, :TT], gt[:, :TT], ALU.mult, [Bbk, Bgt], [Bbig[oc]])
                            else:
                                tm_, Btm = tmpf()
                                tt('dve', tm_[:, :TT], bk[:, :TT], gt[:, :TT], ALU.mult, [Bbk, Bgt], [Btm])
                                if b == 1:
                                    tt('pool', big[:, oc, :TT], big[:, oc, :TT], tm_[:, :TT], ALU.add, [Bbig[oc], Btm], [Bbig[oc]])
                                else:
                                    tt('pool', bfp[:, 24 + oc, :TT], big[:, oc, :TT], tm_[:, :TT], ALU.add, [Bbig[oc], Btm], [Bbfp[24 + oc]])
                if DBG and grp == 0 and l == 0:
                    dma('pool', 'R', dbg_m, bfp[:, 24:32, :], reads=[Bbfp[24 + i] for i in range(8)])
                for hf in range(2):
                    wt, Bw = wpiece(l, PI_OUT + hf)
                    for o4 in range(4):
                        oc = hf * 4 + o4
                        rh = [bfp[:, 24 + kc, :TT] for kc in range(8)]
                        bk, Bbk = big_mm(wt, Bw, o4, rh, [Bbfp[24 + kc] for kc in range(8)], TT)
                        tt('dve', x_t[:, oc, :TT], x_t[:, oc, :TT], bk[:, :TT], ALU.add, [Bx[oc], Bbk], [Bx[oc]])
                if DBG and grp == 0 and l == 0:
                    dma('pool', 'R', dbg_x, x_t[:], reads=Bx)
                checkpoint('merge')
                rmsnorm_to_h(l, V_FFNG, TT)
                for i in range(8):
                    wt, Bw = wpiece(l, PI_UP + i)
                    for o4 in range(4):
                        oc = i * 4 + o4
                        bk, Bbk = big_mm(wt, Bw, o4, hrhs(TT), Bh, TT)
                        rl, Brl = tmpf()
                        act(rl[:, :TT], bk[:, :TT], AF.Relu, [Bbk], [Brl])
                        tt('pool', bfp[:, oc, :TT], rl[:, :TT], rl[:, :TT], ALU.mult, [Brl], [Bbfp[oc]])
                for g in range(2):
                    banks = [bank() for _ in range(4)]
                    for q in range(4):
                        wt, Bw = wpiece(l, PI_DN + g * 4 + q)
                        for o4 in range(4):
                            rh = [bfp[:, q * 8 + kc, :TT] for kc in range(8)]
                            big_mm(wt, Bw, o4, rh, [Bbfp[q * 8 + kc] for kc in range(8)], TT, out_bk=banks[o4], start=(q == 0), stop=(q == 3))
                    for o4 in range(4):
                        oc = g * 4 + o4
                        tt('dve', x_t[:, oc, :TT], x_t[:, oc, :TT], banks[o4][0][:, :TT], ALU.add, [Bx[oc], banks[o4][1]], [Bx[oc]])

            def dbg_x2_dump():
                if DBG:
                    dma('pool', 'R', dbg_x2, x_t[:], reads=Bx)

            def load_x(src, TT):
                NTB_ = max(1, TT // 128); TB = min(TT, 128)
                for tb_ in range(NTB_):
                    dma('pool', 'W', big[0:TB, 2 * tb_:2 * tb_ + 2, :].rearrange("p a n -> p (a n)"), src[tb_ * 128: tb_ * 128 + TB, :],
                        writes=[Bbig[2 * tb_], Bbig[2 * tb_ + 1]])
                for c in range(8):
                    bk, Bbk = bank()
                    for tb_ in range(NTB_):
                        tr(bk[:, tb_ * 128: tb_ * 128 + TB], big[0:TB, 2 * tb_ + c // 4, (c % 4) * 128:(c % 4 + 1) * 128], identf[0:TB, 0:TB],
                           [Bbig[2 * tb_ + c // 4], Bconst], [Bbk] if tb_ == NTB_ - 1 else None)
                    cp('act', x_t[:, c, :TT], bk[:, :TT], [Bbk], [Bx[c]])

            def store_y(dst, TT):
                NTB_ = max(1, TT // 128); TB = min(TT, 128)
                bk, Bbk = bank()
                for c in range(8):
                    sq, Bsq = tmpb()
                    act(sq[:, :TT], x_t[:, c, :TT], AF.Square, [Bx[c]], [Bsq])
                    mm(bk[:, :TT], onesb[:], sq[:, :TT], c == 0, c == 7, [Bsq, Bconst], [Bbk] if c == 7 else None)
                rs, Brs = tmpf()
                rsqrt(rs[:, :TT], bk[:, :TT], [Bbk, Bconst], [Brs], scale=1.0 / D, bias=epsc[:, 0:1])
                for c in range(8):
                    stt(big[:, c, :TT], x_t[:, c, :TT], fing_t[:, c:c + 1], rs[:, :TT], ALU.mult, ALU.mult, [Bx[c], Brs, Bfing], [Bbig[c]])
                for tb_ in range(NTB_):
                    for half in range(2):
                        bk, Bbk = bank()
                        for j in range(4):
                            cc = half * 4 + j
                            tr(bk[0:TB, j * 128:(j + 1) * 128], big[:, cc, tb_ * 128: tb_ * 128 + TB], identf[:], [Bbig[cc], Bconst], [Bbk] if j == 3 else None)
                        stg, Bstg = tmpf()
                        cp('act', stg[0:TB, 0:512], bk[0:TB, :], [Bbk], [Bstg])
                        dma('pool', 'R', dst[tb_ * 128: tb_ * 128 + TB, half * 512:(half + 1) * 512], stg[0:TB, 0:512], reads=[Bstg])

            def store_state(l, owkv, oshift):
                for c in range(8):
                    bk, Bbk = bank()
                    tr(bk[0:64, 0:128], S32[:, l, c, :], identf[:], [BS32[l][c], Bconst], [Bbk])
                    stg, Bstg = tmpf()
                    cp('act', stg[0:64, 0:128], bk[0:64, 0:128], [Bbk], [Bstg])
                    dma('pool', 'R', owkv[l, 2 * c:2 * c + 2].rearrange("h v k -> v h k"), stg[0:64, 0:128].rearrange("p (h k) -> p h k", h=2), reads=[Bstg])
                bk, Bbk = bank()
                tr(bk[0:26, 0:128], prev[:, l, :], identf[:], [Bprev[l], Bconst], [Bbk])
                stg, Bstg = tmpf()
                cp('act', stg[0:26, 0:128], bk[0:26, 0:128], [Bbk], [Bstg])
                dma('pool', 'R', oshift[l].rearrange("(j p) -> j p", p=128), stg[0:26, 0:128], reads=[Bstg])

            memset('pool', cmask[:], 1.0, [Bconst])
            memset('pool', cmask[:].rearrange("p (a b) -> p a b", b=128)[:, :, 0:1], 0.0, [Bconst])

            Bm_ = [Bbig[i] for i in range(4)]
            dma('pool', 'W', big[:, 0:4, :].rearrange("p (mb a) n -> p mb (a n)", mb=2), memp.rearrange("(mb p) n -> p mb n", p=128), writes=Bm_)
            memf = big[:, 4:8, :].rearrange("p a (b n) -> p (a b) n", b=2)
            Bmemf = [Bbig[i] for i in range(4, 8)]
            for mb in range(2):
                ssq, Bssq = tmpf()
                for a in range(2):
                    jk, Bjk = tmpf()
                    rec.op('act', lambda e, a=a, mb=mb, jk=jk, ssq=ssq: e.activation(out=jk[:, 0:512], in_=big[:, mb * 2 + a, :], func=AF.Square, accum_out=ssq[:, a:a + 1]),
                           reads=[Bbig[mb * 2 + a]], writes=[Bjk, Bssq])
                tt('dve', ssq[:, 2:3], ssq[:, 0:1], ssq[:, 1:2], ALU.add, [Bssq], [Bssq])
                act(ssq[:, 3:4], ssq[:, 2:3], AF.Sqrt, [Bssq, Bconst], [Bssq], scale=1.0 / D, bias=epsc[:, 0:1])
                recip(ssq[:, 4:5], ssq[:, 3:4], [Bssq], [Bssq])
                for a in range(2):
                    ts('dve', big[:, mb * 2 + a, :], big[:, mb * 2 + a, :], ssq[:, 4:5], None, ALU.mult, None, [Bbig[mb * 2 + a], Bssq], [Bbig[mb * 2 + a]])
            for c in range(8):
                bk, Bbk = bank()
                for mb in range(2):
                    tr(bk[:, mb * 128:(mb + 1) * 128], big[:, mb * 2 + c // 4, (c % 4) * 128:(c % 4 + 1) * 128], identf[:], Bm_ + [Bconst], [Bbk] if mb == 1 else None)
                cp('act', memf[:, c, :], bk[:, 0:256], [Bbk], Bmemf)
            for l in range(L):
                mnt = [(bfp[:, 8 + i, :], Bbfp[8 + i]) for i in range(4)]
                for c in range(8):
                    t_, B_ = mnt[c // 2]
                    ts('dve', t_[:, (c % 2) * 256:(c % 2 + 1) * 256], memf[:, c, :], vcol(l, V_MEMG, c), None, ALU.mult, None, Bmemf + [Bvec], [B_])
                mrhs = lambda c: mnt[c // 2][0][:, (c % 2) * 256:(c % 2 + 1) * 256]
                kvst = [(bfp[:, i, :], Bbfp[i]) for i in range(8)]
                for i in range(4):
                    wt, Bw = wpiece(l, PI_MKV + i)
                    for mb in range(2):
                        bk, Bbk = bank()
                        for kc in range(8):
                            mm(bk[:, :], mrhs(kc)[:, mb * 128:(mb + 1) * 128], wt[:, kc, :], kc == 0, kc == 7, [mnt[kc // 2][1], Bw], [Bbk] if kc == 7 else None)
                        stg, Bstg = tmpf()
                        cp('act', stg[:, 0:512], bk[:, :], [Bbk], [Bstg])
                        dst = (omk if i < 2 else omv)[l][mb * 128:(mb + 1) * 128, (i % 2) * 512:(i % 2 + 1) * 512]
                        dma('pool', 'R', dst, stg[:, 0:512], reads=[Bstg])
                        if i >= 2:
                            s_, Bs_ = kvst[4 + mb * 2 + (i - 2)]
                            cp('pool', s_, stg[:, 0:512], [Bstg], [Bs_])
                    if i < 2:
                        for o4 in range(4):
                            j = i * 4 + o4
                            bk, Bbk = bank()
                            for kc in range(8):
                                mm(bk[:, 0:256], wt[:, kc, o4 * 128:(o4 + 1) * 128], mrhs(kc), kc == 0, kc == 7, [mnt[kc // 2][1], Bw], [Bbk] if kc == 7 else None)
                            s_, Bs_ = kvst[j // 2]
                            cp('act', s_[:, (j % 2) * 256:(j % 2 + 1) * 256], bk[:, 0:256], [Bbk], [Bs_])
                for i in range(8):
                    Bd = Buf(); Bkv[l][0].append(Bd)
                    dma('pool', 'R', kvsc[l, 0][:, i * 512:(i + 1) * 512], kvst[i][0], reads=[kvst[i][1]], writes=[Bd])

            checkpoint('memkv')
            for l in range(L):
                for c in range(8):
                    stg, Bstg = tmpf()
                    dma('pool', 'W', stg[0:64, 0:128].rearrange("p (h k) -> p h k", h=2), swkv[l, 2 * c:2 * c + 2].rearrange("h v k -> v h k"), writes=[Bstg])
                    bk, Bbk = bank()
                    tr(bk[:, 0:64], stg[0:64, 0:128], identf[0:64, 0:64], [Bstg, Bconst], [Bbk])
                    cp('act', S32[:, l, c, :], bk[:, 0:64], [Bbk], [BS32[l][c]])
                stg, Bstg = tmpf()
                dma('pool', 'W', stg[0:26, 0:128], sshift[l].rearrange("(j p) -> j p", p=128), writes=[Bstg])
                bk, Bbk = bank()
                tr(bk[:, 0:26], stg[0:26, 0:128], identf[0:26, 0:26], [Bstg, Bconst], [Bbk])
                cp('act', prev[:, l, :], bk[:, 0:26], [Bbk], [Bprev[l]])
            load_x(xs, TS)
            for l in range(L):
                tile_layer(l, 1, TS, TS, 5)
            store_y(ys, TS)
            for l in range(L):
                store_state(l, owkv_s, oshift_s)

            checkpoint('sample')
            for l in range(L):
                for c in range(8):
                    memset('pool', S32[:, l, c, :], 0.0, [BS32[l][c]])
                memset('pool', prev[:, l, :], 0.0, [Bprev[l]])
            for t_ in range(NT):
                load_x(xp[t_ * T:(t_ + 1) * T], T)
                for l in range(L):
                    tile_layer(l, 0, T, 128, 7)
                    if l == 0 and t_ == 0:
                        dbg_x2_dump()
                store_y(yp[t_ * T:(t_ + 1) * T], T)
            for l in range(L):
                store_state(l, owkv_p, oshift_p)

        except _Stop:
            pass
        finish()
    return nc


def _vec_table(inp, L):
    def fm(a, n):
        return np.ascontiguousarray(a.reshape(L, n, 128).transpose(2, 0, 1))
    mu = inp['rwkv_mu'][:L]
    ka = inp['rwkv_k_a'][:L]
    parts = [fm(inp['norm_mix_g'][:L], 8), fm(inp['norm_ffn_g'][:L], 8), fm(inp['norm_mem_g'][:L], 8),
             fm(mu, 26), np.zeros((128, L, 26), np.float32), fm(inp['rwkv_w0'][:L], 8), fm(inp['rwkv_a0'][:L], 8),
             fm(inp['rwkv_k_k'][:L], 8), fm(ka, 8), np.zeros((128, L, 8), np.float32), fm(inp['rwkv_r_k'][:L].reshape(L, 1024), 8),
             fm(inp['sgu_ln_g'][:L], 8), fm(inp['sgu_ln_b'][:L], 8), fm(inp['rwkv_lnx_g'][:L], 8), fm(inp['rwkv_lnx_b'][:L], 8)]
    return np.ascontiguousarray(np.concatenate(parts, axis=2).astype(np.float32))


def run(inp, SEQ, L, n_cores=8, trace=False, DBG=False, STOP=None):
    nc = build(SEQ, L, DBG=DBG, STOP=STOP)
    f = lambda a: np.ascontiguousarray(a, dtype=np.float32)
    vecs = _vec_table(inp, L)
    fing = np.ascontiguousarray(inp['norm_final_g'].reshape(8, 128).T.astype(np.float32))
    shared = {
        'vecs': vecs, 'fing': fing, 'w_in': f(inp['w_in'][:L]), 'w_mkv': f(inp['w_mem_kv'][:L]),
        'w2': f(inp['rwkv_w2'][:L]), 'a2': f(inp['rwkv_a2'][:L]), 'g2': f(inp['rwkv_g2'][:L]),
        'w_s': f(inp['sgu_w_s'][:L]), 'b_s': f(inp['sgu_b_s'][:L].reshape(L, 1024)),
        'w_br': f(inp['w_branch'][:L]), 'w_out': f(inp['w_out'][:L]), 'w_up': f(inp['w_ffn_up'][:L]), 'w_dn': f(inp['w_ffn_down'][:L]),
    }
    in_maps = []
    for b in range(n_cores):
        m = dict(shared)
        m['xp'] = f(inp['x_prompt'][b, :SEQ]); m['xs'] = f(inp['x_sample'][b])
        m['ck'] = f(inp['cache_mem_k'][:L, b].reshape(L, 256, 1024)); m['cv'] = f(inp['cache_mem_v'][:L, b].reshape(L, 256, 1024))
        m['swkv'] = f(inp['state_wkv'][:L, b]); m['sshift'] = f(inp['state_shift'][:L, b, 0])
        m['memp'] = f(inp['mem_prompt'][b])
        in_maps.append(m)
    res = run_bass_kernel_spmd(nc, in_maps, core_ids=list(range(n_cores)), trace=trace)
    R = res.results
    st = lambda k: np.stack([np.asarray(r[k]) for r in R])
    y_p = st('yp'); y_s = st('ys')
    mk = st('omk').transpose(1, 0, 2, 3).reshape(L, n_cores, 256, 4, 256)
    mv = st('omv').transpose(1, 0, 2, 3).reshape(L, n_cores, 256, 4, 256)
    wkv_p = st('owkv_p').transpose(1, 0, 2, 3, 4)
    sh_p = st('oshift_p').transpose(1, 0, 2)[:, :, None, :]
    wkv_s = st('owkv_s').transpose(1, 0, 2, 3, 4)
    sh_s = st('oshift_s').transpose(1, 0, 2)[:, :, None, :]
    sgu = st('osgu').transpose(1, 0, 2, 3)
    outs = (y_p, y_s, mk, mv, wkv_p, sh_p, wkv_s, sh_s, sgu)
    return tuple(np.ascontiguousarray(o.astype(np.float32)) for o in outs), res


def kernel(**inputs):
    inp = {k: np.asarray(v) for k, v in inputs.items()}
    outs, _ = run(inp, 8192, 4, 8)
    return outs
```

```python
import contextlib
import numpy as np
import concourse.bass as bass
import concourse.mybir as mybir
from concourse.bass_utils import run_bass_kernel_spmd

F32 = mybir.dt.float32
BF16 = mybir.dt.bfloat16
AF = mybir.ActivationFunctionType
ALU = mybir.AluOpType
AX = mybir.AxisListType

ENGS = ('pe', 'act', 'dve', 'pool', 'sp')
EMBED_ENGS = ('act', 'dve', 'pool')


class Buf:
    __slots__ = ('w', 'r', 'id')
    _n = [0]

    def __init__(self):
        self.w = None
        self.r = {}
        Buf._n[0] += 1
        self.id = Buf._n[0]


class Rec:
    def __init__(self, nc):
        self.nc = nc
        self.streams = {e: [] for e in ENGS}
        self.cnt = {e: 0 for e in ENGS}
        self.dcnt = {}
        self.waited = {e: {} for e in ENGS}
        self.waited_w = {e: {} for e in ENGS}
        self.pending = {e: [] for e in ENGS}

    def _wait(self, eng, tok, kind='r'):
        if tok is None:
            return
        key, val = tok
        if self.waited[eng].get(key, 0) >= val:
            return
        if eng == 'pe' and kind == 'w':
            if self.waited_w[eng].get(key, 0) >= val:
                return
            self.waited_w[eng][key] = val
        else:
            self.waited[eng][key] = val
        self.streams[eng].append(('w', key, val, kind))

    def _deps(self, eng, reads, writes, extra):
        for t in extra:
            self._wait(eng, t)
        for b in reads:
            self._wait(eng, b.w)
        for b in writes:
            self._wait(eng, b.w, 'w')
            for t in b.r.values():
                self._wait(eng, t, 'w')
            for e2 in ENGS:
                if e2 != eng and any(pb_ is b for pb_ in self.pending[e2]):
                    raise RuntimeError("write to buffer %d with pending unmarked reads on %s" % (b.id, e2))

    def op(self, eng, fn, reads=(), writes=(), mark=True, extra=(), wdeps=()):
        self._deps(eng, reads, tuple(writes) + tuple(wdeps), extra)
        if mark:
            self.cnt[eng] += 1
            tok = (eng, self.cnt[eng])
            self.streams[eng].append(('i', fn, True))
            for b in self.pending[eng]:
                b.r[eng] = tok
            self.pending[eng] = []
            for b in reads:
                b.r[eng] = tok
            for b in writes:
                b.w = tok
                b.r = {}
            return tok
        assert not writes
        self.streams[eng].append(('i', fn, False))
        self.pending[eng].extend(reads)
        return None

    def dma(self, q, sem, fn, reads=(), writes=(), extra=()):
        self._deps(q, reads, writes, extra)
        self.dcnt[sem] = self.dcnt.get(sem, 0) + 16
        tok = (sem, self.dcnt[sem])
        self.streams[q].append(('d', fn, sem))
        for b in reads:
            b.r[sem] = tok
        for b in writes:
            b.w = tok
            b.r = {}
        return tok

    def emit(self, stack):
        nc = self.nc
        sems = {}
        for e in ENGS:
            sems[e] = stack.enter_context(nc.semaphore('s_' + e))
        for d in self.dcnt:
            sems[d] = stack.enter_context(nc.semaphore('d_' + d))
        block = stack.enter_context(nc.Block())
        dec = {'pe': block.tensor, 'act': block.scalar, 'dve': block.vector, 'pool': block.gpsimd, 'sp': block.sync}
        for e in ENGS:
            stream = self.streams[e]
            own = sems[e]

            def body(engine, stream=stream, own=own, embed=(e in EMBED_ENGS), is_pe=(e == 'pe')):
                n = len(stream)
                pend = None
                for k, item in enumerate(stream):
                    if item[0] == 'w':
                        if (embed or (is_pe and item[3] == 'w')) and k + 1 < n and stream[k + 1][0] == 'i':
                            pend = item
                        else:
                            engine.wait_ge(sems[item[1]], item[2])
                    elif item[0] == 'i':
                        ins = item[1](engine)
                        if pend is not None:
                            ins._wait_ge(sems[pend[1]], pend[2])
                            pend = None
                        if item[2]:
                            ins.then_inc(own, 1)
                    else:
                        item[1](engine).then_inc(sems[item[2]], 16)
            dec[e](body)


D = 1024
NCH = 8
RW = 3328
IN_COLS = 9472
NP_ = 49
PI_LORA, PI_RKV, PI_SGU, PI_Q, PI_GATE, PI_BR, PI_OUT, PI_UP, PI_DN, PI_MKV = 0, 1, 9, 13, 15, 21, 27, 29, 37, 45
C0 = -0.6065306597126334
V_MIXG, V_FFNG, V_MEMG, V_MU, V_OMM, V_W0, V_A0, V_KK, V_KA, V_OMKA, V_RK, V_SLG, V_SLB, V_LXG, V_LXB = (
    0, 8, 16, 24, 50, 76, 84, 92, 100, 108, 116, 124, 132, 140, 148)
VW = 156


class _Stop(Exception):
    pass


def build(SEQ, DEPTH, MEMN=256, TS=32, DBG=False, STOP=None):
    nc = bass.Bass("TRN2", target_bir_lowering=False)
    L = DEPTH
    T = 512
    NT = SEQ // T
    din = lambda name, shape, dt=F32: nc.dram_tensor(name, shape, dt, kind="ExternalInput").ap()
    dout = lambda name, shape, dt=F32: nc.dram_tensor(name, shape, dt, kind="ExternalOutput").ap()
    dscr = lambda name, shape, dt: nc.dram_tensor(name, shape, dt, kind="Internal").ap()
    xp = din("xp", [SEQ, D]); xs = din("xs", [TS, D])
    ck = din("ck", [L, MEMN, D]); cv = din("cv", [L, MEMN, D])
    swkv = din("swkv", [L, 16, 64, 64]); sshift = din("sshift", [L, RW]); memp = din("memp", [MEMN, D])
    vecs = din("vecs", [128, L, VW]); fing = din("fing", [128, 8])
    w_in = din("w_in", [L, D, IN_COLS]); w_mkv = din("w_mkv", [L, D, 2 * D])
    w2 = din("w2", [L, 64, D]); a2 = din("a2", [L, 64, D]); g2 = din("g2", [L, 128, D])
    w_s = din("w_s", [L, 8, 128, 128]); b_s = din("b_s", [L, 8 * 128])
    w_br = din("w_br", [L, 3, D, D]); w_out = din("w_out", [L, D, D])
    w_up = din("w_up", [L, D, 4 * D]); w_dn = din("w_dn", [L, 4 * D, D])
    yp = dout("yp", [SEQ, D]); ys = dout("ys", [TS, D])
    omk = dout("omk", [L, MEMN, D]); omv = dout("omv", [L, MEMN, D])
    owkv_p = dout("owkv_p", [L, 16, 64, 64]); oshift_p = dout("oshift_p", [L, RW])
    owkv_s = dout("owkv_s", [L, 16, 64, 64]); oshift_s = dout("oshift_s", [L, RW])
    osgu = dout("osgu", [L, TS, D])
    if DBG:
        dbg_y = dout("dbg_y", [128, 24, 512], BF16); dbg_x = dout("dbg_x", [128, 8, 512]); dbg_x2 = dout("dbg_x2", [128, 8, 512]); dbg_m = dout("dbg_m", [128, 8, 512], BF16)
    wsc = dscr("wsc", [L, NP_, 128, 4096], BF16)
    kvsc = dscr("kvsc", [L, 2, 128, 4096], BF16)
    smsc = dscr("smsc", [L, 128, 3072], BF16)

    rec = Rec(nc)
    st = contextlib.ExitStack()
    with st:
        def sb(name, shape, dt):
            return st.enter_context(nc.sbuf_tensor(name, shape, dt))

        def ps(name, shape, dt):
            return st.enter_context(nc.psum_tensor(name, shape, dt))

        def mm(out, lhsT, rhs, start, stop, reads, writes=None):
            rec.op('pe', lambda e: e.matmul(out, lhsT=lhsT, rhs=rhs, start=start, stop=stop),
                   reads=reads, writes=writes or (), mark=writes is not None, wdeps=() if writes is not None else psum_bufs[out.name])

        def tr(out, in_, ident, reads, writes=None):
            rec.op('pe', lambda e: e.transpose(out=out, in_=in_, identity=ident),
                   reads=reads, writes=writes or (), mark=writes is not None, wdeps=() if writes is not None else psum_bufs[out.name])

        def act(out, in_, func, reads, writes, scale=1.0, bias=None):
            if bias is None:
                rec.op('act', lambda e: e.activation(out=out, in_=in_, func=func, scale=scale), reads=reads, writes=writes)
            else:
                rec.op('act', lambda e: e.activation(out=out, in_=in_, func=func, scale=scale, bias=bias), reads=reads, writes=writes)

        def tt(eng, out, in0, in1, op, reads, writes):
            rec.op(eng, lambda e: e.tensor_tensor(out=out, in0=in0, in1=in1, op=op), reads=reads, writes=writes)

        def ts(eng, out, in0, s1, s2, op0, op1, reads, writes):
            if op1 is None and eng == 'pool' and op0 == ALU.mult:
                s2, op1 = 0.0, ALU.add
            if op1 is None:
                rec.op(eng, lambda e: e.tensor_scalar(out=out, in0=in0, scalar1=s1, scalar2=None, op0=op0), reads=reads, writes=writes)
            else:
                rec.op(eng, lambda e: e.tensor_scalar(out=out, in0=in0, scalar1=s1, scalar2=s2, op0=op0, op1=op1), reads=reads, writes=writes)

        def stt(out, in0, scalar, in1, op0, op1, reads, writes):
            rec.op('dve', lambda e: e.scalar_tensor_tensor(out=out, in0=in0, scalar=scalar, in1=in1, op0=op0, op1=op1), reads=reads, writes=writes)

        def cp(eng, out, in_, reads, writes):
            if eng == 'act':
                act(out, in_, AF.Copy, reads, writes)
            else:
                rec.op(eng, lambda e: e.tensor_copy(out=out, in_=in_), reads=reads, writes=writes)

        def rsqrt(out, in_, reads, writes, scale=1.0, bias=None, power=-0.5):
            act(out, in_, AF.Ln, reads, writes, scale=scale, bias=bias)
            act(out, out, AF.Exp, writes, writes, scale=power)

        def cpred(out, mask, data, reads, writes):
            rec.op('dve', lambda e: e.copy_predicated(out=out, mask=mask, data=data), reads=reads, writes=writes)

        def recip(out, in_, reads, writes):
            rec.op('dve', lambda e: e.reciprocal(out=out, in_=in_), reads=reads, writes=writes)

        def memset(eng, ap, val, writes):
            rec.op(eng, lambda e: e.memset(ap, val), writes=writes)

        def dma(q, sem, out, in_, reads=(), writes=(), slow=False):
            if sem == 'W':
                sem = 'b%d' % writes[0].id
            elif sem == 'R':
                sem = 'b%d' % reads[0].id
            if slow:
                return rec.dma(q, sem, lambda e: e.dma_start(out=out, in_=in_, allow_slow_non_contiguous=True), reads=reads, writes=writes)
            return rec.dma(q, sem, lambda e: e.dma_start(out=out, in_=in_), reads=reads, writes=writes)

        x_t = sb("x_t", [128, 8, T], F32); Bx = [Buf() for _ in range(8)]
        h_t = sb("h_t", [128, 8, T], BF16); Bh = [Buf() for _ in range(8)]
        NS = 3
        ws_t = [sb(f"ws{i}", [128, 4096], BF16) for i in range(NS)]; Bws = [Buf() for _ in range(NS)]
        big = sb("big", [128, 8, T], F32); Bbig = [Buf() for _ in range(8)]
        ub = sb("ub", [128, 8, T], BF16); Bub = [Buf() for _ in range(8)]
        bfp = sb("bfp", [128, 32, T], BF16); Bbfp = [Buf() for _ in range(32)]
        NTF = 11
        tf_t = sb("tf_t", [128, NTF, T + 1], F32); Btf = [Buf() for _ in range(NTF)]
        NTB = 13
        tb_t = sb("tb_t", [128, NTB, T], BF16); Btb = [Buf() for _ in range(NTB)]
        tfi = [0]; tbi = [0]

        def tmpf():
            i = tfi[0] % NTF; tfi[0] += 1
            return tf_t[:, i, :], Btf[i]

        def tmpb():
            i = tbi[0] % NTB; tbi[0] += 1
            return tb_t[:, i, :], Btb[i]

        S32 = sb("S32", [128, L, 8, 64], F32); BS32 = [[Buf() for _ in range(8)] for _ in range(L)]
        Sbd = sb("Sbd", [128, 8, 128], BF16); BSbd = [Buf() for _ in range(8)]
        prev = sb("prev", [128, L, 26], F32); Bprev = [Buf() for _ in range(L)]
        vec_t = sb("vec_t", [128, L, VW], F32); Bvec = Buf()
        fing_t = sb("fing_t", [128, 8], F32); Bfing = Buf()
        lw = sb("lw", [128, T], BF16); Blw = Buf(); sgl = sb("sgl", [128, T], BF16); Bsgl = Buf()
        smallw = sb("smallw", [128, 3072], BF16); Bsmall = Buf()
        bsb = sb("bsb", [128, 1024], F32); Bbsb = Buf()
        identf = sb("identf", [128, 128], F32); identb = sb("identb", [128, 128], BF16)
        onesb = sb("onesb", [128, 128], BF16); blkb = sb("blkb", [128, 128], BF16)
        onesf = sb("onesf", [128, 128], F32)
        msu = sb("msu", [128, 128], F32); miu = sb("miu", [128, 128], F32); msl = sb("msl", [128, 128], F32)
        cmask = sb("cmask", [128, T], F32)
        epsc = sb("epsc", [128, 4], F32)
        Bconst = Buf()
        vTp = [sb(f"vT{i}", [128, 4, 128], BF16) for i in range(2)]; kTp = [sb(f"kT{i}", [128, 4, 128], BF16) for i in range(2)]
        bTp = [sb(f"bT{i}", [128, 4, 128], BF16) for i in range(2)]
        BvTp, BkTp, BbTp = [Buf(), Buf()], [Buf(), Buf()], [Buf(), Buf()]
        bvt = [sb(f"bvt{i}", [128, T], BF16) for i in range(2)]; Bbvt = [Buf(), Buf()]
        y1t = sb("y1t", [128, T], F32); By1t = Buf()
        NQt = [sb(f"NQt{i}", [128, 2, 2, 128], BF16) for i in range(4)]; BNQ = [Buf() for _ in range(4)]
        MTt = [sb(f"MTt{i}", [128, 2, 2, 128], BF16) for i in range(4)]; BMT = [Buf() for _ in range(4)]
        Xbt = [sb(f"Xbt{i}", [128, 2, 2, 128], BF16) for i in range(4)]; BXb = [Buf() for _ in range(4)]
        m2d = sb("m2d", [128, 7, 2, 128], BF16)
        m2d_u16 = m2d[:].bitcast(mybir.dt.uint16)
        AKt = [sb(f"AKt{i}", [128, 2, 128], BF16) for i in range(4)]; BAK = [Buf() for _ in range(4)]
        RRt = [sb(f"RRt{i}", [128, 4, 128], BF16) for i in range(4)]; BRR = [Buf() for _ in range(4)]
        Zb = [sb(f"Zb{i}", [128, 128], BF16) for i in range(2)]; BZb = [Buf(), Buf()]
        Ub = [sb(f"Ub{i}", [128, 128], BF16) for i in range(2)]; BUb = [Buf(), Buf()]
        Swt = [sb(f"Swt{i}", [128, 64], F32) for i in range(2)]; BSw = [Buf(), Buf()]
        wc_p = [sb(f"wc{i}", [128, 4], F32) for i in range(2)]; Bwcp = [Buf(), Buf()]
        yTM = sb("yTM", [128, 4, 128], F32); ByTM = Buf()
        ysq = sb("ysq", [128, 4, 128], F32); Bysq = Buf()
        ynb = sb("ynb", [128, 4, 128], BF16); Bynb = Buf()
        gst = sb("gst", [128, 4, 8], F32); Bgst = Buf()

        NB = 7
        pb = [ps(f"pb{i}", [128, 512], F32) for i in range(NB)]; Bpb = [Buf() for _ in range(NB)]
        pT = ps("pT", [128, 1024], BF16); _bpt = Buf(); BpT = [_bpt, _bpt]
        psum_bufs = {f"pb{i}": [Bpb[i]] for i in range(NB)}
        psum_bufs["pT"] = [_bpt]
        bki = [0]

        def bank():
            i = bki[0] % NB; bki[0] += 1
            return pb[i], Bpb[i]

        pti = [0]

        def ptbank():
            i = pti[0] % 2; pti[0] += 1
            return pT[:, i * 512:(i + 1) * 512], BpT[i]

        gcur = ['']

        def checkpoint(name):
            if STOP == name or STOP == gcur[0] + ':' + name:
                raise _Stop()

        def finish():
            for k_, v_ in list(rec.dcnt.items()):
                rec._wait('sp', (k_, v_))
            rec.emit(st)

        try:
            memset('pool', onesf[:], 1.0, [Bconst])
            memset('pool', identf[:], 0.0, [Bconst])
            rec.op('pool', lambda e: e.affine_select(out=identf[:], in_=identf[:], pattern=[[-1, 128]], compare_op=ALU.not_equal, fill=1.0, base=0, channel_multiplier=1), reads=[Bconst], writes=[Bconst])
            cp('pool', identb[:], identf[:], [Bconst], [Bconst])
            cp('pool', onesb[:], onesf[:], [Bconst], [Bconst])
            memset('pool', blkb[:], 0.0, [Bconst])
            memset('pool', blkb[0:64, 0:64], 1.0, [Bconst])
            memset('pool', blkb[64:128, 64:128], 1.0, [Bconst])
            rec.op('pool', lambda e: e.affine_select(out=msu[:], in_=onesf[:], pattern=[[1, 128]], compare_op=ALU.is_gt, fill=0.0, base=0, channel_multiplier=-1), reads=[Bconst], writes=[Bconst])
            rec.op('pool', lambda e: e.affine_select(out=miu[:], in_=onesf[:], pattern=[[1, 128]], compare_op=ALU.is_ge, fill=0.0, base=0, channel_multiplier=-1), reads=[Bconst], writes=[Bconst])
            rec.op('pool', lambda e: e.affine_select(out=msl[:], in_=onesf[:], pattern=[[-1, 128]], compare_op=ALU.is_gt, fill=0.0, base=0, channel_multiplier=1), reads=[Bconst], writes=[Bconst])
            Ea, Eb, Ec = tf_t[:, 0, 0:128], tf_t[:, 1, 0:128], tf_t[:, 2, 0:128]
            BE_ = [Btf[0], Btf[1], Btf[2]]
            cp('pool', Ea, identf[:], [Bconst], [BE_[0]])
            Ecur, Bcur, Enxt, Bnxt = Ea, BE_[0], Eb, BE_[1]
            for j in range(7):
                s_ = 2 ** (j + 1); nb_ = 128 // s_
                if j == 6:
                    cp('pool', Enxt, onesf[:], [Bconst], [Bnxt])
                else:
                    def sel1(e, out=Enxt, s_=s_, nb_=nb_):
                        return e.affine_select(out=out.rearrange("p (a b) -> p a b", b=s_), in_=onesf[:].rearrange("p (a b) -> p a b", b=s_),
                                               pattern=[[-s_, nb_], [0, s_]], compare_op=ALU.is_ge, fill=0.0, base=0, channel_multiplier=1)

                    def sel2(e, out=Enxt, s_=s_, nb_=nb_):
                        return e.affine_select(out=out.rearrange("p (a b) -> p a b", b=s_), in_=out.rearrange("p (a b) -> p a b", b=s_),
                                               pattern=[[s_, nb_], [0, s_]], compare_op=ALU.is_ge, fill=0.0, base=s_ - 1, channel_multiplier=-1)
                    rec.op('pool', sel1, reads=[Bconst], writes=[Bnxt])
                    rec.op('pool', sel2, reads=[Bnxt], writes=[Bnxt])
                tt('pool', Ec, Enxt, Ecur, ALU.subtract, [Bnxt, Bcur], [BE_[2]])
                tt('pool', m2d[:, j, 0, :], Ec, msu[:], ALU.mult, [BE_[2], Bconst], [Bconst])
                tt('pool', m2d[:, j, 1, :], Ec, msl[:], ALU.mult, [BE_[2], Bconst], [Bconst])
                Ecur, Bcur, Enxt, Bnxt = Enxt, Bnxt, Ecur, Bcur
            memset('pool', epsc[:, 0:1], 1e-6, [Bconst])
            memset('pool', epsc[:, 1:2], 1e-5, [Bconst])
            memset('pool', epsc[:, 2:3], 64e-5, [Bconst])
            memset('pool', epsc[:, 3:4], 1e-24, [Bconst])
            dma('pool', 'W', vec_t[:], vecs, writes=[Bvec])
            dma('pool', 'W', fing_t[:], fing, writes=[Bfing])
            ts('dve', vec_t[:, :, V_OMM:V_OMM + 26], vec_t[:, :, V_MU:V_MU + 26], -1.0, 1.0, ALU.mult, ALU.add, [Bvec], [Bvec])
            ts('dve', vec_t[:, :, V_OMKA:V_OMKA + 8], vec_t[:, :, V_KA:V_KA + 8], -1.0, 1.0, ALU.mult, ALU.add, [Bvec], [Bvec])

            def vcol(l, base, c):
                return vec_t[:, l, base + c:base + c + 1]

            checkpoint('const')
            _bw = Buf()
            Bwsc = [_bw] * L
            BwscM = [_bw] * L
            Bsm = [[] for _ in range(L)]
            Bkv = [[[], []] for _ in range(L)]

            def conv(l, pi, src2d, ncols, col0=0):
                dst = wsc[l, pi].rearrange("p (kc n) -> p kc n", kc=8)[:, :, col0:col0 + ncols]
                if pi >= PI_MKV:
                    dma('pool', 'cv', dst, src2d.rearrange("(kc p) n -> p kc n", p=128), writes=[BwscM[l]])
                else:
                    dma('pool', 'cv', dst, src2d.rearrange("(kc p) n -> p kc n", p=128), writes=[Bwsc[l]])

            for l in range(L):
                for i in range(4):
                    conv(l, PI_MKV + i, w_mkv[l][:, i * 512:(i + 1) * 512], 512)
            for l in range(L):
                conv(l, PI_LORA, w_in[l][:, 3072:3328], 256)
                for c in range(8):
                    for j in range(3):
                        conv(l, PI_RKV + c, w_in[l][:, j * 1024 + c * 128: j * 1024 + (c + 1) * 128], 128, col0=j * 128)
                for i in range(4):
                    conv(l, PI_SGU + i, w_in[l][:, 3328 + i * 512: 3328 + (i + 1) * 512], 512)
                for i in range(2):
                    conv(l, PI_Q + i, w_in[l][:, 5376 + i * 512: 5376 + (i + 1) * 512], 512)
                for i in range(6):
                    conv(l, PI_GATE + i, w_in[l][:, 6400 + i * 512: 6400 + (i + 1) * 512], 512)
                for b in range(3):
                    for hf in range(2):
                        conv(l, PI_BR + b * 2 + hf, w_br[l, b][:, hf * 512:(hf + 1) * 512], 512)
                for hf in range(2):
                    conv(l, PI_OUT + hf, w_out[l][:, hf * 512:(hf + 1) * 512], 512)
                for i in range(8):
                    conv(l, PI_UP + i, w_up[l][:, i * 512:(i + 1) * 512], 512)
                for g in range(2):
                    for q in range(4):
                        conv(l, PI_DN + g * 4 + q, w_dn[l][q * 1024:(q + 1) * 1024, g * 512:(g + 1) * 512], 512)
                dma('pool', 'cv', smsc[l][0:64, 1024:2048], w2[l], writes=[Bwsc[l]])
                dma('pool', 'cv', smsc[l][64:128, 1024:2048], a2[l], writes=[Bwsc[l]])
                dma('pool', 'cv', smsc[l][:, 2048:3072], g2[l], writes=[Bwsc[l]])
                dma('pool', 'cv', kvsc[l, 1][:, 2048:4096].rearrange("p (mb n) -> p mb n", mb=2),
                    cv[l].rearrange("(mb p) n -> p mb n", p=128), writes=[Bwsc[l]])
            checkpoint('conv')
            for l in range(L):
                wsf, Bw_ = big[:, 0:2, :], [Bbig[0], Bbig[1]]
                dma('pool', 'W', big[:, 0:2, :].rearrange("p a (g j) -> p (a g) j", j=128), w_s[l].rearrange("g i j -> i g j"), writes=Bw_)
                for g in range(8):
                    if g % 4 == 0:
                        stg, Bstg = tmpb()
                    bk, Bbk = bank()
                    tr(bk[:, 0:128], big[:, g // 4, (g % 4) * 128:(g % 4 + 1) * 128], identf[:], reads=Bw_ + [Bconst], writes=[Bbk])
                    tt('dve', stg[:, (g % 4) * 128:(g % 4 + 1) * 128], bk[:, 0:128], miu[:], ALU.mult, [Bbk, Bconst], [Bstg])
                    if g % 4 == 3:
                        Bd = Buf(); Bsm[l].append(Bd)
                        dma('pool', 'R', smsc[l][:, (g // 4) * 512:(g // 4 + 1) * 512], stg, reads=[Bstg], writes=[Bd])
            for l in range(L):
                Bk_ = [Bbig[i] for i in range(4)]
                dma('pool', 'W', big[:, 0:4, :].rearrange("p (mb a) n -> p mb (a n)", mb=2), ck[l].rearrange("(mb p) n -> p mb n", p=128), writes=Bk_)
                for j in range(8):
                    if j % 2 == 0:
                        stg, Bstg = tmpb()
                    bk, Bbk = bank()
                    for mb in range(2):
                        tr(bk[:, mb * 128:(mb + 1) * 128], big[:, mb * 2 + j // 4, (j % 4) * 128:(j % 4 + 1) * 128], identf[:],
                           reads=Bk_ + [Bconst], writes=[Bbk] if mb == 1 else None)
                    cp('act', stg[:, (j % 2) * 256:(j % 2 + 1) * 256], bk[:, 0:256], [Bbk], [Bstg])
                    if j % 2 == 1:
                        Bd = Buf(); Bkv[l][1].append(Bd)
                        dma('pool', 'R', kvsc[l, 1][:, (j - 1) * 256:(j + 1) * 256], stg, reads=[Bstg], writes=[Bd])

            checkpoint('prep')
            uses = []
            state = {'k': 0, 'loaded': 0}

            def plan_tile_layer(l, grp):
                seq = [('w', l, PI_LORA)] + [('w', l, PI_RKV + c) for c in range(8)] + [('w', l, PI_SGU + i) for i in range(4)]
                seq += [('w', l, PI_Q), ('w', l, PI_Q + 1), ('kv', l, grp)]
                for b in range(3):
                    for hf in range(2):
                        seq += [('w', l, PI_GATE + b * 2 + hf), ('w', l, PI_BR + b * 2 + hf)]
                seq += [('w', l, PI_OUT), ('w', l, PI_OUT + 1)] + [('w', l, PI_UP + i) for i in range(8)]
                seq += [('w', l, PI_DN + i) for i in range(8)]
                return seq

            for l in range(L):
                uses += [('w', l, PI_MKV + i) for i in range(4)]
            for l in range(L):
                uses += plan_tile_layer(l, 1)
            for t_ in range(NT):
                for l in range(L):
                    uses += plan_tile_layer(l, 0)
            def _issue_load(k):
                kind, l, idx = uses[k]
                slot = k % NS
                if kind == 'w':
                    ncols = 256 if idx == PI_LORA else (384 if PI_RKV <= idx < PI_RKV + 8 else 512)
                    if ncols == 512:
                        dma('sp', f'w{slot}', ws_t[slot][:], wsc[l, idx], reads=[BwscM[l] if idx >= PI_MKV else Bwsc[l]], writes=[Bws[slot]])
                    else:
                        dma('sp', f'w{slot}', ws_t[slot][:].rearrange("p (kc n) -> p kc n", kc=8)[:, :, 0:ncols],
                            wsc[l, idx].rearrange("p (kc n) -> p kc n", kc=8)[:, :, 0:ncols], reads=[Bwsc[l]], writes=[Bws[slot]])
                else:
                    dma('sp', f'w{slot}', ws_t[slot][:], kvsc[l, idx], reads=[Bwsc[l]] + Bkv[l][idx], writes=[Bws[slot]])

            def piece(kind, l, idx):
                k = state['k']
                assert uses[k] == (kind, l, idx), (uses[k], kind, l, idx)
                while state['loaded'] < min(len(uses), k + NS - 1):
                    _issue_load(state['loaded']); state['loaded'] += 1
                state['k'] += 1
                slot = k % NS
                return ws_t[slot], Bws[slot]

            def wpiece(l, idx):
                t_, B_ = piece('w', l, idx)
                return t_[:].rearrange("p (kc n) -> p kc n", kc=8), B_

            def rmsnorm_to_h(l, gbase, TT):
                bk, Bbk = bank()
                for c in range(8):
                    sq, Bsq = tmpb()
                    act(sq[:, :TT], x_t[:, c, :TT], AF.Square, [Bx[c]], [Bsq])
                    mm(bk[:, :TT], onesb[:], sq[:, :TT], c == 0, c == 7, [Bsq, Bconst], [Bbk] if c == 7 else None)
                rs, Brs = tmpf()
                rsqrt(rs[:, :TT], bk[:, :TT], [Bbk, Bconst], [Brs], scale=1.0 / D, bias=epsc[:, 0:1])
                for c in range(8):
                    gcol = vcol(l, gbase, c) if l is not None else fing_t[:, c:c + 1]
                    yield_out = h_t[:, c, :TT]
                    stt(yield_out, x_t[:, c, :TT], gcol, rs[:, :TT], ALU.mult, ALU.mult, [Bx[c], Brs, Bvec], [Bh[c]])

            def big_mm(wt, Bw, oc4, rhs_list, Brhs, TT, out_bk=None, start=True, stop=True, kc0=0):
                if out_bk is None:
                    bk, Bbk = bank()
                else:
                    bk, Bbk = out_bk
                n = len(rhs_list)
                for kc in range(n):
                    last = (kc == n - 1)
                    mm(bk[:, :TT], wt[:, kc, oc4 * 128:(oc4 + 1) * 128], rhs_list[kc], start and kc == 0, stop and last,
                       [Bw, Brhs[kc]], [Bbk] if last else None)
                return bk, Bbk

            def hrhs(TT):
                return [h_t[:, kc, :TT] for kc in range(8)]

            def rwkv_gen(l, c, TT, C):
                NC_ = TT // C
                wt, Bw = wpiece(l, PI_RKV + c)
                w2a2 = smallw[:, 1024:2048]; g2t = smallw[:, 2048:3072]
                F = lambda i: (tf_t[:, i, :], Btf[i])
                Bq = lambda i: (tb_t[:, i, :], Btb[i])
                pp = c % 2
                CB = [F(0), F(0)]
                (r_, Br), (k_, Bk), (v_, Bv) = F(1), F(2), F(3)
                (sig, Bsig), (a_, Ba), (kk, Bkk), (Lc, BL), (E, BE), (E2, BE2), (t1, Bt1) = F(4), F(5), F(6), F(7), F(8), F(9), F(10)
                (g_, Bg), (rt, Brt), (kt, Bkt), (bt, Bbt), (at, Bat) = [Bq(pp * 5 + i) for i in range(5)]
                (sq, Bsq), (pr, Bpr), (vb, Bvb) = Bq(10), Bq(11), Bq(12)
                vT, kT, bT, BvT, BkT, BbT = vTp[pp], kTp[pp], bTp[pp], BvTp[pp], BkTp[pp], BbTp[pp]
                wc_t, Bwc = wc_p[pp], Bwcp[pp]
                mixed = [(r_, Br), (k_, Bk), (v_, Bv)]
                for j in range(3):
                    col = j * 8 + c
                    bk, Bbk = big_mm(wt, Bw, j, hrhs(TT), Bh, TT)
                    cb, Bcb = CB[j % 2]
                    mx, Bmx = mixed[j]
                    cp('act', cb[:, 1:TT + 1], bk[:, :TT], [Bbk], [Bcb])
                    cp('pool', cb[:, 0:1], prev[:, l, col:col + 1], [Bprev[l]], [Bcb])
                    ts('dve', t1[:, :TT], cb[:, 0:TT], vcol(l, V_MU, col), None, ALU.mult, None, [Bcb, Bvec], [Bt1])
                    stt(mx[:, :TT], cb[:, 1:TT + 1], vcol(l, V_OMM, col), t1[:, :TT], ALU.mult, ALU.add, [Bcb, Bt1, Bvec], [Bmx])
                    cp('pool', prev[:, l, col:col + 1], cb[:, TT:TT + 1], [Bcb], [Bprev[l]])
                    yield 's'
                bk, Bbk = bank()
                mm(bk[:, :TT], w2a2[0:64, c * 128:(c + 1) * 128], lw[0:64, :TT], True, True, [Bsmall, Blw], [Bbk])
                act(sig[:, :TT], bk[:, :TT], AF.Sigmoid, [Bbk, Bvec], [Bsig], bias=vcol(l, V_W0, c))
                yield 's'
                bk, Bbk = bank()
                mm(bk[:, :TT], w2a2[64:128, c * 128:(c + 1) * 128], lw[64:128, :TT], True, True, [Bsmall, Blw], [Bbk])
                act(a_[:, :TT], bk[:, :TT], AF.Sigmoid, [Bbk, Bvec], [Ba], bias=vcol(l, V_A0, c))
                yield 's'
                bk, Bbk = bank()
                mm(bk[:, :TT], g2t[:, c * 128:(c + 1) * 128], sgl[:, :TT], True, True, [Bsmall, Bsgl], [Bbk])
                cp('act', g_[:, :TT], bk[:, :TT], [Bbk], [Bg])
                yield 's'
                ts('pool', kk[:, :TT], k_[:, :TT], vcol(l, V_KK, c), None, ALU.mult, None, [Bk, Bvec], [Bkk])
                act(sq[:, :TT], kk[:, :TT], AF.Square, [Bkk], [Bsq])
                bk, Bbk = bank()
                mm(bk[:, :TT], blkb[:], sq[:, :TT], True, True, [Bconst, Bsq], [Bbk])
                rn, Brn = E2, BE2
                rsqrt(rn[:, :TT], bk[:, :TT], [Bbk, Bconst], [Brn], bias=epsc[:, 3:4])
                yield 's'
                yield 's'
                tt('pool', kk[:, :TT], kk[:, :TT], rn[:, :TT], ALU.mult, [Bkk, Brn], [Bkk])
                ts('dve', t1[:, :TT], a_[:, :TT], vcol(l, V_KA, c), vcol(l, V_OMKA, c), ALU.mult, ALU.add, [Ba, Bvec], [Bt1])
                tt('pool', k_[:, :TT], k_[:, :TT], t1[:, :TT], ALU.mult, [Bk, Bt1], [Bk])
                yield 's'
                scan_mask = cmask[:, :TT] if C == 128 else onesf[:, :TT]
                rec.op('dve', lambda e: e.tensor_tensor_scan(out=Lc[:, :TT], data0=scan_mask, data1=sig[:, :TT],
                                                             initial=0.0, op0=ALU.mult, op1=ALU.add), reads=[Bsig, Bconst], writes=[BL])
                act(E[:, :TT], Lc[:, :TT], AF.Exp, [BL], [BE], scale=C0)
                tt('dve', rt[:, :TT], r_[:, :TT], E[:, :TT], ALU.mult, [Br, BE], [Brt])
                cp('pool', wc_t[:, 0:NC_], E[:, :TT].rearrange("p (a b) -> p a b", b=C)[:, :, C - 1], [BE], [Bwc])
                yield 's'
                act(E2[:, :TT], Lc[:, :TT], AF.Exp, [BL], [BE2], scale=-C0)
                tt('dve', kt[:, :TT], k_[:, :TT], E2[:, :TT], ALU.mult, [Bk, BE2], [Bkt])
                tt('pool', t1[:, :TT], kk[:, :TT], a_[:, :TT], ALU.mult, [Bkk, Ba], [Bt1])
                tt('dve', bt[:, :TT], t1[:, :TT], E2[:, :TT], ALU.mult, [Bt1, BE2], [Bbt])
                yield 's'
                tt('pool', Lc[:, :TT], Lc[:, :TT], sig[:, :TT], ALU.subtract, [BL, Bsig], [BL])
                act(E[:, :TT], Lc[:, :TT], AF.Exp, [BL], [BE], scale=C0)
                stt(at[:, :TT], kk[:, :TT], -1.0, E[:, :TT], ALU.mult, ALU.mult, [Bkk, BE], [Bat])
                yield 's'
                stt(pr[:, :TT], r_[:, :TT], vcol(l, V_RK, c), k_[:, :TT], ALU.mult, ALU.mult, [Br, Bk, Bvec], [Bpr])
                bk, Bbk = bank()
                mm(bk[:, :TT], blkb[:], pr[:, :TT], True, True, [Bconst, Bpr], [Bbk])
                bv, Bbv = bvt[pp], Bbvt[pp]
                tt('dve', bv[:, :TT], bk[:, :TT], v_[:, :TT], ALU.mult, [Bbk, Bv], [Bbv])
                cp('pool', vb[:, :TT], v_[:, :TT], [Bv], [Bvb])
                yield 's'
                checkpoint('pelem')
                for (src, Bsrc, dst, Bdst) in ((vb, Bvb, vT, BvT), (kt, Bkt, kT, BkT), (bt, Bbt, bT, BbT)):
                    pt, Bpt = ptbank()
                    for ci in range(NC_):
                        tr(pt[0:C, ci * 128:(ci + 1) * 128], src[:, ci * C:(ci + 1) * C], identb[:], [Bsrc, Bconst], [Bpt] if ci == NC_ - 1 else None)
                    cp('act', dst[0:C, 0:NC_, :], pt[0:C, 0:NC_ * 128].rearrange("p (a b) -> p a b", b=128), [Bpt], [Bdst])
                    yield 's'
                checkpoint('ptm')
                memset('pool', Sbd[:, c, :], 0.0, [BSbd[c]])
                for hd in range(2):
                    cp('pool', Sbd[hd * 64:(hd + 1) * 64, c, hd * 64:(hd + 1) * 64], S32[hd * 64:(hd + 1) * 64, l, c, :], [BS32[l][c]], [BSbd[c]])
                checkpoint('psbd')
                yield 'frontdone'
                hr = lambda hd: slice(hd * 64, (hd + 1) * 64)
                NL = C.bit_length() - 1
                for ci in range(NC_):
                    cs = slice(ci * C, (ci + 1) * C)
                    NQ, MT, AK, RR = NQt[ci], MTt[ci], AKt[ci], RRt[ci]
                    for hd in range(2):
                        bkX, BX = bank()
                        X3 = bkX[0:C, :].rearrange("p (a b) -> p a b", b=128)
                        mm(X3[:, 0, 0:C], bt[hr(hd), cs], at[hr(hd), cs], True, True, [Bbt, Bat])
                        mm(X3[:, 1, 0:C], kt[hr(hd), cs], at[hr(hd), cs], True, True, [Bkt, Bat])
                        mm(X3[:, 2, 0:C], bt[hr(hd), cs], rt[hr(hd), cs], True, True, [Bbt, Brt])
                        mm(X3[:, 3, 0:C], kt[hr(hd), cs], rt[hr(hd), cs], True, True, [Bkt, Brt], [BX])
                        bkY, BY = bank()
                        mm(bkY[0:C, 0:C], at[hr(hd), cs], bt[hr(hd), cs], True, True, [Bbt, Bat], [BY])
                        tt('dve', NQ[0:C, 0, hd, 0:C], X3[:, 0, 0:C], msu[0:C, 0:C], ALU.mult, [BX, Bconst], [BNQ[ci]])
                        tt('dve', AK[0:C, hd, 0:C], X3[:, 1, 0:C], msu[0:C, 0:C], ALU.mult, [BX, Bconst], [BAK[ci]])
                        tt('dve', RR[0:C, hd:4:2, 0:C], X3[:, 2:4, 0:C], miu[0:C, 0:C].unsqueeze(1).to_broadcast([C, 2, C]), ALU.mult, [BX, Bconst], [BRR[ci]])
                        tt('dve', NQ[0:C, 1, hd, 0:C], bkY[0:C, 0:C], msl[0:C, 0:C], ALU.mult, [BY, Bconst], [BNQ[ci]])
                    for f_ in range(2):
                        tt('dve', MT[0:C, f_, :, 0:C], NQ[0:C, f_, :, 0:C], m2d[0:C, 0, f_, 0:C].unsqueeze(1).to_broadcast([C, 2, C]), ALU.mult,
                           [BNQ[ci], Bconst], [BMT[ci]])
                    tt('dve', MT[0:C].rearrange("p f h q -> p (f h) q")[:, :, 0:C], MT[0:C].rearrange("p f h q -> p (f h) q")[:, :, 0:C],
                       identb[0:C, 0:C].unsqueeze(1).to_broadcast([C, 4, C]), ALU.add, [BMT[ci], Bconst], [BMT[ci]])
                    yield 's'
                checkpoint('patype')
                for j in range(1, NL):
                    last = j == NL - 1
                    nf = 1 if last else 2
                    xs_ = []
                    for ci in range(NC_):
                        NQ, MT = NQt[ci], MTt[ci]
                        bkX, BX = bank()
                        Xv = bkX[0:C, :].rearrange("p (f h q) -> p f h q", f=2, h=2)
                        for hd in range(2):
                            mm(Xv[:, 0, hd, 0:C], NQ[0:C, 1, hd, 0:C], MT[0:C, 0, hd, 0:C], True, True, [BNQ[ci], BMT[ci]], [BX] if hd == 1 else None)
                        xs_.append((Xv, BX))
                    ys_ = []
                    for ci in range(NC_):
                        MT, XB_ = MTt[ci], Xbt[ci]
                        Xv, BX = xs_[ci]
                        cp('act', XB_[0:C, 0, :, 0:C], Xv[:, 0, :, 0:C], [BX], [BXb[ci]])
                        bkY, BY = bank()
                        Yv = bkY[0:C, :].rearrange("p (f h q) -> p f h q", f=2, h=2)
                        for hd in range(2):
                            mm(Yv[:, 0, hd, 0:C], MT[0:C, 1, hd, 0:C], XB_[0:C, 0, hd, 0:C], True, True, [BMT[ci], BXb[ci]], [BY] if (last and hd == 1) else None)
                            if not last:
                                mm(Yv[:, 1, hd, 0:C], XB_[0:C, 0, hd, 0:C], MT[0:C, 1, hd, 0:C], True, True, [BMT[ci], BXb[ci]], [BY] if hd == 1 else None)
                        ys_.append((Yv, BY))
                    for ci in range(NC_):
                        MT = MTt[ci]
                        Yv, BY = ys_[ci]
                        cpred(MT[0:C, 0:nf, :, 0:C], m2d_u16[0:C, j, 0:nf, 0:C].unsqueeze(2).to_broadcast([C, nf, 2, C]), Yv[:, 0:nf, :, 0:C],
                              [BY, Bconst, BMT[ci]], [BMT[ci]])
                    yield 's'
                yield 'p2done'
                for ci in range(NC_):
                    par = ci % 2
                    cs = slice(ci * C, (ci + 1) * C)
                    AK, RR, MT = AKt[ci], RRt[ci], MTt[ci]
                    checkpoint('pdbl')
                    Rw, BRw = Swt[par], BSw[par]
                    for hd in range(2):
                        tt('dve', Rw[hr(hd), :], S32[hr(hd), l, c, :], Sbd[hr(hd), c, hr(hd)], ALU.subtract, [BS32[l][c], BSbd[c]], [BRw])
                    ts('dve', Rw[:, :], Rw[:, :], wc_t[:, ci:ci + 1], None, ALU.mult, None, [BRw, Bwc], [BRw])
                    bkW, BW = bank()
                    mm(bkW[0:C, 0:128], at[:, cs], Sbd[:, c, :], True, False, [Bat, BSbd[c]])
                    for hd in range(2):
                        mm(bkW[0:C, hr(hd)], AK[0:C, hd, 0:C], vT[0:C, ci, hr(hd)], False, hd == 1, [BAK[ci], BvT], [BW] if hd == 1 else None)
                    cp('act', Zb[par][0:C, :], bkW[0:C, 0:128], [BW], [BZb[par]])
                    yield 's'
                    bkU, BU = bank()
                    for hd in range(2):
                        mm(bkU[0:C, hr(hd)], MT[0:C, 0, hd, 0:C], Zb[par][0:C, hr(hd)], True, True, [BMT[ci], BZb[par]], [BU] if hd == 1 else None)
                    cp('act', Ub[par][0:C, :], bkU[0:C, 0:128], [BU], [BUb[par]])
                    yield 's'
                    bkS, BS = bank()
                    mm(bkS[:, 0:128], bT[0:C, ci, :], Ub[par][0:C, :], True, False, [BbT, BUb[par]])
                    mm(bkS[:, 0:128], kT[0:C, ci, :], vT[0:C, ci, :], False, False, [BkT, BvT])
                    mm(bkS[:, 0:128], identb[:, :], Sbd[:, c, :], False, True, [Bconst, BSbd[c]], [BS])
                    bkY, BY = bank()
                    mm(bkY[0:C, 0:128], rt[:, cs], Sbd[:, c, :], True, False, [Brt, BSbd[c]])
                    for hd in range(2):
                        mm(bkY[0:C, hr(hd)], RR[0:C, hd, 0:C], Ub[par][0:C, hr(hd)], False, False, [BRR[ci], BUb[par]])
                        mm(bkY[0:C, hr(hd)], RR[0:C, 2 + hd, 0:C], vT[0:C, ci, hr(hd)], False, hd == 1, [BRR[ci], BvT], [BY] if hd == 1 else None)
                    if ci < NC_ - 1:
                        for hd in range(2):
                            act(Sbd[hr(hd), c, hr(hd)], bkS[hr(hd), hr(hd)], AF.Identity, [BS, Bwc], [BSbd[c]], scale=wc_t[hr(hd), ci:ci + 1])
                    for hd in range(2):
                        stt(S32[hr(hd), l, c, :], bkS[hr(hd), hr(hd)], wc_t[hr(hd), ci:ci + 1], Rw[hr(hd), :], ALU.mult, ALU.add,
                            [BS, Bwc, BRw, BSbd[c]], [BS32[l][c]])
                    cp('act', yTM[0:C, ci, :], bkY[0:C, 0:128], [BY], [ByTM])
                    yield 's'
                checkpoint('pseq')
                yield 'p3done'
                NG = NC_ * 2
                y3 = yTM[0:C, 0:NC_, :].rearrange("p a (h v) -> p (a h) v", v=64)
                rec.op('dve', lambda e: e.tensor_reduce(out=gst[0:C, 0, 0:NG], in_=y3, axis=AX.X, op=ALU.add), reads=[ByTM], writes=[Bgst])
                tt('pool', ysq[0:C, 0:NC_, :], yTM[0:C, 0:NC_, :], yTM[0:C, 0:NC_, :], ALU.mult, [ByTM], [Bysq])
                s3 = ysq[0:C, 0:NC_, :].rearrange("p a (h v) -> p (a h) v", v=64)
                rec.op('dve', lambda e: e.tensor_reduce(out=gst[0:C, 1, 0:NG], in_=s3, axis=AX.X, op=ALU.add), reads=[Bysq], writes=[Bgst])
                ts('dve', gst[0:C, 2, 0:NG], gst[0:C, 0, 0:NG], 1.0 / 64, None, ALU.mult, None, [Bgst], [Bgst])
                tt('dve', gst[0:C, 3, 0:NG], gst[0:C, 2, 0:NG], gst[0:C, 2, 0:NG], ALU.mult, [Bgst], [Bgst])
                stt(gst[0:C, 3, 0:NG], gst[0:C, 1, 0:NG], 1.0 / 64, gst[0:C, 3, 0:NG], ALU.mult, ALU.subtract, [Bgst], [Bgst])
                rsqrt(gst[0:C, 3, 0:NG], gst[0:C, 3, 0:NG], [Bgst, Bconst], [Bgst], bias=epsc[0:C, 2:3])
                yield 's'
                tt('dve', ysq[0:C, 0:NC_, :].rearrange("p a (h v) -> p (a h) v", v=64), y3,
                   gst[0:C, 2, 0:NG].unsqueeze(2).to_broadcast([C, NG, 64]), ALU.subtract, [ByTM, Bgst], [Bysq])
                tt('dve', ynb[0:C, 0:NC_, :].rearrange("p a (h v) -> p (a h) v", v=64), s3,
                   gst[0:C, 3, 0:NG].unsqueeze(2).to_broadcast([C, NG, 64]), ALU.mult, [Bysq, Bgst], [Bynb])
                pt, Bpt = ptbank()
                for ci in range(NC_):
                    tr(pt[:, ci * C:(ci + 1) * C], ynb[0:C, ci, :], identb[0:C, 0:C], [Bynb, Bconst], [Bpt] if ci == NC_ - 1 else None)
                y1, By1 = y1t, By1t
                act(y1[:, :TT], pt[:, :TT], AF.Identity, [Bpt, Bvec], [By1], scale=vcol(l, V_LXG, c), bias=vcol(l, V_LXB, c))
                tt('pool', y1[:, :TT], y1[:, :TT], bv[:, :TT], ALU.add, [By1, Bbv], [By1])
                tt('dve', bfp[:, c, :TT], y1[:, :TT], g_[:, :TT], ALU.mult, [By1, Bg], [Bbfp[c]])

            def tile_layer(l, grp, TT, C, NSTEP):
                gcur[0] = 'sp'[1 - grp] if False else ('p' if grp == 0 else 's')
                NTB_ = max(1, TT // 128)
                TB = min(TT, 128)
                dma('pool', 'W', smallw[:], smsc[l], reads=[Bwsc[l]] + Bsm[l], writes=[Bsmall])
                dma('pool', 'W', bsb[:], b_s[l].unsqueeze(0).partition_broadcast(128).rearrange("p o n -> p (o n)"), writes=[Bbsb])
                rmsnorm_to_h(l, V_MIXG, TT)
                wt, Bw = wpiece(l, PI_LORA)
                for j in range(2):
                    col = 24 + j
                    bk, Bbk = big_mm(wt, Bw, j, hrhs(TT), Bh, TT)
                    cb, Bcb = tmpf()
                    cp('act', cb[:, 1:TT + 1], bk[:, :TT], [Bbk], [Bcb])
                    cp('pool', cb[:, 0:1], prev[:, l, col:col + 1], [Bprev[l]], [Bcb])
                    t1, Bt1 = tmpf()
                    ts('dve', t1[:, :TT], cb[:, 0:TT], vcol(l, V_MU, col), None, ALU.mult, None, [Bcb, Bvec], [Bt1])
                    mx, Bmx = tmpf()
                    stt(mx[:, :TT], cb[:, 1:TT + 1], vcol(l, V_OMM, col), t1[:, :TT], ALU.mult, ALU.add, [Bcb, Bt1, Bvec], [Bmx])
                    cp('pool', prev[:, l, col:col + 1], cb[:, TT:TT + 1], [Bcb], [Bprev[l]])
                    if j == 0:
                        act(lw[0:64, :TT], mx[0:64, :TT], AF.Tanh, [Bmx], [Blw])
                        cp('act', lw[64:128, :TT], mx[64:128, :TT], [Bmx], [Blw])
                    else:
                        act(sgl[:, :TT], mx[:, :TT], AF.Sigmoid, [Bmx], [Bsgl])
                checkpoint('lora')
                gens, stt_ = {}, {}
                n_started = 0
                while n_started < 8 or any(v != 'done' for v in stt_.values()):
                    if n_started < 8:
                        i_ = n_started
                        ok = (i_ == 0 or stt_[i_ - 1] != 'front') and (i_ < 2 or stt_[i_ - 2] == 'done')
                        if ok:
                            gens[i_] = rwkv_gen(l, i_, TT, C); stt_[i_] = 'front'; n_started += 1
                    in_p3 = any(v == 'chunk3' for v in stt_.values())
                    in_p12 = any(v == 'chunk12' for v in stt_.values())
                    progressed = False
                    for i_ in sorted(gens):
                        st_ = stt_[i_]
                        if st_ == 'done':
                            continue
                        if st_ == 'wait':
                            if i_ == 0 or stt_[i_ - 1] in ('final', 'done'):
                                stt_[i_] = st_ = 'chunk12'
                            else:
                                continue
                        reps = 1
                        for _r in range(reps):
                            try:
                                tok = next(gens[i_])
                            except StopIteration:
                                stt_[i_] = 'done'; progressed = True
                                break
                            progressed = True
                            if tok == 'frontdone':
                                stt_[i_] = 'wait'; break
                            elif tok == 'p2done':
                                stt_[i_] = 'chunk3'; break
                            elif tok == 'p3done':
                                stt_[i_] = 'final'; break
                    assert progressed or n_started < 8
                checkpoint('rwkv')
                for i in range(4):
                    wt, Bw = wpiece(l, PI_SGU + i)
                    for o4 in range(4):
                        oc = i * 4 + o4
                        bk, Bbk = big_mm(wt, Bw, o4, hrhs(TT), Bh, TT)
                        if oc < 8:
                            act(ub[:, oc, :TT], bk[:, :TT], AF.Gelu_apprx_tanh, [Bbk], [Bub[oc]])
                        else:
                            act(big[:, oc - 8, :TT], bk[:, :TT], AF.Gelu_apprx_tanh, [Bbk], [Bbig[oc - 8]])
                bk1, Bb1 = bank(); bk2, Bb2 = bank()
                for c in range(8):
                    vb, Bvb = tmpb(); vs, Bvs = tmpb()
                    cp('pool', vb[:, :TT], big[:, c, :TT], [Bbig[c]], [Bvb])
                    act(vs[:, :TT], big[:, c, :TT], AF.Square, [Bbig[c]], [Bvs])
                    mm(bk1[:, :TT], onesb[:], vb[:, :TT], c == 0, c == 7, [Bvb, Bconst], [Bb1])
                    mm(bk2[:, :TT], onesb[:], vs[:, :TT], c == 0, c == 7, [Bvs, Bconst], [Bb2])
                mean, Bmean = tmpf(); rstd, Brstd = tmpf()
                act(mean[:, :TT], bk1[:, :TT], AF.Copy, [Bb1], [Bmean], scale=1.0 / D)
                tt('pool', rstd[:, :TT], mean[:, :TT], mean[:, :TT], ALU.mult, [Bmean], [Brstd])
                stt(rstd[:, :TT], bk2[:, :TT], 1.0 / D, rstd[:, :TT], ALU.mult, ALU.subtract, [Bb2, Brstd], [Brstd])
                rsqrt(rstd[:, :TT], rstd[:, :TT], [Brstd, Bconst], [Brstd], bias=epsc[:, 1:2])
                vnb = []
                for c in range(8):
                    vc = big[:, c, :TT]
                    tt('dve', vc, vc, mean[:, :TT], ALU.subtract, [Bbig[c], Bmean], [Bbig[c]])
                    tt('pool', vc, vc, rstd[:, :TT], ALU.mult, [Bbig[c], Brstd], [Bbig[c]])
                    ts('dve', vc, vc, vcol(l, V_SLG, c), vcol(l, V_SLB, c), ALU.mult, ALU.add, [Bbig[c], Bvec], [Bbig[c]])
                    cp('pool', bfp[:, 16 + c, :TT], vc, [Bbig[c]], [Bbfp[16 + c]])
                    vnb.append((bfp[:, 16 + c, :], Bbfp[16 + c]))
                if grp == 1:
                    for half in range(2):
                        bk, Bbk = bank()
                        for j in range(4):
                            tr(bk[0:TT, j * 128:(j + 1) * 128], big[:, half * 4 + j, :TT], identf[:], [Bbig[half * 4 + j], Bconst], [Bbk] if j == 3 else None)
                        stg, Bstg = tmpf()
                        cp('act', stg[0:TT, 0:512], bk[0:TT, :], [Bbk], [Bstg])
                        dma('pool', 'R', osgu[l][:, half * 512:(half + 1) * 512], stg[0:TT, 0:512], reads=[Bstg])
                vtm = []
                for tb_ in range(NTB_):
                    halves = []
                    for half in range(2):
                        pt, Bpt = ptbank()
                        for j in range(4):
                            cc = half * 4 + j
                            tr(pt[0:TB, j * 128:(j + 1) * 128], vnb[cc][0][:, tb_ * 128: tb_ * 128 + TB], identb[:], [vnb[cc][1], Bconst], [Bpt] if j == 3 else None)
                        vt_, Bvt_ = bfp[:, 24 + tb_ * 2 + half, :], Bbfp[24 + tb_ * 2 + half]
                        cp('act', vt_[0:TB, :], pt[0:TB, :], [Bpt], [Bvt_])
                        halves.append((vt_, Bvt_))
                    vtm.append(halves)
                wsT = smallw[:, 0:1024]
                for c in range(8):
                    bk, Bbk = bank()
                    for tb_ in range(NTB_):
                        vt_, Bvt_ = vtm[tb_][c // 4]
                        mm(bk[:, tb_ * 128: tb_ * 128 + TB], vt_[0:TB, (c % 4) * 128:(c % 4 + 1) * 128], wsT[0:TB, c * 128: c * 128 + TB], True, True,
                           [Bvt_, Bsmall], [Bbk] if tb_ == NTB_ - 1 else None)
                    sv, Bsv = tmpf()
                    if NTB_ > 1:
                        tt('dve', sv[:, :TT].rearrange("p (a b) -> p a b", b=128), bk[:, :TT].rearrange("p (a b) -> p a b", b=128),
                           bsb[:, c * 128:(c + 1) * 128].unsqueeze(1).to_broadcast([128, NTB_, 128]), ALU.add, [Bbk, Bbsb], [Bsv])
                    else:
                        tt('dve', sv[:, :TT], bk[:, :TT], bsb[:, c * 128: c * 128 + TT], ALU.add, [Bbk, Bbsb], [Bsv])
                    tt('pool', bfp[:, 8 + c, :TT], sv[:, :TT], ub[:, c, :TT], ALU.mult, [Bsv, Bub[c]], [Bbfp[8 + c]])
                checkpoint('sgu')
                for i in range(2):
                    wt, Bw = wpiece(l, PI_Q + i)
                    for o4 in range(4):
                        oc = i * 4 + o4
                        bk, Bbk = big_mm(wt, Bw, o4, hrhs(TT), Bh, TT)
                        cp('act', bfp[:, 24 + oc, :TT], bk[:, :TT], [Bbk], [Bbfp[24 + oc]])
                kvt, Bkvp = piece('kv', l, grp)
                KT = kvt[:, 0:2048].rearrange("p (j m) -> p j m", j=8)
                VV = kvt[:, 2048:4096].rearrange("p (mb n) -> p mb n", mb=2)
                for hh in range(4):
                    pTs = []
                    for mc in range(2):
                        bk, Bbk = bank()
                        for dc in range(2):
                            mm(bk[:, :TT], KT[:, 2 * hh + dc, mc * 128:(mc + 1) * 128], bfp[:, 24 + 2 * hh + dc, :TT], dc == 0, dc == 1,
                               [Bkvp, Bbfp[24 + 2 * hh + dc]], [Bbk] if dc == 1 else None)
                        p_, Bp_ = tmpb()
                        act(p_[:, :TT], bk[:, :TT], AF.Exp, [Bbk], [Bp_], scale=1.0 / 16)
                        pTs.append((p_, Bp_))
                    bk, Bbk = bank()
                    for mc in range(2):
                        mm(bk[:, :TT], onesb[:], pTs[mc][0][:, :TT], mc == 0, mc == 1, [pTs[mc][1], Bconst], [Bbk] if mc == 1 else None)
                    rd, Brd = tmpf()
                    rsqrt(rd[:, :TT], bk[:, :TT], [Bbk], [Brd], power=-1.0)
                    for dvc in range(2):
                        bk, Bbk = bank()
                        for mc in range(2):
                            mm(bk[:, :TT], VV[:, mc, hh * 256 + dvc * 128: hh * 256 + (dvc + 1) * 128], pTs[mc][0][:, :TT], mc == 0, mc == 1,
                               [Bkvp, pTs[mc][1]], [Bbk] if mc == 1 else None)
                        tt('dve', bfp[:, 16 + 2 * hh + dvc, :TT], bk[:, :TT], rd[:, :TT], ALU.mult, [Bbk, Brd], [Bbfp[16 + 2 * hh + dvc]])
                if DBG and grp == 0 and l == 0:
                    dma('pool', 'R', dbg_y, bfp[:, 0:24, :], reads=[Bbfp[i] for i in range(24)])
                checkpoint('attn')
                for b in range(3):
                    for hf in range(2):
                        wg, Bwg = wpiece(l, PI_GATE + b * 2 + hf)
                        wb_, Bwb = wpiece(l, PI_BR + b * 2 + hf)
                        for o4 in range(4):
                            oc = hf * 4 + o4
                            bk, Bbk = big_mm(wg, Bwg, o4, hrhs(TT), Bh, TT)
                            gt, Bgt = tmpb()
                            act(gt[:, :TT], bk[:, :TT], AF.Sigmoid, [Bbk], [Bgt])
                            rh = [bfp[:, b * 8 + kc, :TT] for kc in range(8)]
                            bk, Bbk = big_mm(wb_, Bwb, o4, rh, [Bbfp[b * 8 + kc] for kc in range(8)], TT)
                            if b == 0:
                                tt('dve', big[:, oc, :TT], bk[:, :TT], gt[:, :TT], ALU.mult, [Bbk, Bgt], [Bbig[oc]])
                            else:
                                tm_, Btm = tmpf()
                                tt('dve', tm_[:, :TT], bk[:, :TT], gt[:, :TT], ALU.mult, [Bbk, Bgt], [Btm])
                                if b == 1:
                                    tt('pool', big[:, oc, :TT], big[:, oc, :TT], tm_[:, :TT], ALU.add, [Bbig[oc], Btm], [Bbig[oc]])
                                else:
                                    tt('pool', bfp[:, 24 + oc, :TT], big[:, oc, :TT], tm_[:, :TT], ALU.add, [Bbig[oc], Btm], [Bbfp[24 + oc]])
                if DBG and grp == 0 and l == 0:
                    dma('pool', 'R', dbg_m, bfp[:, 24:32, :], reads=[Bbfp[24 + i] for i in range(8)])
                for hf in range(2):
                    wt, Bw = wpiece(l, PI_OUT + hf)
                    for o4 in range(4):
                        oc = hf * 4 + o4
                        rh = [bfp[:, 24 + kc, :TT] for kc in range(8)]
                        bk, Bbk = big_mm(wt, Bw, o4, rh, [Bbfp[24 + kc] for kc in range(8)], TT)
                        tt('dve', x_t[:, oc, :TT], x_t[:, oc, :TT], bk[:, :TT], ALU.add, [Bx[oc], Bbk], [Bx[oc]])
                if DBG and grp == 0 and l == 0:
                    dma('pool', 'R', dbg_x, x_t[:], reads=Bx)
                checkpoint('merge')
                rmsnorm_to_h(l, V_FFNG, TT)
                for i in range(8):
                    wt, Bw = wpiece(l, PI_UP + i)
                    for o4 in range(4):
                        oc = i * 4 + o4
                        bk, Bbk = big_mm(wt, Bw, o4, hrhs(TT), Bh, TT)
                        rl, Brl = tmpf()
                        act(rl[:, :TT], bk[:, :TT], AF.Relu, [Bbk], [Brl])
                        tt('pool', bfp[:, oc, :TT], rl[:, :TT], rl[:, :TT], ALU.mult, [Brl], [Bbfp[oc]])
                for g in range(2):
                    banks = [bank() for _ in range(4)]
                    for q in range(4):
                        wt, Bw = wpiece(l, PI_DN + g * 4 + q)
                        for o4 in range(4):
                            rh = [bfp[:, q * 8 + kc, :TT] for kc in range(8)]
                            big_mm(wt, Bw, o4, rh, [Bbfp[q * 8 + kc] for kc in range(8)], TT, out_bk=banks[o4], start=(q == 0), stop=(q == 3))
                    for o4 in range(4):
                        oc = g * 4 + o4
                        tt('dve', x_t[:, oc, :TT], x_t[:, oc, :TT], banks[o4][0][:, :TT], ALU.add, [Bx[oc], banks[o4][1]], [Bx[oc]])

            def dbg_x2_dump():
                if DBG:
                    dma('pool', 'R', dbg_x2, x_t[:], reads=Bx)

            def load_x(src, TT):
                NTB_ = max(1, TT // 128); TB = min(TT, 128)
                for tb_ in range(NTB_):
                    dma('pool', 'W', big[0:TB, 2 * tb_:2 * tb_ + 2, :].rearrange("p a n -> p (a n)"), src[tb_ * 128: tb_ * 128 + TB, :],
                        writes=[Bbig[2 * tb_], Bbig[2 * tb_ + 1]])
                for c in range(8):
                    bk, Bbk = bank()
                    for tb_ in range(NTB_):
                        tr(bk[:, tb_ * 128: tb_ * 128 + TB], big[0:TB, 2 * tb_ + c // 4, (c % 4) * 128:(c % 4 + 1) * 128], identf[0:TB, 0:TB],
                           [Bbig[2 * tb_ + c // 4], Bconst], [Bbk] if tb_ == NTB_ - 1 else None)
                    cp('act', x_t[:, c, :TT], bk[:, :TT], [Bbk], [Bx[c]])

            def store_y(dst, TT):
                NTB_ = max(1, TT // 128); TB = min(TT, 128)
                bk, Bbk = bank()
                for c in range(8):
                    sq, Bsq = tmpb()
                    act(sq[:, :TT], x_t[:, c, :TT], AF.Square, [Bx[c]], [Bsq])
                    mm(bk[:, :TT], onesb[:], sq[:, :TT], c == 0, c == 7, [Bsq, Bconst], [Bbk] if c == 7 else None)
                rs, Brs = tmpf()
                rsqrt(rs[:, :TT], bk[:, :TT], [Bbk, Bconst], [Brs], scale=1.0 / D, bias=epsc[:, 0:1])
                for c in range(8):
                    stt(big[:, c, :TT], x_t[:, c, :TT], fing_t[:, c:c + 1], rs[:, :TT], ALU.mult, ALU.mult, [Bx[c], Brs, Bfing], [Bbig[c]])
                for tb_ in range(NTB_):
                    for half in range(2):
                        bk, Bbk = bank()
                        for j in range(4):
                            cc = half * 4 + j
                            tr(bk[0:TB, j * 128:(j + 1) * 128], big[:, cc, tb_ * 128: tb_ * 128 + TB], identf[:], [Bbig[cc], Bconst], [Bbk] if j == 3 else None)
                        stg, Bstg = tmpf()
                        cp('act', stg[0:TB, 0:512], bk[0:TB, :], [Bbk], [Bstg])
                        dma('pool', 'R', dst[tb_ * 128: tb_ * 128 + TB, half * 512:(half + 1) * 512], stg[0:TB, 0:512], reads=[Bstg])

            def store_state(l, owkv, oshift):
                for c in range(8):
                    bk, Bbk = bank()
                    tr(bk[0:64, 0:128], S32[:, l, c, :], identf[:], [BS32[l][c], Bconst], [Bbk])
                    stg, Bstg = tmpf()
                    cp('act', stg[0:64, 0:128], bk[0:64, 0:128], [Bbk], [Bstg])
                    dma('pool', 'R', owkv[l, 2 * c:2 * c + 2].rearrange("h v k -> v h k"), stg[0:64, 0:128].rearrange("p (h k) -> p h k", h=2), reads=[Bstg])
                bk, Bbk = bank()
                tr(bk[0:26, 0:128], prev[:, l, :], identf[:], [Bprev[l], Bconst], [Bbk])
                stg, Bstg = tmpf()
                cp('act', stg[0:26, 0:128], bk[0:26, 0:128], [Bbk], [Bstg])
                dma('pool', 'R', oshift[l].rearrange("(j p) -> j p", p=128), stg[0:26, 0:128], reads=[Bstg])

            memset('pool', cmask[:], 1.0, [Bconst])
            memset('pool', cmask[:].rearrange("p (a b) -> p a b", b=128)[:, :, 0:1], 0.0, [Bconst])

            Bm_ = [Bbig[i] for i in range(4)]
            dma('pool', 'W', big[:, 0:4, :].rearrange("p (mb a) n -> p mb (a n)", mb=2), memp.rearrange("(mb p) n -> p mb n", p=128), writes=Bm_)
            memf = big[:, 4:8, :].rearrange("p a (b n) -> p (a b) n", b=2)
            Bmemf = [Bbig[i] for i in range(4, 8)]
            for mb in range(2):
                ssq, Bssq = tmpf()
                for a in range(2):
                    jk, Bjk = tmpf()
                    rec.op('act', lambda e, a=a, mb=mb, jk=jk, ssq=ssq: e.activation(out=jk[:, 0:512], in_=big[:, mb * 2 + a, :], func=AF.Square, accum_out=ssq[:, a:a + 1]),
                           reads=[Bbig[mb * 2 + a]], writes=[Bjk, Bssq])
                tt('dve', ssq[:, 2:3], ssq[:, 0:1], ssq[:, 1:2], ALU.add, [Bssq], [Bssq])
                act(ssq[:, 3:4], ssq[:, 2:3], AF.Sqrt, [Bssq, Bconst], [Bssq], scale=1.0 / D, bias=epsc[:, 0:1])
                recip(ssq[:, 4:5], ssq[:, 3:4], [Bssq], [Bssq])
                for a in range(2):
                    ts('dve', big[:, mb * 2 + a, :], big[:, mb * 2 + a, :], ssq[:, 4:5], None, ALU.mult, None, [Bbig[mb * 2 + a], Bssq], [Bbig[mb * 2 + a]])
            for c in range(8):
                bk, Bbk = bank()
                for mb in range(2):
                    tr(bk[:, mb * 128:(mb + 1) * 128], big[:, mb * 2 + c // 4, (c % 4) * 128:(c % 4 + 1) * 128], identf[:], Bm_ + [Bconst], [Bbk] if mb == 1 else None)
                cp('act', memf[:, c, :], bk[:, 0:256], [Bbk], Bmemf)
            for l in range(L):
                mnt = [(bfp[:, 8 + i, :], Bbfp[8 + i]) for i in range(4)]
                for c in range(8):
                    t_, B_ = mnt[c // 2]
                    ts('dve', t_[:, (c % 2) * 256:(c % 2 + 1) * 256], memf[:, c, :], vcol(l, V_MEMG, c), None, ALU.mult, None, Bmemf + [Bvec], [B_])
                mrhs = lambda c: mnt[c // 2][0][:, (c % 2) * 256:(c % 2 + 1) * 256]
                kvst = [(bfp[:, i, :], Bbfp[i]) for i in range(8)]
                for i in range(4):
                    wt, Bw = wpiece(l, PI_MKV + i)
                    for mb in range(2):
                        bk, Bbk = bank()
                        for kc in range(8):
                            mm(bk[:, :], mrhs(kc)[:, mb * 128:(mb + 1) * 128], wt[:, kc, :], kc == 0, kc == 7, [mnt[kc // 2][1], Bw], [Bbk] if kc == 7 else None)
                        stg, Bstg = tmpf()
                        cp('act', stg[:, 0:512], bk[:, :], [Bbk], [Bstg])
                        dst = (omk if i < 2 else omv)[l][mb * 128:(mb + 1) * 128, (i % 2) * 512:(i % 2 + 1) * 512]
                        dma('pool', 'R', dst, stg[:, 0:512], reads=[Bstg])
                        if i >= 2:
                            s_, Bs_ = kvst[4 + mb * 2 + (i - 2)]
                            cp('pool', s_, stg[:, 0:512], [Bstg], [Bs_])
                    if i < 2:
                        for o4 in range(4):
                            j = i * 4 + o4
                            bk, Bbk = bank()
                            for kc in range(8):
                                mm(bk[:, 0:256], wt[:, kc, o4 * 128:(o4 + 1) * 128], mrhs(kc), kc == 0, kc == 7, [mnt[kc // 2][1], Bw], [Bbk] if kc == 7 else None)
                            s_, Bs_ = kvst[j // 2]
                            cp('act', s_[:, (j % 2) * 256:(j % 2 + 1) * 256], bk[:, 0:256], [Bbk], [Bs_])
                for i in range(8):
                    Bd = Buf(); Bkv[l][0].append(Bd)
                    dma('pool', 'R', kvsc[l, 0][:, i * 512:(i + 1) * 512], kvst[i][0], reads=[kvst[i][1]], writes=[Bd])

            checkpoint('memkv')
            for l in range(L):
                for c in range(8):
                    stg, Bstg = tmpf()
                    dma('pool', 'W', stg[0:64, 0:128].rearrange("p (h k) -> p h k", h=2), swkv[l, 2 * c:2 * c + 2].rearrange("h v k -> v h k"), writes=[Bstg])
                    bk, Bbk = bank()
                    tr(bk[:, 0:64], stg[0:64, 0:128], identf[0:64, 0:64], [Bstg, Bconst], [Bbk])
                    cp('act', S32[:, l, c, :], bk[:, 0:64], [Bbk], [BS32[l][c]])
                stg, Bstg = tmpf()
                dma('pool', 'W', stg[0:26, 0:128], sshift[l].rearrange("(j p) -> j p", p=128), writes=[Bstg])
                bk, Bbk = bank()
                tr(bk[:, 0:26], stg[0:26, 0:128], identf[0:26, 0:26], [Bstg, Bconst], [Bbk])
                cp('act', prev[:, l, :], bk[:, 0:26], [Bbk], [Bprev[l]])
            load_x(xs, TS)
            for l in range(L):
                tile_layer(l, 1, TS, TS, 5)
            store_y(ys, TS)
            for l in range(L):
                store_state(l, owkv_s, oshift_s)

            checkpoint('sample')
            for l in range(L):
                for c in range(8):
                    memset('pool', S32[:, l, c, :], 0.0, [BS32[l][c]])
                memset('pool', prev[:, l, :], 0.0, [Bprev[l]])
            for t_ in range(NT):
                load_x(xp[t_ * T:(t_ + 1) * T], T)
                for l in range(L):
                    tile_layer(l, 0, T, 128, 7)
                    if l == 0 and t_ == 0:
                        dbg_x2_dump()
                store_y(yp[t_ * T:(t_ + 1) * T], T)
            for l in range(L):
                store_state(l, owkv_p, oshift_p)

        except _Stop:
            pass
        finish()
    return nc


def _vec_table(inp, L):
    def fm(a, n):
        return np.ascontiguousarray(a.reshape(L, n, 128).transpose(2, 0, 1))
    mu = inp['rwkv_mu'][:L]
    ka = inp['rwkv_k_a'][:L]
    parts = [fm(inp['norm_mix_g'][:L], 8), fm(inp['norm_ffn_g'][:L], 8), fm(inp['norm_mem_g'][:L], 8),
             fm(mu, 26), np.zeros((128, L, 26), np.float32), fm(inp['rwkv_w0'][:L], 8), fm(inp['rwkv_a0'][:L], 8),
             fm(inp['rwkv_k_k'][:L], 8), fm(ka, 8), np.zeros((128, L, 8), np.float32), fm(inp['rwkv_r_k'][:L].reshape(L, 1024), 8),
             fm(inp['sgu_ln_g'][:L], 8), fm(inp['sgu_ln_b'][:L], 8), fm(inp['rwkv_lnx_g'][:L], 8), fm(inp['rwkv_lnx_b'][:L], 8)]
    return np.ascontiguousarray(np.concatenate(parts, axis=2).astype(np.float32))


def run(inp, SEQ, L, n_cores=8, trace=False, DBG=False, STOP=None):
    nc = build(SEQ, L, DBG=DBG, STOP=STOP)
    f = lambda a: np.ascontiguousarray(a, dtype=np.float32)
    vecs = _vec_table(inp, L)
    fing = np.ascontiguousarray(inp['norm_final_g'].reshape(8, 128).T.astype(np.float32))
    shared = {
        'vecs': vecs, 'fing': fing, 'w_in': f(inp['w_in'][:L]), 'w_mkv': f(inp['w_mem_kv'][:L]),
        'w2': f(inp['rwkv_w2'][:L]), 'a2': f(inp['rwkv_a2'][:L]), 'g2': f(inp['rwkv_g2'][:L]),
        'w_s': f(inp['sgu_w_s'][:L]), 'b_s': f(inp['sgu_b_s'][:L].reshape(L, 1024)),
        'w_br': f(inp['w_branch'][:L]), 'w_out': f(inp['w_out'][:L]), 'w_up': f(inp['w_ffn_up'][:L]), 'w_dn': f(inp['w_ffn_down'][:L]),
    }
    in_maps = []
    for b in range(n_cores):
        m = dict(shared)
        m['xp'] = f(inp['x_prompt'][b, :SEQ]); m['xs'] = f(inp['x_sample'][b])
        m['ck'] = f(inp['cache_mem_k'][:L, b].reshape(L, 256, 1024)); m['cv'] = f(inp['cache_mem_v'][:L, b].reshape(L, 256, 1024))
        m['swkv'] = f(inp['state_wkv'][:L, b]); m['sshift'] = f(inp['state_shift'][:L, b, 0])
        m['memp'] = f(inp['mem_prompt'][b])
        in_maps.append(m)
    res = run_bass_kernel_spmd(nc, in_maps, core_ids=list(range(n_cores)), trace=trace)
    R = res.results
    st = lambda k: np.stack([np.asarray(r[k]) for r in R])
    y_p = st('yp'); y_s = st('ys')
    mk = st('omk').transpose(1, 0, 2, 3).reshape(L, n_cores, 256, 4, 256)
    mv = st('omv').transpose(1, 0, 2, 3).reshape(L, n_cores, 256, 4, 256)
    wkv_p = st('owkv_p').transpose(1, 0, 2, 3, 4)
    sh_p = st('oshift_p').transpose(1, 0, 2)[:, :, None, :]
    wkv_s = st('owkv_s').transpose(1, 0, 2, 3, 4)
    sh_s = st('oshift_s').transpose(1, 0, 2)[:, :, None, :]
    sgu = st('osgu').transpose(1, 0, 2, 3)
    outs = (y_p, y_s, mk, mv, wkv_p, sh_p, wkv_s, sh_s, sgu)
    return tuple(np.ascontiguousarray(o.astype(np.float32)) for o in outs), res


def kernel(**inputs):
    inp = {k: np.asarray(v) for k, v in inputs.items()}
    outs, _ = run(inp, 8192, 4, 8)
    return outs
```
